# Optimizing a Trainium2 kernel written in Bass

```python
import functools
import jax
import jax.numpy as jnp
from jax import lax
import numpy as np

D_MODEL = 2048
BATCH = 4
SEQ = 2048
DEPTH = 2
DEC_BATCH = 8
DEC_SEQ = 1
PAST_LEN = 16384
PAGE_SIZE = 128

N_EVEN = (DEPTH + 1) // 2
N_ODD = DEPTH // 2
HALF = D_MODEL // 2

MLSTM_HEADS = 4
MLSTM_DH = HALF // MLSTM_HEADS
MLSTM_CHUNK = 64

ATT_HEADS = 8
ATT_DH = HALF // ATT_HEADS
ATT_KV_HEADS = 2
IDX_HEADS = 8
IDX_DIM = 64
TOPK_MAX = 256
Q_BLOCK = 128

HGRN_HEADS = 8
HGRN_DK = 128
HGRN_DV = HALF // HGRN_HEADS
HGRN_CHUNK = 64

RET_HEADS = 4
RET_DK = HALF // RET_HEADS
RET_DV = HALF // RET_HEADS
RET_CHUNK = 64
RET_THETA = 10000.0

D_FF = 5632
P_DIM = 256
ALPHA = (2 * DEPTH) ** 0.25
BETA = (8 * DEPTH) ** -0.25
LN_EPS = 1e-5

EVEN_SPLITS = (HALF, HALF, HALF, HALF, MLSTM_HEADS, MLSTM_HEADS,
               ATT_HEADS * ATT_DH, ATT_KV_HEADS * ATT_DH, ATT_KV_HEADS * ATT_DH,
               IDX_HEADS * IDX_DIM, IDX_DIM, IDX_HEADS)
EVEN_WIDTH = sum(EVEN_SPLITS)
ODD_SPLITS = (HGRN_HEADS * HGRN_DK, HGRN_HEADS * HGRN_DK, HGRN_HEADS * HGRN_DV, HGRN_HEADS * HGRN_DV,
              RET_HEADS * RET_DK, RET_HEADS * RET_DK, RET_HEADS * RET_DV, RET_HEADS * RET_DV)
ODD_WIDTH = sum(ODD_SPLITS)

kernel_name = 'hybrid_mlstm_dsa_hgrn2_retention_step'


def split_cols(z, sizes):
    return jnp.split(z, np.cumsum(sizes)[:-1].tolist(), axis=-1)


def heads_first(a, n_heads):
    B, L, _ = a.shape
    return a.reshape(B, L, n_heads, -1).transpose(0, 2, 1, 3)


def heads_last(a):
    B, H, L, d = a.shape
    return a.transpose(0, 2, 1, 3).reshape(B, L, H * d)


def layer_norm(x, g, b):
    xf = x.astype(jnp.float32)
    mu = jnp.mean(xf, -1, keepdims=True)
    var = jnp.mean(jnp.square(xf - mu), -1, keepdims=True)
    return ((xf - mu) * lax.rsqrt(var + LN_EPS) * g.astype(jnp.float32) + b.astype(jnp.float32)).astype(x.dtype)


def head_rms_norm(h, gain):
    hf = h.astype(jnp.float32)
    hf = hf * lax.rsqrt(jnp.mean(hf * hf, -1, keepdims=True) + LN_EPS)
    return heads_last(hf) * gain.astype(jnp.float32)


def swiglu(x, w_up, w_down):
    g, u = jnp.split(x @ w_up, 2, axis=-1)
    return (jax.nn.silu(g) * u) @ w_down


def rotate(x, pos):
    d = x.shape[-1]
    inv = 1.0 / (RET_THETA ** jnp.linspace(0.0, 1.0, d // 2, dtype=jnp.float32))
    ang = pos.astype(jnp.float32)[:, None] * inv[None, :]
    cos, sin = jnp.cos(ang), jnp.sin(ang)
    x1, x2 = x[..., :d // 2], x[..., d // 2:]
    return jnp.concatenate([x1 * cos - x2 * sin, x1 * sin + x2 * cos], -1)


def run_chunks(fn, state, xs, chunk, chunked):
    if not chunked:
        return fn(state, *xs)
    B, H, S = xs[0].shape[:3]
    n = S // chunk

    def to_chunks(a):
        return jnp.moveaxis(a.reshape((B, H, n, chunk) + a.shape[3:]), 2, 0)

    state, out = lax.scan(lambda st, xc: fn(st, *xc), state, tuple(to_chunks(a) for a in xs))
    out = jnp.moveaxis(out, 0, 2)
    return state, out.reshape((B, H, S) + out.shape[4:])


def mlstm_chunk(state, q, k, v, ig, lf):
    C, n, m = state
    L = q.shape[2]
    b = jnp.cumsum(lf, axis=-1)
    causal = jnp.tril(jnp.ones((L, L), dtype=bool))
    dlog = jnp.where(causal, b[..., :, None] - b[..., None, :] + ig[..., None, :], -jnp.inf)
    g = b + m[..., None]
    m_t = jnp.maximum(g, jnp.max(dlog, -1))
    dw = jnp.exp(dlog - m_t[..., None])
    gw = jnp.exp(g - m_t)
    s = jnp.einsum('bhtd,bhsd->bhts', q, k) * dw
    num = jnp.einsum('bhts,bhsv->bhtv', s, v) + gw[..., None] * jnp.einsum('bhtd,bhdv->bhtv', q, C)
    den = jnp.sum(s, -1) + gw * jnp.einsum('bhtd,bhd->bht', q, n)
    h = num / jnp.maximum(jnp.abs(den), jnp.exp(-m_t))[..., None]
    bL = b[..., -1]
    wlog = bL[..., None] - b + ig
    m_new = jnp.maximum(bL + m, jnp.max(wlog, -1))
    w = jnp.exp(wlog - m_new[..., None])
    decay = jnp.exp(bL + m - m_new)
    C_new = decay[..., None, None] * C + jnp.einsum('bhsd,bhsv->bhdv', k * w[..., None], v)
    n_new = decay[..., None] * n + jnp.einsum('bhs,bhsd->bhd', w, k)
    return (C_new, n_new, m_new), h


def hgrn_chunk(S, q, k, v, lf):
    L = q.shape[2]
    b = jnp.cumsum(lf, axis=2)
    causal = jnp.tril(jnp.ones((L, L), dtype=bool))[:, :, None]
    dlog = jnp.where(causal, b[:, :, :, None, :] - b[:, :, None, :, :], -jnp.inf)
    attn = jnp.einsum('bhtd,bhsd,bhtsd->bhts', q, k, jnp.exp(dlog))
    o = jnp.einsum('bhts,bhsv->bhtv', attn, v) + jnp.einsum('bhtd,bhdv->bhtv', q * jnp.exp(b), S)
    bL = b[:, :, -1:, :]
    S_new = jnp.exp(bL[:, :, 0, :, None]) * S + jnp.einsum('bhsd,bhsv->bhdv', k * jnp.exp(bL - b), v)
    return S_new, o


def retention_chunk(S, q, k, v):
    H, L = q.shape[1], q.shape[2]
    log_g = jnp.log1p(-jnp.exp2(-5.0 - jnp.arange(H, dtype=jnp.float32)))
    pos = jnp.arange(L, dtype=jnp.float32)
    rel = pos[:, None] - pos[None, :]
    dmask = jnp.where(rel >= 0, jnp.exp(log_g[:, None, None] * jnp.maximum(rel, 0.0)), 0.0)
    scores = jnp.einsum('bhtd,bhsd->bhts', q, k) * dmask
    inter = jnp.exp(log_g[:, None] * (pos + 1.0))
    o = jnp.einsum('bhts,bhsv->bhtv', scores, v) + jnp.einsum('bhtd,bhdv->bhtv', q, S) * inter[None, :, :, None]
    tail = jnp.exp(log_g[:, None] * (L - 1.0 - pos))
    S_new = jnp.exp(log_g * L)[None, :, None, None] * S + jnp.einsum('bhsd,bhsv->bhdv', k * tail[None, :, :, None], v)
    return S_new, o


def indexer_scores(qi, ki, wi):
    rel = jax.nn.relu(jnp.einsum('bthd,bsd->bths', qi, ki) * IDX_DIM ** -0.5)
    return jnp.einsum('bths,bth->bts', rel, wi * IDX_HEADS ** -0.5)


def sparse_attend(q, kg, vg, valid):
    B, T = q.shape[:2]
    qg = q.reshape(B, T, ATT_KV_HEADS, ATT_HEADS // ATT_KV_HEADS, ATT_DH)
    logits = jnp.einsum('btkgd,btjkd->btkgj', qg, kg) * ATT_DH ** -0.5
    logits = jnp.where(valid[:, :, None, None, :], logits, -jnp.inf)
    p = jax.nn.softmax(logits.astype(jnp.float32), axis=-1)
    out = jnp.einsum('btkgj,btjkd->btkgd', p, vg)
    return out.reshape(B, T, ATT_HEADS * ATT_DH)


def dsa_prompt(q, k, v, qi, ki, wi):
    B, S = q.shape[:2]
    n_sel = min(TOPK_MAX, S // 4)
    nb = S // Q_BLOCK
    key_pos = jnp.arange(S)

    def blocks(a):
        return jnp.moveaxis(a.reshape((B, nb, Q_BLOCK) + a.shape[2:]), 1, 0)

    def one_block(args):
        qb, qib, wib, t0 = args
        qpos = t0 + jnp.arange(Q_BLOCK)
        score = indexer_scores(qib, ki, wib)
        score = jnp.where(key_pos[None, None, :] <= qpos[None, :, None], score, -jnp.inf)
        _, idx = lax.top_k(score, n_sel)
        valid = idx <= qpos[None, :, None]
        kg = jax.vmap(lambda kk, ii: kk[ii])(k, idx)
        vg = jax.vmap(lambda vv, ii: vv[ii])(v, idx)
        return sparse_attend(qb, kg, vg, valid)

    out = lax.map(one_block, (blocks(q), blocks(qi), blocks(wi), jnp.arange(nb) * Q_BLOCK))
    return jnp.moveaxis(out, 0, 1).reshape(B, S, ATT_HEADS * ATT_DH)


def dsa_sample(q, k, v, qi, ki, wi, cache_k, cache_v, cache_ik, page_table):
    B, T = q.shape[:2]
    past = page_table.shape[1] * PAGE_SIZE
    L = past + T
    n_sel = min(TOPK_MAX, L // 4)
    ki_past = cache_ik[page_table].reshape(B, past, IDX_DIM).astype(jnp.float32)
    ki_all = jnp.concatenate([ki_past, ki], 1)
    qpos = past + jnp.arange(T)
    score = indexer_scores(qi, ki_all, wi)
    score = jnp.where(jnp.arange(L)[None, None, :] <= qpos[None, :, None], score, -jnp.inf)
    _, idx = lax.top_k(score, n_sel)
    valid = idx <= qpos[None, :, None]
    in_past = idx < past
    ip = jnp.minimum(idx, past - 1)
    phys = jax.vmap(lambda pt, ii: pt[ii])(page_table, ip // PAGE_SIZE)
    off = ip % PAGE_SIZE
    inew = jnp.clip(idx - past, 0, T - 1)

    def pick(pool, new):
        from_past = pool[phys, off].astype(jnp.float32)
        from_new = jax.vmap(lambda nn, ii: nn[ii])(new, inew)
        return jnp.where(in_past[..., None, None], from_past, from_new)

    return sparse_attend(q, pick(cache_k, k), pick(cache_v, v), valid)


def mixer_even(x, w_in, b_gate, g_mlstm, w_out, mstate, attend, chunked):
    B, L, _ = x.shape
    z = (x @ w_in).astype(jnp.float32)
    qa, ka, va, oa, ia, fa, qb, kb, vb, qi, ki, wi = split_cols(z, EVEN_SPLITS)
    gates = jnp.concatenate([ia, fa], -1) + b_gate.astype(jnp.float32)
    ia, fa = jnp.split(gates, 2, axis=-1)
    q = heads_first(qa, MLSTM_HEADS) * MLSTM_DH ** -0.5
    k = heads_first(ka, MLSTM_HEADS)
    v = heads_first(va, MLSTM_HEADS)
    ig = ia.transpose(0, 2, 1)
    lf = jax.nn.log_sigmoid(fa).transpose(0, 2, 1)
    new_m, h = run_chunks(mlstm_chunk, mstate, (q, k, v, ig, lf), MLSTM_CHUNK, chunked)
    ya = head_rms_norm(h, g_mlstm) * jax.nn.sigmoid(oa)
    qb = qb.reshape(B, L, ATT_HEADS, ATT_DH)
    kb = kb.reshape(B, L, ATT_KV_HEADS, ATT_DH)
    vb = vb.reshape(B, L, ATT_KV_HEADS, ATT_DH)
    qi = qi.reshape(B, L, IDX_HEADS, IDX_DIM)
    yb = attend(qb, kb, vb, qi, ki, wi)
    y = jnp.concatenate([ya, yb], -1).astype(x.dtype) @ w_out
    return y, new_m, (kb, vb, ki)


def mixer_odd(x, w_in, lb, g_hgrn, g_ret, w_out, hstate, rstate, pos, chunked):
    z = (x @ w_in).astype(jnp.float32)
    qc, fc, ic, gc, qd, kd, vd, gd = split_cols(z, ODD_SPLITS)
    f = lb + (1.0 - lb) * jax.nn.sigmoid(fc)
    q = heads_first(jax.nn.silu(qc), HGRN_HEADS)
    k = heads_first(1.0 - f, HGRN_HEADS)
    lf = heads_first(jnp.log(f), HGRN_HEADS)
    v = heads_first(ic, HGRN_HEADS)
    new_h, oc = run_chunks(hgrn_chunk, hstate, (q, k, v, lf), HGRN_CHUNK, chunked)
    yc = head_rms_norm(oc, g_hgrn) * jax.nn.silu(gc)
    qr = rotate(heads_first(qd, RET_HEADS), pos)
    kr = rotate(heads_first(kd, RET_HEADS), pos) * RET_DK ** -0.5
    vr = heads_first(vd, RET_HEADS)
    new_r, od = run_chunks(retention_chunk, rstate, (qr, kr, vr), RET_CHUNK, chunked)
    yd = head_rms_norm(od, g_ret) * jax.nn.silu(gd)
    y = jnp.concatenate([yc, yd], -1).astype(x.dtype) @ w_out
    return y, new_h, new_r


def trunk(x, p, W, m_states, h_states, r_states, attends, pos, chunked, lb_all):
    new_m, new_kv, new_h, new_r = [], [], [], []
    for i in range(DEPTH):
        j = i // 2
        x = layer_norm(ALPHA * x + 0.5 * swiglu(x, W['w_ffn_up'][i, 0], W['w_ffn_down'][i, 0]), W['ln_g'][i, 0], W['ln_b'][i, 0])
        if i % 2 == 0:
            y, st, kv = mixer_even(x, W['w_in_even'][j], W['b_gate_mlstm'][j], W['g_mlstm'][j], W['w_out'][i],
                                   m_states[j], attends[j], chunked)
            new_m.append(st)
            new_kv.append(kv)
        else:
            y, hs, rs = mixer_odd(x, W['w_in_odd'][j], lb_all[i], W['g_hgrn'][j], W['g_ret'][j], W['w_out'][i],
                                  h_states[j], r_states[j], pos, chunked)
            new_h.append(hs)
            new_r.append(rs)
        x = layer_norm(ALPHA * x + y, W['ln_g'][i, 1], W['ln_b'][i, 1])
        x = layer_norm(ALPHA * x + 0.5 * swiglu(x, W['w_ffn_up'][i, 1], W['w_ffn_down'][i, 1]), W['ln_g'][i, 2], W['ln_b'][i, 2])
        x = x + jax.nn.sigmoid(x @ W['w_pe_gate'][i]) * (p[i] @ W['w_pe_proj'][i])
    return x, new_m, new_kv, new_h, new_r


def setup_inputs(seed: int = 0) -> dict:
    key = jax.random.key(seed)
    keys = iter(jax.random.split(key, 48))

    def nrm(shape, scale=1.0):
        return jax.random.normal(next(keys), shape, jnp.float32) * scale

    n_pages = PAST_LEN // PAGE_SIZE
    n_used = DEC_BATCH * n_pages
    n_pool = (5 * n_used) // 4
    perm = jax.random.permutation(next(keys), n_pool)
    page_table = perm[:n_used].reshape(DEC_BATCH, n_pages).astype(jnp.int32)
    b_gate = jnp.concatenate([nrm((N_EVEN, MLSTM_HEADS), 0.1),
                              jnp.linspace(3.0, 6.0, MLSTM_HEADS, dtype=jnp.float32)[None, :] + nrm((N_EVEN, MLSTM_HEADS), 0.1)], -1)
    return {
        'x_prompt': nrm((BATCH, SEQ, D_MODEL)),
        'x_sample': nrm((DEC_BATCH, DEC_SEQ, D_MODEL)),
        'state_mlstm_C': nrm((N_EVEN, DEC_BATCH, MLSTM_HEADS, MLSTM_DH, MLSTM_DH), 0.5),
        'state_mlstm_n': nrm((N_EVEN, DEC_BATCH, MLSTM_HEADS, MLSTM_DH), 0.5),
        'state_mlstm_m': nrm((N_EVEN, DEC_BATCH, MLSTM_HEADS)),
        'cache_k': nrm((N_EVEN, n_pool, PAGE_SIZE, ATT_KV_HEADS, ATT_DH)),
        'cache_v': nrm((N_EVEN, n_pool, PAGE_SIZE, ATT_KV_HEADS, ATT_DH)),
        'cache_idx_k': nrm((N_EVEN, n_pool, PAGE_SIZE, IDX_DIM)),
        'state_hgrn': nrm((N_ODD, DEC_BATCH, HGRN_HEADS, HGRN_DK, HGRN_DV), 0.5),
        'state_ret': nrm((N_ODD, DEC_BATCH, RET_HEADS, RET_DK, RET_DV), 0.5),
        'page_table': page_table,
        'p_prompt': nrm((DEPTH, BATCH, SEQ, P_DIM)),
        'p_sample': nrm((DEPTH, DEC_BATCH, DEC_SEQ, P_DIM)),
        'ln_g': 1.0 + nrm((DEPTH, 3, D_MODEL), 0.02),
        'ln_b': nrm((DEPTH, 3, D_MODEL), 0.02),
        'w_ffn_up': nrm((DEPTH, 2, D_MODEL, 2 * D_FF), D_MODEL ** -0.5),
        'w_ffn_down': nrm((DEPTH, 2, D_FF, D_MODEL), BETA * D_FF ** -0.5),
        'w_in_even': nrm((N_EVEN, D_MODEL, EVEN_WIDTH), D_MODEL ** -0.5),
        'b_gate_mlstm': b_gate,
        'g_mlstm': 1.0 + nrm((N_EVEN, HALF), 0.02),
        'w_in_odd': nrm((N_ODD, D_MODEL, ODD_WIDTH), D_MODEL ** -0.5),
        'hgrn_lb': nrm((DEPTH, HGRN_HEADS * HGRN_DK), 0.1),
        'g_hgrn': 1.0 + nrm((N_ODD, HALF), 0.02),
        'g_ret': 1.0 + nrm((N_ODD, HALF), 0.02),
        'w_out': nrm((DEPTH, 2 * HALF, D_MODEL), BETA * (2 * HALF) ** -0.5),
        'w_pe_gate': nrm((DEPTH, D_MODEL, D_MODEL), D_MODEL ** -0.5),
        'w_pe_proj': nrm((DEPTH, P_DIM, D_MODEL), P_DIM ** -0.5),
    }


def reference(x_prompt, x_sample, state_mlstm_C, state_mlstm_n, state_mlstm_m, cache_k, cache_v, cache_idx_k,
              state_hgrn, state_ret, page_table, p_prompt, p_sample, ln_g, ln_b, w_ffn_up, w_ffn_down,
              w_in_even, b_gate_mlstm, g_mlstm, w_in_odd, hgrn_lb, g_hgrn, g_ret, w_out, w_pe_gate, w_pe_proj):
    f32 = jnp.float32
    W = {'ln_g': ln_g, 'ln_b': ln_b, 'w_ffn_up': w_ffn_up, 'w_ffn_down': w_ffn_down,
         'w_in_even': w_in_even, 'b_gate_mlstm': b_gate_mlstm, 'g_mlstm': g_mlstm,
         'w_in_odd': w_in_odd, 'g_hgrn': g_hgrn, 'g_ret': g_ret, 'w_out': w_out,
         'w_pe_gate': w_pe_gate, 'w_pe_proj': w_pe_proj}
    lb_cum = jnp.cumsum(jax.nn.softmax(hgrn_lb.astype(f32), axis=0), axis=0)
    lb_all = lb_cum - lb_cum[:1]
    B, S = x_prompt.shape[:2]
    T = x_sample.shape[1]
    past = page_table.shape[1] * PAGE_SIZE

    m0 = (jnp.zeros((B, MLSTM_HEADS, MLSTM_DH, MLSTM_DH), f32), jnp.zeros((B, MLSTM_HEADS, MLSTM_DH), f32),
          jnp.zeros((B, MLSTM_HEADS), f32))
    y_prompt, m_p, kv_p, h_p, r_p = trunk(
        x_prompt, p_prompt, W, [m0] * N_EVEN,
        [jnp.zeros((B, HGRN_HEADS, HGRN_DK, HGRN_DV), f32)] * N_ODD,
        [jnp.zeros((B, RET_HEADS, RET_DK, RET_DV), f32)] * N_ODD,
        [dsa_prompt] * N_EVEN, jnp.arange(S), True, lb_all)

    m_init = [(state_mlstm_C[j].astype(f32), state_mlstm_n[j].astype(f32), state_mlstm_m[j].astype(f32))
              for j in range(N_EVEN)]
    att_s = [functools.partial(dsa_sample, cache_k=cache_k[j], cache_v=cache_v[j], cache_ik=cache_idx_k[j],
                               page_table=page_table) for j in range(N_EVEN)]
    y_sample, m_s, kv_s, h_s, r_s = trunk(
        x_sample, p_sample, W, m_init,
        [state_hgrn[j].astype(f32) for j in range(N_ODD)],
        [state_ret[j].astype(f32) for j in range(N_ODD)],
        att_s, past + jnp.arange(T), False, lb_all)

    def stk(items, idx, like):
        return jnp.stack([it[idx] for it in items]).astype(like.dtype)

    mC_p, mC_s = stk(m_p, 0, state_mlstm_C), stk(m_s, 0, state_mlstm_C)
    mn_p, mn_s = stk(m_p, 1, state_mlstm_n), stk(m_s, 1, state_mlstm_n)
    mm_p, mm_s = stk(m_p, 2, state_mlstm_m), stk(m_s, 2, state_mlstm_m)
    k_p, k_s = stk(kv_p, 0, cache_k), stk(kv_s, 0, cache_k)
    v_p, v_s = stk(kv_p, 1, cache_v), stk(kv_s, 1, cache_v)
    ik_p, ik_s = stk(kv_p, 2, cache_idx_k), stk(kv_s, 2, cache_idx_k)
    hg_p, hg_s = jnp.stack(h_p).astype(state_hgrn.dtype), jnp.stack(h_s).astype(state_hgrn.dtype)
    rt_p, rt_s = jnp.stack(r_p).astype(state_ret.dtype), jnp.stack(r_s).astype(state_ret.dtype)
    return (y_prompt, y_sample, mC_p, mC_s, mn_p, mn_s, mm_p, mm_s, k_p, k_s, v_p, v_s, ik_p, ik_s, hg_p, hg_s, rt_p, rt_s)
```

```python
import numpy as np
from contextlib import ExitStack
import concourse.bass as bass
import concourse.mybir as mybir
from concourse.bass_utils import run_bass_kernel_spmd

F32 = mybir.dt.float32
BF16 = mybir.dt.bfloat16
I32 = mybir.dt.int32
AF = mybir.ActivationFunctionType
ALU = mybir.AluOpType
AX = mybir.AxisListType

ENGS = ("pe", "act", "dve", "pool", "sp")

D = 2048
SEQ = 2048
NT = SEQ + 1
DFF = 5632
NFC = DFF // 128
DEPTH = 2
ALPHA = (2 * DEPTH) ** 0.25
LN_EPS = 1e-5
TILES = [(0, 512), (512, 512), (1024, 512), (1536, 513)]


def blocks(tw):
    out = []
    o = 0
    while o < tw:
        n = min(512, tw - o)
        out.append((o, n))
        o += n
    return out


class _Op:
    __slots__ = ("eng", "fn", "dma", "deps", "signaled", "sem", "val", "amt", "pre")

    def __init__(self, eng, fn, dma):
        self.eng = eng
        self.fn = fn
        self.dma = dma
        self.deps = []
        self.signaled = False
        self.sem = None
        self.val = 0
        self.amt = 0
        self.pre = None


class Sched:
    def __init__(self, nc, n_dma_sems=12):
        self.nc = nc
        self.ops = {e: [] for e in ENGS}
        self.last_w = {}
        self.readers = {}
        self.n_dma_sems = n_dma_sems

    def add(self, eng, fn, r=(), w=(), dma=False):
        op = _Op(eng, fn, dma)
        deps = set()
        for k in r:
            lw = self.last_w.get(k)
            if lw is not None:
                deps.add(lw)
        for k in w:
            lw = self.last_w.get(k)
            if lw is not None:
                deps.add(lw)
            for rd in self.readers.get(k, ()):
                deps.add(rd)
        op.deps = list(deps)
        for d in op.deps:
            d.signaled = True
        for k in w:
            self.last_w[k] = op
            self.readers[k] = []
        for k in r:
            if k not in w:
                self.readers.setdefault(k, []).append(op)
        if dma:
            op.signaled = True
        self.ops[eng].append(op)
        return op

    def barrier(self):
        pend = set()
        for v in self.last_w.values():
            if v is not None:
                pend.add(v)
        for v in self.readers.values():
            for rd in v:
                pend.add(rd)
        for e in ENGS:
            op = _Op(e, None, False)
            op.deps = list(pend)
            self.ops[e].append(op)
        for d in pend:
            d.signaled = True
        self.last_w = {}
        self.readers = {}

    def emit(self, stack):
        nc = self.nc
        sems = {e: stack.enter_context(nc.semaphore("s_" + e)) for e in ENGS}
        dsems = {}
        for e in ("sp", "pool", "act"):
            dsems[e] = [stack.enter_context(nc.semaphore("d_%s_%d" % (e, i))) for i in range(self.n_dma_sems)]
        cnt = {e: 0 for e in ENGS}
        dtot = {e: [0] * self.n_dma_sems for e in dsems}
        drr = {e: 0 for e in dsems}
        for e in ENGS:
            for op in self.ops[e]:
                if op.dma:
                    j = drr[e]
                    drr[e] = (j + 1) % self.n_dma_sems
                    op.sem = dsems[e][j]
                    if dtot[e][j] > 0:
                        op.pre = (op.sem, dtot[e][j])
                    dtot[e][j] += 16
                    op.val = dtot[e][j]
                    op.amt = 16
                elif op.signaled:
                    cnt[e] += 1
                    op.sem = sems[e]
                    op.val = cnt[e]
                    op.amt = 1
        block = stack.enter_context(nc.Block())

        def run(e, eng):
            waited = {}
            for op in self.ops[e]:
                need = {}
                if op.pre is not None:
                    need[id(op.pre[0])] = (op.pre[0], op.pre[1])
                for d in op.deps:
                    if d.sem is None:
                        continue
                    if d.eng == "pe" and e == "pe" and not d.dma:
                        continue
                    cur = need.get(id(d.sem))
                    if cur is None or cur[1] < d.val:
                        need[id(d.sem)] = (d.sem, d.val)
                for sid, (sem, val) in need.items():
                    if waited.get(sid, 0) >= val:
                        continue
                    eng.wait_ge(sem, val)
                    waited[sid] = val
                if op.fn is not None:
                    ins = op.fn(eng)
                    if op.signaled:
                        ins.then_inc(op.sem, op.amt)
                elif op.signaled:
                    eng.nop().then_inc(op.sem, op.amt)

        @block.tensor
        def _(eng):
            run("pe", eng)

        @block.scalar
        def _(eng):
            run("act", eng)

        @block.vector
        def _(eng):
            run("dve", eng)

        @block.gpsimd
        def _(eng):
            run("pool", eng)

        @block.sync
        def _(eng):
            run("sp", eng)


class Prog:
    ARENA = 94 * 1024

    def __init__(self, debug_out=None):
        self.nc = bass.Bass("TRN2", target_bir_lowering=False)
        nc = self.nc
        self.S = Sched(nc)
        self.arena = nc.alloc_sbuf_tensor("arena", [128, self.ARENA], BF16)
        self.aoff = 0
        self.PS = [nc.alloc_psum_tensor("ps%d" % i, [128, 1024], F32) for i in range(4)]
        self.ins = {}
        self.outs = {}
        self.debug_out = debug_out or ()
        self.uid = 0

    def reset_arena(self):
        self.S.barrier()
        self.aoff = 0

    def carve(self, dtype, fshape, parts=128):
        n = int(np.prod(fshape))
        nb = n * (2 if dtype == F32 or dtype == I32 else 1)
        nb = (nb + 15) // 16 * 16
        assert self.aoff + nb <= self.ARENA, ("arena overflow", self.aoff, nb)
        ap = self.arena[0:parts, self.aoff:self.aoff + nb]
        self.aoff += nb
        if dtype != BF16:
            ap = ap.bitcast(dtype)
        ap = ap[:, 0:n]
        if len(fshape) == 2:
            ap = ap.rearrange("p (a b) -> p a b", a=fshape[0])
        elif len(fshape) == 3:
            ap = ap.rearrange("p (a b c) -> p a b c", a=fshape[0], b=fshape[1])
        return ap

    def key(self, name):
        self.uid += 1
        return "%s#%d" % (name, self.uid)

    def inp(self, name, shape, dtype=F32):
        t = self.nc.dram_tensor(name, list(shape), dtype, kind="ExternalInput").ap()
        self.ins[name] = t
        return t

    def out(self, name, shape, dtype=F32):
        t = self.nc.dram_tensor(name, list(shape), dtype, kind="ExternalOutput").ap()
        self.outs[name] = t
        return t

    def scratch(self, name, shape, dtype=F32):
        kind = "ExternalOutput" if name in self.debug_out else "Internal"
        t = self.nc.dram_tensor(name, list(shape), dtype, kind=kind).ap()
        if kind == "ExternalOutput":
            self.outs[name] = t
        return t

    def dma(self, eng, out, in_, r, w):
        return self.S.add(eng, lambda e: e.dma_start(out=out, in_=in_, allow_slow_non_contiguous=True), r=r, w=w, dma=True)

    def mm(self, out, lhsT, rhs, start, stop, r, w):
        return self.S.add("pe", lambda e: e.matmul(out, lhsT, rhs, start=start, stop=stop), r=r, w=w)

    def act(self, out, in_, func, r, w, bias=None, scale=None, accum_out=None):
        kw = {}
        if bias is not None:
            kw["bias"] = bias
        if scale is not None:
            kw["scale"] = scale
        if accum_out is not None:
            kw["accum_out"] = accum_out
        return self.S.add("act", lambda e: e.activation(out, in_, func, **kw), r=r, w=w)

    def tt(self, eng, out, in0, in1, op, r, w):
        return self.S.add(eng, lambda e: e.tensor_tensor(out, in0, in1, op), r=r, w=w)

    def ts(self, eng, out, in0, s1, s2, op0, op1, r, w, accum_out=None):
        if op1 is None:
            return self.S.add(eng, lambda e: e.tensor_scalar(out, in0, s1, None, op0), r=r, w=w)
        if accum_out is not None:
            return self.S.add(eng, lambda e: e.tensor_scalar(out, in0, s1, s2, op0, op1, accum_out), r=r, w=w)
        return self.S.add(eng, lambda e: e.tensor_scalar(out, in0, s1, s2, op0, op1), r=r, w=w)

    def stt(self, out, in0, scalar, in1, op0, op1, r, w, accum_out=None):
        if accum_out is not None:
            return self.S.add("dve", lambda e: e.scalar_tensor_tensor(out, in0, scalar, in1, op0, op1, accum_out), r=r, w=w)
        return self.S.add("dve", lambda e: e.scalar_tensor_tensor(out, in0, scalar, in1, op0, op1), r=r, w=w)

    def copy(self, eng, out, in_, r, w):
        if eng == "act":
            return self.S.add("act", lambda e: e.copy(out, in_), r=r, w=w)
        return self.S.add(eng, lambda e: e.tensor_copy(out, in_), r=r, w=w)

    def memset(self, eng, ap, val, w):
        return self.S.add(eng, lambda e: e.memset(ap, val), w=w)


def fm(ap):
    return ap.rearrange("(c p) t -> p c t", p=128)


def load_cast_tile(P, src, c0, tw, xb, nch=16):
    S = P.S
    sv = fm(src)
    keys = []
    grp = 2
    for q in range(0, nch, grp):
        n = min(grp, nch - q)
        i = (q // grp) % 2
        stg = P.xstg[i]
        ks = "xstg%d" % i
        P.dma("sp", stg[:, 0:n, 0:tw], sv[:, q:q + n, c0:c0 + tw], r=[("dram", src.name)], w=[ks])
        kx = ("xb", id(xb), q)
        eng = "dve" if (q // grp) % 2 == 0 else "act"
        P.copy(eng, xb[:, q:q + n, 0:tw], stg[:, 0:n, 0:tw], r=[ks], w=[kx])
        keys.append(kx)
    return keys


def proj_res_ln(P, tilei, c0, tw, inT, in_keys, nk, wview, wcol0, res_src, scale, lng, lnb, dst,
                dst_bf=None):
    S = P.S
    blks = blocks(tw)
    PS = P.PS
    rv = fm(res_src)
    dv = fm(dst)
    s1, s2 = PS[2], PS[3]
    pend = None

    def post(dmc, o_ps, ko):
        xf = P.xf[dmc % 2]
        kxf = "xf%d" % (dmc % 2)
        P.dma("sp", xf[:, 0:tw], rv[:, dmc, c0:c0 + tw], r=[("dram", res_src.name)], w=[kxf])
        kv = ("v", dmc)
        xa = P.xa[dmc % 2]
        kxa = "xa%d" % (dmc % 2)
        P.act(xa[:, 0:tw], xf[:, 0:tw], AF.Copy, r=[kxf], w=[kxa], scale=float(ALPHA))
        P.stt(P.v[:, dmc, 0:tw], o_ps[:, 0:tw], float(scale), xa[:, 0:tw], ALU.mult, ALU.add, r=[ko, kxa], w=[kv])
        sq = P.sq[dmc % 2]
        ksq = "sq%d" % (dmc % 2)
        P.act(sq[:, 0:tw], P.v[:, dmc, 0:tw], AF.Square, r=[kv], w=[ksq])
        return kv, ksq

    def stats(dmc, kv, ksq):
        sq = P.sq[dmc % 2]
        for (o, n) in blks:
            P.mm(s1[:, o:o + n], P.ones[:, :], P.v[:, dmc, o:o + n], dmc == 0, dmc == 15, r=[kv, "ones"], w=["ps2"])
            P.mm(s2[:, o:o + n], P.ones[:, :], sq[:, o:o + n], dmc == 0, dmc == 15, r=[ksq, "ones"], w=["ps3"])

    for dmc in range(16):
        wd = P.wD[dmc % 2]
        kw = "wD%d" % (dmc % 2)
        P.dma("pool", wd[:, 0:nk, :], wview[:, :, wcol0 + dmc * 128: wcol0 + (dmc + 1) * 128], r=[], w=[kw])
        o_ps = PS[dmc % 2]
        ko = "ps%d" % (dmc % 2)
        for (o, n) in blks:
            for k in range(nk):
                P.mm(o_ps[:, o:o + n], wd[:, k, :], inT[:, k, o:o + n], k == 0, k == nk - 1,
                     r=[kw] + list(in_keys), w=[ko])
        if pend is not None:
            stats(*pend)
        kv, ksq = post(dmc, o_ps, ko)
        pend = (dmc, kv, ksq)
    stats(*pend)
    mean, msq, var, rstd = P.lnt[0], P.lnt[1], P.lnt[2], P.lnt[3]
    P.ts("dve", mean[:, 0:tw], s1[:, 0:tw], 1.0 / D, None, ALU.mult, None, r=["ps2"], w=["mean"])
    P.tt("dve", msq[:, 0:tw], mean[:, 0:tw], mean[:, 0:tw], ALU.mult, r=["mean"], w=["msq"])
    P.stt(var[:, 0:tw], s2[:, 0:tw], 1.0 / D, msq[:, 0:tw], ALU.mult, ALU.subtract, r=["ps3", "msq"], w=["var"])
    P.ts("dve", var[:, 0:tw], var[:, 0:tw], float(LN_EPS), None, ALU.add, None, r=["var"], w=["var"])
    P.act(var[:, 0:tw], var[:, 0:tw], AF.Sqrt, r=["var"], w=["var"])
    P.S.add("dve", lambda e: e.reciprocal(rstd[:, 0:tw], var[:, 0:tw]), r=["var"], w=["rstd"])
    for dmc in range(16):
        t1 = P.t1[dmc % 2]
        k1 = "t1%d" % (dmc % 2)
        yo = P.yo[dmc % 2]
        ky = "yo%d" % (dmc % 2)
        P.tt("dve", t1[:, 0:tw], P.v[:, dmc, 0:tw], mean[:, 0:tw], ALU.subtract, r=[("v", dmc), "mean"], w=[k1])
        P.tt("pool", t1[:, 0:tw], t1[:, 0:tw], rstd[:, 0:tw], ALU.mult, r=[k1, "rstd"], w=[k1])
        P.act(yo[:, 0:tw], t1[:, 0:tw], AF.Identity, r=[k1, "lnp"], w=[ky],
              scale=lng[:, dmc:dmc + 1], bias=lnb[:, dmc:dmc + 1])
        P.dma("sp", dv[:, dmc, c0:c0 + tw], yo[:, 0:tw], r=[ky], w=[("dram", dst.name)])


def dense_bufs(P, lng_d, lnb_d, full=True):
    P.reset_arena()
    P.xb = P.carve(BF16, [16, 513])
    if full:
        P.hT = P.carve(BF16, [NFC, 513])
        P.v = P.carve(F32, [16, 513])
        P.wA = [P.carve(BF16, [16, 256]) for _ in range(2)]
        P.wB = [P.carve(BF16, [16, 256]) for _ in range(2)]
    P.wD = [P.carve(BF16, [NFC, 128]) for _ in range(2)]
    P.xstg = [P.carve(F32, [2, 513]) for _ in range(2)]
    P.xf = [P.carve(F32, [513]) for _ in range(2)]
    P.xa = [P.carve(F32, [513]) for _ in range(2)]
    P.sq = [P.carve(F32, [513]) for _ in range(2)]
    P.sg = [P.carve(BF16, [513]) for _ in range(2)]
    P.lnt = [P.carve(F32, [513]) for _ in range(4)]
    P.t1 = [P.carve(F32, [513]) for _ in range(2)]
    P.yo = [P.carve(F32, [513]) for _ in range(2)]
    P.ones = P.carve(F32, [128])
    P.lng = P.carve(F32, [16])
    P.lnb = P.carve(F32, [16])
    P.memset("dve", P.ones[:, :], 1.0, w=["ones"])
    if lng_d is not None:
        P.dma("sp", P.lng[:, :], lng_d, r=[], w=["lnp"])
        P.dma("sp", P.lnb[:, :], lnb_d, r=[], w=["lnp"])


def outproj_phase(P, yT, res_src, dst, w_out, lng_d, lnb_d):
    dense_bufs(P, lng_d, lnb_d)
    wv = fm(w_out)
    yv = fm(yT)
    for ti, (c0, tw) in enumerate(TILES):
        P.dma("sp", P.xb[:, :, 0:tw], yv[:, :, c0:c0 + tw], r=[("dram", yT.name)], w=["xb_y"])
        proj_res_ln(P, ti, c0, tw, P.xb, ["xb_y"], 16, wv, 0, res_src, 1.0, P.lng, P.lnb, dst)


def pegate_phase(P, src, dst, pT, w_gate, w_proj):
    dense_bufs(P, None, None, full=False)
    pb = P.carve(BF16, [2, 513])
    pst = P.carve(F32, [2, 513])
    wp = P.carve(BF16, [2, 2048])
    wgv = fm(w_gate)
    wpv = fm(w_proj)
    pv = fm(pT)
    sv = fm(src)
    dv = fm(dst)
    PS = P.PS
    P.dma("pool", wp[:, :, :], wpv[:, :, :], r=[], w=["wp"])
    for ti, (c0, tw) in enumerate(TILES):
        blks = blocks(tw)
        xkeys = load_cast_tile(P, src, c0, tw, P.xb)
        P.dma("sp", pst[:, :, 0:tw], pv[:, :, c0:c0 + tw], r=[], w=["pst"])
        P.copy("dve", pb[:, :, 0:tw], pst[:, :, 0:tw], r=["pst"], w=["pb"])
        for dmc in range(16):
            wd = P.wD[dmc % 2]
            kw = "wD%d" % (dmc % 2)
            P.dma("pool", wd[:, 0:16, :], wgv[:, :, dmc * 128:(dmc + 1) * 128], r=[], w=[kw])
            g_ps, p_ps = PS[(dmc % 2) * 2], PS[(dmc % 2) * 2 + 1]
            kg, kp = "ps%d" % ((dmc % 2) * 2), "ps%d" % ((dmc % 2) * 2 + 1)
            for (o, n) in blks:
                for dc in range(16):
                    P.mm(g_ps[:, o:o + n], wd[:, dc, :], P.xb[:, dc, o:o + n], dc == 0, dc == 15,
                         r=[kw] + xkeys, w=[kg])
                for j in range(2):
                    P.mm(p_ps[:, o:o + n], wp[:, j, dmc * 128:(dmc + 1) * 128], pb[:, j, o:o + n], j == 0, j == 1,
                         r=["wp", "pb"], w=[kp])
            xf = P.xf[dmc % 2]
            kxf = "xf%d" % (dmc % 2)
            P.dma("sp", xf[:, 0:tw], sv[:, dmc, c0:c0 + tw], r=[("dram", src.name)], w=[kxf])
            sgm = P.xa[dmc % 2]
            ksg = "xa%d" % (dmc % 2)
            P.act(sgm[:, 0:tw], g_ps[:, 0:tw], AF.Sigmoid, r=[kg], w=[ksg])
            t1 = P.t1[dmc % 2]
            k1 = "t1%d" % (dmc % 2)
            P.tt("dve", t1[:, 0:tw], sgm[:, 0:tw], p_ps[:, 0:tw], ALU.mult, r=[ksg, kp], w=[k1])
            yo = P.yo[dmc % 2]
            ky = "yo%d" % (dmc % 2)
            P.tt("pool", yo[:, 0:tw], t1[:, 0:tw], xf[:, 0:tw], ALU.add, r=[k1, kxf], w=[ky])
            P.dma("sp", dv[:, dmc, c0:c0 + tw], yo[:, 0:tw], r=[ky], w=[("dram", dst.name)])


def zero_dram(P, dst, nrows, ncols, dtype):
    P.reset_arena()
    z = P.carve(dtype, [ncols])
    P.memset("dve", z[:, :], 0.0, w=["z"])
    for r0 in range(0, nrows, 128):
        n = min(128, nrows - r0)
        P.dma("sp", dst[r0:r0 + n, :], z[0:n, :], r=["z"], w=[("dram", dst.name)])


def ffn_phase(P, src, dst, wup, wdn, lng_d, lnb_d):
    dense_bufs(P, lng_d, lnb_d)
    lng, lnb = P.lng, P.lnb
    wuv = fm(wup)
    wdv = fm(wdn)
    PS = P.PS
    for ti, (c0, tw) in enumerate(TILES):
        blks = blocks(tw)
        xkeys = load_cast_tile(P, src, c0, tw, P.xb)
        for fp in range(NFC // 2):
            wg, wu = P.wA[fp % 2], P.wB[fp % 2]
            kg, ku = "wA%d" % (fp % 2), "wB%d" % (fp % 2)
            P.dma("pool", wg[:, :, :], wuv[:, :, fp * 256:(fp + 1) * 256], r=[], w=[kg])
            P.dma("pool", wu[:, :, :], wuv[:, :, DFF + fp * 256: DFF + (fp + 1) * 256], r=[], w=[ku])
            for j in range(2):
                fc = fp * 2 + j
                g_ps, u_ps = PS[(fc % 2) * 2], PS[(fc % 2) * 2 + 1]
                kgp, kup = "ps%d" % ((fc % 2) * 2), "ps%d" % ((fc % 2) * 2 + 1)
                for (o, n) in blks:
                    for dc in range(16):
                        P.mm(g_ps[:, o:o + n], wg[:, dc, j * 128:(j + 1) * 128], P.xb[:, dc, o:o + n],
                             dc == 0, dc == 15, r=[kg] + xkeys, w=[kgp])
                for (o, n) in blks:
                    for dc in range(16):
                        P.mm(u_ps[:, o:o + n], wu[:, dc, j * 128:(j + 1) * 128], P.xb[:, dc, o:o + n],
                             dc == 0, dc == 15, r=[ku] + xkeys, w=[kup])
                sg = P.sg[fc % 2]
                ksg = "sg%d" % (fc % 2)
                P.act(sg[:, 0:tw], g_ps[:, 0:tw], AF.Silu, r=[kgp], w=[ksg])
                P.tt("dve", P.hT[:, fc, 0:tw], sg[:, 0:tw], u_ps[:, 0:tw], ALU.mult, r=[ksg, kup], w=[("hT", fc)])
        hkeys = [("hT", fc) for fc in range(NFC)]
        proj_res_ln(P, ti, c0, tw, P.hT, hkeys, NFC, wdv, 0, src, 0.5, lng, lnb, dst)


TOKBLKS = [(i * 128, 128) for i in range(16)] + [(2048, 1)]
COLBLKS = [(0, 512), (512, 512), (1024, 512), (1536, 512), (2048, 1)]


def load_all_bf16(P, src, xb_all):
    keys = []
    sv = fm(src)
    i = 0
    for (c0, tw) in TILES:
        for q in range(0, 16, 2):
            stg = P.xstg[i % 2]
            ks = "xstg%d" % (i % 2)
            P.dma("sp", stg[:, 0:2, 0:tw], sv[:, q:q + 2, c0:c0 + tw], r=[("dram", src.name)], w=[ks])
            kx = ("xball", q, c0)
            P.copy("dve" if i % 2 == 0 else "act", xb_all[:, q:q + 2, c0:c0 + tw], stg[:, 0:2, 0:tw], r=[ks], w=[kx])
            keys.append(kx)
            i += 1
    return keys


def proj_phase(P, src, wv, groups):
    P.reset_arena()
    xb = P.carve(BF16, [16, NT])
    wb = [P.carve(BF16, [16, 512]) for _ in range(2)]
    P.xstg = [P.carve(F32, [2, 513]) for _ in range(2)]
    stg_f = [P.carve(F32, [NT]) for _ in range(2)]
    stg_t = [P.carve(F32, [512]) for _ in range(2)]
    xkeys = load_all_bf16(P, src, xb)
    PS = P.PS
    wi = 0
    si = 0
    pi = 0
    for g in groups:
        col0, ncols = g["col0"], g["ncols"]
        for c in range(0, ncols, 512):
            n = min(512, ncols - c)
            w = wb[wi % 2]
            kw = "wb%d" % (wi % 2)
            wi += 1
            P.dma("pool", w[:, :, 0:n], wv[:, :, col0 + c: col0 + c + n], r=[], w=[kw])
            if g["mode"] == "fm":
                for cc in range(0, n, 128):
                    m = min(128, n - cc)
                    stg = stg_f[si % 2]
                    kst = "stgf%d" % (si % 2)
                    si += 1
                    sdt = stg if g["dtype"] == F32 else stg.bitcast(BF16)
                    for (o, nn) in COLBLKS:
                        ps = PS[pi % 4]
                        kp = "ps%d" % (pi % 4)
                        pi += 1
                        for dc in range(16):
                            P.mm(ps[0:m, 0:nn], w[:, dc, cc:cc + m], xb[:, dc, o:o + nn], dc == 0, dc == 15,
                                 r=[kw] + xkeys, w=[kp])
                        kwargs = {}
                        if g.get("bias") is not None:
                            P.act(sdt[0:m, o:o + nn], ps[0:m, 0:nn], AF.Identity, r=[kp, "bias"], w=[kst],
                                  bias=g["bias"][0:m, 0:1], scale=float(g.get("scale", 1.0)))
                        elif pi % 2 == 0:
                            P.act(sdt[0:m, o:o + nn], ps[0:m, 0:nn], AF.Copy, r=[kp], w=[kst],
                                  scale=float(g.get("scale", 1.0)))
                        else:
                            P.ts("dve", sdt[0:m, o:o + nn], ps[0:m, 0:nn], float(g.get("scale", 1.0)), None,
                                 ALU.mult, None, r=[kp], w=[kst])
                    dst = g["dst"]
                    P.dma("sp", dst[c + cc: c + cc + m, :], sdt[0:m, 0:NT], r=[kst], w=[("dram", dst.name)])
            else:
                for (t0, tn) in TOKBLKS:
                    ps = PS[pi % 4]
                    kp = "ps%d" % (pi % 4)
                    pi += 1
                    for dc in range(16):
                        P.mm(ps[0:tn, 0:n], xb[:, dc, t0:t0 + tn], w[:, dc, 0:n], dc == 0, dc == 15,
                             r=[kw] + xkeys, w=[kp])
                    stg = stg_t[si % 2]
                    kst = "stgt%d" % (si % 2)
                    si += 1
                    sdt = stg if g["dtype"] == F32 else stg.bitcast(BF16)
                    if pi % 2 == 0:
                        P.act(sdt[0:tn, 0:n], ps[0:tn, 0:n], AF.Copy, r=[kp], w=[kst])
                    else:
                        P.copy("dve", sdt[0:tn, 0:n], ps[0:tn, 0:n], r=[kp], w=[kst])
                    for (dst, dcol0, scol0, scn, prompt_only, sample_only) in g["dsts"]:
                        lo = max(scol0, c)
                        hi = min(scol0 + scn, c + n)
                        if lo >= hi:
                            continue
                        if tn == 1:
                            if prompt_only:
                                continue
                            row = 0 if sample_only else t0
                        else:
                            if sample_only:
                                continue
                            row = t0
                        P.dma("sp", dst[row:row + tn, dcol0 + lo - scol0: dcol0 + hi - scol0],
                              sdt[0:tn, lo - c: hi - c], r=[kst], w=[("dram", dst.name)])


def even_proj(P, x1, w_in, bgate_d, O):
    wv = fm(w_in)
    groups = [
        dict(mode="fm", col0=0, ncols=1024, dst=O["QAT"], dtype=BF16, scale=1.0 / 16.0),
        dict(mode="fm", col0=1024, ncols=1024, dst=O["KAT"], dtype=BF16),
        dict(mode="fm", col0=4096, ncols=8, dst=O["GT"], dtype=F32, bias="BG"),
        dict(mode="fm", col0=4104, ncols=1024, dst=O["QBT"], dtype=BF16, scale=128.0 ** -0.5),
        dict(mode="fm", col0=5128, ncols=256, dst=O["KBT"], dtype=BF16),
        dict(mode="fm", col0=5640, ncols=512, dst=O["QIT"], dtype=BF16, scale=64.0 ** -0.5),
        dict(mode="fm", col0=6152, ncols=64, dst=O["KIT"], dtype=BF16),
        dict(mode="tm", col0=1024, ncols=1024, dtype=BF16, dsts=[(O["KA"], 0, 0, 1024, False, False)]),
        dict(mode="tm", col0=2048, ncols=1024, dtype=BF16, dsts=[(O["VA"], 0, 0, 1024, False, False)]),
        dict(mode="tm", col0=3072, ncols=1024, dtype=F32, dsts=[(O["OA"], 0, 0, 1024, False, False)]),
        dict(mode="tm", col0=5128, ncols=512, dtype=F32, dsts=[
            (O["k_rows_p"], 0, 0, 256, True, False), (O["k_rows_s"], 0, 0, 256, False, True),
            (O["v_rows_p"], 0, 256, 256, True, False), (O["v_rows_s"], 0, 256, 256, False, True)]),
        dict(mode="tm", col0=6152, ncols=72, dtype=F32, dsts=[
            (O["ik_rows_p"], 0, 0, 64, True, False), (O["ik_rows_s"], 0, 0, 64, False, True),
            (O["WI"], 0, 64, 8, False, False)]),
    ]
    P._bg_d = bgate_d
    _proj_with_bias(P, x1, wv, groups)


def _proj_with_bias(P, x1, wv, groups):
    orig_reset = P.reset_arena

    def reset_and_bias():
        orig_reset()
        bg = P.carve(F32, [1])
        P.dma("sp", bg[0:8, 0:1], P._bg_d, r=[], w=["bias"])
        for g in groups:
            if g.get("bias") == "BG":
                g["bias"] = bg

    P.reset_arena = reset_and_bias
    try:
        proj_phase(P, x1, wv, groups)
    finally:
        P.reset_arena = orig_reset


def pc_layout(v):
    sh = v.shape[:-1]
    return np.ascontiguousarray(v.reshape(sh + (16, 128)).swapaxes(-1, -2))


def odd_proj(P, x5, w_in, O):
    wv = fm(w_in)
    groups = [
        dict(mode="fm", col0=0, ncols=1024, dst=O["QCT"], dtype=F32),
        dict(mode="fm", col0=1024, ncols=1024, dst=O["FCT"], dtype=F32),
        dict(mode="fm", col0=4096, ncols=1024, dst=O["QDT"], dtype=F32),
        dict(mode="fm", col0=5120, ncols=1024, dst=O["KDT"], dtype=F32, scale=1.0 / 16.0),
        dict(mode="tm", col0=1024, ncols=1024, dtype=F32, dsts=[(O["FC"], 0, 0, 1024, False, False)]),
        dict(mode="tm", col0=2048, ncols=1024, dtype=BF16, dsts=[(O["IC"], 0, 0, 1024, False, False)]),
        dict(mode="tm", col0=3072, ncols=1024, dtype=F32, dsts=[(O["GC"], 0, 0, 1024, False, False)]),
        dict(mode="tm", col0=5120, ncols=1024, dtype=F32, dsts=[(O["KD"], 0, 0, 1024, False, False)]),
        dict(mode="tm", col0=6144, ncols=1024, dtype=BF16, dsts=[(O["VD"], 0, 0, 1024, False, False)]),
        dict(mode="tm", col0=7168, ncols=1024, dtype=F32, dsts=[(O["GD"], 0, 0, 1024, False, False)]),
    ]
    proj_phase(P, x5, wv, groups)


def build(stop_after=None, debug_out=(), mixers=True):
    P = Prog(debug_out=debug_out)
    xin = P.inp("xin", [D, NT])
    wup = P.inp("w_ffn_up", [2, 2, D, 2 * DFF])
    wdn = P.inp("w_ffn_down", [2, 2, DFF, D])
    lng = P.inp("ln_g", [2, 3, 128, 16])
    lnb = P.inp("ln_b", [2, 3, 128, 16])
    w_in_even = P.inp("w_in_even", [D, 6224])
    w_in_odd = P.inp("w_in_odd", [D, 8192])
    w_out = P.inp("w_out", [2, D, D])
    w_pg = P.inp("w_pe_gate", [2, D, D])
    w_pp = P.inp("w_pe_proj", [2, 256, D])
    pT = P.inp("pT", [2, 256, NT])
    bgate = P.inp("b_gate", [8, 1])
    x = [None] + [P.scratch("x%d" % i, [D, NT]) for i in range(1, 8)]
    yT_out = P.out("yT_out", [D, NT])
    YT0 = P.scratch("YT0", [D, NT], BF16)
    YT1 = P.scratch("YT1", [D, NT], BF16)
    O = {}
    for nm, shp, dt in [("QAT", [1024, NT], BF16), ("KAT", [1024, NT], BF16), ("GT", [8, NT], F32),
                        ("QBT", [1024, NT], BF16), ("KBT", [256, NT], BF16), ("QIT", [512, NT], BF16),
                        ("KIT", [64, NT], BF16), ("KA", [NT, 1024], BF16), ("VA", [NT, 1024], BF16),
                        ("OA", [NT, 1024], F32), ("WI", [NT, 8], F32),
                        ("QCT", [1024, NT], F32), ("FCT", [1024, NT], F32), ("QDT", [1024, NT], F32),
                        ("KDT", [1024, NT], F32), ("FC", [NT, 1024], F32), ("IC", [NT, 1024], BF16),
                        ("GC", [NT, 1024], F32), ("KD", [NT, 1024], F32), ("VD", [NT, 1024], BF16),
                        ("GD", [NT, 1024], F32)]:
        O[nm] = P.scratch(nm, shp, dt)
    for nm, shp in [("k_rows_p", [SEQ, 256]), ("k_rows_s", [1, 256]), ("v_rows_p", [SEQ, 256]),
                    ("v_rows_s", [1, 256]), ("ik_rows_p", [SEQ, 64]), ("ik_rows_s", [1, 64])]:
        O[nm] = P.out(nm, shp)
    P.O = O
    Cn = {}
    Cn["ident"] = P.inp("c_ident", [128, 128])
    Cn["maskT"] = P.inp("c_maskT", [128, 128])
    Cn["sel4"] = P.inp("c_sel4", [4, 4, 128])
    Cn["g_mlstm_bc"] = P.inp("g_mlstm_bc", [128, 1024])
    Cn["cmask"] = P.inp("c_cmask", [128, 128])
    Cn["page_col"] = P.inp("page_col", [128, 1], I32)
    Cn["cosT"] = P.inp("c_cosT", [128, NT])
    Cn["sinT"] = P.inp("c_sinT", [128, NT])
    Cn["cosK"] = P.inp("c_cosK", [NT, 128])
    Cn["sinK"] = P.inp("c_sinK", [NT, 128])
    Cn["ret_D"] = P.inp("c_ret_D", [4, 2, 128, 128])
    Cn["ret_tails"] = P.inp("c_ret_tails", [128, 4, 17])
    Cn["g_ret_bc"] = P.inp("g_ret_bc", [128, 1024])
    Cn["st_ret"] = P.inp("st_ret", [4, 256, 256])
    Cn["rt_p"] = P.out("rt_p", [4, 256, 256])
    Cn["rt_s"] = P.out("rt_s", [4, 256, 256])
    Cn["lb_bc"] = P.inp("lb_bc", [128, 2, 1024])
    Cn["lb_col"] = P.inp("lb_col", [128, 2, 8])
    Cn["g_hgrn_bc"] = P.inp("g_hgrn_bc", [128, 1024])
    Cn["st_hg"] = P.inp("st_hg", [8, 128, 128])
    Cn["hg_p"] = P.out("hg_p", [8, 128, 128])
    Cn["hg_s"] = P.out("hg_s", [8, 128, 128])
    Cn["cache_ik"] = P.inp("cache_ik", [1280, 8192])
    Cn["cache_k4"] = P.inp("cache_k4", [5120, 8192])
    Cn["cache_v4"] = P.inp("cache_v4", [5120, 8192])
    Cn["k_rows_s"] = O["k_rows_s"]
    Cn["v_rows_s"] = O["v_rows_s"]
    Cn["ik_rows_s"] = O["ik_rows_s"]
    Cn["mC_p"] = P.out("mC_p", [4, 256, 256])
    Cn["mn_p"] = P.out("mn_p", [4, 256])
    Cn["mm_p"] = P.out("mm_p", [4, 1])
    Cn["mC_s"] = P.out("mC_s", [4, 256, 256])
    Cn["mn_s"] = P.out("mn_s", [4, 256])
    Cn["mm_s"] = P.out("mm_s", [1, 4])
    Cn["st_C"] = P.inp("st_C", [4, 256, 256])
    Cn["st_n"] = P.inp("st_n", [4, 256])
    Cn["st_m"] = P.inp("st_m", [1, 4])
    P.C = Cn
    ffn_phase(P, xin, x[1], wup[0, 0], wdn[0, 0], lng[0, 0], lnb[0, 0])
    if stop_after == "ffn1":
        return P
    even_proj(P, x[1], w_in_even, bgate, O)
    if stop_after == "proj0":
        return P
    zero_dram(P, YT0, D, NT, BF16)
    zero_dram(P, YT1, D, NT, BF16)
    if mixers:
        mixers_even(P, O, YT0)
    if stop_after == "mix0":
        return P
    outproj_phase(P, YT0, x[1], x[2], w_out[0], lng[0, 1], lnb[0, 1])
    ffn_phase(P, x[2], x[3], wup[0, 1], wdn[0, 1], lng[0, 2], lnb[0, 2])
    pegate_phase(P, x[3], x[4], pT[0], w_pg[0], w_pp[0])
    ffn_phase(P, x[4], x[5], wup[1, 0], wdn[1, 0], lng[1, 0], lnb[1, 0])
    odd_proj(P, x[5], w_in_odd, O)
    if mixers:
        mixers_odd(P, O, YT1)
    outproj_phase(P, YT1, x[5], x[6], w_out[1], lng[1, 1], lnb[1, 1])
    ffn_phase(P, x[6], x[7], wup[1, 1], wdn[1, 1], lng[1, 2], lnb[1, 2])
    pegate_phase(P, x[7], yT_out, pT[1], w_pg[1], w_pp[1])
    return P


def scan_rows(P, src, tmp, n, op, ksrc, ktmp):
    cur, nxt, kc, kn = src, tmp, ksrc, ktmp
    k = 1
    while k < n:
        P.tt("dve", nxt[0:4, k:n], cur[0:4, k:n], cur[0:4, 0:n - k], op, r=[kc], w=[kn])
        P.copy("act", nxt[0:4, 0:k], cur[0:4, 0:k], r=[kc], w=[kn])
        cur, nxt, kc, kn = nxt, cur, kn, kc
        k *= 2
    return cur, kc


def mlstm_phase(P, O, YT0, C):
    P.reset_arena()
    PS = P.PS
    S_ = SEQ
    row = lambda: P.carve(F32, [NT])
    ig, fa, r0, r1, bt, at, Mt, negM, wrow, Erow = [row() for _ in range(10)]
    cols = P.carve(F32, [3, 16, 4])
    sel = P.carve(F32, [4, 128])
    ident = P.carve(F32, [128])
    identb = P.carve(BF16, [128])
    maskT = P.carve(F32, [128])
    gbc = P.carve(F32, [1024])
    QT = P.carve(BF16, [2, NT])
    KT = P.carve(BF16, [2, NT])
    KAt = P.carve(BF16, [17, 256])
    KW = P.carve(BF16, [16, 256])
    VA1 = P.carve(BF16, [17, 257])
    oat = [P.carve(F32, [256]) for _ in range(2)]
    dwt = [P.carve(F32, [128]) for _ in range(3)]
    pt = [P.carve(BF16, [128]) for _ in range(3)]
    hh = [P.carve(F32, [256]) for _ in range(2)]
    junk = P.carve(F32, [256])
    sm = [P.carve(F32, [8]) for _ in range(2)]
    yab = [P.carve(BF16, [256]) for _ in range(2)]
    yT = [P.carve(BF16, [2, 128]) for _ in range(2)]
    cst = [P.carve(F32, [257]) for _ in range(2)]
    Cin = P.carve(F32, [8, 256])
    Cinb = P.carve(BF16, [8, 256])
    nin = P.carve(F32, [8])
    ninb = P.carve(BF16, [8])
    srow = P.carve(F32, [64])
    bc8 = P.carve(F32, [8])
    qcol = P.carve(BF16, [8])
    kcolf = P.carve(F32, [8])
    nnew = P.carve(F32, [8])
    one1 = P.carve(F32, [1])
    one1b = P.carve(BF16, [1])
    yrow = P.carve(BF16, [1024])
    hrow = P.carve(F32, [1024])
    oarow = P.carve(F32, [1024])

    P.dma("sp", ident[:, :], C["ident"], r=[], w=["ident"])
    P.copy("dve", identb[:, :], ident[:, :], r=["ident"], w=["identb"])
    P.dma("sp", maskT[:, :], C["maskT"], r=[], w=["maskT"])
    P.dma("sp", sel[0:4, :, :], C["sel4"], r=[], w=["sel"])
    P.dma("sp", gbc[:, :], C["g_mlstm_bc"], r=[], w=["gbc"])
    P.dma("sp", ig[0:4, :], O["GT"][0:4, :], r=[("dram", "GT")], w=["ig"])
    P.dma("sp", fa[0:4, :], O["GT"][4:8, :], r=[("dram", "GT")], w=["fa"])
    P.memset("dve", one1[:, :], 1.0, w=["one1"])
    P.memset("dve", one1b[:, :], 1.0, w=["one1b"])
    P.act(fa[0:4, :], fa[0:4, :], AF.Exp, r=["fa"], w=["fa"], scale=-1.0)
    P.act(fa[0:4, :], fa[0:4, :], AF.Ln, r=["fa"], w=["fa"], bias=1.0)
    P.ts("dve", fa[0:4, :], fa[0:4, :], -1.0, None, ALU.mult, None, r=["fa"], w=["fa"])
    P.copy("dve", r0[0:4, 0:S_], fa[0:4, 0:S_], r=["fa"], w=["r0"])
    bres, kb = scan_rows(P, r0, r1, S_, ALU.add, "r0", "r1")
    P.copy("dve", bt[0:4, 0:S_], bres[0:4, 0:S_], r=[kb], w=["bt"])
    P.tt("dve", at[0:4, 0:S_], ig[0:4, 0:S_], bt[0:4, 0:S_], ALU.subtract, r=["ig", "bt"], w=["at"])
    P.copy("dve", r0[0:4, 0:S_], at[0:4, 0:S_], r=["at", kb], w=["r0"])
    P.S.add("dve", None, r=["r1"], w=["r1"])
    mres, km = scan_rows(P, r0, r1, S_, ALU.max, "r0", "r1")
    P.ts("dve", Mt[0:4, 0:S_], mres[0:4, 0:S_], 0.0, None, ALU.max, None, r=[km], w=["Mt"])
    P.ts("dve", negM[0:4, 0:S_], Mt[0:4, 0:S_], -1.0, None, ALU.mult, None, r=["Mt"], w=["negM"])
    P.tt("dve", Erow[0:4, 0:S_], bt[0:4, 0:S_], Mt[0:4, 0:S_], ALU.add, r=["bt", "Mt"], w=["Erow"])
    P.dma("sp", C["mm_p"], Erow[0:4, S_ - 1:S_], r=["Erow"], w=[("dram", "mm_p")])
    P.act(Erow[0:4, 0:S_], Erow[0:4, 0:S_], AF.Exp, r=["Erow"], w=["Erow"], scale=-1.0)
    P.act(wrow[0:4, 0:S_], at[0:4, 0:S_], AF.Exp, r=["at", "negM"], w=["wrow"], bias=negM[0:4, S_ - 1:S_])
    tp = PS[0]
    for qi_, (rt, kr) in enumerate([(at, "at"), (wrow, "wrow"), (Erow, "Erow")]):
        for blk in range(16):
            o = (qi_ * 16 + blk) * 4
            P.mm(tp[:, o:o + 4], rt[0:4, blk * 128:(blk + 1) * 128], ident[0:4, 0:4], True, True,
                 r=[kr, "ident"], w=["ps0_0"])
    P.copy("dve", cols[:, :, :, :].rearrange("p a b c -> p (a b c)"), tp[:, 0:192], r=["ps0_0"], w=["cols"])

    rp = PS[1][0:1, 0:8]
    P.mm(rp[:, 0:4], ig[0:4, S_:NT], ident[0:4, 0:4], True, True, r=["ig", "ident"], w=["ps1a"])
    P.mm(rp[:, 4:8], fa[0:4, S_:NT], ident[0:4, 0:4], True, True, r=["fa", "ident"], w=["ps1a"])
    P.copy("dve", srow[0:1, 0:8], rp, r=["ps1a"], w=["srow"])
    P.dma("sp", srow[0:1, 8:12], C["st_m"], r=[], w=["srow"])
    P.tt("dve", srow[0:1, 12:16], srow[0:1, 4:8], srow[0:1, 8:12], ALU.add, r=["srow"], w=["srow"])
    P.tt("dve", srow[0:1, 16:20], srow[0:1, 12:16], srow[0:1, 0:4], ALU.max, r=["srow"], w=["srow"])
    P.tt("dve", srow[0:1, 20:24], srow[0:1, 0:4], srow[0:1, 16:20], ALU.subtract, r=["srow"], w=["srow"])
    P.tt("dve", srow[0:1, 24:28], srow[0:1, 12:16], srow[0:1, 16:20], ALU.subtract, r=["srow"], w=["srow"])
    P.act(srow[0:1, 20:28], srow[0:1, 20:28], AF.Exp, r=["srow"], w=["srow"])
    P.act(srow[0:1, 28:32], srow[0:1, 16:20], AF.Exp, r=["srow"], w=["srow"], scale=-1.0)
    P.dma("sp", C["mm_s"], srow[0:1, 16:20], r=["srow"], w=[("dram", "mm_s")])
    bp = PS[1][:, 0:8]
    P.mm(bp, maskT[0:1, :], srow[0:1, 20:28], True, True, r=["maskT", "srow"], w=["ps1a"])
    P.copy("dve", bc8[:, :], bp, r=["ps1a"], w=["bc8"])
    P.dma("sp", oarow[0:1, :], O["OA"][S_:NT, :], r=[("dram", "OA")], w=["oarow"])
    P.act(oarow[0:1, :], oarow[0:1, :], AF.Sigmoid, r=["oarow"], w=["oarow"])
    P.tt("dve", oarow[0:1, :], oarow[0:1, :], gbc[0:1, :], ALU.mult, r=["oarow", "gbc"], w=["oarow"])

    QATv, KATv = fm(O["QAT"]), fm(O["KAT"])
    KAv = O["KA"][0:S_, :].rearrange("(b p) c -> p b c", p=128)
    VAv = O["VA"][0:S_, :].rearrange("(b p) c -> p b c", p=128)
    OAv = O["OA"][0:S_, :].rearrange("(b p) c -> p b c", p=128)
    YTv = fm(YT0)
    it = 0
    for h in range(4):
        hs = slice(h * 256, (h + 1) * 256)
        P.dma("sp", QT[:, :, :], QATv[:, 2 * h:2 * h + 2, :], r=[("dram", "QAT")], w=["QT"])
        P.dma("sp", KT[:, :, :], KATv[:, 2 * h:2 * h + 2, :], r=[("dram", "KAT")], w=["KT"])
        P.dma("sp", KAt[:, 0:16, :], KAv[:, :, hs], r=[("dram", "KA")], w=["KAt"])
        P.dma("sp", KAt[0:1, 16, :], O["KA"][S_:NT, hs], r=[("dram", "KA")], w=["KAt"])
        P.dma("sp", VA1[:, 0:16, 0:256], VAv[:, :, hs], r=[("dram", "VA")], w=["VA1"])
        P.dma("sp", VA1[0:1, 16, 0:256], O["VA"][S_:NT, hs], r=[("dram", "VA")], w=["VA1"])
        P.memset("pool", VA1[:, :, 256:257], 1.0, w=["VA1"])
        for tq in range(16):
            ts_ = slice(tq * 128, (tq + 1) * 128)
            nb_ps = PS[1][:, 0:128]
            P.mm(nb_ps, sel[0:4, h, :], negM[0:4, ts_], True, True, r=["sel", "negM"], w=["ps1a"])
            num_ps = PS[2 + (tq % 2)][:, 0:257]
            knum = "ps%d" % (2 + tq % 2)
            for sb in range(tq + 1):
                ss = slice(sb * 128, (sb + 1) * 128)
                sT = PS[0][:, (it % 2) * 512:(it % 2) * 512 + 128]
                ksT = "ps0_%d" % (it % 2)
                P.mm(sT, KT[:, 0, ss], QT[:, 0, ts_], True, False, r=["KT", "QT"], w=[ksT])
                P.mm(sT, KT[:, 1, ss], QT[:, 1, ts_], False, True, r=["KT", "QT"], w=[ksT])
                dw = dwt[it % 3]
                kdw = "dw%d" % (it % 3)
                P.act(dw[:, :], nb_ps, AF.Exp, r=["ps1a", "cols"], w=[kdw], bias=cols[:, 0, sb, h:h + 1])
                if sb == tq:
                    P.tt("pool", dw[:, :], dw[:, :], maskT[:, :], ALU.mult, r=[kdw, "maskT"], w=[kdw])
                p_ = pt[it % 3]
                kp = "pt%d" % (it % 3)
                P.tt("dve", p_[:, :], sT, dw[:, :], ALU.mult, r=[ksT, kdw], w=[kp])
                P.mm(num_ps, p_[:, :], VA1[:, sb, :], sb == 0, sb == tq, r=[kp, "VA1"], w=[knum])
                it += 1
            i2 = tq % 2
            smt = sm[i2]
            ksm = "sm%d" % i2
            P.act(smt[:, 0:1], num_ps[:, 256:257], AF.Abs, r=[knum], w=[ksm])
            P.tt("dve", smt[:, 1:2], smt[:, 0:1], cols[:, 2, tq, h:h + 1], ALU.max, r=[ksm, "cols"], w=[ksm])
            P.S.add("dve", lambda e, a=smt: e.reciprocal(a[:, 2:3], a[:, 1:2]), r=[ksm], w=[ksm])
            hht = hh[i2]
            khh = "hh%d" % i2
            P.ts("dve", hht[:, :], num_ps[:, 0:256], smt[:, 2:3], None, ALU.mult, None, r=[knum, ksm], w=[khh])
            P.act(junk[:, :], hht[:, :], AF.Square, r=[khh], w=["junk", ksm], accum_out=smt[:, 3:4])
            P.ts("dve", smt[:, 4:5], smt[:, 3:4], 1.0 / 256.0, float(LN_EPS), ALU.mult, ALU.add, r=[ksm], w=[ksm])
            P.act(smt[:, 4:5], smt[:, 4:5], AF.Sqrt, r=[ksm], w=[ksm])
            P.S.add("dve", lambda e, a=smt: e.reciprocal(a[:, 5:6], a[:, 4:5]), r=[ksm], w=[ksm])
            oa = oat[i2]
            koa = "oa%d" % i2
            P.dma("sp", oa[:, :], OAv[:, tq, hs], r=[("dram", "OA")], w=[koa])
            P.act(oa[:, :], oa[:, :], AF.Sigmoid, r=[koa], w=[koa])
            P.tt("pool", oa[:, :], oa[:, :], gbc[:, hs], ALU.mult, r=[koa, "gbc"], w=[koa])
            ya = yab[i2]
            kya = "ya%d" % i2
            P.stt(ya[:, :], hht[:, :], smt[:, 5:6], oa[:, :], ALU.mult, ALU.mult, r=[khh, ksm, koa], w=[kya])
            tps = PS[1][:, 512:768].bitcast(BF16)
            yTt = yT[i2]
            kyT = "yT%d" % i2
            for j in range(2):
                P.S.add("pe", lambda e, o=tps[:, j * 128:(j + 1) * 128], i=ya[:, j * 128:(j + 1) * 128]:
                        e.transpose(o, i, identb[:, :]), r=[kya, "identb"], w=["ps1b"])
            P.copy("act", yTt[:, :, :].rearrange("p a b -> p (a b)"), tps[:, 0:256], r=["ps1b"], w=[kyT])
            P.dma("sp", YTv[:, 2 * h:2 * h + 2, ts_], yTt[:, :, :], r=[kyT], w=[("dram", YT0.name)])
        for sb in range(16):
            P.ts("dve", KW[:, sb, :], KAt[:, sb, :], cols[:, 1, sb, h:h + 1], None, ALU.mult, None,
                 r=["KAt", "cols"], w=["KW"])
        for dcn in range(2):
            c_ps = PS[2 + dcn][:, 0:257]
            kc = "ps%d" % (2 + dcn)
            for sb in range(16):
                P.mm(c_ps, KW[:, sb, dcn * 128:(dcn + 1) * 128], VA1[:, sb, :], sb == 0, sb == 15,
                     r=["KW", "VA1"], w=[kc])
            ct = cst[dcn]
            kct = "cst%d" % dcn
            P.copy("act", ct[:, :], c_ps, r=[kc], w=[kct])
            P.dma("sp", C["mC_p"][h, dcn * 128:(dcn + 1) * 128, :], ct[:, 0:256], r=[kct], w=[("dram", "mC_p")])
            P.dma("sp", C["mn_p"][h:h + 1, dcn * 128:(dcn + 1) * 128].rearrange("a d -> d a"), ct[:, 256:257],
                  r=[kct], w=[("dram", "mn_p")])
        P.dma("sp", Cin[:, 0:2, :], C["st_C"][h].rearrange("(c p) v -> p c v", p=128), r=[], w=["Cin"])
        P.dma("sp", nin[:, 0:2], C["st_n"][h:h + 1, :].rearrange("a (c p) -> p (a c)", p=128), r=[], w=["nin"])
        P.copy("act", Cinb[:, 0:2, :], Cin[:, 0:2, :], r=["Cin"], w=["Cinb"])
        P.copy("dve", ninb[:, 0:2], nin[:, 0:2], r=["nin"], w=["ninb"])
        P.copy("dve", kcolf[:, 0:2], KT[:, :, S_], r=["KT"], w=["kcolf"])
        rps = PS[1][0:1, 0:512]
        for dcn in range(2):
            P.mm(rps[:, 0:256], QT[:, dcn, S_:NT], Cinb[:, dcn, :], dcn == 0, dcn == 1, r=["QT", "Cinb"], w=["ps1a"])
        for dcn in range(2):
            P.mm(rps[:, 256:257], QT[:, dcn, S_:NT], KT[:, dcn, S_:NT], dcn == 0, dcn == 1, r=["QT", "KT"], w=["ps1a"])
        for dcn in range(2):
            P.mm(rps[:, 257:258], QT[:, dcn, S_:NT], ninb[:, dcn:dcn + 1], dcn == 0, dcn == 1, r=["QT", "ninb"], w=["ps1a"])
        dwh, gwh, emh = srow[0:1, 20 + h:21 + h], srow[0:1, 24 + h:25 + h], srow[0:1, 28 + h:29 + h]
        sc = srow[0:1, 32:48]
        P.ts("dve", sc[:, 0:1], rps[:, 256:257], dwh, None, ALU.mult, None, r=["ps1a", "srow"], w=["sc"])
        P.ts("dve", sc[:, 1:2], rps[:, 257:258], gwh, None, ALU.mult, None, r=["ps1a", "srow"], w=["sc"])
        P.tt("dve", sc[:, 2:3], sc[:, 0:1], sc[:, 1:2], ALU.add, r=["sc"], w=["sc"])
        P.act(sc[:, 3:4], sc[:, 2:3], AF.Abs, r=["sc"], w=["sc"])
        P.tt("dve", sc[:, 4:5], sc[:, 3:4], emh, ALU.max, r=["sc", "srow"], w=["sc"])
        P.S.add("dve", lambda e, a=sc: e.reciprocal(a[:, 5:6], a[:, 4:5]), r=["sc"], w=["sc"])
        P.ts("dve", hrow[0:1, hs], rps[:, 0:256], gwh, None, ALU.mult, None, r=["ps1a", "srow"], w=["hrow"])
        P.stt(hrow[0:1, hs], VA1[0:1, 16, 0:256], sc[:, 0:1], hrow[0:1, hs], ALU.mult, ALU.add,
              r=["VA1", "sc", "hrow"], w=["hrow"])
        P.ts("dve", hrow[0:1, hs], hrow[0:1, hs], sc[:, 5:6], None, ALU.mult, None, r=["hrow", "sc"], w=["hrow"])
        P.act(junk[0:1, :], hrow[0:1, hs], AF.Square, r=["hrow"], w=["junk", "sc"], accum_out=sc[:, 6:7])
        P.ts("dve", sc[:, 7:8], sc[:, 6:7], 1.0 / 256.0, float(LN_EPS), ALU.mult, ALU.add, r=["sc"], w=["sc"])
        P.act(sc[:, 7:8], sc[:, 7:8], AF.Sqrt, r=["sc"], w=["sc"])
        P.S.add("dve", lambda e, a=sc: e.reciprocal(a[:, 8:9], a[:, 7:8]), r=["sc"], w=["sc"])
        P.stt(yrow[0:1, hs], hrow[0:1, hs], sc[:, 8:9], oarow[0:1, hs], ALU.mult, ALU.mult,
              r=["hrow", "sc", "oarow"], w=["yrow"])
        cps = PS[1][:, 512:514]
        for j in range(2):
            P.mm(cps[:, j:j + 1], yrow[0:1, h * 256 + j * 128: h * 256 + (j + 1) * 128], one1b[0:1, 0:1], True, True,
                 r=["yrow", "one1b"], w=["ps1b"])
        P.copy("dve", qcol[:, 0:2], cps, r=["ps1b"], w=["qcol"])
        P.dma("sp", YTv[:, 2 * h:2 * h + 2, S_], qcol[:, 0:2], r=["qcol"], w=[("dram", YT0.name)])
        P.ts("dve", yrow[0:1, 0:256] if False else KW[0:1, 0, :], KAt[0:1, 16, :], dwh, None, ALU.mult, None,
             r=["KAt", "srow", "KW"], w=["KW"])
        for dcn in range(2):
            kv_ps = PS[2 + dcn][:, 0:256]
            kc = "ps%d" % (2 + dcn)
            P.mm(kv_ps, KW[0:1, 0, dcn * 128:(dcn + 1) * 128], VA1[0:1, 16, 0:256], True, True,
                 r=["KW", "VA1"], w=[kc])
            ct = cst[dcn]
            kct = "cst%d" % dcn
            P.stt(ct[:, 0:256], Cin[:, dcn, :], bc8[:, 4 + h:5 + h], kv_ps, ALU.mult, ALU.add,
                  r=["Cin", "bc8", kc], w=[kct])
            P.dma("sp", C["mC_s"][h, dcn * 128:(dcn + 1) * 128, :], ct[:, 0:256], r=[kct], w=[("dram", "mC_s")])
        P.ts("dve", kcolf[:, 0:2], kcolf[:, 0:2], bc8[:, h:h + 1], None, ALU.mult, None, r=["kcolf", "bc8"], w=["kcolf"])
        P.stt(nnew[:, 0:2], nin[:, 0:2], bc8[:, 4 + h:5 + h], kcolf[:, 0:2], ALU.mult, ALU.add,
              r=["nin", "bc8", "kcolf"], w=["nnew"])
        P.dma("sp", C["mn_s"][h:h + 1, :].rearrange("a (c p) -> p (a c)", p=128), nnew[:, 0:2], r=["nnew"],
              w=[("dram", "mn_s")])
    P.mlstm_sample_done = True


def dsa_prompt_phase(P, O, YT0, C):
    P.reset_arena()
    PS = P.PS
    S_ = SEQ
    NSEL = 256
    QI = P.carve(BF16, [4, NT])
    KI2 = P.carve(BF16, [NT])
    QB = P.carve(BF16, [8, NT])
    KB = P.carve(BF16, [2, NT])
    VBf = P.carve(F32, [2, 256])
    VB = P.carve(BF16, [16, 256])
    WIc = P.carve(F32, [16, 8])
    acc = P.carve(F32, [S_])
    rl = [P.carve(F32, [S_]) for _ in range(2)]
    junk = P.carve(BF16, [S_])
    mask = P.carve(BF16, [S_])
    pp = [P.carve(BF16, [S_]) for _ in range(2)]
    pm = [P.carve(BF16, [S_]) for _ in range(2)]
    pT = [P.carve(BF16, [8, 128]) for _ in range(2)]
    ybt = P.carve(BF16, [1024])
    yT = P.carve(BF16, [8, 128])
    cmask = P.carve(F32, [128])
    ident = P.carve(F32, [128])
    identb = P.carve(BF16, [128])
    sm = P.carve(F32, [16])
    hsm = [P.carve(F32, [4]) for _ in range(2)]
    P.dma("sp", ident[:, :], C["ident"], r=[], w=["ident"])
    P.copy("dve", identb[:, :], ident[:, :], r=["ident"], w=["identb"])
    P.dma("sp", cmask[:, :], C["cmask"], r=[], w=["cmask"])
    P.dma("sp", QI[:, :, :], fm(O["QIT"]), r=[("dram", "QIT")], w=["QI"])
    P.dma("sp", KI2[0:64, :], O["KIT"], r=[("dram", "KIT")], w=["KI2"])
    P.dma("sp", KI2[64:128, :], O["KIT"], r=[("dram", "KIT")], w=["KI2"])
    P.dma("sp", QB[:, :, :], fm(O["QBT"]), r=[("dram", "QBT")], w=["QB"])
    P.dma("sp", KB[:, :, :], fm(O["KBT"]), r=[("dram", "KBT")], w=["KB"])
    P.dma("sp", WIc[:, :, :], O["WI"][0:S_, :].rearrange("(b p) h -> p b h", p=128), r=[("dram", "WI")], w=["WIc"])
    vv = O["v_rows_p"].rearrange("(b p) c -> p b c", p=128)
    for b2 in range(0, 16, 2):
        P.dma("sp", VBf[:, :, :], vv[:, b2:b2 + 2, :], r=[("dram", "v_rows_p")], w=["VBf"])
        P.copy("dve", VB[:, b2:b2 + 2, :], VBf[:, :, :], r=["VBf"], w=["VB"])
    YTv = fm(YT0)
    for tq in range(16):
        ts_ = slice(tq * 128, (tq + 1) * 128)
        nk = (tq + 1) * 128
        halves = [(o, min(1024, nk - o)) for o in range(0, nk, 1024)]
        for h in range(8):
            pb = (h % 2) * 64
            r_ = rl[h % 2]
            kr = "rl%d" % (h % 2)
            for (o, n) in halves:
                ps = PS[(h % 2) * 2 + o // 1024]
                kps = "ps%d" % ((h % 2) * 2 + o // 1024)
                for (bo, bn) in blocks(n):
                    P.mm(ps[:, bo:bo + bn], QI[pb:pb + 64, h // 2, ts_], KI2[pb:pb + 64, o + bo:o + bo + bn], True, True,
                         r=["QI", "KI2"], w=[kps])
                P.act(r_[:, o:o + n], ps[:, 0:n], AF.Relu, r=[kps], w=[kr])
            if h == 0:
                P.ts("dve", acc[:, 0:nk], r_[:, 0:nk], WIc[:, tq, 0:1], None, ALU.mult, None, r=[kr, "WIc"], w=["acc"])
            else:
                P.stt(acc[:, 0:nk], r_[:, 0:nk], WIc[:, tq, h:h + 1], acc[:, 0:nk], ALU.mult, ALU.add,
                      r=[kr, "WIc", "acc"], w=["acc"])
        P.S.add("dve", lambda e, a=acc, n_=nk: e.tensor_reduce(sm[:, 0:1], a[:, 0:n_], AX.X, ALU.min), r=["acc"], w=["sm"])
        P.S.add("dve", lambda e, a=acc, n_=nk: e.tensor_reduce(sm[:, 1:2], a[:, 0:n_], AX.X, ALU.max), r=["acc"], w=["sm"])
        P.tt("dve", acc[:, nk - 128:nk], acc[:, nk - 128:nk], cmask[:, :], ALU.add, r=["acc", "cmask"], w=["acc"])
        if tq >= 2:
            P.tt("dve", sm[:, 2:3], sm[:, 1:2], sm[:, 0:1], ALU.subtract, r=["sm"], w=["sm"])
            for it in range(26):
                ci = 0.5 ** (it + 1)
                P.ts("dve", sm[:, 3:4], sm[:, 2:3], float(ci), sm[:, 0:1], ALU.mult, ALU.add, r=["sm"], w=["sm"])
                P.ts("dve", junk[:, 0:nk], acc[:, 0:nk], sm[:, 3:4], 0.0, ALU.is_ge, ALU.add, r=["acc", "sm"],
                     w=["junk", "sm"], accum_out=sm[:, 4:5])
                P.ts("dve", sm[:, 5:6], sm[:, 4:5], NSEL - 0.5, float(ci), ALU.is_ge, ALU.mult, r=["sm"], w=["sm"])
                P.stt(sm[:, 0:1], sm[:, 5:6], sm[:, 2:3], sm[:, 0:1], ALU.mult, ALU.add, r=["sm"], w=["sm"])
        else:
            P.memset("dve", sm[:, 0:1], -1.0e29, w=["sm"])
        P.ts("dve", mask[:, 0:nk], acc[:, 0:nk], sm[:, 0:1], None, ALU.is_ge, None, r=["acc", "sm"], w=["mask"])
        for hq in range(8):
            kv = hq // 4
            hs_ = hsm[hq % 2]
            khs = "hsm%d" % (hq % 2)
            lps = []
            for (o, n) in halves:
                ps = PS[o // 1024]
                kps = "ps%d" % (o // 1024)
                for (bo, bn) in blocks(n):
                    P.mm(ps[:, bo:bo + bn], QB[:, hq, ts_], KB[:, kv, o + bo:o + bo + bn], True, True,
                         r=["QB", "KB"], w=[kps])
                lps.append((ps, kps, o, n))
            for i_, (ps, kps, o, n) in enumerate(lps):
                P.S.add("dve", lambda e, d=hs_[:, i_:i_ + 1], a=ps[:, 0:n]: e.tensor_reduce(d, a, AX.X, ALU.max),
                        r=[kps], w=[khs])
            if len(lps) == 2:
                P.tt("dve", hs_[:, 0:1], hs_[:, 0:1], hs_[:, 1:2], ALU.max, r=[khs], w=[khs])
            P.ts("dve", hs_[:, 2:3], hs_[:, 0:1], -1.0, None, ALU.mult, None, r=[khs], w=[khs])
            p_ = pp[hq % 2]
            kp = "pp%d" % (hq % 2)
            for (ps, kps, o, n) in lps:
                P.act(p_[:, o:o + n], ps[:, 0:n], AF.Exp, r=[kps, khs], w=[kp], bias=hs_[:, 2:3])
            pm_ = pm[hq % 2]
            kpm = "pm%d" % (hq % 2)
            P.stt(pm_[:, 0:nk], p_[:, 0:nk], 1.0, mask[:, 0:nk], ALU.mult, ALU.mult, r=[kp, "mask"], w=[kpm, khs],
                  accum_out=hs_[:, 3:4])
            o_ps = PS[3][:, 0:128]
            nsb = tq + 1
            for g0 in range(0, nsb, 8):
                gn = min(8, nsb - g0)
                gi = (g0 // 8) % 2
                tps = PS[2][:, gi * 512:(gi + 1) * 512].bitcast(BF16)
                ktp = "ps2_%d" % gi
                for j in range(gn):
                    sb = g0 + j
                    P.S.add("pe", lambda e, o_=tps[:, j * 128:(j + 1) * 128], i=pm_[:, sb * 128:(sb + 1) * 128]:
                            e.transpose(o_, i, identb[:, :]), r=[kpm, "identb"], w=[ktp])
                pT_ = pT[gi]
                kpT = "pT%d" % gi
                P.copy("act" if gi == 0 else "dve", pT_[:, 0:gn, :].rearrange("p a b -> p (a b)"), tps[:, 0:gn * 128],
                       r=[ktp], w=[kpT])
                for j in range(gn):
                    sb = g0 + j
                    P.mm(o_ps, pT_[:, j, :], VB[:, sb, kv * 128:(kv + 1) * 128], sb == 0, sb == nsb - 1,
                         r=[kpT, "VB"], w=["ps3a"])
            P.S.add("dve", lambda e, a=hs_: e.reciprocal(a[:, 3:4], a[:, 3:4]), r=[khs], w=[khs])
            P.ts("dve", ybt[:, hq * 128:(hq + 1) * 128], o_ps, hs_[:, 3:4], None, ALU.mult, None, r=["ps3a", khs],
                 w=["ybt"])
        tps = PS[3][:, 512:1024].bitcast(BF16)
        for hq in range(8):
            P.S.add("pe", lambda e, o_=tps[:, hq * 128:(hq + 1) * 128], i=ybt[:, hq * 128:(hq + 1) * 128]:
                    e.transpose(o_, i, identb[:, :]), r=["ybt", "identb"], w=["ps3b"])
        P.copy("act", yT[:, :, :].rearrange("p a b -> p (a b)"), tps[:, 0:1024], r=["ps3b"], w=["yT"])
        P.dma("sp", YTv[:, 8:16, ts_], yT[:, :, :], r=["yT"], w=[("dram", YT0.name)])


def row_from_cols(P, cols_bf, nchunk, identb, ps_row, kps, dst_row, kdst, rkeys):
    for i, cap in enumerate(cols_bf):
        P.mm(ps_row[0:1, i * 128:(i + 1) * 128], cap, identb[:, :], True, True, r=list(rkeys) + ["identb"], w=[kps])
    P.copy("dve", dst_row[0:1, 0:nchunk * 128], ps_row[0:1, 0:nchunk * 128], r=[kps], w=[kdst])


def dsa_sample_phase(P, O, YT0, C):
    P.reset_arena()
    PS = P.PS
    S_ = SEQ
    NSEL = 256
    KIg = P.carve(F32, [128, 64])
    tmp = P.carve(F32, [8192])
    KV = [P.carve(F32, [32, 256]) for _ in range(2)]
    sc = P.carve(F32, [129])
    dot = P.carve(F32, [128])
    lg = P.carve(F32, [8, 129])
    pmat = P.carve(F32, [8, 129])
    mask = P.carve(F32, [129])
    qicol = P.carve(BF16, [4])
    qbcol = P.carve(BF16, [8])
    qirow = P.carve(F32, [512])
    qbrow = P.carve(F32, [1024])
    qibc = P.carve(F32, [512])
    qbbc = P.carve(F32, [1024])
    wrow = P.carve(F32, [8])
    wbc = P.carve(F32, [8])
    kis = P.carve(F32, [64])
    ksf = P.carve(F32, [256])
    vsf = P.carve(F32, [256])
    srow = P.carve(F32, [64])
    ident = P.carve(F32, [128])
    identb = P.carve(BF16, [128])
    ones = P.carve(F32, [128])
    pt_i = P.carve(I32, [1])
    pt_f = P.carve(F32, [1])
    idx = [P.carve(I32, [1]) for _ in range(4)]
    sm = P.carve(F32, [16])
    sm8 = P.carve(F32, [32])
    dg = P.carve(F32, [8])
    yrow = P.carve(F32, [1024])
    yrowb = P.carve(BF16, [1024])
    ycol = P.carve(BF16, [8])
    junk = P.carve(F32, [129])
    P.dma("sp", ident[:, :], C["ident"], r=[], w=["ident"])
    P.copy("dve", identb[:, :], ident[:, :], r=["ident"], w=["identb"])
    P.memset("dve", ones[:, :], 1.0, w=["ones"])
    P.dma("sp", pt_i[:, :], C["page_col"], r=[], w=["pt_i"])
    P.copy("dve", pt_f[:, :], pt_i[:, :], r=["pt_i"], w=["pt_f"])
    for q in range(4):
        P.ts("dve", idx[q][:, :], pt_f[:, :], 4.0, float(q), ALU.mult, ALU.add, r=["pt_f"], w=["idx%d" % q])
    P.S.add("pool", lambda e: e.indirect_dma_start(
        out=KIg[:, :, :].rearrange("p a b -> p (a b)"), out_offset=None, in_=C["cache_ik"],
        in_offset=bass.IndirectOffsetOnAxis(ap=pt_i[:, :], axis=0)), r=["pt_i"], w=["KIg"], dma=True)
    P.dma("sp", qicol[:, :], fm(O["QIT"])[:, :, S_], r=[("dram", "QIT")], w=["qicol"])
    P.dma("sp", qbcol[:, :], fm(O["QBT"])[:, :, S_], r=[("dram", "QBT")], w=["qbcol"])
    rp = PS[0]
    row_from_cols(P, [qicol[:, i:i + 1] for i in range(4)], 4, identb, rp[:, 0:512], "ps0", qirow, "qirow", ["qicol"])
    row_from_cols(P, [qbcol[:, i:i + 1] for i in range(8)], 8, identb, rp[:, 0:1024] if False else PS[1][:, 0:1024],
                  "ps1", qbrow, "qbrow", ["qbcol"])
    P.dma("sp", wrow[0:1, :], O["WI"][S_:NT, :], r=[("dram", "WI")], w=["wrow"])
    P.mm(PS[2][:, 0:512], ones[0:1, :], qirow[0:1, :], True, True, r=["ones", "qirow"], w=["ps2"])
    P.copy("act", qibc[:, :], PS[2][:, 0:512], r=["ps2"], w=["qibc"])
    for i in range(2):
        P.mm(PS[3][:, i * 512:(i + 1) * 512], ones[0:1, :], qbrow[0:1, i * 512:(i + 1) * 512], True, True,
             r=["ones", "qbrow"], w=["ps3"])
    P.copy("act", qbbc[:, :], PS[3][:, 0:1024], r=["ps3"], w=["qbbc"])
    P.mm(PS[2][:, 512:520], ones[0:1, :], wrow[0:1, :], True, True, r=["ones", "wrow"], w=["ps2b"])
    P.copy("dve", wbc[:, :], PS[2][:, 512:520], r=["ps2b"], w=["wbc"])
    qi3 = qibc[:, :].rearrange("p (h d) -> p h d", h=8)
    t3 = tmp[:, :].rearrange("p (a b) -> p a b", a=128)
    for h in range(8):
        P.tt("dve", t3, KIg[:, :, :], qi3[:, h:h + 1, :].broadcast_to([128, 128, 64]), ALU.mult,
             r=["KIg", "qibc"], w=["tmp"])
        P.S.add("dve", lambda e: e.tensor_reduce(dot[:, :], t3, AX.X, ALU.add), r=["tmp"], w=["dot"])
        P.act(dot[:, :], dot[:, :], AF.Relu, r=["dot"], w=["dot"])
        if h == 0:
            P.ts("dve", sc[:, 0:128], dot[:, :], wbc[:, 0:1], None, ALU.mult, None, r=["dot", "wbc"], w=["sc"])
        else:
            P.stt(sc[:, 0:128], dot[:, :], wbc[:, h:h + 1], sc[:, 0:128], ALU.mult, ALU.add, r=["dot", "wbc", "sc"], w=["sc"])
    P.dma("sp", kis[0:1, :], C["ik_rows_s"], r=[("dram", "ik_rows_s")], w=["kis"])
    q3r = qirow[0:1, :].rearrange("p (h d) -> p h d", h=8)
    P.tt("dve", tmp[0:1, 0:512].rearrange("p (h d) -> p h d", h=8), q3r, kis[0:1, :].rearrange("p (a d) -> p a d", a=1).broadcast_to([1, 8, 64]),
         ALU.mult, r=["qirow", "kis", "tmp"], w=["tmp"])
    P.S.add("dve", lambda e: e.tensor_reduce(srow[0:1, 0:8], tmp[0:1, 0:512].rearrange("p (h d) -> p h d", h=8), AX.X, ALU.add),
            r=["tmp"], w=["srow"])
    P.act(srow[0:1, 0:8], srow[0:1, 0:8], AF.Relu, r=["srow"], w=["srow"])
    P.tt("dve", srow[0:1, 0:8], srow[0:1, 0:8], wrow[0:1, :], ALU.mult, r=["srow", "wrow"], w=["srow"])
    P.memset("dve", sc[:, 128:129], -1.0e30, w=["sc"])
    P.S.add("dve", lambda e: e.tensor_reduce(sc[0:1, 128:129], srow[0:1, 0:8], AX.X, ALU.add), r=["srow", "sc"], w=["sc"])
    P.S.add("dve", lambda e: e.tensor_reduce(sm[:, 0:1], sc[:, 0:128], AX.X, ALU.min), r=["sc"], w=["sm"])
    P.S.add("dve", lambda e: e.tensor_reduce(sm[:, 1:2], sc[:, 0:129], AX.X, ALU.max), r=["sc"], w=["sm"])
    P.mm(PS[0][0:1, 0:128], sm[:, 0:1], ident[:, :], True, True, r=["sm", "ident"], w=["ps0"])
    P.mm(PS[0][0:1, 128:256], sm[:, 1:2], ident[:, :], True, True, r=["sm", "ident"], w=["ps0"])
    P.S.add("dve", lambda e: e.tensor_reduce(srow[0:1, 8:9], PS[0][0:1, 0:128], AX.X, ALU.min), r=["ps0"], w=["srow"])
    P.S.add("dve", lambda e: e.tensor_reduce(srow[0:1, 9:10], PS[0][0:1, 128:256], AX.X, ALU.max), r=["ps0"], w=["srow"])
    P.tt("dve", srow[0:1, 9:10], srow[0:1, 9:10], srow[0:1, 8:9], ALU.subtract, r=["srow"], w=["srow"])
    P.mm(PS[0][:, 512:514], ones[0:1, :], srow[0:1, 8:10], True, True, r=["ones", "srow"], w=["ps0b"])
    P.copy("dve", sm[:, 0:1], PS[0][:, 512:513], r=["ps0b"], w=["sm"])
    P.copy("dve", sm[:, 2:3], PS[0][:, 513:514], r=["ps0b"], w=["sm"])
    for it in range(26):
        ci = 0.5 ** (it + 1)
        P.ts("dve", sm[:, 3:4], sm[:, 2:3], float(ci), sm[:, 0:1], ALU.mult, ALU.add, r=["sm"], w=["sm"])
        P.ts("dve", junk[:, 0:129], sc[:, 0:129], sm[:, 3:4], 0.0, ALU.is_ge, ALU.add, r=["sc", "sm"],
             w=["junk", "sm"], accum_out=sm[:, 4:5])
        P.mm(PS[0][:, 520:521], ones[:, :], sm[:, 4:5], True, True, r=["ones", "sm"], w=["ps0b"])
        P.ts("dve", sm[:, 5:6], PS[0][:, 520:521], NSEL - 0.5, float(ci), ALU.is_ge, ALU.mult, r=["ps0b"], w=["sm"])
        P.stt(sm[:, 0:1], sm[:, 5:6], sm[:, 2:3], sm[:, 0:1], ALU.mult, ALU.add, r=["sm"], w=["sm"])
    P.ts("dve", mask[:, 0:129], sc[:, 0:129], sm[:, 0:1], None, ALU.is_ge, None, r=["sc", "sm"], w=["mask"])
    qb3 = qbbc[:, :].rearrange("p (h d) -> p h d", h=8)
    ck = C["cache_k4"]
    cv = C["cache_v4"]
    tk = tmp[:, 0:4096].rearrange("p (a b) -> p a b", a=32)
    for q in range(4):
        kvb = KV[q % 2]
        kk = "KV%d" % (q % 2)
        P.S.add("pool", lambda e, kvb=kvb, q=q: e.indirect_dma_start(
            out=kvb[:, :, :].rearrange("p a b -> p (a b)"), out_offset=None, in_=ck,
            in_offset=bass.IndirectOffsetOnAxis(ap=idx[q][:, :], axis=0)), r=["idx%d" % q], w=[kk], dma=True)
        for hq in range(8):
            kv = hq // 4
            P.tt("dve", tk, kvb[:, :, kv * 128:(kv + 1) * 128], qb3[:, hq:hq + 1, :].broadcast_to([128, 32, 128]),
                 ALU.mult, r=[kk, "qbbc"], w=["tmp"])
            P.S.add("dve", lambda e, hq=hq, q=q: e.tensor_reduce(lg[:, hq, q * 32:(q + 1) * 32], tk, AX.X, ALU.add),
                    r=["tmp"], w=["lg"])
    P.dma("sp", ksf[0:1, :], C["k_rows_s"], r=[("dram", "k_rows_s")], w=["ksf"])
    P.dma("sp", vsf[0:1, :], C["v_rows_s"], r=[("dram", "v_rows_s")], w=["vsf"])
    P.memset("dve", lg[:, :, 128:129], 0.0, w=["lg"])
    for kv in range(2):
        P.tt("dve", tmp[0:1, kv * 512:(kv + 1) * 512].rearrange("p (h d) -> p h d", h=4),
             qbrow[0:1, kv * 512:(kv + 1) * 512].rearrange("p (h d) -> p h d", h=4),
             ksf[0:1, kv * 128:(kv + 1) * 128].rearrange("p (a d) -> p a d", a=1).broadcast_to([1, 4, 128]),
             ALU.mult, r=["qbrow", "ksf", "tmp"], w=["tmp"])
    P.S.add("dve", lambda e: e.tensor_reduce(lg[0:1, :, 128:129].rearrange("p h a -> p (h a)"),
                                             tmp[0:1, 0:1024].rearrange("p (h d) -> p h d", h=8), AX.X, ALU.add),
            r=["tmp", "lg"], w=["lg"])
    for hq in range(8):
        P.S.add("dve", lambda e, hq=hq: e.tensor_reduce(sm8[:, hq:hq + 1], lg[:, hq, :], AX.X, ALU.max), r=["lg"], w=["sm8"])
    P.mm(PS[1][0:8, 0:128], sm8[:, 0:8], ident[:, :], True, True, r=["sm8", "ident"], w=["ps1"])
    P.S.add("dve", lambda e: e.tensor_reduce(sm8[0:8, 8:9], PS[1][0:8, 0:128], AX.X, ALU.max), r=["ps1"], w=["sm8"])
    P.ts("dve", dg[0:8, :], ident[0:8, 0:8], sm8[0:8, 8:9], None, ALU.mult, None, r=["ident", "sm8"], w=["dg"])
    P.mm(PS[1][:, 512:520], ones[0:8, :], dg[0:8, :], True, True, r=["ones", "dg"], w=["ps1b"])
    P.ts("dve", sm8[:, 16:24], PS[1][:, 512:520], -1.0, None, ALU.mult, None, r=["ps1b"], w=["sm8"])
    for hq in range(8):
        P.act(pmat[:, hq, :], lg[:, hq, :], AF.Exp, r=["lg", "sm8"], w=["pmat"], bias=sm8[:, 16 + hq:17 + hq])
        P.stt(pmat[:, hq, :], pmat[:, hq, :], 1.0, mask[:, :], ALU.mult, ALU.mult, r=["pmat", "mask"], w=["pmat", "sm8"],
              accum_out=sm8[:, 24 + hq:25 + hq])
    P.mm(PS[1][:, 520:528], ones[:, :], sm8[:, 24:32], True, True, r=["ones", "sm8"], w=["ps1b"])
    P.copy("dve", srow[0:1, 16:24], PS[1][0:1, 520:528], r=["ps1b"], w=["srow"])
    P.S.add("dve", lambda e: e.reciprocal(srow[0:1, 16:24], srow[0:1, 16:24]), r=["srow"], w=["srow"])
    acc_reg = [PS[hq // 2][0:1, (hq % 2) * 512:(hq % 2) * 512 + 128] for hq in range(8)]
    acc_key = ["psacc%d" % hq for hq in range(8)]
    for q in range(4):
        kvb = KV[q % 2]
        kk = "KV%d" % (q % 2)
        P.S.add("pool", lambda e, kvb=kvb, q=q: e.indirect_dma_start(
            out=kvb[:, :, :].rearrange("p a b -> p (a b)"), out_offset=None, in_=cv,
            in_offset=bass.IndirectOffsetOnAxis(ap=idx[q][:, :], axis=0)), r=["idx%d" % q], w=[kk], dma=True)
        for hq in range(8):
            kv = hq // 4
            dst = acc_reg[hq]
            for o in range(32):
                P.mm(dst, pmat[:, hq, q * 32 + o:q * 32 + o + 1], kvb[:, o, kv * 128:(kv + 1) * 128],
                     q == 0 and o == 0, q == 3 and o == 31, r=["pmat", kk, "ps0", "ps1", "ps1b", "ps0b", "ps2", "ps3", "ps2b"],
                     w=[acc_key[hq]])
    for hq in range(8):
        kv = hq // 4
        src = acc_reg[hq]
        P.stt(yrow[0:1, hq * 128:(hq + 1) * 128], vsf[0:1, kv * 128:(kv + 1) * 128], pmat[0:1, hq, 128:129], src,
              ALU.mult, ALU.add, r=["vsf", "pmat", acc_key[hq]], w=["yrow"])
        P.ts("dve", yrowb[0:1, hq * 128:(hq + 1) * 128], yrow[0:1, hq * 128:(hq + 1) * 128], srow[0:1, 16 + hq:17 + hq],
             None, ALU.mult, None, r=["yrow", "srow"], w=["yrowb"])
    one1b = identb[0:1, 0:1]
    for hq in range(8):
        P.mm(PS[0][:, 600 + hq:601 + hq], yrowb[0:1, hq * 128:(hq + 1) * 128], one1b, True, True,
             r=["yrowb", "identb"], w=["ps0c"] + acc_key)
    P.copy("dve", ycol[:, :], PS[0][:, 600:608], r=["ps0c"], w=["ycol"])
    P.dma("sp", fm(YT0)[:, 8:16, S_], ycol[:, :], r=["ycol"], w=[("dram", YT0.name)])


def mixers_even(P, O, YT0):
    mlstm_phase(P, O, YT0, P.C)
    dsa_prompt_phase(P, O, YT0, P.C)
    dsa_sample_phase(P, O, YT0, P.C)


def hgrn_phase(P, O, YT1, C):
    P.reset_arena()
    PS = P.PS
    ident = P.carve(F32, [128])
    identb = P.carve(BF16, [128])
    maskT = P.carve(F32, [128])
    mask8 = P.carve(F32, [8, 128])
    lbb = P.carve(F32, [2, 1024])
    lbc = P.carve(F32, [2, 8])
    omlc = P.carve(F32, [8])
    gbc = P.carve(F32, [1024])
    St = P.carve(F32, [8, 128])
    Stmp = P.carve(F32, [8, 128])
    Sb = P.carve(BF16, [8, 128])
    junk = P.carve(F32, [1024])
    fct = [P.carve(F32, [1024]) for _ in range(2)]
    eb = [P.carve(F32, [8, 128]) for _ in range(2)]
    enb = [P.carve(F32, [8, 128]) for _ in range(2)]
    qct = [P.carve(F32, [8, 128]) for _ in range(2)]
    fcT = [P.carve(F32, [8, 128]) for _ in range(2)]
    gdt = [P.carve(F32, [1024]) for _ in range(2)]
    sm = [P.carve(F32, [32]) for _ in range(2)]
    QT = [P.carve(BF16, [8, 128]) for _ in range(2)]
    KT = [P.carve(BF16, [8, 128]) for _ in range(2)]
    Ktm = [P.carve(BF16, [8, 128]) for _ in range(2)]
    vt = [P.carve(BF16, [1024]) for _ in range(2)]
    AT = [P.carve(BF16, [8, 128]) for _ in range(2)]
    yab = [P.carve(BF16, [8, 128]) for _ in range(2)]
    yT = [P.carve(BF16, [8, 128]) for _ in range(2)]

    def flat(t):
        return t[:, :, :].rearrange("p a b -> p (a b)")

    P.dma("sp", ident[:, :], C["ident"], r=[], w=["ident"])
    P.copy("dve", identb[:, :], ident[:, :], r=["ident"], w=["identb"])
    P.dma("sp", maskT[:, :], C["maskT"], r=[], w=["maskT"])
    for h in range(8):
        P.copy("dve" if h % 2 == 0 else "pool", mask8[:, h, :], maskT[:, :], r=["maskT"], w=["mask8"])
    P.dma("sp", gbc[:, :], C["g_hgrn_bc"], r=[], w=["gbc"])
    P.dma("sp", lbb[:, :, :], C["lb_bc"], r=[], w=["lbb"])
    P.tt("dve", lbb[:, 0, :], lbb[:, 1, :], lbb[:, 0, :], ALU.subtract, r=["lbb"], w=["lbb"])
    P.act(lbb[:, 1, :], lbb[:, 0, :], AF.Sigmoid, r=["lbb"], w=["lbb"], scale=-1.0)
    P.act(lbb[:, 0, :], lbb[:, 0, :], AF.Sigmoid, r=["lbb"], w=["lbb"])
    P.dma("sp", lbc[:, :, :], C["lb_col"], r=[], w=["lbc"])
    P.tt("dve", lbc[:, 0, :], lbc[:, 1, :], lbc[:, 0, :], ALU.subtract, r=["lbc"], w=["lbc"])
    P.act(omlc[:, 0:8], lbc[:, 0, :], AF.Sigmoid, r=["lbc"], w=["omlc"], scale=-1.0)
    P.memset("dve", flat(St), 0.0, w=["St"])
    P.memset("pool", flat(Sb), 0.0, w=["Sb"])
    QCv, FCTv = fm(O["QCT"]), fm(O["FCT"])
    YTv = fm(YT1)
    b3 = PS[0][:, :].rearrange("p (h t) -> p h t", h=8)
    a3 = PS[1][:, :].rearrange("p (h t) -> p h t", h=8)
    s3 = PS[3][:, :].rearrange("p (h t) -> p h t", h=8)
    for blk, (c0, tw) in enumerate(TOKBLKS):
        i2 = blk % 2
        sfx = "%d" % i2
        if blk == 16:
            P.dma("sp", St[:, :, :], C["st_hg"].rearrange("h d v -> d h v"), r=[], w=["St"])
            P.copy("act", flat(Sb), flat(St), r=["St"], w=["Sb"])
        f_ = fct[i2]
        kf = "fct" + sfx
        P.dma("sp", f_[0:tw, :], O["FC"][c0:c0 + tw, :], r=[("dram", "FC")], w=[kf])
        P.act(f_[0:tw, :], f_[0:tw, :], AF.Sigmoid, r=[kf], w=[kf])
        P.tt("dve", f_[0:tw, :], f_[0:tw, :], lbb[0:tw, 1, :], ALU.mult, r=[kf, "lbb"], w=[kf])
        P.tt("pool", f_[0:tw, :], f_[0:tw, :], lbb[0:tw, 0, :], ALU.add, r=[kf, "lbb"], w=[kf])
        P.act(f_[0:tw, :], f_[0:tw, :], AF.Ln, r=[kf], w=[kf])
        for h in range(8):
            P.mm(PS[0][:, h * 128:h * 128 + tw], f_[0:tw, h * 128:(h + 1) * 128], maskT[0:tw, 0:tw], True, True,
                 r=[kf, "maskT"], w=["ps0"])
        e_, en_ = eb[i2], enb[i2]
        ke, ken = "eb" + sfx, "enb" + sfx
        P.act(e_[:, :, 0:tw], b3[:, :, 0:tw], AF.Exp, r=["ps0"], w=[ke])
        P.act(en_[:, :, 0:tw], b3[:, :, 0:tw], AF.Exp, r=["ps0"], w=[ken], scale=-1.0)
        q_ = qct[i2]
        kq = "qct" + sfx
        P.dma("pool", q_[:, :, 0:tw], QCv[:, 0:8, c0:c0 + tw], r=[("dram", "QCT")], w=[kq])
        P.act(q_[:, :, 0:tw], q_[:, :, 0:tw], AF.Silu, r=[kq], w=[kq])
        QT_ = QT[i2]
        kQT = "QT" + sfx
        P.tt("dve", QT_[:, :, 0:tw], q_[:, :, 0:tw], e_[:, :, 0:tw], ALU.mult, r=[kq, ke], w=[kQT])
        fT = fcT[i2]
        kfT = "fcT" + sfx
        P.dma("pool", fT[:, :, 0:tw], FCTv[:, 0:8, c0:c0 + tw], r=[("dram", "FCT")], w=[kfT])
        P.act(fT[:, :, 0:tw], fT[:, :, 0:tw], AF.Sigmoid, r=[kfT], w=[kfT], scale=-1.0)
        KT_ = KT[i2]
        kKT = "KT" + sfx
        for h in range(8):
            P.stt(KT_[:, h, 0:tw], fT[:, h, 0:tw], omlc[:, h:h + 1], en_[:, h, 0:tw], ALU.mult, ALU.mult,
                  r=[kfT, "omlc", ken], w=[kKT])
        for h in range(8):
            P.mm(PS[0][0:tw, h * 128:(h + 1) * 128], KT_[:, h, 0:tw], identb[:, :], True, True,
                 r=[kKT, "identb"], w=["ps0"])
        Ktm_ = Ktm[i2]
        kKtm = "Ktm" + sfx
        P.copy("act", flat(Ktm_)[0:tw, :], PS[0][0:tw, :], r=["ps0"], w=[kKtm])
        v_ = vt[i2]
        kv = "vt" + sfx
        P.dma("sp", v_[0:tw, :], O["IC"][c0:c0 + tw, :], r=[("dram", "IC")], w=[kv])
        for h in range(8):
            P.mm(PS[1][0:tw, h * 128:h * 128 + tw], KT_[:, h, 0:tw], QT_[:, h, 0:tw], True, True,
                 r=[kKT, kQT], w=["ps1"])
        AT_ = AT[i2]
        kAT = "AT" + sfx
        P.tt("dve", AT_[0:tw, :, 0:tw], a3[0:tw, :, 0:tw], mask8[0:tw, :, 0:tw], ALU.mult, r=["ps1", "mask8"], w=[kAT])
        for h in range(8):
            hs = slice(h * 128, (h + 1) * 128)
            P.mm(PS[2][0:tw, hs], AT_[0:tw, h, 0:tw], v_[0:tw, hs], True, False, r=[kAT, kv], w=["ps2"])
            P.mm(PS[2][0:tw, hs], QT_[:, h, 0:tw], Sb[:, h, :], False, True, r=[kQT, "Sb"], w=["ps2"])
        for h in range(8):
            hs = slice(h * 128, (h + 1) * 128)
            P.mm(PS[3][:, hs], Ktm_[0:tw, h, :], v_[0:tw, hs], True, True, r=[kKtm, kv], w=["ps3"])
        P.tt("dve", Stmp[:, :, :], St[:, :, :], s3, ALU.add, r=["St", "ps3"], w=["Stmp"])
        for h in range(8):
            P.ts("dve", St[:, h, :], Stmp[:, h, :], e_[:, h, tw - 1:tw], None, ALU.mult, None,
                 r=["Stmp", ke], w=["St"])
        P.copy("act", flat(Sb), flat(St), r=["St"], w=["Sb"])
        sm_ = sm[i2]
        ksm = "sm" + sfx
        P.act(junk[0:tw, :], PS[2][0:tw, :], AF.Square, r=["ps2"], w=["junk"])
        P.S.add("dve", lambda e, a=sm_, n=tw: e.tensor_reduce(
            a[0:n, 0:8], junk[0:n, :].rearrange("p (h v) -> p h v", h=8), AX.X, ALU.add), r=["junk"], w=[ksm])
        P.ts("dve", sm_[0:tw, 8:16], sm_[0:tw, 0:8], 1.0 / 128.0, float(LN_EPS), ALU.mult, ALU.add, r=[ksm], w=[ksm])
        P.act(sm_[0:tw, 8:16], sm_[0:tw, 8:16], AF.Sqrt, r=[ksm], w=[ksm])
        P.S.add("dve", lambda e, a=sm_, n=tw: e.reciprocal(a[0:n, 16:24], a[0:n, 8:16]), r=[ksm], w=[ksm])
        g_ = gdt[i2]
        kg = "gdt" + sfx
        P.dma("sp", g_[0:tw, :], O["GC"][c0:c0 + tw, :], r=[("dram", "GC")], w=[kg])
        P.act(g_[0:tw, :], g_[0:tw, :], AF.Silu, r=[kg], w=[kg])
        P.tt("pool", g_[0:tw, :], g_[0:tw, :], gbc[0:tw, :], ALU.mult, r=[kg, "gbc"], w=[kg])
        ya_ = yab[i2]
        kya = "ya" + sfx
        for h in range(8):
            hs = slice(h * 128, (h + 1) * 128)
            P.stt(ya_[0:tw, h, :], PS[2][0:tw, hs], sm_[0:tw, 16 + h:17 + h], g_[0:tw, hs], ALU.mult, ALU.mult,
                  r=["ps2", ksm, kg], w=[kya])
        for h in range(8):
            P.mm(PS[1][:, h * 128:h * 128 + tw], ya_[0:tw, h, :], identb[0:tw, 0:tw], True, True,
                 r=[kya, "identb"], w=["ps1"])
        yT_ = yT[i2]
        kyT = "yT" + sfx
        P.copy("act", yT_[:, :, 0:tw], a3[:, :, 0:tw], r=["ps1"], w=[kyT])
        P.dma("sp", YTv[:, 0:8, c0:c0 + tw], yT_[:, :, 0:tw], r=[kyT], w=[("dram", YT1.name)])
        if blk == 15:
            P.dma("sp", C["hg_p"].rearrange("h d v -> d h v"), St[:, :, :], r=["St"], w=[("dram", "hg_p")])
        if blk == 16:
            P.dma("sp", C["hg_s"].rearrange("h d v -> d h v"), St[:, :, :], r=["St"], w=[("dram", "hg_s")])


RET_GAMMA = [1.0 - 2.0 ** (-5.0 - h) for h in range(4)]


def ret_phase(P, O, YT1, C):
    P.reset_arena()
    PS = P.PS
    S_ = SEQ
    cosT = P.carve(F32, [NT])
    sinT = P.carve(F32, [NT])
    cosK = P.carve(F32, [17, 128])
    sinK = P.carve(F32, [17, 128])
    x1 = P.carve(F32, [NT])
    x2 = P.carve(F32, [NT])
    ta = P.carve(F32, [NT])
    tb = P.carve(F32, [NT])
    QR = P.carve(BF16, [2, NT])
    KR = P.carve(BF16, [2, NT])
    KDt = P.carve(F32, [17, 256])
    KRt = P.carve(F32, [17, 256])
    tk = P.carve(F32, [17, 128])
    KW = P.carve(BF16, [17, 256])
    VDt = P.carve(BF16, [17, 256])
    RD = P.carve(F32, [2, 128])
    tails = P.carve(F32, [4, 17])
    gbc = P.carve(F32, [1024])
    ident = P.carve(F32, [128])
    identb = P.carve(BF16, [128])
    gdt = [P.carve(F32, [256]) for _ in range(2)]
    pt = [P.carve(BF16, [128]) for _ in range(3)]
    junk = P.carve(F32, [256])
    sm = [P.carve(F32, [8]) for _ in range(2)]
    yab = [P.carve(BF16, [256]) for _ in range(2)]
    yT = [P.carve(BF16, [2, 128]) for _ in range(2)]
    cst = [P.carve(F32, [256]) for _ in range(2)]
    Sin = P.carve(F32, [2, 256])
    Sinb = P.carve(BF16, [2, 256])
    srow = P.carve(F32, [16])
    hrow = P.carve(F32, [256])
    gdrow = P.carve(F32, [1024])
    yrow = P.carve(BF16, [256])
    ycol = P.carve(BF16, [2])
    P.dma("sp", ident[:, :], C["ident"], r=[], w=["ident"])
    P.copy("dve", identb[:, :], ident[:, :], r=["ident"], w=["identb"])
    P.dma("sp", cosT[:, :], C["cosT"], r=[], w=["cosT"])
    P.dma("sp", sinT[:, :], C["sinT"], r=[], w=["sinT"])
    P.dma("sp", cosK[:, 0:16, :], C["cosK"][0:S_, :].rearrange("(b p) j -> p b j", p=128), r=[], w=["cosK"])
    P.dma("sp", cosK[0:1, 16, :], C["cosK"][S_:NT, :], r=[], w=["cosK"])
    P.dma("sp", sinK[:, 0:16, :], C["sinK"][0:S_, :].rearrange("(b p) j -> p b j", p=128), r=[], w=["sinK"])
    P.dma("sp", sinK[0:1, 16, :], C["sinK"][S_:NT, :], r=[], w=["sinK"])
    P.dma("sp", tails[:, :, :], C["ret_tails"], r=[], w=["tails"])
    P.dma("sp", gbc[:, :], C["g_ret_bc"], r=[], w=["gbc"])
    P.dma("sp", gdrow[0:1, :], O["GD"][S_:NT, :], r=[("dram", "GD")], w=["gdrow"])
    P.act(gdrow[0:1, :], gdrow[0:1, :], AF.Silu, r=["gdrow"], w=["gdrow"])
    P.tt("dve", gdrow[0:1, :], gdrow[0:1, :], gbc[0:1, :], ALU.mult, r=["gdrow", "gbc"], w=["gdrow"])
    QDv, KDv = fm(O["QDT"]), fm(O["KDT"])
    KDtv = O["KD"][0:S_, :].rearrange("(b p) c -> p b c", p=128)
    VDtv = O["VD"][0:S_, :].rearrange("(b p) c -> p b c", p=128)
    GDv = O["GD"][0:S_, :].rearrange("(b p) c -> p b c", p=128)
    YTv = fm(YT1)
    it = 0
    for h in range(4):
        hs = slice(h * 256, (h + 1) * 256)
        gam = RET_GAMMA[h]
        P.dma("sp", RD[:, :, :], C["ret_D"][h].rearrange("a s t -> s a t"), r=[], w=["RD"])
        for (srcv, dst, kd, nm) in ((QDv, QR, "QR", "QDT"), (KDv, KR, "KR", "KDT")):
            P.dma("sp", x1[:, :], srcv[:, 2 * h, :], r=[("dram", nm)], w=["x1"])
            P.dma("sp", x2[:, :], srcv[:, 2 * h + 1, :], r=[("dram", nm)], w=["x2"])
            P.tt("dve", ta[:, :], x1[:, :], cosT[:, :], ALU.mult, r=["x1", "cosT"], w=["ta"])
            P.tt("pool", tb[:, :], x2[:, :], sinT[:, :], ALU.mult, r=["x2", "sinT"], w=["tb"])
            P.tt("dve", dst[:, 0, :], ta[:, :], tb[:, :], ALU.subtract, r=["ta", "tb"], w=[kd])
            P.tt("dve", ta[:, :], x1[:, :], sinT[:, :], ALU.mult, r=["x1", "sinT"], w=["ta"])
            P.tt("pool", tb[:, :], x2[:, :], cosT[:, :], ALU.mult, r=["x2", "cosT"], w=["tb"])
            P.tt("dve", dst[:, 1, :], ta[:, :], tb[:, :], ALU.add, r=["ta", "tb"], w=[kd])
        P.dma("sp", KDt[:, 0:16, :], KDtv[:, :, hs], r=[("dram", "KD")], w=["KDt"])
        P.dma("sp", KDt[0:1, 16, :], O["KD"][S_:NT, hs], r=[("dram", "KD")], w=["KDt"])
        P.dma("sp", VDt[:, 0:16, :], VDtv[:, :, hs], r=[("dram", "VD")], w=["VDt"])
        P.dma("sp", VDt[0:1, 16, :], O["VD"][S_:NT, hs], r=[("dram", "VD")], w=["VDt"])
        for (np_, bsl) in ((128, slice(0, 16)), (1, slice(16, 17))):
            k1, k2 = KDt[0:np_, bsl, 0:128], KDt[0:np_, bsl, 128:256]
            cK, sK = cosK[0:np_, bsl, :], sinK[0:np_, bsl, :]
            P.tt("dve", KRt[0:np_, bsl, 0:128], k1, cK, ALU.mult, r=["KDt", "cosK"], w=["KRt"])
            P.tt("pool", tk[0:np_, bsl, :], k2, sK, ALU.mult, r=["KDt", "sinK"], w=["tk"])
            P.tt("dve", KRt[0:np_, bsl, 0:128], KRt[0:np_, bsl, 0:128], tk[0:np_, bsl, :], ALU.subtract, r=["KRt", "tk"], w=["KRt"])
            P.tt("dve", KRt[0:np_, bsl, 128:256], k1, sK, ALU.mult, r=["KDt", "sinK"], w=["KRt"])
            P.tt("pool", tk[0:np_, bsl, :], k2, cK, ALU.mult, r=["KDt", "cosK", "KRt"], w=["tk"])
            P.tt("dve", KRt[0:np_, bsl, 128:256], KRt[0:np_, bsl, 128:256], tk[0:np_, bsl, :], ALU.add, r=["KRt", "tk"], w=["KRt"])
        for sb in range(17):
            np_ = 128 if sb < 16 else 1
            P.ts("dve", KW[0:np_, sb, :], KRt[0:np_, sb, :], tails[0:np_, h, sb:sb + 1], None, ALU.mult, None,
                 r=["KRt", "tails"], w=["KW"])
        for tq in range(16):
            ts_ = slice(tq * 128, (tq + 1) * 128)
            o_ps = PS[2 + (tq % 2)][:, 0:256]
            ko = "ps%d" % (2 + tq % 2)
            for sb in range(tq + 1):
                ss = slice(sb * 128, (sb + 1) * 128)
                sT = PS[0][:, (it % 2) * 512:(it % 2) * 512 + 128]
                ksT = "ps0_%d" % (it % 2)
                P.mm(sT, KR[:, 0, ss], QR[:, 0, ts_], True, False, r=["KR", "QR"], w=[ksT])
                P.mm(sT, KR[:, 1, ss], QR[:, 1, ts_], False, True, r=["KR", "QR"], w=[ksT])
                p_ = pt[it % 3]
                kp = "pt%d" % (it % 3)
                if sb == tq:
                    P.tt("dve", p_[:, :], sT, RD[:, 1, :], ALU.mult, r=[ksT, "RD"], w=[kp])
                else:
                    P.stt(p_[:, :], sT, float(gam ** (128 * (tq - sb))), RD[:, 0, :], ALU.mult, ALU.mult,
                          r=[ksT, "RD"], w=[kp])
                P.mm(o_ps, p_[:, :], VDt[:, sb, :], sb == 0, sb == tq, r=[kp, "VDt"], w=[ko])
                it += 1
            i2 = tq % 2
            smt = sm[i2]
            ksm = "sm%d" % i2
            P.act(junk[:, :], o_ps, AF.Square, r=[ko], w=["junk", ksm], accum_out=smt[:, 3:4])
            P.ts("dve", smt[:, 4:5], smt[:, 3:4], 1.0 / 256.0, float(LN_EPS), ALU.mult, ALU.add, r=[ksm], w=[ksm])
            P.act(smt[:, 4:5], smt[:, 4:5], AF.Sqrt, r=[ksm], w=[ksm])
            P.S.add("dve", lambda e, a=smt: e.reciprocal(a[:, 5:6], a[:, 4:5]), r=[ksm], w=[ksm])
            gd = gdt[i2]
            kgd = "gd%d" % i2
            P.dma("sp", gd[:, :], GDv[:, tq, hs], r=[("dram", "GD")], w=[kgd])
            P.act(gd[:, :], gd[:, :], AF.Silu, r=[kgd], w=[kgd])
            P.tt("pool", gd[:, :], gd[:, :], gbc[:, hs], ALU.mult, r=[kgd, "gbc"], w=[kgd])
            ya = yab[i2]
            kya = "ya%d" % i2
            P.stt(ya[:, :], o_ps, smt[:, 5:6], gd[:, :], ALU.mult, ALU.mult, r=[ko, ksm, kgd], w=[kya])
            tps = PS[1][:, 512:768].bitcast(BF16)
            yTt = yT[i2]
            kyT = "yT%d" % i2
            for j in range(2):
                P.S.add("pe", lambda e, o=tps[:, j * 128:(j + 1) * 128], i=ya[:, j * 128:(j + 1) * 128]:
                        e.transpose(o, i, identb[:, :]), r=[kya, "identb"], w=["ps1b"])
            P.copy("act", yTt[:, :, :].rearrange("p a b -> p (a b)"), tps[:, 0:256], r=["ps1b"], w=[kyT])
            P.dma("sp", YTv[:, 8 + 2 * h:8 + 2 * h + 2, ts_], yTt[:, :, :], r=[kyT], w=[("dram", YT1.name)])
        for dcn in range(2):
            c_ps = PS[2 + dcn][:, 0:256]
            kc = "ps%d" % (2 + dcn)
            for sb in range(16):
                P.mm(c_ps, KW[:, sb, dcn * 128:(dcn + 1) * 128], VDt[:, sb, :], sb == 0, sb == 15,
                     r=["KW", "VDt"], w=[kc])
            ct = cst[dcn]
            kct = "cst%d" % dcn
            P.copy("act", ct[:, :], c_ps, r=[kc], w=[kct])
            P.dma("sp", C["rt_p"][h, dcn * 128:(dcn + 1) * 128, :], ct[:, :], r=[kct], w=[("dram", "rt_p")])
        P.dma("sp", Sin[:, 0:2, :], C["st_ret"][h].rearrange("(c p) v -> p c v", p=128), r=[], w=["Sin"])
        P.copy("act", Sinb[:, 0:2, :], Sin[:, 0:2, :], r=["Sin"], w=["Sinb"])
        rps = PS[1][0:1, 0:512]
        for dcn in range(2):
            P.mm(rps[:, 0:256], QR[:, dcn, S_:NT], Sinb[:, dcn, :], dcn == 0, dcn == 1, r=["QR", "Sinb"], w=["ps1a"])
        for dcn in range(2):
            P.mm(rps[:, 256:257], QR[:, dcn, S_:NT], KR[:, dcn, S_:NT], dcn == 0, dcn == 1, r=["QR", "KR"], w=["ps1a"])
        P.copy("dve", srow[0:1, 0:1], rps[:, 256:257], r=["ps1a"], w=["srow"])
        P.ts("dve", hrow[0:1, :], rps[:, 0:256], float(gam), None, ALU.mult, None, r=["ps1a"], w=["hrow"])
        P.stt(hrow[0:1, :], VDt[0:1, 16, :], srow[0:1, 0:1], hrow[0:1, :], ALU.mult, ALU.add,
              r=["VDt", "srow", "hrow"], w=["hrow"])
        P.act(junk[0:1, :], hrow[0:1, :], AF.Square, r=["hrow"], w=["junk", "srow"], accum_out=srow[0:1, 1:2])
        P.ts("dve", srow[0:1, 2:3], srow[0:1, 1:2], 1.0 / 256.0, float(LN_EPS), ALU.mult, ALU.add, r=["srow"], w=["srow"])
        P.act(srow[0:1, 2:3], srow[0:1, 2:3], AF.Sqrt, r=["srow"], w=["srow"])
        P.S.add("dve", lambda e: e.reciprocal(srow[0:1, 3:4], srow[0:1, 2:3]), r=["srow"], w=["srow"])
        P.stt(yrow[0:1, :], hrow[0:1, :], srow[0:1, 3:4], gdrow[0:1, hs], ALU.mult, ALU.mult,
              r=["hrow", "srow", "gdrow"], w=["yrow"])
        cps = PS[1][:, 512:514]
        for j in range(2):
            P.mm(cps[:, j:j + 1], yrow[0:1, j * 128:(j + 1) * 128], identb[0:1, 0:1], True, True,
                 r=["yrow", "identb"], w=["ps1b"])
        P.copy("dve", ycol[:, 0:2], cps, r=["ps1b"], w=["ycol"])
        P.dma("sp", YTv[:, 8 + 2 * h:8 + 2 * h + 2, S_], ycol[:, 0:2], r=["ycol"], w=[("dram", YT1.name)])
        for dcn in range(2):
            kv_ps = PS[2 + dcn][:, 0:256]
            kc = "ps%d" % (2 + dcn)
            P.mm(kv_ps, KW[0:1, 16, dcn * 128:(dcn + 1) * 128], VDt[0:1, 16, :], True, True, r=["KW", "VDt"], w=[kc])
            ct = cst[dcn]
            kct = "cst%d" % dcn
            P.stt(ct[:, :], Sin[:, dcn, :], float(gam), kv_ps, ALU.mult, ALU.add, r=["Sin", kc], w=[kct])
            P.dma("sp", C["rt_s"][h, dcn * 128:(dcn + 1) * 128, :], ct[:, :], r=[kct], w=[("dram", "rt_s")])


def mixers_odd(P, O, YT1):
    hgrn_phase(P, O, YT1, P.C)
    ret_phase(P, O, YT1, P.C)


def finish(P):
    S = P.S
    S.barrier()
    st = ExitStack()
    S.emit(st)
    st.close()
    return P.nc


def make_in_maps(inputs):
    f = np.float32
    A = lambda k: np.ascontiguousarray(np.asarray(inputs[k], f))
    xp, xs = A("x_prompt"), A("x_sample")
    pp, ps = A("p_prompt"), A("p_sample")
    shared = {
        "w_ffn_up": A("w_ffn_up"), "w_ffn_down": A("w_ffn_down"),
        "ln_g": pc_layout(A("ln_g")), "ln_b": pc_layout(A("ln_b")),
        "w_in_even": np.ascontiguousarray(A("w_in_even")[0]),
        "w_in_odd": np.ascontiguousarray(A("w_in_odd")[0]),
        "w_out": A("w_out"), "w_pe_gate": A("w_pe_gate"), "w_pe_proj": A("w_pe_proj"),
        "b_gate": np.ascontiguousarray(A("b_gate_mlstm")[0].reshape(8, 1)),
        "c_ident": np.eye(128, dtype=f),
        "cache_ik": np.ascontiguousarray(A("cache_idx_k")[0].reshape(1280, 8192)),
        "cache_k4": np.ascontiguousarray(A("cache_k")[0].reshape(5120, 8192)),
        "cache_v4": np.ascontiguousarray(A("cache_v")[0].reshape(5120, 8192)),
        "c_maskT": np.triu(np.ones((128, 128), f)),
        "c_cmask": np.where(np.tril(np.ones((128, 128), bool)), 0.0, -1.0e30).astype(f),
        "c_sel4": np.ascontiguousarray(np.broadcast_to(np.eye(4, dtype=f)[:, :, None], (4, 4, 128))),
        "g_mlstm_bc": np.ascontiguousarray(np.broadcast_to(A("g_mlstm")[0][None, :], (128, 1024))),
    }
    pos = np.concatenate([np.arange(SEQ, dtype=np.float32), np.array([16384.0], np.float32)])
    inv = (1.0 / (np.float32(10000.0) ** np.linspace(0.0, 1.0, 128, dtype=np.float32))).astype(np.float32)
    ang = (pos[:, None] * inv[None, :]).astype(np.float32)
    shared["c_cosK"] = np.ascontiguousarray(np.cos(ang).astype(f))
    shared["c_sinK"] = np.ascontiguousarray(np.sin(ang).astype(f))
    shared["c_cosT"] = np.ascontiguousarray(shared["c_cosK"].T)
    shared["c_sinT"] = np.ascontiguousarray(shared["c_sinK"].T)
    lg = np.log1p(-np.exp2(-5.0 - np.arange(4, dtype=np.float64)))
    sl = np.arange(128)
    dexp = (sl[None, :] - sl[:, None]).astype(np.float64)
    D0 = np.exp(lg[:, None, None] * dexp[None])
    Dd = np.where(dexp[None] >= 0, D0, 0.0)
    shared["c_ret_D"] = np.ascontiguousarray(np.stack([D0, Dd], 1).astype(f))
    tl = np.zeros((128, 4, 17), np.float64)
    for hh in range(4):
        for bb in range(16):
            tl[:, hh, bb] = np.exp(lg[hh] * (SEQ - 1 - (bb * 128 + sl))) / 16.0
        tl[:, hh, 16] = 1.0 / 16.0
    shared["c_ret_tails"] = tl.astype(f)
    shared["g_ret_bc"] = np.ascontiguousarray(np.broadcast_to(A("g_ret")[0][None, :], (128, 1024)))
    shared["g_hgrn_bc"] = np.ascontiguousarray(np.broadcast_to(A("g_hgrn")[0][None, :], (128, 1024)))
    shared["lb_bc"] = np.ascontiguousarray(np.broadcast_to(A("hgrn_lb")[None], (128, 2, 1024)))
    shared["lb_col"] = np.ascontiguousarray(A("hgrn_lb").reshape(2, 8, 128).transpose(2, 0, 1))
    maps = []
    for c in range(8):
        b = c % 4
        m = dict(shared)
        m["st_ret"] = np.ascontiguousarray(A("state_ret")[0, c])
        m["st_hg"] = np.ascontiguousarray(A("state_hgrn")[0, c])
        m["xin"] = np.ascontiguousarray(np.concatenate([xp[b].T, xs[c].T], axis=1))
        m["page_col"] = np.ascontiguousarray(np.asarray(inputs["page_table"], np.int32)[c].reshape(128, 1))
        m["st_C"] = np.ascontiguousarray(A("state_mlstm_C")[0, c])
        m["st_n"] = np.ascontiguousarray(A("state_mlstm_n")[0, c])
        m["st_m"] = np.ascontiguousarray(A("state_mlstm_m")[0, c].reshape(1, 4))
        m["pT"] = np.ascontiguousarray(np.stack(
            [np.concatenate([pp[i, b].T, ps[i, c].T], axis=1) for i in range(2)]))
        maps.append(m)
    return maps


_NC_CACHE = {}


def kernel(**inputs):
    if "nc" not in _NC_CACHE:
        P = build()
        _NC_CACHE["nc"] = finish(P)
    nc = _NC_CACHE["nc"]
    maps = make_in_maps(inputs)
    res = run_bass_kernel_spmd(nc, maps, core_ids=list(range(8)))
    R = res.results
    f = np.float32

    def get(c, name, shape):
        if name in R[c]:
            return np.asarray(R[c][name], f).reshape(shape)
        return np.zeros(shape, f)

    y_p = np.stack([np.asarray(R[b]["yT_out"], f)[:, :SEQ].T for b in range(4)])
    y_s = np.stack([np.asarray(R[c]["yT_out"], f)[:, SEQ:].T for c in range(8)])
    P4, S8 = range(4), range(8)
    mC_p = np.stack([get(b, "mC_p", (4, 256, 256)) for b in P4])[None]
    mC_s = np.stack([get(c, "mC_s", (4, 256, 256)) for c in S8])[None]
    mn_p = np.stack([get(b, "mn_p", (4, 256)) for b in P4])[None]
    mn_s = np.stack([get(c, "mn_s", (4, 256)) for c in S8])[None]
    mm_p = np.stack([get(b, "mm_p", (4,)) for b in P4])[None]
    mm_s = np.stack([get(c, "mm_s", (4,)) for c in S8])[None]
    k_p = np.stack([get(b, "k_rows_p", (SEQ, 2, 128)) for b in P4])[None]
    k_s = np.stack([get(c, "k_rows_s", (1, 2, 128)) for c in S8])[None]
    v_p = np.stack([get(b, "v_rows_p", (SEQ, 2, 128)) for b in P4])[None]
    v_s = np.stack([get(c, "v_rows_s", (1, 2, 128)) for c in S8])[None]
    ik_p = np.stack([get(b, "ik_rows_p", (SEQ, 64)) for b in P4])[None]
    ik_s = np.stack([get(c, "ik_rows_s", (1, 64)) for c in S8])[None]
    hg_p = np.stack([get(b, "hg_p", (8, 128, 128)) for b in P4])[None]
    hg_s = np.stack([get(c, "hg_s", (8, 128, 128)) for c in S8])[None]
    rt_p = np.stack([get(b, "rt_p", (4, 256, 256)) for b in P4])[None]
    rt_s = np.stack([get(c, "rt_s", (4, 256, 256)) for c in S8])[None]
    return (y_p, y_s, mC_p, mC_s, mn_p, mn_s, mm_p, mm_s, k_p, k_s, v_p, v_s, ik_p, ik_s,
            hg_p, hg_s, rt_p, rt_s)
```

```python
import numpy as np
from contextlib import ExitStack
import concourse.bass as bass
import concourse.mybir as mybir
from concourse.bass_utils import run_bass_kernel_spmd

F32 = mybir.dt.float32
BF16 = mybir.dt.bfloat16
I32 = mybir.dt.int32
AF = mybir.ActivationFunctionType
ALU = mybir.AluOpType
AX = mybir.AxisListType

ENGS = ("pe", "act", "dve", "pool", "sp")

D = 2048
SEQ = 2048
NT = SEQ + 1
DFF = 5632
NFC = DFF // 128
DEPTH = 2
ALPHA = (2 * DEPTH) ** 0.25
LN_EPS = 1e-5
TILES = [(0, 512), (512, 512), (1024, 512), (1536, 513)]


def blocks(tw):
    out = []
    o = 0
    while o < tw:
        n = min(512, tw - o)
        out.append((o, n))
        o += n
    return out


class _Op:
    __slots__ = ("eng", "fn", "dma", "deps", "signaled", "sem", "val", "amt", "pre")

    def __init__(self, eng, fn, dma):
        self.eng = eng
        self.fn = fn
        self.dma = dma
        self.deps = []
        self.signaled = False
        self.sem = None
        self.val = 0
        self.amt = 0
        self.pre = None


class Sched:
    def __init__(self, nc, n_dma_sems=12):
        self.nc = nc
        self.ops = {e: [] for e in ENGS}
        self.last_w = {}
        self.readers = {}
        self.n_dma_sems = n_dma_sems

    def add(self, eng, fn, r=(), w=(), dma=False):
        op = _Op(eng, fn, dma)
        deps = set()
        for k in r:
            lw = self.last_w.get(k)
            if lw is not None:
                deps.add(lw)
        for k in w:
            lw = self.last_w.get(k)
            if lw is not None:
                deps.add(lw)
            for rd in self.readers.get(k, ()):
                deps.add(rd)
        op.deps = list(deps)
        for d in op.deps:
            d.signaled = True
        for k in w:
            self.last_w[k] = op
            self.readers[k] = []
        for k in r:
            if k not in w:
                self.readers.setdefault(k, []).append(op)
        if dma:
            op.signaled = True
        self.ops[eng].append(op)
        return op

    def barrier(self):
        pend = set()
        for v in self.last_w.values():
            if v is not None:
                pend.add(v)
        for v in self.readers.values():
            for rd in v:
                pend.add(rd)
        for e in ENGS:
            op = _Op(e, None, False)
            op.deps = list(pend)
            self.ops[e].append(op)
        for d in pend:
            d.signaled = True
        self.last_w = {}
        self.readers = {}

    def emit(self, stack):
        nc = self.nc
        sems = {e: stack.enter_context(nc.semaphore("s_" + e)) for e in ENGS}
        dsems = {}
        for e in ("sp", "pool", "act"):
            dsems[e] = [stack.enter_context(nc.semaphore("d_%s_%d" % (e, i))) for i in range(self.n_dma_sems)]
        cnt = {e: 0 for e in ENGS}
        dtot = {e: [0] * self.n_dma_sems for e in dsems}
        drr = {e: 0 for e in dsems}
        for e in ENGS:
            for op in self.ops[e]:
                if op.dma:
                    j = drr[e]
                    drr[e] = (j + 1) % self.n_dma_sems
                    op.sem = dsems[e][j]
                    if dtot[e][j] > 0:
                        op.pre = (op.sem, dtot[e][j])
                    dtot[e][j] += 16
                    op.val = dtot[e][j]
                    op.amt = 16
                elif op.signaled:
                    cnt[e] += 1
                    op.sem = sems[e]
                    op.val = cnt[e]
                    op.amt = 1
        block = stack.enter_context(nc.Block())

        def run(e, eng):
            waited = {}
            for op in self.ops[e]:
                need = {}
                if op.pre is not None:
                    need[id(op.pre[0])] = (op.pre[0], op.pre[1])
                for d in op.deps:
                    if d.sem is None:
                        continue
                    if d.eng == "pe" and e == "pe" and not d.dma:
                        continue
                    cur = need.get(id(d.sem))
                    if cur is None or cur[1] < d.val:
                        need[id(d.sem)] = (d.sem, d.val)
                for sid, (sem, val) in need.items():
                    if waited.get(sid, 0) >= val:
                        continue
                    eng.wait_ge(sem, val)
                    waited[sid] = val
                if op.fn is not None:
                    ins = op.fn(eng)
                    if op.signaled:
                        ins.then_inc(op.sem, op.amt)
                elif op.signaled:
                    eng.nop().then_inc(op.sem, op.amt)

        @block.tensor
        def _(eng):
            run("pe", eng)

        @block.scalar
        def _(eng):
            run("act", eng)

        @block.vector
        def _(eng):
            run("dve", eng)

        @block.gpsimd
        def _(eng):
            run("pool", eng)

        @block.sync
        def _(eng):
            run("sp", eng)


class Prog:
    ARENA = 94 * 1024

    def __init__(self, debug_out=None):
        self.nc = bass.Bass("TRN2", target_bir_lowering=False)
        nc = self.nc
        self.S = Sched(nc)
        self.arena = nc.alloc_sbuf_tensor("arena", [128, self.ARENA], BF16)
        self.aoff = 0
        self.PS = [nc.alloc_psum_tensor("ps%d" % i, [128, 1024], F32) for i in range(4)]
        self.ins = {}
        self.outs = {}
        self.debug_out = debug_out or ()
        self.uid = 0

    def reset_arena(self):
        self.S.barrier()
        self.aoff = 0

    def carve(self, dtype, fshape, parts=128):
        n = int(np.prod(fshape))
        nb = n * (2 if dtype == F32 or dtype == I32 else 1)
        nb = (nb + 15) // 16 * 16
        assert self.aoff + nb <= self.ARENA, ("arena overflow", self.aoff, nb)
        ap = self.arena[0:parts, self.aoff:self.aoff + nb]
        self.aoff += nb
        if dtype != BF16:
            ap = ap.bitcast(dtype)
        ap = ap[:, 0:n]
        if len(fshape) == 2:
            ap = ap.rearrange("p (a b) -> p a b", a=fshape[0])
        elif len(fshape) == 3:
            ap = ap.rearrange("p (a b c) -> p a b c", a=fshape[0], b=fshape[1])
        return ap

    def key(self, name):
        self.uid += 1
        return "%s#%d" % (name, self.uid)

    def inp(self, name, shape, dtype=F32):
        t = self.nc.dram_tensor(name, list(shape), dtype, kind="ExternalInput").ap()
        self.ins[name] = t
        return t

    def out(self, name, shape, dtype=F32):
        t = self.nc.dram_tensor(name, list(shape), dtype, kind="ExternalOutput").ap()
        self.outs[name] = t
        return t

    def scratch(self, name, shape, dtype=F32):
        kind = "ExternalOutput" if name in self.debug_out else "Internal"
        t = self.nc.dram_tensor(name, list(shape), dtype, kind=kind).ap()
        if kind == "ExternalOutput":
            self.outs[name] = t
        return t

    def dma(self, eng, out, in_, r, w):
        return self.S.add(eng, lambda e: e.dma_start(out=out, in_=in_, allow_slow_non_contiguous=True), r=r, w=w, dma=True)

    def mm(self, out, lhsT, rhs, start, stop, r, w):
        return self.S.add("pe", lambda e: e.matmul(out, lhsT, rhs, start=start, stop=stop), r=r, w=w)

    def act(self, out, in_, func, r, w, bias=None, scale=None, accum_out=None):
        kw = {}
        if bias is not None:
            kw["bias"] = bias
        if scale is not None:
            kw["scale"] = scale
        if accum_out is not None:
            kw["accum_out"] = accum_out
        return self.S.add("act", lambda e: e.activation(out, in_, func, **kw), r=r, w=w)

    def tt(self, eng, out, in0, in1, op, r, w):
        return self.S.add(eng, lambda e: e.tensor_tensor(out, in0, in1, op), r=r, w=w)

    def ts(self, eng, out, in0, s1, s2, op0, op1, r, w, accum_out=None):
        if op1 is None:
            return self.S.add(eng, lambda e: e.tensor_scalar(out, in0, s1, None, op0), r=r, w=w)
        if accum_out is not None:
            return self.S.add(eng, lambda e: e.tensor_scalar(out, in0, s1, s2, op0, op1, accum_out), r=r, w=w)
        return self.S.add(eng, lambda e: e.tensor_scalar(out, in0, s1, s2, op0, op1), r=r, w=w)

    def stt(self, out, in0, scalar, in1, op0, op1, r, w, accum_out=None):
        if accum_out is not None:
            return self.S.add("dve", lambda e: e.scalar_tensor_tensor(out, in0, scalar, in1, op0, op1, accum_out), r=r, w=w)
        return self.S.add("dve", lambda e: e.scalar_tensor_tensor(out, in0, scalar, in1, op0, op1), r=r, w=w)

    def copy(self, eng, out, in_, r, w):
        if eng == "act":
            return self.S.add("act", lambda e: e.copy(out, in_), r=r, w=w)
        return self.S.add(eng, lambda e: e.tensor_copy(out, in_), r=r, w=w)

    def memset(self, eng, ap, val, w):
        return self.S.add(eng, lambda e: e.memset(ap, val), w=w)


def fm(ap):
    return ap.rearrange("(c p) t -> p c t", p=128)


def load_cast_tile(P, src, c0, tw, xb, nch=16):
    S = P.S
    sv = fm(src)
    keys = []
    grp = 2
    for q in range(0, nch, grp):
        n = min(grp, nch - q)
        i = (q // grp) % 2
        stg = P.xstg[i]
        ks = "xstg%d" % i
        P.dma("sp", stg[:, 0:n, 0:tw], sv[:, q:q + n, c0:c0 + tw], r=[("dram", src.name)], w=[ks])
        kx = ("xb", id(xb), q)
        eng = "dve" if (q // grp) % 2 == 0 else "act"
        P.copy(eng, xb[:, q:q + n, 0:tw], stg[:, 0:n, 0:tw], r=[ks], w=[kx])
        keys.append(kx)
    return keys


def proj_res_ln(P, tilei, c0, tw, inT, in_keys, nk, wview, wcol0, res_src, scale, lng, lnb, dst,
                dst_bf=None):
    S = P.S
    blks = blocks(tw)
    PS = P.PS
    rv = fm(res_src)
    dv = fm(dst)
    s1, s2 = PS[2], PS[3]
    pend = None

    def post(dmc, o_ps, ko):
        xf = P.xf[dmc % 2]
        kxf = "xf%d" % (dmc % 2)
        P.dma("sp", xf[:, 0:tw], rv[:, dmc, c0:c0 + tw], r=[("dram", res_src.name)], w=[kxf])
        kv = ("v", dmc)
        xa = P.xa[dmc % 2]
        kxa = "xa%d" % (dmc % 2)
        P.act(xa[:, 0:tw], xf[:, 0:tw], AF.Copy, r=[kxf], w=[kxa], scale=float(ALPHA))
        P.stt(P.v[:, dmc, 0:tw], o_ps[:, 0:tw], float(scale), xa[:, 0:tw], ALU.mult, ALU.add, r=[ko, kxa], w=[kv])
        sq = P.sq[dmc % 2]
        ksq = "sq%d" % (dmc % 2)
        P.act(sq[:, 0:tw], P.v[:, dmc, 0:tw], AF.Square, r=[kv], w=[ksq])
        return kv, ksq

    def stats(dmc, kv, ksq):
        sq = P.sq[dmc % 2]
        for (o, n) in blks:
            P.mm(s1[:, o:o + n], P.ones[:, :], P.v[:, dmc, o:o + n], dmc == 0, dmc == 15, r=[kv, "ones"], w=["ps2"])
            P.mm(s2[:, o:o + n], P.ones[:, :], sq[:, o:o + n], dmc == 0, dmc == 15, r=[ksq, "ones"], w=["ps3"])

    for dmc in range(16):
        wd = P.wD[dmc % 2]
        kw = "wD%d" % (dmc % 2)
        P.dma("pool", wd[:, 0:nk, :], wview[:, :, wcol0 + dmc * 128: wcol0 + (dmc + 1) * 128], r=[], w=[kw])
        o_ps = PS[dmc % 2]
        ko = "ps%d" % (dmc % 2)
        for (o, n) in blks:
            for k in range(nk):
                P.mm(o_ps[:, o:o + n], wd[:, k, :], inT[:, k, o:o + n], k == 0, k == nk - 1,
                     r=[kw] + list(in_keys), w=[ko])
        if pend is not None:
            stats(*pend)
        kv, ksq = post(dmc, o_ps, ko)
        pend = (dmc, kv, ksq)
    stats(*pend)
    mean, msq, var, rstd = P.lnt[0], P.lnt[1], P.lnt[2], P.lnt[3]
    P.ts("dve", mean[:, 0:tw], s1[:, 0:tw], 1.0 / D, None, ALU.mult, None, r=["ps2"], w=["mean"])
    P.tt("dve", msq[:, 0:tw], mean[:, 0:tw], mean[:, 0:tw], ALU.mult, r=["mean"], w=["msq"])
    P.stt(var[:, 0:tw], s2[:, 0:tw], 1.0 / D, msq[:, 0:tw], ALU.mult, ALU.subtract, r=["ps3", "msq"], w=["var"])
    P.ts("dve", var[:, 0:tw], var[:, 0:tw], float(LN_EPS), None, ALU.add, None, r=["var"], w=["var"])
    P.act(var[:, 0:tw], var[:, 0:tw], AF.Sqrt, r=["var"], w=["var"])
    P.S.add("dve", lambda e: e.reciprocal(rstd[:, 0:tw], var[:, 0:tw]), r=["var"], w=["rstd"])
    for dmc in range(16):
        t1 = P.t1[dmc % 2]
        k1 = "t1%d" % (dmc % 2)
        yo = P.yo[dmc % 2]
        ky = "yo%d" % (dmc % 2)
        P.tt("dve", t1[:, 0:tw], P.v[:, dmc, 0:tw], mean[:, 0:tw], ALU.subtract, r=[("v", dmc), "mean"], w=[k1])
        P.tt("dve", t1[:, 0:tw], t1[:, 0:tw], rstd[:, 0:tw], ALU.mult, r=[k1, "rstd"], w=[k1])
        P.act(yo[:, 0:tw], t1[:, 0:tw], AF.Identity, r=[k1, "lnp"], w=[ky],
              scale=lng[:, dmc:dmc + 1], bias=lnb[:, dmc:dmc + 1])
        P.dma("sp", dv[:, dmc, c0:c0 + tw], yo[:, 0:tw], r=[ky], w=[("dram", dst.name)])


def dense_bufs(P, lng_d, lnb_d, full=True):
    P.reset_arena()
    P.xb = P.carve(BF16, [16, 513])
    if full:
        P.hT = P.carve(BF16, [NFC, 513])
        P.v = P.carve(F32, [16, 513])
        P.wA = [P.carve(BF16, [16, 256]) for _ in range(2)]
        P.wB = [P.carve(BF16, [16, 256]) for _ in range(2)]
    P.wD = [P.carve(BF16, [NFC, 128]) for _ in range(2)]
    P.xstg = [P.carve(F32, [2, 513]) for _ in range(2)]
    P.xf = [P.carve(F32, [513]) for _ in range(2)]
    P.xa = [P.carve(F32, [513]) for _ in range(2)]
    P.sq = [P.carve(F32, [513]) for _ in range(2)]
    P.sg = [P.carve(BF16, [513]) for _ in range(2)]
    P.lnt = [P.carve(F32, [513]) for _ in range(4)]
    P.t1 = [P.carve(F32, [513]) for _ in range(2)]
    P.yo = [P.carve(F32, [513]) for _ in range(2)]
    P.ones = P.carve(F32, [128])
    P.lng = P.carve(F32, [16])
    P.lnb = P.carve(F32, [16])
    P.memset("dve", P.ones[:, :], 1.0, w=["ones"])
    if lng_d is not None:
        P.dma("sp", P.lng[:, :], lng_d, r=[], w=["lnp"])
        P.dma("sp", P.lnb[:, :], lnb_d, r=[], w=["lnp"])


def outproj_phase(P, yT, res_src, dst, w_out, lng_d, lnb_d):
    dense_bufs(P, lng_d, lnb_d)
    wv = fm(w_out)
    yv = fm(yT)
    for ti, (c0, tw) in enumerate(TILES):
        P.dma("sp", P.xb[:, :, 0:tw], yv[:, :, c0:c0 + tw], r=[("dram", yT.name)], w=["xb_y"])
        proj_res_ln(P, ti, c0, tw, P.xb, ["xb_y"], 16, wv, 0, res_src, 1.0, P.lng, P.lnb, dst)


def pegate_phase(P, src, dst, pT, w_gate, w_proj):
    P.reset_arena()
    xb = P.carve(BF16, [16, NT])
    pb = P.carve(BF16, [2, NT])
    P.xstg = [P.carve(F32, [2, 513]) for _ in range(2)]
    pst = [P.carve(F32, [2, 513]) for _ in range(2)]
    wp = P.carve(BF16, [2, 2048])
    wD = [P.carve(BF16, [16, 128]) for _ in range(3)]
    NB = 4
    xf = [P.carve(F32, [513]) for _ in range(NB)]
    sg = [P.carve(F32, [513]) for _ in range(2)]
    t1 = [P.carve(F32, [513]) for _ in range(2)]
    yo = [P.carve(F32, [513]) for _ in range(NB)]
    wgv = fm(w_gate)
    wpv = fm(w_proj)
    pv = fm(pT)
    sv = fm(src)
    dv = fm(dst)
    PS = P.PS
    P.dma("pool", wp[:, :, :], wpv[:, :, :], r=[], w=["wp"])
    xkeys = load_all_bf16(P, src, xb)
    for ti, (c0, tw) in enumerate(TILES):
        ps_, kps = pst[ti % 2], "pst%d" % (ti % 2)
        P.dma("sp", ps_[:, :, 0:tw], pv[:, :, c0:c0 + tw], r=[], w=[kps])
        P.copy("dve" if ti % 2 == 0 else "act", pb[:, :, c0:c0 + tw], ps_[:, :, 0:tw], r=[kps], w=[("pb", ti)])
    its = [(dmc, ti, c0, tw) for dmc in range(16) for ti, (c0, tw) in enumerate(TILES)]
    PF = 2

    def load_res(j):
        dmc, ti, c0, tw = its[j]
        P.dma("sp", xf[j % NB][:, 0:tw], sv[:, dmc, c0:c0 + tw], r=[("dram", src.name)], w=["xf%d" % (j % NB)])

    for j in range(min(PF, len(its))):
        load_res(j)
    for it, (dmc, ti, c0, tw) in enumerate(its):
        wd, kw = wD[dmc % 3], "wD%d" % (dmc % 3)
        if ti == 0:
            P.dma("pool", wd[:, :, :], wgv[:, :, dmc * 128:(dmc + 1) * 128], r=[], w=[kw])
        if it + PF < len(its):
            load_res(it + PF)
        g_ps, p_ps = PS[(it % 2) * 2], PS[(it % 2) * 2 + 1]
        kg, kp = "ps%d" % ((it % 2) * 2), "ps%d" % ((it % 2) * 2 + 1)
        for (o, n) in blocks(tw):
            for dc in range(16):
                P.mm(g_ps[:, o:o + n], wd[:, dc, :], xb[:, dc, c0 + o:c0 + o + n], dc == 0, dc == 15,
                     r=[kw] + xkeys, w=[kg])
            for j in range(2):
                P.mm(p_ps[:, o:o + n], wp[:, j, dmc * 128:(dmc + 1) * 128], pb[:, j, c0 + o:c0 + o + n], j == 0, j == 1,
                     r=["wp", ("pb", ti)], w=[kp])
        sgm, ksg = sg[it % 2], "sg%d" % (it % 2)
        P.act(sgm[:, 0:tw], g_ps[:, 0:tw], AF.Sigmoid, r=[kg], w=[ksg])
        t_, k1 = t1[it % 2], "t1%d" % (it % 2)
        P.tt("dve", t_[:, 0:tw], sgm[:, 0:tw], p_ps[:, 0:tw], ALU.mult, r=[ksg, kp], w=[k1])
        y_, ky = yo[it % NB], "yo%d" % (it % NB)
        P.tt("dve", y_[:, 0:tw], t_[:, 0:tw], xf[it % NB][:, 0:tw], ALU.add, r=[k1, "xf%d" % (it % NB)], w=[ky])
        P.dma("sp", dv[:, dmc, c0:c0 + tw], y_[:, 0:tw], r=[ky], w=[("dram", dst.name)])


def zero_dram(P, dst, nrows, ncols, dtype):
    P.reset_arena()
    z = P.carve(dtype, [ncols])
    P.memset("dve", z[:, :], 0.0, w=["z"])
    for r0 in range(0, nrows, 128):
        n = min(128, nrows - r0)
        P.dma("sp", dst[r0:r0 + n, :], z[0:n, :], r=["z"], w=[("dram", dst.name)])


def ffn_phase(P, src, dst, wup, wdn, lng_d, lnb_d):
    dense_bufs(P, lng_d, lnb_d)
    lng, lnb = P.lng, P.lnb
    wuv = fm(wup)
    wdv = fm(wdn)
    PS = P.PS
    for ti, (c0, tw) in enumerate(TILES):
        blks = blocks(tw)
        xkeys = load_cast_tile(P, src, c0, tw, P.xb)
        for fp in range(NFC // 2):
            wg, wu = P.wA[fp % 2], P.wB[fp % 2]
            kg, ku = "wA%d" % (fp % 2), "wB%d" % (fp % 2)
            P.dma("pool", wg[:, :, :], wuv[:, :, fp * 256:(fp + 1) * 256], r=[], w=[kg])
            P.dma("pool", wu[:, :, :], wuv[:, :, DFF + fp * 256: DFF + (fp + 1) * 256], r=[], w=[ku])
            for j in range(2):
                fc = fp * 2 + j
                g_ps, u_ps = PS[(fc % 2) * 2], PS[(fc % 2) * 2 + 1]
                kgp, kup = "ps%d" % ((fc % 2) * 2), "ps%d" % ((fc % 2) * 2 + 1)
                for (o, n) in blks:
                    for dc in range(16):
                        P.mm(g_ps[:, o:o + n], wg[:, dc, j * 128:(j + 1) * 128], P.xb[:, dc, o:o + n],
                             dc == 0, dc == 15, r=[kg] + xkeys, w=[kgp])
                for (o, n) in blks:
                    for dc in range(16):
                        P.mm(u_ps[:, o:o + n], wu[:, dc, j * 128:(j + 1) * 128], P.xb[:, dc, o:o + n],
                             dc == 0, dc == 15, r=[ku] + xkeys, w=[kup])
                sg = P.sg[fc % 2]
                ksg = "sg%d" % (fc % 2)
                P.act(sg[:, 0:tw], g_ps[:, 0:tw], AF.Silu, r=[kgp], w=[ksg])
                P.tt("dve", P.hT[:, fc, 0:tw], sg[:, 0:tw], u_ps[:, 0:tw], ALU.mult, r=[ksg, kup], w=[("hT", fc)])
        hkeys = [("hT", fc) for fc in range(NFC)]
        proj_res_ln(P, ti, c0, tw, P.hT, hkeys, NFC, wdv, 0, src, 0.5, lng, lnb, dst)


TOKBLKS = [(i * 128, 128) for i in range(16)] + [(2048, 1)]
COLBLKS = [(0, 512), (512, 512), (1024, 512), (1536, 512), (2048, 1)]


def load_all_bf16(P, src, xb_all):
    keys = []
    sv = fm(src)
    i = 0
    for (c0, tw) in TILES:
        for q in range(0, 16, 2):
            stg = P.xstg[i % 2]
            ks = "xstg%d" % (i % 2)
            P.dma("sp", stg[:, 0:2, 0:tw], sv[:, q:q + 2, c0:c0 + tw], r=[("dram", src.name)], w=[ks])
            kx = ("xball", q, c0)
            P.copy("dve" if i % 2 == 0 else "act", xb_all[:, q:q + 2, c0:c0 + tw], stg[:, 0:2, 0:tw], r=[ks], w=[kx])
            keys.append(kx)
            i += 1
    return keys


def proj_phase(P, src, wv, groups):
    P.reset_arena()
    xb = P.carve(BF16, [16, NT])
    wb = [P.carve(BF16, [16, 512]) for _ in range(2)]
    P.xstg = [P.carve(F32, [2, 513]) for _ in range(2)]
    stg_f = [P.carve(F32, [NT]) for _ in range(2)]
    stg_t = [P.carve(F32, [512]) for _ in range(2)]
    xkeys = load_all_bf16(P, src, xb)
    PS = P.PS
    wi = 0
    si = 0
    pi = 0
    for g in groups:
        col0, ncols = g["col0"], g["ncols"]
        for c in range(0, ncols, 512):
            n = min(512, ncols - c)
            w = wb[wi % 2]
            kw = "wb%d" % (wi % 2)
            wi += 1
            P.dma("pool", w[:, :, 0:n], wv[:, :, col0 + c: col0 + c + n], r=[], w=[kw])
            if g["mode"] == "fm":
                for cc in range(0, n, 128):
                    m = min(128, n - cc)
                    stg = stg_f[si % 2]
                    kst = "stgf%d" % (si % 2)
                    si += 1
                    sdt = stg if g["dtype"] == F32 else stg.bitcast(BF16)
                    for (o, nn) in COLBLKS:
                        ps = PS[pi % 4]
                        kp = "ps%d" % (pi % 4)
                        pi += 1
                        for dc in range(16):
                            P.mm(ps[0:m, 0:nn], w[:, dc, cc:cc + m], xb[:, dc, o:o + nn], dc == 0, dc == 15,
                                 r=[kw] + xkeys, w=[kp])
                        kwargs = {}
                        if g.get("bias") is not None:
                            P.act(sdt[0:m, o:o + nn], ps[0:m, 0:nn], AF.Identity, r=[kp, "bias"], w=[kst],
                                  bias=g["bias"][0:m, 0:1], scale=float(g.get("scale", 1.0)))
                        elif pi % 2 == 0:
                            P.act(sdt[0:m, o:o + nn], ps[0:m, 0:nn], AF.Copy, r=[kp], w=[kst],
                                  scale=float(g.get("scale", 1.0)))
                        else:
                            P.ts("dve", sdt[0:m, o:o + nn], ps[0:m, 0:nn], float(g.get("scale", 1.0)), None,
                                 ALU.mult, None, r=[kp], w=[kst])
                    dst = g["dst"]
                    P.dma("sp", dst[c + cc: c + cc + m, :], sdt[0:m, 0:NT], r=[kst], w=[("dram", dst.name)])
            else:
                for (t0, tn) in TOKBLKS:
                    ps = PS[pi % 4]
                    kp = "ps%d" % (pi % 4)
                    pi += 1
                    for dc in range(16):
                        P.mm(ps[0:tn, 0:n], xb[:, dc, t0:t0 + tn], w[:, dc, 0:n], dc == 0, dc == 15,
                             r=[kw] + xkeys, w=[kp])
                    stg = stg_t[si % 2]
                    kst = "stgt%d" % (si % 2)
                    si += 1
                    sdt = stg if g["dtype"] == F32 else stg.bitcast(BF16)
                    if pi % 2 == 0:
                        P.act(sdt[0:tn, 0:n], ps[0:tn, 0:n], AF.Copy, r=[kp], w=[kst])
                    else:
                        P.copy("dve", sdt[0:tn, 0:n], ps[0:tn, 0:n], r=[kp], w=[kst])
                    for (dst, dcol0, scol0, scn, prompt_only, sample_only) in g["dsts"]:
                        lo = max(scol0, c)
                        hi = min(scol0 + scn, c + n)
                        if lo >= hi:
                            continue
                        if tn == 1:
                            if prompt_only:
                                continue
                            row = 0 if sample_only else t0
                        else:
                            if sample_only:
                                continue
                            row = t0
                        P.dma("sp", dst[row:row + tn, dcol0 + lo - scol0: dcol0 + hi - scol0],
                              sdt[0:tn, lo - c: hi - c], r=[kst], w=[("dram", dst.name)])


def even_proj(P, x1, w_in, bgate_d, O):
    wv = fm(w_in)
    groups = [
        dict(mode="fm", col0=0, ncols=1024, dst=O["QAT"], dtype=BF16, scale=1.0 / 16.0),
        dict(mode="fm", col0=1024, ncols=1024, dst=O["KAT"], dtype=BF16),
        dict(mode="fm", col0=4096, ncols=8, dst=O["GT"], dtype=F32, bias="BG"),
        dict(mode="fm", col0=4104, ncols=1024, dst=O["QBT"], dtype=BF16, scale=128.0 ** -0.5),
        dict(mode="fm", col0=5128, ncols=256, dst=O["KBT"], dtype=BF16),
        dict(mode="fm", col0=5640, ncols=512, dst=O["QIT"], dtype=BF16, scale=64.0 ** -0.5),
        dict(mode="fm", col0=6152, ncols=64, dst=O["KIT"], dtype=BF16),
        dict(mode="tm", col0=1024, ncols=1024, dtype=BF16, dsts=[(O["KA"], 0, 0, 1024, False, False)]),
        dict(mode="tm", col0=2048, ncols=1024, dtype=BF16, dsts=[(O["VA"], 0, 0, 1024, False, False)]),
        dict(mode="tm", col0=3072, ncols=1024, dtype=F32, dsts=[(O["OA"], 0, 0, 1024, False, False)]),
        dict(mode="tm", col0=5128, ncols=512, dtype=F32, dsts=[
            (O["k_rows_p"], 0, 0, 256, True, False), (O["k_rows_s"], 0, 0, 256, False, True),
            (O["v_rows_p"], 0, 256, 256, True, False), (O["v_rows_s"], 0, 256, 256, False, True)]),
        dict(mode="tm", col0=6152, ncols=72, dtype=F32, dsts=[
            (O["ik_rows_p"], 0, 0, 64, True, False), (O["ik_rows_s"], 0, 0, 64, False, True),
            (O["WI"], 0, 64, 8, False, False)]),
    ]
    P._bg_d = bgate_d
    _proj_with_bias(P, x1, wv, groups)


def _proj_with_bias(P, x1, wv, groups):
    orig_reset = P.reset_arena

    def reset_and_bias():
        orig_reset()
        bg = P.carve(F32, [1])
        P.dma("sp", bg[0:8, 0:1], P._bg_d, r=[], w=["bias"])
        for g in groups:
            if g.get("bias") == "BG":
                g["bias"] = bg

    P.reset_arena = reset_and_bias
    try:
        proj_phase(P, x1, wv, groups)
    finally:
        P.reset_arena = orig_reset


def pc_layout(v):
    sh = v.shape[:-1]
    return np.ascontiguousarray(v.reshape(sh + (16, 128)).swapaxes(-1, -2))


def odd_proj(P, x5, w_in, O):
    wv = fm(w_in)
    groups = [
        dict(mode="fm", col0=0, ncols=1024, dst=O["QCT"], dtype=F32),
        dict(mode="fm", col0=1024, ncols=1024, dst=O["FCT"], dtype=F32),
        dict(mode="fm", col0=4096, ncols=1024, dst=O["QDT"], dtype=F32),
        dict(mode="fm", col0=5120, ncols=1024, dst=O["KDT"], dtype=F32, scale=1.0 / 16.0),
        dict(mode="tm", col0=1024, ncols=1024, dtype=F32, dsts=[(O["FC"], 0, 0, 1024, False, False)]),
        dict(mode="tm", col0=2048, ncols=1024, dtype=BF16, dsts=[(O["IC"], 0, 0, 1024, False, False)]),
        dict(mode="tm", col0=3072, ncols=1024, dtype=F32, dsts=[(O["GC"], 0, 0, 1024, False, False)]),
        dict(mode="tm", col0=5120, ncols=1024, dtype=F32, dsts=[(O["KD"], 0, 0, 1024, False, False)]),
        dict(mode="tm", col0=6144, ncols=1024, dtype=BF16, dsts=[(O["VD"], 0, 0, 1024, False, False)]),
        dict(mode="tm", col0=7168, ncols=1024, dtype=F32, dsts=[(O["GD"], 0, 0, 1024, False, False)]),
    ]
    proj_phase(P, x5, wv, groups)


def build(stop_after=None, debug_out=(), mixers=True):
    P = Prog(debug_out=debug_out)
    xin = P.inp("xin", [D, NT])
    wup = P.inp("w_ffn_up", [2, 2, D, 2 * DFF])
    wdn = P.inp("w_ffn_down", [2, 2, DFF, D])
    lng = P.inp("ln_g", [2, 3, 128, 16])
    lnb = P.inp("ln_b", [2, 3, 128, 16])
    w_in_even = P.inp("w_in_even", [D, 6224])
    w_in_odd = P.inp("w_in_odd", [D, 8192])
    w_out = P.inp("w_out", [2, D, D])
    w_pg = P.inp("w_pe_gate", [2, D, D])
    w_pp = P.inp("w_pe_proj", [2, 256, D])
    pT = P.inp("pT", [2, 256, NT])
    bgate = P.inp("b_gate", [8, 1])
    x = [None] + [P.scratch("x%d" % i, [D, NT]) for i in range(1, 8)]
    yT_out = P.out("yT_out", [D, NT])
    YT0 = P.scratch("YT0", [D, NT], BF16)
    YT1 = P.scratch("YT1", [D, NT], BF16)
    O = {}
    for nm, shp, dt in [("QAT", [1024, NT], BF16), ("KAT", [1024, NT], BF16), ("GT", [8, NT], F32),
                        ("QBT", [1024, NT], BF16), ("KBT", [256, NT], BF16), ("QIT", [512, NT], BF16),
                        ("KIT", [64, NT], BF16), ("KA", [NT, 1024], BF16), ("VA", [NT, 1024], BF16),
                        ("OA", [NT, 1024], F32), ("WI", [NT, 8], F32),
                        ("QCT", [1024, NT], F32), ("FCT", [1024, NT], F32), ("QDT", [1024, NT], F32),
                        ("KDT", [1024, NT], F32), ("FC", [NT, 1024], F32), ("IC", [NT, 1024], BF16),
                        ("GC", [NT, 1024], F32), ("KD", [NT, 1024], F32), ("VD", [NT, 1024], BF16),
                        ("GD", [NT, 1024], F32)]:
        O[nm] = P.scratch(nm, shp, dt)
    for nm, shp in [("k_rows_p", [SEQ, 256]), ("k_rows_s", [1, 256]), ("v_rows_p", [SEQ, 256]),
                    ("v_rows_s", [1, 256]), ("ik_rows_p", [SEQ, 64]), ("ik_rows_s", [1, 64])]:
        O[nm] = P.out(nm, shp)
    P.O = O
    Cn = {}
    Cn["ident"] = P.inp("c_ident", [128, 128])
    Cn["maskT"] = P.inp("c_maskT", [128, 128])
    Cn["sel4"] = P.inp("c_sel4", [4, 4, 128])
    Cn["g_mlstm_bc"] = P.inp("g_mlstm_bc", [128, 1024])
    Cn["cmask"] = P.inp("c_cmask", [128, 128])
    Cn["page_col"] = P.inp("page_col", [128, 1], I32)
    Cn["cosT"] = P.inp("c_cosT", [128, NT])
    Cn["sinT"] = P.inp("c_sinT", [128, NT])
    Cn["cosK"] = P.inp("c_cosK", [NT, 128])
    Cn["sinK"] = P.inp("c_sinK", [NT, 128])
    Cn["ret_D"] = P.inp("c_ret_D", [4, 2, 128, 128])
    Cn["ret_tails"] = P.inp("c_ret_tails", [128, 4, 17])
    Cn["g_ret_bc"] = P.inp("g_ret_bc", [128, 1024])
    Cn["st_ret"] = P.inp("st_ret", [4, 256, 256])
    Cn["rt_p"] = P.out("rt_p", [4, 256, 256])
    Cn["rt_s"] = P.out("rt_s", [4, 256, 256])
    Cn["lb_bc"] = P.inp("lb_bc", [128, 2, 1024])
    Cn["lb_col"] = P.inp("lb_col", [128, 2, 8])
    Cn["g_hgrn_bc"] = P.inp("g_hgrn_bc", [128, 1024])
    Cn["st_hg"] = P.inp("st_hg", [8, 128, 128])
    Cn["hg_p"] = P.out("hg_p", [8, 128, 128])
    Cn["hg_s"] = P.out("hg_s", [8, 128, 128])
    Cn["cache_ik"] = P.inp("cache_ik", [1280, 8192])
    Cn["cache_k4"] = P.inp("cache_k4", [5120, 8192])
    Cn["cache_v4"] = P.inp("cache_v4", [5120, 8192])
    Cn["k_rows_s"] = O["k_rows_s"]
    Cn["v_rows_s"] = O["v_rows_s"]
    Cn["ik_rows_s"] = O["ik_rows_s"]
    Cn["mC_p"] = P.out("mC_p", [4, 256, 256])
    Cn["mn_p"] = P.out("mn_p", [4, 256])
    Cn["mm_p"] = P.out("mm_p", [4, 1])
    Cn["mC_s"] = P.out("mC_s", [4, 256, 256])
    Cn["mn_s"] = P.out("mn_s", [4, 256])
    Cn["mm_s"] = P.out("mm_s", [1, 4])
    Cn["st_C"] = P.inp("st_C", [4, 256, 256])
    Cn["st_n"] = P.inp("st_n", [4, 256])
    Cn["st_m"] = P.inp("st_m", [1, 4])
    P.C = Cn
    ffn_phase(P, xin, x[1], wup[0, 0], wdn[0, 0], lng[0, 0], lnb[0, 0])
    if stop_after == "ffn1":
        return P
    even_proj(P, x[1], w_in_even, bgate, O)
    if stop_after == "proj0":
        return P
    zero_dram(P, YT0, D, NT, BF16)
    zero_dram(P, YT1, D, NT, BF16)
    if mixers:
        mixers_even(P, O, YT0)
    if stop_after == "mix0":
        return P
    outproj_phase(P, YT0, x[1], x[2], w_out[0], lng[0, 1], lnb[0, 1])
    ffn_phase(P, x[2], x[3], wup[0, 1], wdn[0, 1], lng[0, 2], lnb[0, 2])
    pegate_phase(P, x[3], x[4], pT[0], w_pg[0], w_pp[0])
    ffn_phase(P, x[4], x[5], wup[1, 0], wdn[1, 0], lng[1, 0], lnb[1, 0])
    odd_proj(P, x[5], w_in_odd, O)
    if mixers:
        mixers_odd(P, O, YT1)
    outproj_phase(P, YT1, x[5], x[6], w_out[1], lng[1, 1], lnb[1, 1])
    ffn_phase(P, x[6], x[7], wup[1, 1], wdn[1, 1], lng[1, 2], lnb[1, 2])
    pegate_phase(P, x[7], yT_out, pT[1], w_pg[1], w_pp[1])
    return P


def scan_rows(P, src, tmp, n, op, ksrc, ktmp):
    cur, nxt, kc, kn = src, tmp, ksrc, ktmp
    k = 1
    while k < n:
        P.tt("dve", nxt[0:4, k:n], cur[0:4, k:n], cur[0:4, 0:n - k], op, r=[kc], w=[kn])
        P.copy("act", nxt[0:4, 0:k], cur[0:4, 0:k], r=[kc], w=[kn])
        cur, nxt, kc, kn = nxt, cur, kn, kc
        k *= 2
    return cur, kc


def mlstm_phase(P, O, YT0, C):
    P.reset_arena()
    PS = P.PS
    S_ = SEQ
    row = lambda: P.carve(F32, [NT])
    ig, fa, r0, r1, bt, at, Mt, negM, wrow, Erow = [row() for _ in range(10)]
    cols = P.carve(F32, [3, 16, 4])
    sel = P.carve(F32, [4, 128])
    ident = P.carve(F32, [128])
    identb = P.carve(BF16, [128])
    maskT = P.carve(F32, [128])
    gbc = P.carve(F32, [1024])
    QT = P.carve(BF16, [2, NT])
    KT = P.carve(BF16, [2, NT])
    KAt = P.carve(BF16, [17, 256])
    KW = P.carve(BF16, [16, 256])
    VA1 = P.carve(BF16, [17, 257])
    oat = [P.carve(F32, [256]) for _ in range(2)]
    dwt = [P.carve(F32, [128]) for _ in range(3)]
    pt = [P.carve(BF16, [128]) for _ in range(3)]
    hh = [P.carve(F32, [256]) for _ in range(2)]
    junk = P.carve(F32, [256])
    sm = [P.carve(F32, [8]) for _ in range(2)]
    yab = [P.carve(BF16, [256]) for _ in range(2)]
    yT = [P.carve(BF16, [2, 128]) for _ in range(2)]
    cst = [P.carve(F32, [257]) for _ in range(2)]
    Cin = P.carve(F32, [8, 256])
    Cinb = P.carve(BF16, [8, 256])
    nin = P.carve(F32, [8])
    ninb = P.carve(BF16, [8])
    srow = P.carve(F32, [64])
    bc8 = P.carve(F32, [8])
    qcol = P.carve(BF16, [8])
    kcolf = P.carve(F32, [8])
    nnew = P.carve(F32, [8])
    one1 = P.carve(F32, [1])
    one1b = P.carve(BF16, [1])
    yrow = P.carve(BF16, [1024])
    hrow = P.carve(F32, [1024])
    oarow = P.carve(F32, [1024])

    P.dma("sp", ident[:, :], C["ident"], r=[], w=["ident"])
    P.copy("dve", identb[:, :], ident[:, :], r=["ident"], w=["identb"])
    P.dma("sp", maskT[:, :], C["maskT"], r=[], w=["maskT"])
    P.dma("sp", sel[0:4, :, :], C["sel4"], r=[], w=["sel"])
    P.dma("sp", gbc[:, :], C["g_mlstm_bc"], r=[], w=["gbc"])
    P.dma("sp", ig[0:4, :], O["GT"][0:4, :], r=[("dram", "GT")], w=["ig"])
    P.dma("sp", fa[0:4, :], O["GT"][4:8, :], r=[("dram", "GT")], w=["fa"])
    P.memset("dve", one1[:, :], 1.0, w=["one1"])
    P.memset("dve", one1b[:, :], 1.0, w=["one1b"])
    P.act(fa[0:4, :], fa[0:4, :], AF.Exp, r=["fa"], w=["fa"], scale=-1.0)
    P.act(fa[0:4, :], fa[0:4, :], AF.Ln, r=["fa"], w=["fa"], bias=1.0)
    P.ts("dve", fa[0:4, :], fa[0:4, :], -1.0, None, ALU.mult, None, r=["fa"], w=["fa"])
    P.copy("dve", r0[0:4, 0:S_], fa[0:4, 0:S_], r=["fa"], w=["r0"])
    bres, kb = scan_rows(P, r0, r1, S_, ALU.add, "r0", "r1")
    P.copy("dve", bt[0:4, 0:S_], bres[0:4, 0:S_], r=[kb], w=["bt"])
    P.tt("dve", at[0:4, 0:S_], ig[0:4, 0:S_], bt[0:4, 0:S_], ALU.subtract, r=["ig", "bt"], w=["at"])
    P.copy("dve", r0[0:4, 0:S_], at[0:4, 0:S_], r=["at", kb], w=["r0"])
    P.S.add("dve", None, r=["r1"], w=["r1"])
    mres, km = scan_rows(P, r0, r1, S_, ALU.max, "r0", "r1")
    P.ts("dve", Mt[0:4, 0:S_], mres[0:4, 0:S_], 0.0, None, ALU.max, None, r=[km], w=["Mt"])
    P.ts("dve", negM[0:4, 0:S_], Mt[0:4, 0:S_], -1.0, None, ALU.mult, None, r=["Mt"], w=["negM"])
    P.tt("dve", Erow[0:4, 0:S_], bt[0:4, 0:S_], Mt[0:4, 0:S_], ALU.add, r=["bt", "Mt"], w=["Erow"])
    P.dma("sp", C["mm_p"], Erow[0:4, S_ - 1:S_], r=["Erow"], w=[("dram", "mm_p")])
    P.act(Erow[0:4, 0:S_], Erow[0:4, 0:S_], AF.Exp, r=["Erow"], w=["Erow"], scale=-1.0)
    P.act(wrow[0:4, 0:S_], at[0:4, 0:S_], AF.Exp, r=["at", "negM"], w=["wrow"], bias=negM[0:4, S_ - 1:S_])
    tp = PS[0]
    for qi_, (rt, kr) in enumerate([(at, "at"), (wrow, "wrow"), (Erow, "Erow")]):
        for blk in range(16):
            o = (qi_ * 16 + blk) * 4
            P.mm(tp[:, o:o + 4], rt[0:4, blk * 128:(blk + 1) * 128], ident[0:4, 0:4], True, True,
                 r=[kr, "ident"], w=["ps0_0"])
    P.copy("dve", cols[:, :, :, :].rearrange("p a b c -> p (a b c)"), tp[:, 0:192], r=["ps0_0"], w=["cols"])

    rp = PS[1][0:1, 0:8]
    P.mm(rp[:, 0:4], ig[0:4, S_:NT], ident[0:4, 0:4], True, True, r=["ig", "ident"], w=["ps1a"])
    P.mm(rp[:, 4:8], fa[0:4, S_:NT], ident[0:4, 0:4], True, True, r=["fa", "ident"], w=["ps1a"])
    P.copy("dve", srow[0:1, 0:8], rp, r=["ps1a"], w=["srow"])
    P.dma("sp", srow[0:1, 8:12], C["st_m"], r=[], w=["srow"])
    P.tt("dve", srow[0:1, 12:16], srow[0:1, 4:8], srow[0:1, 8:12], ALU.add, r=["srow"], w=["srow"])
    P.tt("dve", srow[0:1, 16:20], srow[0:1, 12:16], srow[0:1, 0:4], ALU.max, r=["srow"], w=["srow"])
    P.tt("dve", srow[0:1, 20:24], srow[0:1, 0:4], srow[0:1, 16:20], ALU.subtract, r=["srow"], w=["srow"])
    P.tt("dve", srow[0:1, 24:28], srow[0:1, 12:16], srow[0:1, 16:20], ALU.subtract, r=["srow"], w=["srow"])
    P.act(srow[0:1, 20:28], srow[0:1, 20:28], AF.Exp, r=["srow"], w=["srow"])
    P.act(srow[0:1, 28:32], srow[0:1, 16:20], AF.Exp, r=["srow"], w=["srow"], scale=-1.0)
    P.dma("sp", C["mm_s"], srow[0:1, 16:20], r=["srow"], w=[("dram", "mm_s")])
    bp = PS[1][:, 0:8]
    P.mm(bp, maskT[0:1, :], srow[0:1, 20:28], True, True, r=["maskT", "srow"], w=["ps1a"])
    P.copy("dve", bc8[:, :], bp, r=["ps1a"], w=["bc8"])
    P.dma("sp", oarow[0:1, :], O["OA"][S_:NT, :], r=[("dram", "OA")], w=["oarow"])
    P.act(oarow[0:1, :], oarow[0:1, :], AF.Sigmoid, r=["oarow"], w=["oarow"])
    P.tt("dve", oarow[0:1, :], oarow[0:1, :], gbc[0:1, :], ALU.mult, r=["oarow", "gbc"], w=["oarow"])

    QATv, KATv = fm(O["QAT"]), fm(O["KAT"])
    KAv = O["KA"][0:S_, :].rearrange("(b p) c -> p b c", p=128)
    VAv = O["VA"][0:S_, :].rearrange("(b p) c -> p b c", p=128)
    OAv = O["OA"][0:S_, :].rearrange("(b p) c -> p b c", p=128)
    YTv = fm(YT0)
    it = 0
    for h in range(4):
        hs = slice(h * 256, (h + 1) * 256)
        P.dma("sp", QT[:, :, :], QATv[:, 2 * h:2 * h + 2, :], r=[("dram", "QAT")], w=["QT"])
        P.dma("sp", KT[:, :, :], KATv[:, 2 * h:2 * h + 2, :], r=[("dram", "KAT")], w=["KT"])
        P.dma("sp", KAt[:, 0:16, :], KAv[:, :, hs], r=[("dram", "KA")], w=["KAt"])
        P.dma("sp", KAt[0:1, 16, :], O["KA"][S_:NT, hs], r=[("dram", "KA")], w=["KAt"])
        P.dma("sp", VA1[:, 0:16, 0:256], VAv[:, :, hs], r=[("dram", "VA")], w=["VA1"])
        P.dma("sp", VA1[0:1, 16, 0:256], O["VA"][S_:NT, hs], r=[("dram", "VA")], w=["VA1"])
        P.memset("pool", VA1[:, :, 256:257], 1.0, w=["VA1"])
        for tq in range(16):
            ts_ = slice(tq * 128, (tq + 1) * 128)
            nb_ps = PS[1][:, 0:128]
            P.mm(nb_ps, sel[0:4, h, :], negM[0:4, ts_], True, True, r=["sel", "negM"], w=["ps1a"])
            num_ps = PS[2 + (tq % 2)][:, 0:257]
            knum = "ps%d" % (2 + tq % 2)
            for sb in range(tq + 1):
                ss = slice(sb * 128, (sb + 1) * 128)
                sT = PS[0][:, (it % 2) * 512:(it % 2) * 512 + 128]
                ksT = "ps0_%d" % (it % 2)
                P.mm(sT, KT[:, 0, ss], QT[:, 0, ts_], True, False, r=["KT", "QT"], w=[ksT])
                P.mm(sT, KT[:, 1, ss], QT[:, 1, ts_], False, True, r=["KT", "QT"], w=[ksT])
                dw = dwt[it % 3]
                kdw = "dw%d" % (it % 3)
                P.act(dw[:, :], nb_ps, AF.Exp, r=["ps1a", "cols"], w=[kdw], bias=cols[:, 0, sb, h:h + 1])
                if sb == tq:
                    P.tt("pool", dw[:, :], dw[:, :], maskT[:, :], ALU.mult, r=[kdw, "maskT"], w=[kdw])
                p_ = pt[it % 3]
                kp = "pt%d" % (it % 3)
                P.tt("dve", p_[:, :], sT, dw[:, :], ALU.mult, r=[ksT, kdw], w=[kp])
                P.mm(num_ps, p_[:, :], VA1[:, sb, :], sb == 0, sb == tq, r=[kp, "VA1"], w=[knum])
                it += 1
            i2 = tq % 2
            smt = sm[i2]
            ksm = "sm%d" % i2
            P.act(smt[:, 0:1], num_ps[:, 256:257], AF.Abs, r=[knum], w=[ksm])
            P.tt("dve", smt[:, 1:2], smt[:, 0:1], cols[:, 2, tq, h:h + 1], ALU.max, r=[ksm, "cols"], w=[ksm])
            P.S.add("dve", lambda e, a=smt: e.reciprocal(a[:, 2:3], a[:, 1:2]), r=[ksm], w=[ksm])
            hht = hh[i2]
            khh = "hh%d" % i2
            P.ts("dve", hht[:, :], num_ps[:, 0:256], smt[:, 2:3], None, ALU.mult, None, r=[knum, ksm], w=[khh])
            P.act(junk[:, :], hht[:, :], AF.Square, r=[khh], w=["junk", ksm], accum_out=smt[:, 3:4])
            P.ts("dve", smt[:, 4:5], smt[:, 3:4], 1.0 / 256.0, float(LN_EPS), ALU.mult, ALU.add, r=[ksm], w=[ksm])
            P.act(smt[:, 4:5], smt[:, 4:5], AF.Sqrt, r=[ksm], w=[ksm])
            P.S.add("dve", lambda e, a=smt: e.reciprocal(a[:, 5:6], a[:, 4:5]), r=[ksm], w=[ksm])
            oa = oat[i2]
            koa = "oa%d" % i2
            P.dma("sp", oa[:, :], OAv[:, tq, hs], r=[("dram", "OA")], w=[koa])
            P.act(oa[:, :], oa[:, :], AF.Sigmoid, r=[koa], w=[koa])
            P.tt("pool", oa[:, :], oa[:, :], gbc[:, hs], ALU.mult, r=[koa, "gbc"], w=[koa])
            ya = yab[i2]
            kya = "ya%d" % i2
            P.stt(ya[:, :], hht[:, :], smt[:, 5:6], oa[:, :], ALU.mult, ALU.mult, r=[khh, ksm, koa], w=[kya])
            tps = PS[1][:, 512:768].bitcast(BF16)
            yTt = yT[i2]
            kyT = "yT%d" % i2
            for j in range(2):
                P.S.add("pe", lambda e, o=tps[:, j * 128:(j + 1) * 128], i=ya[:, j * 128:(j + 1) * 128]:
                        e.transpose(o, i, identb[:, :]), r=[kya, "identb"], w=["ps1b"])
            P.copy("act", yTt[:, :, :].rearrange("p a b -> p (a b)"), tps[:, 0:256], r=["ps1b"], w=[kyT])
            P.dma("sp", YTv[:, 2 * h:2 * h + 2, ts_], yTt[:, :, :], r=[kyT], w=[("dram", YT0.name)])
        for sb in range(16):
            P.ts("dve", KW[:, sb, :], KAt[:, sb, :], cols[:, 1, sb, h:h + 1], None, ALU.mult, None,
                 r=["KAt", "cols"], w=["KW"])
        for dcn in range(2):
            c_ps = PS[2 + dcn][:, 0:257]
            kc = "ps%d" % (2 + dcn)
            for sb in range(16):
                P.mm(c_ps, KW[:, sb, dcn * 128:(dcn + 1) * 128], VA1[:, sb, :], sb == 0, sb == 15,
                     r=["KW", "VA1"], w=[kc])
            ct = cst[dcn]
            kct = "cst%d" % dcn
            P.copy("act", ct[:, :], c_ps, r=[kc], w=[kct])
            P.dma("sp", C["mC_p"][h, dcn * 128:(dcn + 1) * 128, :], ct[:, 0:256], r=[kct], w=[("dram", "mC_p")])
            P.dma("sp", C["mn_p"][h:h + 1, dcn * 128:(dcn + 1) * 128].rearrange("a d -> d a"), ct[:, 256:257],
                  r=[kct], w=[("dram", "mn_p")])
        P.dma("sp", Cin[:, 0:2, :], C["st_C"][h].rearrange("(c p) v -> p c v", p=128), r=[], w=["Cin"])
        P.dma("sp", nin[:, 0:2], C["st_n"][h:h + 1, :].rearrange("a (c p) -> p (a c)", p=128), r=[], w=["nin"])
        P.copy("act", Cinb[:, 0:2, :], Cin[:, 0:2, :], r=["Cin"], w=["Cinb"])
        P.copy("dve", ninb[:, 0:2], nin[:, 0:2], r=["nin"], w=["ninb"])
        P.copy("dve", kcolf[:, 0:2], KT[:, :, S_], r=["KT"], w=["kcolf"])
        rps = PS[1][0:1, 0:512]
        for dcn in range(2):
            P.mm(rps[:, 0:256], QT[:, dcn, S_:NT], Cinb[:, dcn, :], dcn == 0, dcn == 1, r=["QT", "Cinb"], w=["ps1a"])
        for dcn in range(2):
            P.mm(rps[:, 256:257], QT[:, dcn, S_:NT], KT[:, dcn, S_:NT], dcn == 0, dcn == 1, r=["QT", "KT"], w=["ps1a"])
        for dcn in range(2):
            P.mm(rps[:, 257:258], QT[:, dcn, S_:NT], ninb[:, dcn:dcn + 1], dcn == 0, dcn == 1, r=["QT", "ninb"], w=["ps1a"])
        dwh, gwh, emh = srow[0:1, 20 + h:21 + h], srow[0:1, 24 + h:25 + h], srow[0:1, 28 + h:29 + h]
        sc = srow[0:1, 32:48]
        P.ts("dve", sc[:, 0:1], rps[:, 256:257], dwh, None, ALU.mult, None, r=["ps1a", "srow"], w=["sc"])
        P.ts("dve", sc[:, 1:2], rps[:, 257:258], gwh, None, ALU.mult, None, r=["ps1a", "srow"], w=["sc"])
        P.tt("dve", sc[:, 2:3], sc[:, 0:1], sc[:, 1:2], ALU.add, r=["sc"], w=["sc"])
        P.act(sc[:, 3:4], sc[:, 2:3], AF.Abs, r=["sc"], w=["sc"])
        P.tt("dve", sc[:, 4:5], sc[:, 3:4], emh, ALU.max, r=["sc", "srow"], w=["sc"])
        P.S.add("dve", lambda e, a=sc: e.reciprocal(a[:, 5:6], a[:, 4:5]), r=["sc"], w=["sc"])
        P.ts("dve", hrow[0:1, hs], rps[:, 0:256], gwh, None, ALU.mult, None, r=["ps1a", "srow"], w=["hrow"])
        P.stt(hrow[0:1, hs], VA1[0:1, 16, 0:256], sc[:, 0:1], hrow[0:1, hs], ALU.mult, ALU.add,
              r=["VA1", "sc", "hrow"], w=["hrow"])
        P.ts("dve", hrow[0:1, hs], hrow[0:1, hs], sc[:, 5:6], None, ALU.mult, None, r=["hrow", "sc"], w=["hrow"])
        P.act(junk[0:1, :], hrow[0:1, hs], AF.Square, r=["hrow"], w=["junk", "sc"], accum_out=sc[:, 6:7])
        P.ts("dve", sc[:, 7:8], sc[:, 6:7], 1.0 / 256.0, float(LN_EPS), ALU.mult, ALU.add, r=["sc"], w=["sc"])
        P.act(sc[:, 7:8], sc[:, 7:8], AF.Sqrt, r=["sc"], w=["sc"])
        P.S.add("dve", lambda e, a=sc: e.reciprocal(a[:, 8:9], a[:, 7:8]), r=["sc"], w=["sc"])
        P.stt(yrow[0:1, hs], hrow[0:1, hs], sc[:, 8:9], oarow[0:1, hs], ALU.mult, ALU.mult,
              r=["hrow", "sc", "oarow"], w=["yrow"])
        cps = PS[1][:, 512:514]
        for j in range(2):
            P.mm(cps[:, j:j + 1], yrow[0:1, h * 256 + j * 128: h * 256 + (j + 1) * 128], one1b[0:1, 0:1], True, True,
                 r=["yrow", "one1b"], w=["ps1b"])
        P.copy("dve", qcol[:, 0:2], cps, r=["ps1b"], w=["qcol"])
        P.dma("sp", YTv[:, 2 * h:2 * h + 2, S_], qcol[:, 0:2], r=["qcol"], w=[("dram", YT0.name)])
        P.ts("dve", yrow[0:1, 0:256] if False else KW[0:1, 0, :], KAt[0:1, 16, :], dwh, None, ALU.mult, None,
             r=["KAt", "srow", "KW"], w=["KW"])
        for dcn in range(2):
            kv_ps = PS[2 + dcn][:, 0:256]
            kc = "ps%d" % (2 + dcn)
            P.mm(kv_ps, KW[0:1, 0, dcn * 128:(dcn + 1) * 128], VA1[0:1, 16, 0:256], True, True,
                 r=["KW", "VA1"], w=[kc])
            ct = cst[dcn]
            kct = "cst%d" % dcn
            P.stt(ct[:, 0:256], Cin[:, dcn, :], bc8[:, 4 + h:5 + h], kv_ps, ALU.mult, ALU.add,
                  r=["Cin", "bc8", kc], w=[kct])
            P.dma("sp", C["mC_s"][h, dcn * 128:(dcn + 1) * 128, :], ct[:, 0:256], r=[kct], w=[("dram", "mC_s")])
        P.ts("dve", kcolf[:, 0:2], kcolf[:, 0:2], bc8[:, h:h + 1], None, ALU.mult, None, r=["kcolf", "bc8"], w=["kcolf"])
        P.stt(nnew[:, 0:2], nin[:, 0:2], bc8[:, 4 + h:5 + h], kcolf[:, 0:2], ALU.mult, ALU.add,
              r=["nin", "bc8", "kcolf"], w=["nnew"])
        P.dma("sp", C["mn_s"][h:h + 1, :].rearrange("a (c p) -> p (a c)", p=128), nnew[:, 0:2], r=["nnew"],
              w=[("dram", "mn_s")])
    P.mlstm_sample_done = True


def dsa_prompt_phase(P, O, YT0, C):
    P.reset_arena()
    PS = P.PS
    S_ = SEQ
    NSEL = 256
    QI = P.carve(BF16, [4, NT])
    KI2 = P.carve(BF16, [NT])
    QB = P.carve(BF16, [8, NT])
    KB = P.carve(BF16, [2, NT])
    VBf = P.carve(F32, [2, 256])
    VB = P.carve(BF16, [16, 256])
    WIc = P.carve(F32, [16, 8])
    acc = P.carve(F32, [S_])
    rl = [P.carve(F32, [S_]) for _ in range(2)]
    junk = P.carve(BF16, [S_])
    mask = P.carve(BF16, [S_])
    pp = [P.carve(BF16, [S_]) for _ in range(2)]
    pm = [P.carve(BF16, [S_]) for _ in range(2)]
    pT = [P.carve(BF16, [8, 128]) for _ in range(2)]
    ybt = P.carve(BF16, [1024])
    yT = P.carve(BF16, [8, 128])
    cmask = P.carve(F32, [128])
    ident = P.carve(F32, [128])
    identb = P.carve(BF16, [128])
    sm = P.carve(F32, [16])
    hsm = [P.carve(F32, [4]) for _ in range(2)]
    P.dma("sp", ident[:, :], C["ident"], r=[], w=["ident"])
    P.copy("dve", identb[:, :], ident[:, :], r=["ident"], w=["identb"])
    P.dma("sp", cmask[:, :], C["cmask"], r=[], w=["cmask"])
    P.dma("sp", QI[:, :, :], fm(O["QIT"]), r=[("dram", "QIT")], w=["QI"])
    P.dma("sp", KI2[0:64, :], O["KIT"], r=[("dram", "KIT")], w=["KI2"])
    P.dma("sp", KI2[64:128, :], O["KIT"], r=[("dram", "KIT")], w=["KI2"])
    P.dma("sp", QB[:, :, :], fm(O["QBT"]), r=[("dram", "QBT")], w=["QB"])
    P.dma("sp", KB[:, :, :], fm(O["KBT"]), r=[("dram", "KBT")], w=["KB"])
    P.dma("sp", WIc[:, :, :], O["WI"][0:S_, :].rearrange("(b p) h -> p b h", p=128), r=[("dram", "WI")], w=["WIc"])
    vv = O["v_rows_p"].rearrange("(b p) c -> p b c", p=128)
    for b2 in range(0, 16, 2):
        P.dma("sp", VBf[:, :, :], vv[:, b2:b2 + 2, :], r=[("dram", "v_rows_p")], w=["VBf"])
        P.copy("dve", VB[:, b2:b2 + 2, :], VBf[:, :, :], r=["VBf"], w=["VB"])
    YTv = fm(YT0)
    for tq in range(16):
        ts_ = slice(tq * 128, (tq + 1) * 128)
        nk = (tq + 1) * 128
        halves = [(o, min(1024, nk - o)) for o in range(0, nk, 1024)]
        for h in range(8):
            pb = (h % 2) * 64
            r_ = rl[h % 2]
            kr = "rl%d" % (h % 2)
            for (o, n) in halves:
                ps = PS[(h % 2) * 2 + o // 1024]
                kps = "ps%d" % ((h % 2) * 2 + o // 1024)
                for (bo, bn) in blocks(n):
                    P.mm(ps[:, bo:bo + bn], QI[pb:pb + 64, h // 2, ts_], KI2[pb:pb + 64, o + bo:o + bo + bn], True, True,
                         r=["QI", "KI2"], w=[kps])
                P.act(r_[:, o:o + n], ps[:, 0:n], AF.Relu, r=[kps], w=[kr])
            if h == 0:
                P.ts("dve", acc[:, 0:nk], r_[:, 0:nk], WIc[:, tq, 0:1], None, ALU.mult, None, r=[kr, "WIc"], w=["acc"])
            else:
                P.stt(acc[:, 0:nk], r_[:, 0:nk], WIc[:, tq, h:h + 1], acc[:, 0:nk], ALU.mult, ALU.add,
                      r=[kr, "WIc", "acc"], w=["acc"])
        P.S.add("dve", lambda e, a=acc, n_=nk: e.tensor_reduce(sm[:, 0:1], a[:, 0:n_], AX.X, ALU.min), r=["acc"], w=["sm"])
        P.S.add("dve", lambda e, a=acc, n_=nk: e.tensor_reduce(sm[:, 1:2], a[:, 0:n_], AX.X, ALU.max), r=["acc"], w=["sm"])
        P.tt("dve", acc[:, nk - 128:nk], acc[:, nk - 128:nk], cmask[:, :], ALU.add, r=["acc", "cmask"], w=["acc"])
        if tq >= 2:
            P.tt("dve", sm[:, 2:3], sm[:, 1:2], sm[:, 0:1], ALU.subtract, r=["sm"], w=["sm"])
            for it in range(26):
                ci = 0.5 ** (it + 1)
                P.ts("dve", sm[:, 3:4], sm[:, 2:3], float(ci), sm[:, 0:1], ALU.mult, ALU.add, r=["sm"], w=["sm"])
                P.ts("dve", junk[:, 0:nk], acc[:, 0:nk], sm[:, 3:4], 0.0, ALU.is_ge, ALU.add, r=["acc", "sm"],
                     w=["junk", "sm"], accum_out=sm[:, 4:5])
                P.ts("dve", sm[:, 5:6], sm[:, 4:5], NSEL - 0.5, float(ci), ALU.is_ge, ALU.mult, r=["sm"], w=["sm"])
                P.stt(sm[:, 0:1], sm[:, 5:6], sm[:, 2:3], sm[:, 0:1], ALU.mult, ALU.add, r=["sm"], w=["sm"])
        else:
            P.memset("dve", sm[:, 0:1], -1.0e29, w=["sm"])
        P.ts("dve", mask[:, 0:nk], acc[:, 0:nk], sm[:, 0:1], None, ALU.is_ge, None, r=["acc", "sm"], w=["mask"])
        for hq in range(8):
            kv = hq // 4
            hs_ = hsm[hq % 2]
            khs = "hsm%d" % (hq % 2)
            lps = []
            for (o, n) in halves:
                ps = PS[o // 1024]
                kps = "ps%d" % (o // 1024)
                for (bo, bn) in blocks(n):
                    P.mm(ps[:, bo:bo + bn], QB[:, hq, ts_], KB[:, kv, o + bo:o + bo + bn], True, True,
                         r=["QB", "KB"], w=[kps])
                lps.append((ps, kps, o, n))
            for i_, (ps, kps, o, n) in enumerate(lps):
                P.S.add("dve", lambda e, d=hs_[:, i_:i_ + 1], a=ps[:, 0:n]: e.tensor_reduce(d, a, AX.X, ALU.max),
                        r=[kps], w=[khs])
            if len(lps) == 2:
                P.tt("dve", hs_[:, 0:1], hs_[:, 0:1], hs_[:, 1:2], ALU.max, r=[khs], w=[khs])
            P.ts("dve", hs_[:, 2:3], hs_[:, 0:1], -1.0, None, ALU.mult, None, r=[khs], w=[khs])
            p_ = pp[hq % 2]
            kp = "pp%d" % (hq % 2)
            for (ps, kps, o, n) in lps:
                P.act(p_[:, o:o + n], ps[:, 0:n], AF.Exp, r=[kps, khs], w=[kp], bias=hs_[:, 2:3])
            pm_ = pm[hq % 2]
            kpm = "pm%d" % (hq % 2)
            P.stt(pm_[:, 0:nk], p_[:, 0:nk], 1.0, mask[:, 0:nk], ALU.mult, ALU.mult, r=[kp, "mask"], w=[kpm, khs],
                  accum_out=hs_[:, 3:4])
            o_ps = PS[3][:, 0:128]
            nsb = tq + 1
            for g0 in range(0, nsb, 8):
                gn = min(8, nsb - g0)
                gi = (g0 // 8) % 2
                tps = PS[2][:, gi * 512:(gi + 1) * 512].bitcast(BF16)
                ktp = "ps2_%d" % gi
                for j in range(gn):
                    sb = g0 + j
                    P.S.add("pe", lambda e, o_=tps[:, j * 128:(j + 1) * 128], i=pm_[:, sb * 128:(sb + 1) * 128]:
                            e.transpose(o_, i, identb[:, :]), r=[kpm, "identb"], w=[ktp])
                pT_ = pT[gi]
                kpT = "pT%d" % gi
                P.copy("act" if gi == 0 else "dve", pT_[:, 0:gn, :].rearrange("p a b -> p (a b)"), tps[:, 0:gn * 128],
                       r=[ktp], w=[kpT])
                for j in range(gn):
                    sb = g0 + j
                    P.mm(o_ps, pT_[:, j, :], VB[:, sb, kv * 128:(kv + 1) * 128], sb == 0, sb == nsb - 1,
                         r=[kpT, "VB"], w=["ps3a"])
            P.S.add("dve", lambda e, a=hs_: e.reciprocal(a[:, 3:4], a[:, 3:4]), r=[khs], w=[khs])
            P.ts("dve", ybt[:, hq * 128:(hq + 1) * 128], o_ps, hs_[:, 3:4], None, ALU.mult, None, r=["ps3a", khs],
                 w=["ybt"])
        tps = PS[3][:, 512:1024].bitcast(BF16)
        for hq in range(8):
            P.S.add("pe", lambda e, o_=tps[:, hq * 128:(hq + 1) * 128], i=ybt[:, hq * 128:(hq + 1) * 128]:
                    e.transpose(o_, i, identb[:, :]), r=["ybt", "identb"], w=["ps3b"])
        P.copy("act", yT[:, :, :].rearrange("p a b -> p (a b)"), tps[:, 0:1024], r=["ps3b"], w=["yT"])
        P.dma("sp", YTv[:, 8:16, ts_], yT[:, :, :], r=["yT"], w=[("dram", YT0.name)])


def row_from_cols(P, cols_bf, nchunk, identb, ps_row, kps, dst_row, kdst, rkeys):
    for i, cap in enumerate(cols_bf):
        P.mm(ps_row[0:1, i * 128:(i + 1) * 128], cap, identb[:, :], True, True, r=list(rkeys) + ["identb"], w=[kps])
    P.copy("dve", dst_row[0:1, 0:nchunk * 128], ps_row[0:1, 0:nchunk * 128], r=[kps], w=[kdst])


def dsa_sample_phase(P, O, YT0, C):
    P.reset_arena()
    PS = P.PS
    S_ = SEQ
    NSEL = 256
    KIg = P.carve(F32, [128, 64])
    tmp = P.carve(F32, [8192])
    KV = [P.carve(F32, [32, 256]) for _ in range(2)]
    sc = P.carve(F32, [129])
    dot = P.carve(F32, [128])
    lg = P.carve(F32, [8, 129])
    pmat = P.carve(F32, [8, 129])
    mask = P.carve(F32, [129])
    qicol = P.carve(BF16, [4])
    qbcol = P.carve(BF16, [8])
    qirow = P.carve(F32, [512])
    qbrow = P.carve(F32, [1024])
    qibc = P.carve(F32, [512])
    qbbc = P.carve(F32, [1024])
    wrow = P.carve(F32, [8])
    wbc = P.carve(F32, [8])
    kis = P.carve(F32, [64])
    ksf = P.carve(F32, [256])
    vsf = P.carve(F32, [256])
    srow = P.carve(F32, [64])
    ident = P.carve(F32, [128])
    identb = P.carve(BF16, [128])
    ones = P.carve(F32, [128])
    pt_i = P.carve(I32, [1])
    pt_f = P.carve(F32, [1])
    idx = [P.carve(I32, [1]) for _ in range(4)]
    sm = P.carve(F32, [16])
    sm8 = P.carve(F32, [32])
    dg = P.carve(F32, [8])
    yrow = P.carve(F32, [1024])
    yrowb = P.carve(BF16, [1024])
    ycol = P.carve(BF16, [8])
    junk = P.carve(F32, [129])
    P.dma("sp", ident[:, :], C["ident"], r=[], w=["ident"])
    P.copy("dve", identb[:, :], ident[:, :], r=["ident"], w=["identb"])
    P.memset("dve", ones[:, :], 1.0, w=["ones"])
    P.dma("sp", pt_i[:, :], C["page_col"], r=[], w=["pt_i"])
    P.copy("dve", pt_f[:, :], pt_i[:, :], r=["pt_i"], w=["pt_f"])
    for q in range(4):
        P.ts("dve", idx[q][:, :], pt_f[:, :], 4.0, float(q), ALU.mult, ALU.add, r=["pt_f"], w=["idx%d" % q])
    P.S.add("pool", lambda e: e.indirect_dma_start(
        out=KIg[:, :, :].rearrange("p a b -> p (a b)"), out_offset=None, in_=C["cache_ik"],
        in_offset=bass.IndirectOffsetOnAxis(ap=pt_i[:, :], axis=0)), r=["pt_i"], w=["KIg"], dma=True)
    P.dma("sp", qicol[:, :], fm(O["QIT"])[:, :, S_], r=[("dram", "QIT")], w=["qicol"])
    P.dma("sp", qbcol[:, :], fm(O["QBT"])[:, :, S_], r=[("dram", "QBT")], w=["qbcol"])
    rp = PS[0]
    row_from_cols(P, [qicol[:, i:i + 1] for i in range(4)], 4, identb, rp[:, 0:512], "ps0", qirow, "qirow", ["qicol"])
    row_from_cols(P, [qbcol[:, i:i + 1] for i in range(8)], 8, identb, rp[:, 0:1024] if False else PS[1][:, 0:1024],
                  "ps1", qbrow, "qbrow", ["qbcol"])
    P.dma("sp", wrow[0:1, :], O["WI"][S_:NT, :], r=[("dram", "WI")], w=["wrow"])
    P.mm(PS[2][:, 0:512], ones[0:1, :], qirow[0:1, :], True, True, r=["ones", "qirow"], w=["ps2"])
    P.copy("act", qibc[:, :], PS[2][:, 0:512], r=["ps2"], w=["qibc"])
    for i in range(2):
        P.mm(PS[3][:, i * 512:(i + 1) * 512], ones[0:1, :], qbrow[0:1, i * 512:(i + 1) * 512], True, True,
             r=["ones", "qbrow"], w=["ps3"])
    P.copy("act", qbbc[:, :], PS[3][:, 0:1024], r=["ps3"], w=["qbbc"])
    P.mm(PS[2][:, 512:520], ones[0:1, :], wrow[0:1, :], True, True, r=["ones", "wrow"], w=["ps2b"])
    P.copy("dve", wbc[:, :], PS[2][:, 512:520], r=["ps2b"], w=["wbc"])
    qi3 = qibc[:, :].rearrange("p (h d) -> p h d", h=8)
    t3 = tmp[:, :].rearrange("p (a b) -> p a b", a=128)
    for h in range(8):
        P.tt("dve", t3, KIg[:, :, :], qi3[:, h:h + 1, :].broadcast_to([128, 128, 64]), ALU.mult,
             r=["KIg", "qibc"], w=["tmp"])
        P.S.add("dve", lambda e: e.tensor_reduce(dot[:, :], t3, AX.X, ALU.add), r=["tmp"], w=["dot"])
        P.act(dot[:, :], dot[:, :], AF.Relu, r=["dot"], w=["dot"])
        if h == 0:
            P.ts("dve", sc[:, 0:128], dot[:, :], wbc[:, 0:1], None, ALU.mult, None, r=["dot", "wbc"], w=["sc"])
        else:
            P.stt(sc[:, 0:128], dot[:, :], wbc[:, h:h + 1], sc[:, 0:128], ALU.mult, ALU.add, r=["dot", "wbc", "sc"], w=["sc"])
    P.dma("sp", kis[0:1, :], C["ik_rows_s"], r=[("dram", "ik_rows_s")], w=["kis"])
    q3r = qirow[0:1, :].rearrange("p (h d) -> p h d", h=8)
    P.tt("dve", tmp[0:1, 0:512].rearrange("p (h d) -> p h d", h=8), q3r, kis[0:1, :].rearrange("p (a d) -> p a d", a=1).broadcast_to([1, 8, 64]),
         ALU.mult, r=["qirow", "kis", "tmp"], w=["tmp"])
    P.S.add("dve", lambda e: e.tensor_reduce(srow[0:1, 0:8], tmp[0:1, 0:512].rearrange("p (h d) -> p h d", h=8), AX.X, ALU.add),
            r=["tmp"], w=["srow"])
    P.act(srow[0:1, 0:8], srow[0:1, 0:8], AF.Relu, r=["srow"], w=["srow"])
    P.tt("dve", srow[0:1, 0:8], srow[0:1, 0:8], wrow[0:1, :], ALU.mult, r=["srow", "wrow"], w=["srow"])
    P.memset("dve", sc[:, 128:129], -1.0e30, w=["sc"])
    P.S.add("dve", lambda e: e.tensor_reduce(sc[0:1, 128:129], srow[0:1, 0:8], AX.X, ALU.add), r=["srow", "sc"], w=["sc"])
    P.S.add("dve", lambda e: e.tensor_reduce(sm[:, 0:1], sc[:, 0:128], AX.X, ALU.min), r=["sc"], w=["sm"])
    P.S.add("dve", lambda e: e.tensor_reduce(sm[:, 1:2], sc[:, 0:129], AX.X, ALU.max), r=["sc"], w=["sm"])
    P.mm(PS[0][0:1, 0:128], sm[:, 0:1], ident[:, :], True, True, r=["sm", "ident"], w=["ps0"])
    P.mm(PS[0][0:1, 128:256], sm[:, 1:2], ident[:, :], True, True, r=["sm", "ident"], w=["ps0"])
    P.S.add("dve", lambda e: e.tensor_reduce(srow[0:1, 8:9], PS[0][0:1, 0:128], AX.X, ALU.min), r=["ps0"], w=["srow"])
    P.S.add("dve", lambda e: e.tensor_reduce(srow[0:1, 9:10], PS[0][0:1, 128:256], AX.X, ALU.max), r=["ps0"], w=["srow"])
    P.tt("dve", srow[0:1, 9:10], srow[0:1, 9:10], srow[0:1, 8:9], ALU.subtract, r=["srow"], w=["srow"])
    P.mm(PS[0][:, 512:514], ones[0:1, :], srow[0:1, 8:10], True, True, r=["ones", "srow"], w=["ps0b"])
    P.copy("dve", sm[:, 0:1], PS[0][:, 512:513], r=["ps0b"], w=["sm"])
    P.copy("dve", sm[:, 2:3], PS[0][:, 513:514], r=["ps0b"], w=["sm"])
    for it in range(26):
        ci = 0.5 ** (it + 1)
        P.ts("dve", sm[:, 3:4], sm[:, 2:3], float(ci), sm[:, 0:1], ALU.mult, ALU.add, r=["sm"], w=["sm"])
        P.ts("dve", junk[:, 0:129], sc[:, 0:129], sm[:, 3:4], 0.0, ALU.is_ge, ALU.add, r=["sc", "sm"],
             w=["junk", "sm"], accum_out=sm[:, 4:5])
        P.mm(PS[0][:, 520:521], ones[:, :], sm[:, 4:5], True, True, r=["ones", "sm"], w=["ps0b"])
        P.ts("dve", sm[:, 5:6], PS[0][:, 520:521], NSEL - 0.5, float(ci), ALU.is_ge, ALU.mult, r=["ps0b"], w=["sm"])
        P.stt(sm[:, 0:1], sm[:, 5:6], sm[:, 2:3], sm[:, 0:1], ALU.mult, ALU.add, r=["sm"], w=["sm"])
    P.ts("dve", mask[:, 0:129], sc[:, 0:129], sm[:, 0:1], None, ALU.is_ge, None, r=["sc", "sm"], w=["mask"])
    qb3 = qbbc[:, :].rearrange("p (h d) -> p h d", h=8)
    ck = C["cache_k4"]
    cv = C["cache_v4"]
    tk = tmp[:, 0:4096].rearrange("p (a b) -> p a b", a=32)
    for q in range(4):
        kvb = KV[q % 2]
        kk = "KV%d" % (q % 2)
        P.S.add("pool", lambda e, kvb=kvb, q=q: e.indirect_dma_start(
            out=kvb[:, :, :].rearrange("p a b -> p (a b)"), out_offset=None, in_=ck,
            in_offset=bass.IndirectOffsetOnAxis(ap=idx[q][:, :], axis=0)), r=["idx%d" % q], w=[kk], dma=True)
        for hq in range(8):
            kv = hq // 4
            P.tt("dve", tk, kvb[:, :, kv * 128:(kv + 1) * 128], qb3[:, hq:hq + 1, :].broadcast_to([128, 32, 128]),
                 ALU.mult, r=[kk, "qbbc"], w=["tmp"])
            P.S.add("dve", lambda e, hq=hq, q=q: e.tensor_reduce(lg[:, hq, q * 32:(q + 1) * 32], tk, AX.X, ALU.add),
                    r=["tmp"], w=["lg"])
    P.dma("sp", ksf[0:1, :], C["k_rows_s"], r=[("dram", "k_rows_s")], w=["ksf"])
    P.dma("sp", vsf[0:1, :], C["v_rows_s"], r=[("dram", "v_rows_s")], w=["vsf"])
    P.memset("dve", lg[:, :, 128:129], 0.0, w=["lg"])
    for kv in range(2):
        P.tt("dve", tmp[0:1, kv * 512:(kv + 1) * 512].rearrange("p (h d) -> p h d", h=4),
             qbrow[0:1, kv * 512:(kv + 1) * 512].rearrange("p (h d) -> p h d", h=4),
             ksf[0:1, kv * 128:(kv + 1) * 128].rearrange("p (a d) -> p a d", a=1).broadcast_to([1, 4, 128]),
             ALU.mult, r=["qbrow", "ksf", "tmp"], w=["tmp"])
    P.S.add("dve", lambda e: e.tensor_reduce(lg[0:1, :, 128:129].rearrange("p h a -> p (h a)"),
                                             tmp[0:1, 0:1024].rearrange("p (h d) -> p h d", h=8), AX.X, ALU.add),
            r=["tmp", "lg"], w=["lg"])
    for hq in range(8):
        P.S.add("dve", lambda e, hq=hq: e.tensor_reduce(sm8[:, hq:hq + 1], lg[:, hq, :], AX.X, ALU.max), r=["lg"], w=["sm8"])
    P.mm(PS[1][0:8, 0:128], sm8[:, 0:8], ident[:, :], True, True, r=["sm8", "ident"], w=["ps1"])
    P.S.add("dve", lambda e: e.tensor_reduce(sm8[0:8, 8:9], PS[1][0:8, 0:128], AX.X, ALU.max), r=["ps1"], w=["sm8"])
    P.ts("dve", dg[0:8, :], ident[0:8, 0:8], sm8[0:8, 8:9], None, ALU.mult, None, r=["ident", "sm8"], w=["dg"])
    P.mm(PS[1][:, 512:520], ones[0:8, :], dg[0:8, :], True, True, r=["ones", "dg"], w=["ps1b"])
    P.ts("dve", sm8[:, 16:24], PS[1][:, 512:520], -1.0, None, ALU.mult, None, r=["ps1b"], w=["sm8"])
    for hq in range(8):
        P.act(pmat[:, hq, :], lg[:, hq, :], AF.Exp, r=["lg", "sm8"], w=["pmat"], bias=sm8[:, 16 + hq:17 + hq])
        P.stt(pmat[:, hq, :], pmat[:, hq, :], 1.0, mask[:, :], ALU.mult, ALU.mult, r=["pmat", "mask"], w=["pmat", "sm8"],
              accum_out=sm8[:, 24 + hq:25 + hq])
    P.mm(PS[1][:, 520:528], ones[:, :], sm8[:, 24:32], True, True, r=["ones", "sm8"], w=["ps1b"])
    P.copy("dve", srow[0:1, 16:24], PS[1][0:1, 520:528], r=["ps1b"], w=["srow"])
    P.S.add("dve", lambda e: e.reciprocal(srow[0:1, 16:24], srow[0:1, 16:24]), r=["srow"], w=["srow"])
    acc_reg = [PS[hq // 2][0:1, (hq % 2) * 512:(hq % 2) * 512 + 128] for hq in range(8)]
    acc_key = ["psacc%d" % hq for hq in range(8)]
    for q in range(4):
        kvb = KV[q % 2]
        kk = "KV%d" % (q % 2)
        P.S.add("pool", lambda e, kvb=kvb, q=q: e.indirect_dma_start(
            out=kvb[:, :, :].rearrange("p a b -> p (a b)"), out_offset=None, in_=cv,
            in_offset=bass.IndirectOffsetOnAxis(ap=idx[q][:, :], axis=0)), r=["idx%d" % q], w=[kk], dma=True)
        for hq in range(8):
            kv = hq // 4
            dst = acc_reg[hq]
            for o in range(32):
                P.mm(dst, pmat[:, hq, q * 32 + o:q * 32 + o + 1], kvb[:, o, kv * 128:(kv + 1) * 128],
                     q == 0 and o == 0, q == 3 and o == 31, r=["pmat", kk, "ps0", "ps1", "ps1b", "ps0b", "ps2", "ps3", "ps2b"],
                     w=[acc_key[hq]])
    for hq in range(8):
        kv = hq // 4
        src = acc_reg[hq]
        P.stt(yrow[0:1, hq * 128:(hq + 1) * 128], vsf[0:1, kv * 128:(kv + 1) * 128], pmat[0:1, hq, 128:129], src,
              ALU.mult, ALU.add, r=["vsf", "pmat", acc_key[hq]], w=["yrow"])
        P.ts("dve", yrowb[0:1, hq * 128:(hq + 1) * 128], yrow[0:1, hq * 128:(hq + 1) * 128], srow[0:1, 16 + hq:17 + hq],
             None, ALU.mult, None, r=["yrow", "srow"], w=["yrowb"])
    one1b = identb[0:1, 0:1]
    for hq in range(8):
        P.mm(PS[0][:, 600 + hq:601 + hq], yrowb[0:1, hq * 128:(hq + 1) * 128], one1b, True, True,
             r=["yrowb", "identb"], w=["ps0c"] + acc_key)
    P.copy("dve", ycol[:, :], PS[0][:, 600:608], r=["ps0c"], w=["ycol"])
    P.dma("sp", fm(YT0)[:, 8:16, S_], ycol[:, :], r=["ycol"], w=[("dram", YT0.name)])


def mixers_even(P, O, YT0):
    mlstm_phase(P, O, YT0, P.C)
    dsa_prompt_phase(P, O, YT0, P.C)
    dsa_sample_phase(P, O, YT0, P.C)


def hgrn_phase(P, O, YT1, C):
    P.reset_arena()
    PS = P.PS
    ident = P.carve(F32, [128])
    identb = P.carve(BF16, [128])
    maskT = P.carve(F32, [128])
    mask8 = P.carve(F32, [8, 128])
    lbb = P.carve(F32, [2, 1024])
    lbc = P.carve(F32, [2, 8])
    omlc = P.carve(F32, [8])
    gbc = P.carve(F32, [1024])
    St = P.carve(F32, [8, 128])
    Stmp = P.carve(F32, [8, 128])
    Sb = P.carve(BF16, [8, 128])
    junk = P.carve(F32, [1024])
    fct = [P.carve(F32, [1024]) for _ in range(2)]
    eb = [P.carve(F32, [8, 128]) for _ in range(2)]
    enb = [P.carve(F32, [8, 128]) for _ in range(2)]
    qct = [P.carve(F32, [8, 128]) for _ in range(2)]
    fcT = [P.carve(F32, [8, 128]) for _ in range(2)]
    gdt = [P.carve(F32, [1024]) for _ in range(2)]
    sm = [P.carve(F32, [32]) for _ in range(2)]
    QT = [P.carve(BF16, [8, 128]) for _ in range(2)]
    KT = [P.carve(BF16, [8, 128]) for _ in range(2)]
    Ktm = [P.carve(BF16, [8, 128]) for _ in range(2)]
    vt = [P.carve(BF16, [1024]) for _ in range(2)]
    AT = [P.carve(BF16, [8, 128]) for _ in range(2)]
    yab = [P.carve(BF16, [8, 128]) for _ in range(2)]
    yT = [P.carve(BF16, [8, 128]) for _ in range(2)]

    def flat(t):
        return t[:, :, :].rearrange("p a b -> p (a b)")

    P.dma("sp", ident[:, :], C["ident"], r=[], w=["ident"])
    P.copy("dve", identb[:, :], ident[:, :], r=["ident"], w=["identb"])
    P.dma("sp", maskT[:, :], C["maskT"], r=[], w=["maskT"])
    for h in range(8):
        P.copy("dve" if h % 2 == 0 else "pool", mask8[:, h, :], maskT[:, :], r=["maskT"], w=["mask8"])
    P.dma("sp", gbc[:, :], C["g_hgrn_bc"], r=[], w=["gbc"])
    P.dma("sp", lbb[:, :, :], C["lb_bc"], r=[], w=["lbb"])
    P.tt("dve", lbb[:, 0, :], lbb[:, 1, :], lbb[:, 0, :], ALU.subtract, r=["lbb"], w=["lbb"])
    P.act(lbb[:, 1, :], lbb[:, 0, :], AF.Sigmoid, r=["lbb"], w=["lbb"], scale=-1.0)
    P.act(lbb[:, 0, :], lbb[:, 0, :], AF.Sigmoid, r=["lbb"], w=["lbb"])
    P.dma("sp", lbc[:, :, :], C["lb_col"], r=[], w=["lbc"])
    P.tt("dve", lbc[:, 0, :], lbc[:, 1, :], lbc[:, 0, :], ALU.subtract, r=["lbc"], w=["lbc"])
    P.act(omlc[:, 0:8], lbc[:, 0, :], AF.Sigmoid, r=["lbc"], w=["omlc"], scale=-1.0)
    P.memset("dve", flat(St), 0.0, w=["St"])
    P.memset("pool", flat(Sb), 0.0, w=["Sb"])
    QCv, FCTv = fm(O["QCT"]), fm(O["FCT"])
    YTv = fm(YT1)
    b3 = PS[0][:, :].rearrange("p (h t) -> p h t", h=8)
    a3 = PS[1][:, :].rearrange("p (h t) -> p h t", h=8)
    s3 = PS[3][:, :].rearrange("p (h t) -> p h t", h=8)
    for blk, (c0, tw) in enumerate(TOKBLKS):
        i2 = blk % 2
        sfx = "%d" % i2
        if blk == 16:
            P.dma("sp", St[:, :, :], C["st_hg"].rearrange("h d v -> d h v"), r=[], w=["St"])
            P.copy("act", flat(Sb), flat(St), r=["St"], w=["Sb"])
        f_ = fct[i2]
        kf = "fct" + sfx
        P.dma("sp", f_[0:tw, :], O["FC"][c0:c0 + tw, :], r=[("dram", "FC")], w=[kf])
        P.act(f_[0:tw, :], f_[0:tw, :], AF.Sigmoid, r=[kf], w=[kf])
        P.tt("dve", f_[0:tw, :], f_[0:tw, :], lbb[0:tw, 1, :], ALU.mult, r=[kf, "lbb"], w=[kf])
        P.tt("pool", f_[0:tw, :], f_[0:tw, :], lbb[0:tw, 0, :], ALU.add, r=[kf, "lbb"], w=[kf])
        P.act(f_[0:tw, :], f_[0:tw, :], AF.Ln, r=[kf], w=[kf])
        for h in range(8):
            P.mm(PS[0][:, h * 128:h * 128 + tw], f_[0:tw, h * 128:(h + 1) * 128], maskT[0:tw, 0:tw], True, True,
                 r=[kf, "maskT"], w=["ps0"])
        e_, en_ = eb[i2], enb[i2]
        ke, ken = "eb" + sfx, "enb" + sfx
        P.act(e_[:, :, 0:tw], b3[:, :, 0:tw], AF.Exp, r=["ps0"], w=[ke])
        P.act(en_[:, :, 0:tw], b3[:, :, 0:tw], AF.Exp, r=["ps0"], w=[ken], scale=-1.0)
        q_ = qct[i2]
        kq = "qct" + sfx
        P.dma("pool", q_[:, :, 0:tw], QCv[:, 0:8, c0:c0 + tw], r=[("dram", "QCT")], w=[kq])
        P.act(q_[:, :, 0:tw], q_[:, :, 0:tw], AF.Silu, r=[kq], w=[kq])
        QT_ = QT[i2]
        kQT = "QT" + sfx
        P.tt("dve", QT_[:, :, 0:tw], q_[:, :, 0:tw], e_[:, :, 0:tw], ALU.mult, r=[kq, ke], w=[kQT])
        fT = fcT[i2]
        kfT = "fcT" + sfx
        P.dma("pool", fT[:, :, 0:tw], FCTv[:, 0:8, c0:c0 + tw], r=[("dram", "FCT")], w=[kfT])
        P.act(fT[:, :, 0:tw], fT[:, :, 0:tw], AF.Sigmoid, r=[kfT], w=[kfT], scale=-1.0)
        KT_ = KT[i2]
        kKT = "KT" + sfx
        for h in range(8):
            P.stt(KT_[:, h, 0:tw], fT[:, h, 0:tw], omlc[:, h:h + 1], en_[:, h, 0:tw], ALU.mult, ALU.mult,
                  r=[kfT, "omlc", ken], w=[kKT])
        for h in range(8):
            P.mm(PS[0][0:tw, h * 128:(h + 1) * 128], KT_[:, h, 0:tw], identb[:, :], True, True,
                 r=[kKT, "identb"], w=["ps0"])
        Ktm_ = Ktm[i2]
        kKtm = "Ktm" + sfx
        P.copy("act", flat(Ktm_)[0:tw, :], PS[0][0:tw, :], r=["ps0"], w=[kKtm])
        v_ = vt[i2]
        kv = "vt" + sfx
        P.dma("sp", v_[0:tw, :], O["IC"][c0:c0 + tw, :], r=[("dram", "IC")], w=[kv])
        for h in range(8):
            P.mm(PS[1][0:tw, h * 128:h * 128 + tw], KT_[:, h, 0:tw], QT_[:, h, 0:tw], True, True,
                 r=[kKT, kQT], w=["ps1"])
        AT_ = AT[i2]
        kAT = "AT" + sfx
        P.tt("dve", AT_[0:tw, :, 0:tw], a3[0:tw, :, 0:tw], mask8[0:tw, :, 0:tw], ALU.mult, r=["ps1", "mask8"], w=[kAT])
        for h in range(8):
            hs = slice(h * 128, (h + 1) * 128)
            P.mm(PS[2][0:tw, hs], AT_[0:tw, h, 0:tw], v_[0:tw, hs], True, False, r=[kAT, kv], w=["ps2"])
            P.mm(PS[2][0:tw, hs], QT_[:, h, 0:tw], Sb[:, h, :], False, True, r=[kQT, "Sb"], w=["ps2"])
        for h in range(8):
            hs = slice(h * 128, (h + 1) * 128)
            P.mm(PS[3][:, hs], Ktm_[0:tw, h, :], v_[0:tw, hs], True, True, r=[kKtm, kv], w=["ps3"])
        P.tt("dve", Stmp[:, :, :], St[:, :, :], s3, ALU.add, r=["St", "ps3"], w=["Stmp"])
        for h in range(8):
            P.ts("dve", St[:, h, :], Stmp[:, h, :], e_[:, h, tw - 1:tw], None, ALU.mult, None,
                 r=["Stmp", ke], w=["St"])
        P.copy("act", flat(Sb), flat(St), r=["St"], w=["Sb"])
        sm_ = sm[i2]
        ksm = "sm" + sfx
        P.act(junk[0:tw, :], PS[2][0:tw, :], AF.Square, r=["ps2"], w=["junk"])
        P.S.add("dve", lambda e, a=sm_, n=tw: e.tensor_reduce(
            a[0:n, 0:8], junk[0:n, :].rearrange("p (h v) -> p h v", h=8), AX.X, ALU.add), r=["junk"], w=[ksm])
        P.ts("dve", sm_[0:tw, 8:16], sm_[0:tw, 0:8], 1.0 / 128.0, float(LN_EPS), ALU.mult, ALU.add, r=[ksm], w=[ksm])
        P.act(sm_[0:tw, 8:16], sm_[0:tw, 8:16], AF.Sqrt, r=[ksm], w=[ksm])
        P.S.add("dve", lambda e, a=sm_, n=tw: e.reciprocal(a[0:n, 16:24], a[0:n, 8:16]), r=[ksm], w=[ksm])
        g_ = gdt[i2]
        kg = "gdt" + sfx
        P.dma("sp", g_[0:tw, :], O["GC"][c0:c0 + tw, :], r=[("dram", "GC")], w=[kg])
        P.act(g_[0:tw, :], g_[0:tw, :], AF.Silu, r=[kg], w=[kg])
        P.tt("pool", g_[0:tw, :], g_[0:tw, :], gbc[0:tw, :], ALU.mult, r=[kg, "gbc"], w=[kg])
        ya_ = yab[i2]
        kya = "ya" + sfx
        for h in range(8):
            hs = slice(h * 128, (h + 1) * 128)
            P.stt(ya_[0:tw, h, :], PS[2][0:tw, hs], sm_[0:tw, 16 + h:17 + h], g_[0:tw, hs], ALU.mult, ALU.mult,
                  r=["ps2", ksm, kg], w=[kya])
        for h in range(8):
            P.mm(PS[1][:, h * 128:h * 128 + tw], ya_[0:tw, h, :], identb[0:tw, 0:tw], True, True,
                 r=[kya, "identb"], w=["ps1"])
        yT_ = yT[i2]
        kyT = "yT" + sfx
        P.copy("act", yT_[:, :, 0:tw], a3[:, :, 0:tw], r=["ps1"], w=[kyT])
        P.dma("sp", YTv[:, 0:8, c0:c0 + tw], yT_[:, :, 0:tw], r=[kyT], w=[("dram", YT1.name)])
        if blk == 15:
            P.dma("sp", C["hg_p"].rearrange("h d v -> d h v"), St[:, :, :], r=["St"], w=[("dram", "hg_p")])
        if blk == 16:
            P.dma("sp", C["hg_s"].rearrange("h d v -> d h v"), St[:, :, :], r=["St"], w=[("dram", "hg_s")])


RET_GAMMA = [1.0 - 2.0 ** (-5.0 - h) for h in range(4)]


def ret_phase(P, O, YT1, C):
    P.reset_arena()
    PS = P.PS
    S_ = SEQ
    cosT = P.carve(F32, [NT])
    sinT = P.carve(F32, [NT])
    cosK = P.carve(F32, [17, 128])
    sinK = P.carve(F32, [17, 128])
    x1 = P.carve(F32, [NT])
    x2 = P.carve(F32, [NT])
    ta = P.carve(F32, [NT])
    tb = P.carve(F32, [NT])
    QR = P.carve(BF16, [2, NT])
    KR = P.carve(BF16, [2, NT])
    QRl = P.carve(BF16, [2, NT])
    KRl = P.carve(BF16, [2, NT])
    KDt = P.carve(F32, [17, 256])
    KRt = P.carve(F32, [17, 256])
    tk = P.carve(F32, [17, 128])
    KW = P.carve(BF16, [17, 256])
    VDt = P.carve(BF16, [17, 256])
    RD = P.carve(F32, [2, 128])
    tails = P.carve(F32, [4, 17])
    gbc = P.carve(F32, [1024])
    ident = P.carve(F32, [128])
    identb = P.carve(BF16, [128])
    gdt = [P.carve(F32, [256]) for _ in range(2)]
    pt = [P.carve(BF16, [128]) for _ in range(3)]
    junk = P.carve(F32, [256])
    sm = [P.carve(F32, [8]) for _ in range(2)]
    yab = [P.carve(BF16, [256]) for _ in range(2)]
    yT = [P.carve(BF16, [2, 128]) for _ in range(2)]
    cst = [P.carve(F32, [256]) for _ in range(2)]
    Sin = P.carve(F32, [2, 256])
    Sinb = P.carve(BF16, [2, 256])
    srow = P.carve(F32, [16])
    hrow = P.carve(F32, [256])
    gdrow = P.carve(F32, [1024])
    yrow = P.carve(BF16, [256])
    ycol = P.carve(BF16, [2])
    P.dma("sp", ident[:, :], C["ident"], r=[], w=["ident"])
    P.copy("dve", identb[:, :], ident[:, :], r=["ident"], w=["identb"])
    P.dma("sp", cosT[:, :], C["cosT"], r=[], w=["cosT"])
    P.dma("sp", sinT[:, :], C["sinT"], r=[], w=["sinT"])
    P.dma("sp", cosK[:, 0:16, :], C["cosK"][0:S_, :].rearrange("(b p) j -> p b j", p=128), r=[], w=["cosK"])
    P.dma("sp", cosK[0:1, 16, :], C["cosK"][S_:NT, :], r=[], w=["cosK"])
    P.dma("sp", sinK[:, 0:16, :], C["sinK"][0:S_, :].rearrange("(b p) j -> p b j", p=128), r=[], w=["sinK"])
    P.dma("sp", sinK[0:1, 16, :], C["sinK"][S_:NT, :], r=[], w=["sinK"])
    P.dma("sp", tails[:, :, :], C["ret_tails"], r=[], w=["tails"])
    P.dma("sp", gbc[:, :], C["g_ret_bc"], r=[], w=["gbc"])
    P.dma("sp", gdrow[0:1, :], O["GD"][S_:NT, :], r=[("dram", "GD")], w=["gdrow"])
    P.act(gdrow[0:1, :], gdrow[0:1, :], AF.Silu, r=["gdrow"], w=["gdrow"])
    P.tt("dve", gdrow[0:1, :], gdrow[0:1, :], gbc[0:1, :], ALU.mult, r=["gdrow", "gbc"], w=["gdrow"])
    QDv, KDv = fm(O["QDT"]), fm(O["KDT"])
    KDtv = O["KD"][0:S_, :].rearrange("(b p) c -> p b c", p=128)
    VDtv = O["VD"][0:S_, :].rearrange("(b p) c -> p b c", p=128)
    GDv = O["GD"][0:S_, :].rearrange("(b p) c -> p b c", p=128)
    YTv = fm(YT1)
    it = 0
    for h in range(4):
        hs = slice(h * 256, (h + 1) * 256)
        gam = RET_GAMMA[h]
        P.dma("sp", RD[:, :, :], C["ret_D"][h].rearrange("a s t -> s a t"), r=[], w=["RD"])
        for (srcv, dst, dstl, kd, kdl, nm) in ((QDv, QR, QRl, "QR", "QRl", "QDT"), (KDv, KR, KRl, "KR", "KRl", "KDT")):
            P.dma("sp", x1[:, :], srcv[:, 2 * h, :], r=[("dram", nm)], w=["x1"])
            P.dma("sp", x2[:, :], srcv[:, 2 * h + 1, :], r=[("dram", nm)], w=["x2"])
            for (half, sa, sb_, op) in ((0, cosT, sinT, ALU.subtract), (1, sinT, cosT, ALU.add)):
                ka, kb = ("cosT", "sinT") if half == 0 else ("sinT", "cosT")
                P.tt("dve", ta[:, :], x1[:, :], sa[:, :], ALU.mult, r=["x1", ka], w=["ta"])
                P.tt("pool", tb[:, :], x2[:, :], sb_[:, :], ALU.mult, r=["x2", kb], w=["tb"])
                P.tt("dve", ta[:, :], ta[:, :], tb[:, :], op, r=["ta", "tb"], w=["ta"])
                P.copy("act", dst[:, half, :], ta[:, :], r=["ta"], w=[kd])
                P.copy("act", tb[:, :], dst[:, half, :], r=[kd], w=["tb"])
                P.tt("dve", dstl[:, half, :], ta[:, :], tb[:, :], ALU.subtract, r=["ta", "tb"], w=[kdl])
        P.dma("sp", KDt[:, 0:16, :], KDtv[:, :, hs], r=[("dram", "KD")], w=["KDt"])
        P.dma("sp", KDt[0:1, 16, :], O["KD"][S_:NT, hs], r=[("dram", "KD")], w=["KDt"])
        P.dma("sp", VDt[:, 0:16, :], VDtv[:, :, hs], r=[("dram", "VD")], w=["VDt"])
        P.dma("sp", VDt[0:1, 16, :], O["VD"][S_:NT, hs], r=[("dram", "VD")], w=["VDt"])
        for (np_, bsl) in ((128, slice(0, 16)), (1, slice(16, 17))):
            k1, k2 = KDt[0:np_, bsl, 0:128], KDt[0:np_, bsl, 128:256]
            cK, sK = cosK[0:np_, bsl, :], sinK[0:np_, bsl, :]
            P.tt("dve", KRt[0:np_, bsl, 0:128], k1, cK, ALU.mult, r=["KDt", "cosK"], w=["KRt"])
            P.tt("pool", tk[0:np_, bsl, :], k2, sK, ALU.mult, r=["KDt", "sinK"], w=["tk"])
            P.tt("dve", KRt[0:np_, bsl, 0:128], KRt[0:np_, bsl, 0:128], tk[0:np_, bsl, :], ALU.subtract, r=["KRt", "tk"], w=["KRt"])
            P.tt("dve", KRt[0:np_, bsl, 128:256], k1, sK, ALU.mult, r=["KDt", "sinK"], w=["KRt"])
            P.tt("pool", tk[0:np_, bsl, :], k2, cK, ALU.mult, r=["KDt", "cosK", "KRt"], w=["tk"])
            P.tt("dve", KRt[0:np_, bsl, 128:256], KRt[0:np_, bsl, 128:256], tk[0:np_, bsl, :], ALU.add, r=["KRt", "tk"], w=["KRt"])
        for sb in range(17):
            np_ = 128 if sb < 16 else 1
            P.ts("dve", KW[0:np_, sb, :], KRt[0:np_, sb, :], tails[0:np_, h, sb:sb + 1], None, ALU.mult, None,
                 r=["KRt", "tails"], w=["KW"])
        for tq in range(16):
            ts_ = slice(tq * 128, (tq + 1) * 128)
            o_ps = PS[2 + (tq % 2)][:, 0:256]
            ko = "ps%d" % (2 + tq % 2)
            for sb in range(tq + 1):
                ss = slice(sb * 128, (sb + 1) * 128)
                sT = PS[0][:, (it % 2) * 512:(it % 2) * 512 + 128]
                ksT = "ps0_%d" % (it % 2)
                diag = (sb == tq)
                P.mm(sT, KR[:, 0, ss], QR[:, 0, ts_], True, False, r=["KR", "QR"], w=[ksT])
                P.mm(sT, KR[:, 1, ss], QR[:, 1, ts_], False, not diag, r=["KR", "QR"], w=[ksT])
                if diag:
                    P.mm(sT, KRl[:, 0, ss], QR[:, 0, ts_], False, False, r=["KRl", "QR"], w=[ksT])
                    P.mm(sT, KRl[:, 1, ss], QR[:, 1, ts_], False, False, r=["KRl", "QR"], w=[ksT])
                    P.mm(sT, KR[:, 0, ss], QRl[:, 0, ts_], False, False, r=["KR", "QRl"], w=[ksT])
                    P.mm(sT, KR[:, 1, ss], QRl[:, 1, ts_], False, True, r=["KR", "QRl"], w=[ksT])
                p_ = pt[it % 3]
                kp = "pt%d" % (it % 3)
                if sb == tq:
                    P.tt("dve", p_[:, :], sT, RD[:, 1, :], ALU.mult, r=[ksT, "RD"], w=[kp])
                else:
                    P.stt(p_[:, :], sT, float(gam ** (128 * (tq - sb))), RD[:, 0, :], ALU.mult, ALU.mult,
                          r=[ksT, "RD"], w=[kp])
                P.mm(o_ps, p_[:, :], VDt[:, sb, :], sb == 0, sb == tq, r=[kp, "VDt"], w=[ko])
                it += 1
            i2 = tq % 2
            smt = sm[i2]
            ksm = "sm%d" % i2
            P.act(junk[:, :], o_ps, AF.Square, r=[ko], w=["junk", ksm], accum_out=smt[:, 3:4])
            P.ts("dve", smt[:, 4:5], smt[:, 3:4], 1.0 / 256.0, float(LN_EPS), ALU.mult, ALU.add, r=[ksm], w=[ksm])
            P.act(smt[:, 4:5], smt[:, 4:5], AF.Sqrt, r=[ksm], w=[ksm])
            P.S.add("dve", lambda e, a=smt: e.reciprocal(a[:, 5:6], a[:, 4:5]), r=[ksm], w=[ksm])
            gd = gdt[i2]
            kgd = "gd%d" % i2
            P.dma("sp", gd[:, :], GDv[:, tq, hs], r=[("dram", "GD")], w=[kgd])
            P.act(gd[:, :], gd[:, :], AF.Silu, r=[kgd], w=[kgd])
            P.tt("pool", gd[:, :], gd[:, :], gbc[:, hs], ALU.mult, r=[kgd, "gbc"], w=[kgd])
            ya = yab[i2]
            kya = "ya%d" % i2
            P.stt(ya[:, :], o_ps, smt[:, 5:6], gd[:, :], ALU.mult, ALU.mult, r=[ko, ksm, kgd], w=[kya])
            tps = PS[1][:, 512:768].bitcast(BF16)
            yTt = yT[i2]
            kyT = "yT%d" % i2
            for j in range(2):
                P.S.add("pe", lambda e, o=tps[:, j * 128:(j + 1) * 128], i=ya[:, j * 128:(j + 1) * 128]:
                        e.transpose(o, i, identb[:, :]), r=[kya, "identb"], w=["ps1b"])
            P.copy("act", yTt[:, :, :].rearrange("p a b -> p (a b)"), tps[:, 0:256], r=["ps1b"], w=[kyT])
            P.dma("sp", YTv[:, 8 + 2 * h:8 + 2 * h + 2, ts_], yTt[:, :, :], r=[kyT], w=[("dram", YT1.name)])
        for dcn in range(2):
            c_ps = PS[2 + dcn][:, 0:256]
            kc = "ps%d" % (2 + dcn)
            for sb in range(16):
                P.mm(c_ps, KW[:, sb, dcn * 128:(dcn + 1) * 128], VDt[:, sb, :], sb == 0, sb == 15,
                     r=["KW", "VDt"], w=[kc])
            ct = cst[dcn]
            kct = "cst%d" % dcn
            P.copy("act", ct[:, :], c_ps, r=[kc], w=[kct])
            P.dma("sp", C["rt_p"][h, dcn * 128:(dcn + 1) * 128, :], ct[:, :], r=[kct], w=[("dram", "rt_p")])
        P.dma("sp", Sin[:, 0:2, :], C["st_ret"][h].rearrange("(c p) v -> p c v", p=128), r=[], w=["Sin"])
        P.copy("act", Sinb[:, 0:2, :], Sin[:, 0:2, :], r=["Sin"], w=["Sinb"])
        rps = PS[1][0:1, 0:512]
        for dcn in range(2):
            P.mm(rps[:, 0:256], QR[:, dcn, S_:NT], Sinb[:, dcn, :], dcn == 0, dcn == 1, r=["QR", "Sinb"], w=["ps1a"])
        for dcn in range(2):
            P.mm(rps[:, 256:257], QR[:, dcn, S_:NT], KR[:, dcn, S_:NT], dcn == 0, dcn == 1, r=["QR", "KR"], w=["ps1a"])
        P.copy("dve", srow[0:1, 0:1], rps[:, 256:257], r=["ps1a"], w=["srow"])
        P.ts("dve", hrow[0:1, :], rps[:, 0:256], float(gam), None, ALU.mult, None, r=["ps1a"], w=["hrow"])
        P.stt(hrow[0:1, :], VDt[0:1, 16, :], srow[0:1, 0:1], hrow[0:1, :], ALU.mult, ALU.add,
              r=["VDt", "srow", "hrow"], w=["hrow"])
        P.act(junk[0:1, :], hrow[0:1, :], AF.Square, r=["hrow"], w=["junk", "srow"], accum_out=srow[0:1, 1:2])
        P.ts("dve", srow[0:1, 2:3], srow[0:1, 1:2], 1.0 / 256.0, float(LN_EPS), ALU.mult, ALU.add, r=["srow"], w=["srow"])
        P.act(srow[0:1, 2:3], srow[0:1, 2:3], AF.Sqrt, r=["srow"], w=["srow"])
        P.S.add("dve", lambda e: e.reciprocal(srow[0:1, 3:4], srow[0:1, 2:3]), r=["srow"], w=["srow"])
        P.stt(yrow[0:1, :], hrow[0:1, :], srow[0:1, 3:4], gdrow[0:1, hs], ALU.mult, ALU.mult,
              r=["hrow", "srow", "gdrow"], w=["yrow"])
        cps = PS[1][:, 512:514]
        for j in range(2):
            P.mm(cps[:, j:j + 1], yrow[0:1, j * 128:(j + 1) * 128], identb[0:1, 0:1], True, True,
                 r=["yrow", "identb"], w=["ps1b"])
        P.copy("dve", ycol[:, 0:2], cps, r=["ps1b"], w=["ycol"])
        P.dma("sp", YTv[:, 8 + 2 * h:8 + 2 * h + 2, S_], ycol[:, 0:2], r=["ycol"], w=[("dram", YT1.name)])
        for dcn in range(2):
            kv_ps = PS[2 + dcn][:, 0:256]
            kc = "ps%d" % (2 + dcn)
            P.mm(kv_ps, KW[0:1, 16, dcn * 128:(dcn + 1) * 128], VDt[0:1, 16, :], True, True, r=["KW", "VDt"], w=[kc])
            ct = cst[dcn]
            kct = "cst%d" % dcn
            P.stt(ct[:, :], Sin[:, dcn, :], float(gam), kv_ps, ALU.mult, ALU.add, r=["Sin", kc], w=[kct])
            P.dma("sp", C["rt_s"][h, dcn * 128:(dcn + 1) * 128, :], ct[:, :], r=[kct], w=[("dram", "rt_s")])


def mixers_odd(P, O, YT1):
    hgrn_phase(P, O, YT1, P.C)
    ret_phase(P, O, YT1, P.C)


def finish(P):
    S = P.S
    S.barrier()
    st = ExitStack()
    S.emit(st)
    st.close()
    return P.nc


def make_in_maps(inputs):
    f = np.float32
    A = lambda k: np.ascontiguousarray(np.asarray(inputs[k], f))
    xp, xs = A("x_prompt"), A("x_sample")
    pp, ps = A("p_prompt"), A("p_sample")
    shared = {
        "w_ffn_up": A("w_ffn_up"), "w_ffn_down": A("w_ffn_down"),
        "ln_g": pc_layout(A("ln_g")), "ln_b": pc_layout(A("ln_b")),
        "w_in_even": np.ascontiguousarray(A("w_in_even")[0]),
        "w_in_odd": np.ascontiguousarray(A("w_in_odd")[0]),
        "w_out": A("w_out"), "w_pe_gate": A("w_pe_gate"), "w_pe_proj": A("w_pe_proj"),
        "b_gate": np.ascontiguousarray(A("b_gate_mlstm")[0].reshape(8, 1)),
        "c_ident": np.eye(128, dtype=f),
        "cache_ik": np.ascontiguousarray(A("cache_idx_k")[0].reshape(1280, 8192)),
        "cache_k4": np.ascontiguousarray(A("cache_k")[0].reshape(5120, 8192)),
        "cache_v4": np.ascontiguousarray(A("cache_v")[0].reshape(5120, 8192)),
        "c_maskT": np.triu(np.ones((128, 128), f)),
        "c_cmask": np.where(np.tril(np.ones((128, 128), bool)), 0.0, -1.0e30).astype(f),
        "c_sel4": np.ascontiguousarray(np.broadcast_to(np.eye(4, dtype=f)[:, :, None], (4, 4, 128))),
        "g_mlstm_bc": np.ascontiguousarray(np.broadcast_to(A("g_mlstm")[0][None, :], (128, 1024))),
    }
    pos = np.concatenate([np.arange(SEQ, dtype=np.float32), np.array([16384.0], np.float32)])
    inv = (1.0 / (np.float32(10000.0) ** np.linspace(0.0, 1.0, 128, dtype=np.float32))).astype(np.float32)
    ang = (pos[:, None] * inv[None, :]).astype(np.float32)
    shared["c_cosK"] = np.ascontiguousarray(np.cos(ang).astype(f))
    shared["c_sinK"] = np.ascontiguousarray(np.sin(ang).astype(f))
    shared["c_cosT"] = np.ascontiguousarray(shared["c_cosK"].T)
    shared["c_sinT"] = np.ascontiguousarray(shared["c_sinK"].T)
    lg = np.log1p(-np.exp2(-5.0 - np.arange(4, dtype=np.float64)))
    sl = np.arange(128)
    dexp = (sl[None, :] - sl[:, None]).astype(np.float64)
    D0 = np.exp(lg[:, None, None] * dexp[None])
    Dd = np.where(dexp[None] >= 0, D0, 0.0)
    shared["c_ret_D"] = np.ascontiguousarray(np.stack([D0, Dd], 1).astype(f))
    tl = np.zeros((128, 4, 17), np.float64)
    for hh in range(4):
        for bb in range(16):
            tl[:, hh, bb] = np.exp(lg[hh] * (SEQ - 1 - (bb * 128 + sl))) / 16.0
        tl[:, hh, 16] = 1.0 / 16.0
    shared["c_ret_tails"] = tl.astype(f)
    shared["g_ret_bc"] = np.ascontiguousarray(np.broadcast_to(A("g_ret")[0][None, :], (128, 1024)))
    shared["g_hgrn_bc"] = np.ascontiguousarray(np.broadcast_to(A("g_hgrn")[0][None, :], (128, 1024)))
    shared["lb_bc"] = np.ascontiguousarray(np.broadcast_to(A("hgrn_lb")[None], (128, 2, 1024)))
    shared["lb_col"] = np.ascontiguousarray(A("hgrn_lb").reshape(2, 8, 128).transpose(2, 0, 1))
    maps = []
    for c in range(8):
        b = c % 4
        m = dict(shared)
        m["st_ret"] = np.ascontiguousarray(A("state_ret")[0, c])
        m["st_hg"] = np.ascontiguousarray(A("state_hgrn")[0, c])
        m["xin"] = np.ascontiguousarray(np.concatenate([xp[b].T, xs[c].T], axis=1))
        m["page_col"] = np.ascontiguousarray(np.asarray(inputs["page_table"], np.int32)[c].reshape(128, 1))
        m["st_C"] = np.ascontiguousarray(A("state_mlstm_C")[0, c])
        m["st_n"] = np.ascontiguousarray(A("state_mlstm_n")[0, c])
        m["st_m"] = np.ascontiguousarray(A("state_mlstm_m")[0, c].reshape(1, 4))
        m["pT"] = np.ascontiguousarray(np.stack(
            [np.concatenate([pp[i, b].T, ps[i, c].T], axis=1) for i in range(2)]))
        maps.append(m)
    return maps


_NC_CACHE = {}


def kernel(**inputs):
    if "nc" not in _NC_CACHE:
        P = build()
        _NC_CACHE["nc"] = finish(P)
    nc = _NC_CACHE["nc"]
    maps = make_in_maps(inputs)
    res = run_bass_kernel_spmd(nc, maps, core_ids=list(range(8)))
    R = res.results
    f = np.float32

    def get(c, name, shape):
        if name in R[c]:
            return np.asarray(R[c][name], f).reshape(shape)
        return np.zeros(shape, f)

    y_p = np.stack([np.asarray(R[b]["yT_out"], f)[:, :SEQ].T for b in range(4)])
    y_s = np.stack([np.asarray(R[c]["yT_out"], f)[:, SEQ:].T for c in range(8)])
    P4, S8 = range(4), range(8)
    mC_p = np.stack([get(b, "mC_p", (4, 256, 256)) for b in P4])[None]
    mC_s = np.stack([get(c, "mC_s", (4, 256, 256)) for c in S8])[None]
    mn_p = np.stack([get(b, "mn_p", (4, 256)) for b in P4])[None]
    mn_s = np.stack([get(c, "mn_s", (4, 256)) for c in S8])[None]
    mm_p = np.stack([get(b, "mm_p", (4,)) for b in P4])[None]
    mm_s = np.stack([get(c, "mm_s", (4,)) for c in S8])[None]
    k_p = np.stack([get(b, "k_rows_p", (SEQ, 2, 128)) for b in P4])[None]
    k_s = np.stack([get(c, "k_rows_s", (1, 2, 128)) for c in S8])[None]
    v_p = np.stack([get(b, "v_rows_p", (SEQ, 2, 128)) for b in P4])[None]
    v_s = np.stack([get(c, "v_rows_s", (1, 2, 128)) for c in S8])[None]
    ik_p = np.stack([get(b, "ik_rows_p", (SEQ, 64)) for b in P4])[None]
    ik_s = np.stack([get(c, "ik_rows_s", (1, 64)) for c in S8])[None]
    hg_p = np.stack([get(b, "hg_p", (8, 128, 128)) for b in P4])[None]
    hg_s = np.stack([get(c, "hg_s", (8, 128, 128)) for c in S8])[None]
    rt_p = np.stack([get(b, "rt_p", (4, 256, 256)) for b in P4])[None]
    rt_s = np.stack([get(c, "rt_s", (4, 256, 256)) for c in S8])[None]
    return (y_p, y_s, mC_p, mC_s, mn_p, mn_s, mm_p, mm_s, k_p, k_s, v_p, v_s, ik_p, ik_s,
            hg_p, hg_s, rt_p, rt_s)
```

```python
import numpy as np
from contextlib import ExitStack
import concourse.bass as bass
import concourse.mybir as mybir
from concourse.bass_utils import run_bass_kernel_spmd

F32 = mybir.dt.float32
BF16 = mybir.dt.bfloat16
I32 = mybir.dt.int32
AF = mybir.ActivationFunctionType
ALU = mybir.AluOpType
AX = mybir.AxisListType

ENGS = ("pe", "act", "dve", "pool", "sp")

D = 2048
SEQ = 2048
NT = SEQ + 1
DFF = 5632
NFC = DFF // 128
DEPTH = 2
ALPHA = (2 * DEPTH) ** 0.25
LN_EPS = 1e-5
TILES = [(0, 512), (512, 512), (1024, 512), (1536, 513)]


def blocks(tw):
    out = []
    o = 0
    while o < tw:
        n = min(512, tw - o)
        out.append((o, n))
        o += n
    return out


class _Op:
    __slots__ = ("eng", "fn", "dma", "deps", "signaled", "sem", "val", "amt", "pre")

    def __init__(self, eng, fn, dma):
        self.eng = eng
        self.fn = fn
        self.dma = dma
        self.deps = []
        self.signaled = False
        self.sem = None
        self.val = 0
        self.amt = 0
        self.pre = None


class Sched:
    def __init__(self, nc, n_dma_sems=12):
        self.nc = nc
        self.ops = {e: [] for e in ENGS}
        self.last_w = {}
        self.readers = {}
        self.n_dma_sems = n_dma_sems

    def add(self, eng, fn, r=(), w=(), dma=False):
        op = _Op(eng, fn, dma)
        deps = set()
        for k in r:
            lw = self.last_w.get(k)
            if lw is not None:
                deps.add(lw)
        for k in w:
            lw = self.last_w.get(k)
            if lw is not None:
                deps.add(lw)
            for rd in self.readers.get(k, ()):
                deps.add(rd)
        op.deps = list(deps)
        for d in op.deps:
            d.signaled = True
        for k in w:
            self.last_w[k] = op
            self.readers[k] = []
        for k in r:
            if k not in w:
                self.readers.setdefault(k, []).append(op)
        if dma:
            op.signaled = True
        self.ops[eng].append(op)
        return op

    def barrier(self):
        pend = set()
        for v in self.last_w.values():
            if v is not None:
                pend.add(v)
        for v in self.readers.values():
            for rd in v:
                pend.add(rd)
        for e in ENGS:
            op = _Op(e, None, False)
            op.deps = list(pend)
            self.ops[e].append(op)
        for d in pend:
            d.signaled = True
        self.last_w = {}
        self.readers = {}

    def emit(self, stack):
        nc = self.nc
        sems = {e: stack.enter_context(nc.semaphore("s_" + e)) for e in ENGS}
        dsems = {}
        for e in ("sp", "pool", "act"):
            dsems[e] = [stack.enter_context(nc.semaphore("d_%s_%d" % (e, i))) for i in range(self.n_dma_sems)]
        cnt = {e: 0 for e in ENGS}
        dtot = {e: [0] * self.n_dma_sems for e in dsems}
        drr = {e: 0 for e in dsems}
        for e in ENGS:
            for op in self.ops[e]:
                if op.dma:
                    j = drr[e]
                    drr[e] = (j + 1) % self.n_dma_sems
                    op.sem = dsems[e][j]
                    if dtot[e][j] > 0:
                        op.pre = (op.sem, dtot[e][j])
                    dtot[e][j] += 16
                    op.val = dtot[e][j]
                    op.amt = 16
                elif op.signaled:
                    cnt[e] += 1
                    op.sem = sems[e]
                    op.val = cnt[e]
                    op.amt = 1
        block = stack.enter_context(nc.Block())

        def run(e, eng):
            waited = {}
            for op in self.ops[e]:
                need = {}
                if op.pre is not None:
                    need[id(op.pre[0])] = (op.pre[0], op.pre[1])
                for d in op.deps:
                    if d.sem is None:
                        continue
                    if d.eng == "pe" and e == "pe" and not d.dma:
                        continue
                    cur = need.get(id(d.sem))
                    if cur is None or cur[1] < d.val:
                        need[id(d.sem)] = (d.sem, d.val)
                for sid, (sem, val) in need.items():
                    if waited.get(sid, 0) >= val:
                        continue
                    eng.wait_ge(sem, val)
                    waited[sid] = val
                if op.fn is not None:
                    ins = op.fn(eng)
                    if op.signaled:
                        ins.then_inc(op.sem, op.amt)
                elif op.signaled:
                    eng.nop().then_inc(op.sem, op.amt)

        @block.tensor
        def _(eng):
            run("pe", eng)

        @block.scalar
        def _(eng):
            run("act", eng)

        @block.vector
        def _(eng):
            run("dve", eng)

        @block.gpsimd
        def _(eng):
            run("pool", eng)

        @block.sync
        def _(eng):
            run("sp", eng)


class Prog:
    ARENA = 94 * 1024

    def __init__(self, debug_out=None):
        self.nc = bass.Bass("TRN2", target_bir_lowering=False)
        nc = self.nc
        self.S = Sched(nc)
        self.arena = nc.alloc_sbuf_tensor("arena", [128, self.ARENA], BF16)
        self.aoff = 0
        self.PS = [nc.alloc_psum_tensor("ps%d" % i, [128, 1024], F32) for i in range(4)]
        self.ins = {}
        self.outs = {}
        self.debug_out = debug_out or ()
        self.uid = 0

    def reset_arena(self):
        self.S.barrier()
        self.aoff = 0

    def carve(self, dtype, fshape, parts=128):
        n = int(np.prod(fshape))
        nb = n * (2 if dtype == F32 or dtype == I32 else 1)
        nb = (nb + 15) // 16 * 16
        assert self.aoff + nb <= self.ARENA, ("arena overflow", self.aoff, nb)
        ap = self.arena[0:parts, self.aoff:self.aoff + nb]
        self.aoff += nb
        if dtype != BF16:
            ap = ap.bitcast(dtype)
        ap = ap[:, 0:n]
        if len(fshape) == 2:
            ap = ap.rearrange("p (a b) -> p a b", a=fshape[0])
        elif len(fshape) == 3:
            ap = ap.rearrange("p (a b c) -> p a b c", a=fshape[0], b=fshape[1])
        return ap

    def key(self, name):
        self.uid += 1
        return "%s#%d" % (name, self.uid)

    def inp(self, name, shape, dtype=F32):
        t = self.nc.dram_tensor(name, list(shape), dtype, kind="ExternalInput").ap()
        self.ins[name] = t
        return t

    def out(self, name, shape, dtype=F32):
        t = self.nc.dram_tensor(name, list(shape), dtype, kind="ExternalOutput").ap()
        self.outs[name] = t
        return t

    def scratch(self, name, shape, dtype=F32):
        kind = "ExternalOutput" if name in self.debug_out else "Internal"
        t = self.nc.dram_tensor(name, list(shape), dtype, kind=kind).ap()
        if kind == "ExternalOutput":
            self.outs[name] = t
        return t

    def dma(self, eng, out, in_, r, w):
        return self.S.add(eng, lambda e: e.dma_start(out=out, in_=in_, allow_slow_non_contiguous=True), r=r, w=w, dma=True)

    def mm(self, out, lhsT, rhs, start, stop, r, w):
        return self.S.add("pe", lambda e: e.matmul(out, lhsT, rhs, start=start, stop=stop), r=r, w=w)

    def act(self, out, in_, func, r, w, bias=None, scale=None, accum_out=None):
        kw = {}
        if bias is not None:
            kw["bias"] = bias
        if scale is not None:
            kw["scale"] = scale
        if accum_out is not None:
            kw["accum_out"] = accum_out
        return self.S.add("act", lambda e: e.activation(out, in_, func, **kw), r=r, w=w)

    def tt(self, eng, out, in0, in1, op, r, w):
        return self.S.add(eng, lambda e: e.tensor_tensor(out, in0, in1, op), r=r, w=w)

    def ts(self, eng, out, in0, s1, s2, op0, op1, r, w, accum_out=None):
        if op1 is None:
            return self.S.add(eng, lambda e: e.tensor_scalar(out, in0, s1, None, op0), r=r, w=w)
        if accum_out is not None:
            return self.S.add(eng, lambda e: e.tensor_scalar(out, in0, s1, s2, op0, op1, accum_out), r=r, w=w)
        return self.S.add(eng, lambda e: e.tensor_scalar(out, in0, s1, s2, op0, op1), r=r, w=w)

    def stt(self, out, in0, scalar, in1, op0, op1, r, w, accum_out=None):
        if accum_out is not None:
            return self.S.add("dve", lambda e: e.scalar_tensor_tensor(out, in0, scalar, in1, op0, op1, accum_out), r=r, w=w)
        return self.S.add("dve", lambda e: e.scalar_tensor_tensor(out, in0, scalar, in1, op0, op1), r=r, w=w)

    def copy(self, eng, out, in_, r, w):
        if eng == "act":
            return self.S.add("act", lambda e: e.copy(out, in_), r=r, w=w)
        return self.S.add(eng, lambda e: e.tensor_copy(out, in_), r=r, w=w)

    def memset(self, eng, ap, val, w):
        return self.S.add(eng, lambda e: e.memset(ap, val), w=w)


def fm(ap):
    return ap.rearrange("(c p) t -> p c t", p=128)


def load_cast_tile(P, src, c0, tw, xb, nch=16):
    S = P.S
    sv = fm(src)
    keys = []
    grp = 2
    for q in range(0, nch, grp):
        n = min(grp, nch - q)
        i = (q // grp) % 2
        stg = P.xstg[i]
        ks = "xstg%d" % i
        P.dma("sp", stg[:, 0:n, 0:tw], sv[:, q:q + n, c0:c0 + tw], r=[("dram", src.name)], w=[ks])
        kx = ("xb", id(xb), q)
        eng = "dve" if (q // grp) % 2 == 0 else "act"
        P.copy(eng, xb[:, q:q + n, 0:tw], stg[:, 0:n, 0:tw], r=[ks], w=[kx])
        keys.append(kx)
    return keys


def load_cast_steps(P, src, c0, tw, xb, nch=16):
    sv = fm(src)
    grp = 2
    keys, steps = [], []
    for q in range(0, nch, grp):
        n = min(grp, nch - q)
        i = (q // grp) % 2
        kx = ("xb", id(xb), q)
        keys.append(kx)

        def step(q=q, n=n, i=i, kx=kx):
            stg = P.xstg[i]
            ks = "xstg%d" % i
            P.dma("sp", stg[:, 0:n, 0:tw], sv[:, q:q + n, c0:c0 + tw], r=[("dram", src.name)], w=[ks])
            P.copy("dve" if i == 0 else "act", xb[:, q:q + n, 0:tw], stg[:, 0:n, 0:tw], r=[ks], w=[kx])
        steps.append(step)
    return keys, steps


def proj_res_ln(P, tilei, c0, tw, inT, in_keys, nk, wview, wcol0, res_src, scale, lng, lnb, dst,
                dst_bf=None, pre_steps=None, defer=False):
    S = P.S
    blks = blocks(tw)
    PS = P.PS
    rv = fm(res_src)
    dv = fm(dst)
    s1, s2 = PS[2], PS[3]
    pend = None

    def post(dmc, o_ps, ko):
        xf = P.xf[dmc % 2]
        kxf = "xf%d" % (dmc % 2)
        P.dma("sp", xf[:, 0:tw], rv[:, dmc, c0:c0 + tw], r=[("dram", res_src.name)], w=[kxf])
        kv = ("v", dmc)
        xa = P.xa[dmc % 2]
        kxa = "xa%d" % (dmc % 2)
        P.act(xa[:, 0:tw], xf[:, 0:tw], AF.Copy, r=[kxf], w=[kxa], scale=float(ALPHA))
        P.stt(P.v[:, dmc, 0:tw], o_ps[:, 0:tw], float(scale), xa[:, 0:tw], ALU.mult, ALU.add, r=[ko, kxa], w=[kv])
        sq = P.sq[dmc % 2]
        ksq = "sq%d" % (dmc % 2)
        P.act(sq[:, 0:tw], P.v[:, dmc, 0:tw], AF.Square, r=[kv], w=[ksq])
        return kv, ksq

    def stats(dmc, kv, ksq):
        sq = P.sq[dmc % 2]
        for (o, n) in blks:
            P.mm(s1[:, o:o + n], P.ones[:, :], P.v[:, dmc, o:o + n], dmc == 0, dmc == 15, r=[kv, "ones"], w=["ps2"])
            P.mm(s2[:, o:o + n], P.ones[:, :], sq[:, o:o + n], dmc == 0, dmc == 15, r=[ksq, "ones"], w=["ps3"])

    pre = list(pre_steps or [])
    for dmc in range(16):
        wd = P.wD[dmc % 2]
        kw = "wD%d" % (dmc % 2)
        P.dma("pool", wd[:, 0:nk, :], wview[:, :, wcol0 + dmc * 128: wcol0 + (dmc + 1) * 128], r=[], w=[kw])
        for _ in range(-(-len(pre) // (16 - dmc))):
            pre.pop(0)()
        o_ps = PS[dmc % 2]
        ko = "ps%d" % (dmc % 2)
        for (o, n) in blks:
            for k in range(nk):
                P.mm(o_ps[:, o:o + n], wd[:, k, :], inT[:, k, o:o + n], k == 0, k == nk - 1,
                     r=[kw] + list(in_keys), w=[ko])
        if pend is not None:
            stats(*pend)
        kv, ksq = post(dmc, o_ps, ko)
        pend = (dmc, kv, ksq)
    stats(*pend)
    mean, msq, var, rstd = P.lnt[0], P.lnt[1], P.lnt[2], P.lnt[3]
    P.ts("dve", mean[:, 0:tw], s1[:, 0:tw], 1.0 / D, None, ALU.mult, None, r=["ps2"], w=["mean"])
    P.tt("dve", msq[:, 0:tw], mean[:, 0:tw], mean[:, 0:tw], ALU.mult, r=["mean"], w=["msq"])
    P.stt(var[:, 0:tw], s2[:, 0:tw], 1.0 / D, msq[:, 0:tw], ALU.mult, ALU.subtract, r=["ps3", "msq"], w=["var"])
    P.ts("dve", var[:, 0:tw], var[:, 0:tw], float(LN_EPS), None, ALU.add, None, r=["var"], w=["var"])
    P.act(var[:, 0:tw], var[:, 0:tw], AF.Sqrt, r=["var"], w=["var"])
    P.S.add("dve", lambda e: e.reciprocal(rstd[:, 0:tw], var[:, 0:tw]), r=["var"], w=["rstd"])
    def body(dmc):
        t1 = P.t1[dmc % 2]
        k1 = "t1%d" % (dmc % 2)
        yo = P.yo[dmc % 2]
        ky = "yo%d" % (dmc % 2)
        P.tt("dve", t1[:, 0:tw], P.v[:, dmc, 0:tw], mean[:, 0:tw], ALU.subtract, r=[("v", dmc), "mean"], w=[k1])
        P.tt("dve", t1[:, 0:tw], t1[:, 0:tw], rstd[:, 0:tw], ALU.mult, r=[k1, "rstd"], w=[k1])
        P.act(yo[:, 0:tw], t1[:, 0:tw], AF.Identity, r=[k1, "lnp"], w=[ky],
              scale=lng[:, dmc:dmc + 1], bias=lnb[:, dmc:dmc + 1])
        P.dma("sp", dv[:, dmc, c0:c0 + tw], yo[:, 0:tw], r=[ky], w=[("dram", dst.name)])

    steps = [(lambda d=dmc: body(d)) for dmc in range(16)]
    if defer:
        return steps
    for st_ in steps:
        st_()
    return []


def dense_bufs(P, lng_d, lnb_d, full=True):
    P.reset_arena()
    P.xb = P.carve(BF16, [16, 513])
    if full:
        P.hT = P.carve(BF16, [NFC, 513])
        P.v = P.carve(F32, [16, 513])
        P.wA = [P.carve(BF16, [16, 256]) for _ in range(2)]
        P.wB = [P.carve(BF16, [16, 256]) for _ in range(2)]
    P.wD = [P.carve(BF16, [NFC, 128]) for _ in range(2)]
    P.xstg = [P.carve(F32, [2, 513]) for _ in range(2)]
    P.xf = [P.carve(F32, [513]) for _ in range(2)]
    P.xa = [P.carve(F32, [513]) for _ in range(2)]
    P.sq = [P.carve(F32, [513]) for _ in range(2)]
    P.sg = [P.carve(BF16, [513]) for _ in range(2)]
    P.lnt = [P.carve(F32, [513]) for _ in range(4)]
    P.t1 = [P.carve(F32, [513]) for _ in range(2)]
    P.yo = [P.carve(F32, [513]) for _ in range(2)]
    P.ones = P.carve(F32, [128])
    P.lng = P.carve(F32, [16])
    P.lnb = P.carve(F32, [16])
    P.memset("dve", P.ones[:, :], 1.0, w=["ones"])
    if lng_d is not None:
        P.dma("sp", P.lng[:, :], lng_d, r=[], w=["lnp"])
        P.dma("sp", P.lnb[:, :], lnb_d, r=[], w=["lnp"])


def outproj_phase(P, yT, res_src, dst, w_out, lng_d, lnb_d):
    dense_bufs(P, lng_d, lnb_d)
    wv = fm(w_out)
    yv = fm(yT)
    xbufs = [P.xb, P.hT[:, 0:16, :]]
    kxb = ["xb_y0", "xb_y1"]

    def load(ti):
        c0, tw = TILES[ti]
        P.dma("sp", xbufs[ti % 2][:, :, 0:tw], yv[:, :, c0:c0 + tw], r=[("dram", yT.name)], w=[kxb[ti % 2]])

    load(0)
    pending = []
    for ti, (c0, tw) in enumerate(TILES):
        pre = []
        if ti + 1 < len(TILES):
            pre.append(lambda t=ti + 1: load(t))
        pre += pending
        pending = proj_res_ln(P, ti, c0, tw, xbufs[ti % 2], [kxb[ti % 2]], 16, wv, 0, res_src, 1.0, P.lng, P.lnb, dst,
                              pre_steps=pre, defer=True)
    for st_ in pending:
        st_()


def pegate_phase(P, src, dst, pT, w_gate, w_proj):
    P.reset_arena()
    xb = P.carve(BF16, [16, NT])
    pb = P.carve(BF16, [2, NT])
    P.xstg = [P.carve(F32, [2, 513]) for _ in range(2)]
    pst = [P.carve(F32, [2, 513]) for _ in range(2)]
    wp = P.carve(BF16, [2, 2048])
    wD = [P.carve(BF16, [16, 128]) for _ in range(3)]
    NB = 4
    xf = [P.carve(F32, [513]) for _ in range(NB)]
    sg = [P.carve(F32, [513]) for _ in range(2)]
    t1 = [P.carve(F32, [513]) for _ in range(2)]
    yo = [P.carve(F32, [513]) for _ in range(NB)]
    wgv = fm(w_gate)
    wpv = fm(w_proj)
    pv = fm(pT)
    sv = fm(src)
    dv = fm(dst)
    PS = P.PS
    P.dma("pool", wp[:, :, :], wpv[:, :, :], r=[], w=["wp"])
    xkeys = load_all_bf16(P, src, xb)
    for ti, (c0, tw) in enumerate(TILES):
        ps_, kps = pst[ti % 2], "pst%d" % (ti % 2)
        P.dma("sp", ps_[:, :, 0:tw], pv[:, :, c0:c0 + tw], r=[], w=[kps])
        P.copy("dve" if ti % 2 == 0 else "act", pb[:, :, c0:c0 + tw], ps_[:, :, 0:tw], r=[kps], w=[("pb", ti)])
    its = [(dmc, ti, c0, tw) for dmc in range(16) for ti, (c0, tw) in enumerate(TILES)]
    PF = 2

    def load_res(j):
        dmc, ti, c0, tw = its[j]
        P.dma("sp", xf[j % NB][:, 0:tw], sv[:, dmc, c0:c0 + tw], r=[("dram", src.name)], w=["xf%d" % (j % NB)])

    for j in range(min(PF, len(its))):
        load_res(j)
    for it, (dmc, ti, c0, tw) in enumerate(its):
        wd, kw = wD[dmc % 3], "wD%d" % (dmc % 3)
        if ti == 0:
            P.dma("pool", wd[:, :, :], wgv[:, :, dmc * 128:(dmc + 1) * 128], r=[], w=[kw])
        if it + PF < len(its):
            load_res(it + PF)
        g_ps, p_ps = PS[(it % 2) * 2], PS[(it % 2) * 2 + 1]
        kg, kp = "ps%d" % ((it % 2) * 2), "ps%d" % ((it % 2) * 2 + 1)
        for (o, n) in blocks(tw):
            for dc in range(16):
                P.mm(g_ps[:, o:o + n], wd[:, dc, :], xb[:, dc, c0 + o:c0 + o + n], dc == 0, dc == 15,
                     r=[kw] + xkeys, w=[kg])
            for j in range(2):
                P.mm(p_ps[:, o:o + n], wp[:, j, dmc * 128:(dmc + 1) * 128], pb[:, j, c0 + o:c0 + o + n], j == 0, j == 1,
                     r=["wp", ("pb", ti)], w=[kp])
        sgm, ksg = sg[it % 2], "sg%d" % (it % 2)
        P.act(sgm[:, 0:tw], g_ps[:, 0:tw], AF.Sigmoid, r=[kg], w=[ksg])
        t_, k1 = t1[it % 2], "t1%d" % (it % 2)
        P.tt("dve", t_[:, 0:tw], sgm[:, 0:tw], p_ps[:, 0:tw], ALU.mult, r=[ksg, kp], w=[k1])
        y_, ky = yo[it % NB], "yo%d" % (it % NB)
        P.tt("dve", y_[:, 0:tw], t_[:, 0:tw], xf[it % NB][:, 0:tw], ALU.add, r=[k1, "xf%d" % (it % NB)], w=[ky])
        P.dma("sp", dv[:, dmc, c0:c0 + tw], y_[:, 0:tw], r=[ky], w=[("dram", dst.name)])


def zero_dram(P, dst, nrows, ncols, dtype):
    P.reset_arena()
    z = P.carve(dtype, [ncols])
    P.memset("dve", z[:, :], 0.0, w=["z"])
    for r0 in range(0, nrows, 128):
        n = min(128, nrows - r0)
        P.dma("sp", dst[r0:r0 + n, :], z[0:n, :], r=["z"], w=[("dram", dst.name)])


def ffn_phase(P, src, dst, wup, wdn, lng_d, lnb_d):
    dense_bufs(P, lng_d, lnb_d)
    lng, lnb = P.lng, P.lnb
    wuv = fm(wup)
    wdv = fm(wdn)
    PS = P.PS
    xkeys, st0 = load_cast_steps(P, src, TILES[0][0], TILES[0][1], P.xb)
    for st_ in st0:
        st_()
    pending = []
    for ti, (c0, tw) in enumerate(TILES):
        blks = blocks(tw)
        for fp in range(NFC // 2):
            wg, wu = P.wA[fp % 2], P.wB[fp % 2]
            kg, ku = "wA%d" % (fp % 2), "wB%d" % (fp % 2)
            P.dma("pool", wg[:, :, :], wuv[:, :, fp * 256:(fp + 1) * 256], r=[], w=[kg])
            P.dma("pool", wu[:, :, :], wuv[:, :, DFF + fp * 256: DFF + (fp + 1) * 256], r=[], w=[ku])
            for j in range(2):
                fc = fp * 2 + j
                g_ps, u_ps = PS[(fc % 2) * 2], PS[(fc % 2) * 2 + 1]
                kgp, kup = "ps%d" % ((fc % 2) * 2), "ps%d" % ((fc % 2) * 2 + 1)
                for (o, n) in blks:
                    for dc in range(16):
                        P.mm(g_ps[:, o:o + n], wg[:, dc, j * 128:(j + 1) * 128], P.xb[:, dc, o:o + n],
                             dc == 0, dc == 15, r=[kg] + xkeys, w=[kgp])
                for (o, n) in blks:
                    for dc in range(16):
                        P.mm(u_ps[:, o:o + n], wu[:, dc, j * 128:(j + 1) * 128], P.xb[:, dc, o:o + n],
                             dc == 0, dc == 15, r=[ku] + xkeys, w=[kup])
                sg = P.sg[fc % 2]
                ksg = "sg%d" % (fc % 2)
                P.act(sg[:, 0:tw], g_ps[:, 0:tw], AF.Silu, r=[kgp], w=[ksg])
                P.tt("dve", P.hT[:, fc, 0:tw], sg[:, 0:tw], u_ps[:, 0:tw], ALU.mult, r=[ksg, kup], w=[("hT", fc)])
            if pending:
                pending.pop(0)()
        while pending:
            pending.pop(0)()
        hkeys = [("hT", fc) for fc in range(NFC)]
        if ti + 1 < len(TILES):
            nkeys, nsteps = load_cast_steps(P, src, TILES[ti + 1][0], TILES[ti + 1][1], P.xb)
        else:
            nkeys, nsteps = None, []
        pending = proj_res_ln(P, ti, c0, tw, P.hT, hkeys, NFC, wdv, 0, src, 0.5, lng, lnb, dst,
                              pre_steps=nsteps, defer=True)
        xkeys = nkeys
    while pending:
        pending.pop(0)()


TOKBLKS = [(i * 128, 128) for i in range(16)] + [(2048, 1)]
COLBLKS = [(0, 512), (512, 512), (1024, 512), (1536, 512), (2048, 1)]


def load_all_bf16(P, src, xb_all):
    keys = []
    sv = fm(src)
    i = 0
    for (c0, tw) in TILES:
        for q in range(0, 16, 2):
            stg = P.xstg[i % 2]
            ks = "xstg%d" % (i % 2)
            P.dma("sp", stg[:, 0:2, 0:tw], sv[:, q:q + 2, c0:c0 + tw], r=[("dram", src.name)], w=[ks])
            kx = ("xball", q, c0)
            P.copy("dve" if i % 2 == 0 else "act", xb_all[:, q:q + 2, c0:c0 + tw], stg[:, 0:2, 0:tw], r=[ks], w=[kx])
            keys.append(kx)
            i += 1
    return keys


def proj_phase(P, src, wv, groups):
    P.reset_arena()
    xb = P.carve(BF16, [16, NT])
    wb = [P.carve(BF16, [16, 512]) for _ in range(2)]
    P.xstg = [P.carve(F32, [2, 513]) for _ in range(2)]
    stg_f = [P.carve(F32, [NT]) for _ in range(2)]
    stg_t = [P.carve(F32, [512]) for _ in range(2)]
    xkeys = load_all_bf16(P, src, xb)
    PS = P.PS
    wi = 0
    si = 0
    pi = 0
    for g in groups:
        col0, ncols = g["col0"], g["ncols"]
        for c in range(0, ncols, 512):
            n = min(512, ncols - c)
            w = wb[wi % 2]
            kw = "wb%d" % (wi % 2)
            wi += 1
            P.dma("pool", w[:, :, 0:n], wv[:, :, col0 + c: col0 + c + n], r=[], w=[kw])
            if g["mode"] == "fm":
                for cc in range(0, n, 128):
                    m = min(128, n - cc)
                    stg = stg_f[si % 2]
                    kst = "stgf%d" % (si % 2)
                    si += 1
                    sdt = stg if g["dtype"] == F32 else stg.bitcast(BF16)
                    for (o, nn) in COLBLKS:
                        ps = PS[pi % 4]
                        kp = "ps%d" % (pi % 4)
                        pi += 1
                        for dc in range(16):
                            P.mm(ps[0:m, 0:nn], w[:, dc, cc:cc + m], xb[:, dc, o:o + nn], dc == 0, dc == 15,
                                 r=[kw] + xkeys, w=[kp])
                        kwargs = {}
                        if g.get("bias") is not None:
                            P.act(sdt[0:m, o:o + nn], ps[0:m, 0:nn], AF.Identity, r=[kp, "bias"], w=[kst],
                                  bias=g["bias"][0:m, 0:1], scale=float(g.get("scale", 1.0)))
                        elif pi % 2 == 0:
                            P.act(sdt[0:m, o:o + nn], ps[0:m, 0:nn], AF.Copy, r=[kp], w=[kst],
                                  scale=float(g.get("scale", 1.0)))
                        else:
                            P.ts("dve", sdt[0:m, o:o + nn], ps[0:m, 0:nn], float(g.get("scale", 1.0)), None,
                                 ALU.mult, None, r=[kp], w=[kst])
                    dst = g["dst"]
                    P.dma("sp", dst[c + cc: c + cc + m, :], sdt[0:m, 0:NT], r=[kst], w=[("dram", dst.name)])
            else:
                for (t0, tn) in TOKBLKS:
                    ps = PS[pi % 4]
                    kp = "ps%d" % (pi % 4)
                    pi += 1
                    for dc in range(16):
                        P.mm(ps[0:tn, 0:n], xb[:, dc, t0:t0 + tn], w[:, dc, 0:n], dc == 0, dc == 15,
                             r=[kw] + xkeys, w=[kp])
                    stg = stg_t[si % 2]
                    kst = "stgt%d" % (si % 2)
                    si += 1
                    sdt = stg if g["dtype"] == F32 else stg.bitcast(BF16)
                    if pi % 2 == 0:
                        P.act(sdt[0:tn, 0:n], ps[0:tn, 0:n], AF.Copy, r=[kp], w=[kst])
                    else:
                        P.copy("dve", sdt[0:tn, 0:n], ps[0:tn, 0:n], r=[kp], w=[kst])
                    for (dst, dcol0, scol0, scn, prompt_only, sample_only) in g["dsts"]:
                        lo = max(scol0, c)
                        hi = min(scol0 + scn, c + n)
                        if lo >= hi:
                            continue
                        if tn == 1:
                            if prompt_only:
                                continue
                            row = 0 if sample_only else t0
                        else:
                            if sample_only:
                                continue
                            row = t0
                        P.dma("sp", dst[row:row + tn, dcol0 + lo - scol0: dcol0 + hi - scol0],
                              sdt[0:tn, lo - c: hi - c], r=[kst], w=[("dram", dst.name)])


def even_proj(P, x1, w_in, bgate_d, O):
    wv = fm(w_in)
    groups = [
        dict(mode="fm", col0=0, ncols=1024, dst=O["QAT"], dtype=BF16, scale=1.0 / 16.0),
        dict(mode="fm", col0=1024, ncols=1024, dst=O["KAT"], dtype=BF16),
        dict(mode="fm", col0=4096, ncols=8, dst=O["GT"], dtype=F32, bias="BG"),
        dict(mode="fm", col0=4104, ncols=1024, dst=O["QBT"], dtype=BF16, scale=128.0 ** -0.5),
        dict(mode="fm", col0=5128, ncols=256, dst=O["KBT"], dtype=BF16),
        dict(mode="fm", col0=5640, ncols=512, dst=O["QIT"], dtype=BF16, scale=64.0 ** -0.5),
        dict(mode="fm", col0=6152, ncols=64, dst=O["KIT"], dtype=BF16),
        dict(mode="tm", col0=1024, ncols=1024, dtype=BF16, dsts=[(O["KA"], 0, 0, 1024, False, False)]),
        dict(mode="tm", col0=2048, ncols=1024, dtype=BF16, dsts=[(O["VA"], 0, 0, 1024, False, False)]),
        dict(mode="tm", col0=3072, ncols=1024, dtype=F32, dsts=[(O["OA"], 0, 0, 1024, False, False)]),
        dict(mode="tm", col0=5128, ncols=512, dtype=F32, dsts=[
            (O["k_rows_p"], 0, 0, 256, True, False), (O["k_rows_s"], 0, 0, 256, False, True),
            (O["v_rows_p"], 0, 256, 256, True, False), (O["v_rows_s"], 0, 256, 256, False, True)]),
        dict(mode="tm", col0=6152, ncols=72, dtype=F32, dsts=[
            (O["ik_rows_p"], 0, 0, 64, True, False), (O["ik_rows_s"], 0, 0, 64, False, True),
            (O["WI"], 0, 64, 8, False, False)]),
    ]
    P._bg_d = bgate_d
    _proj_with_bias(P, x1, wv, groups)


def _proj_with_bias(P, x1, wv, groups):
    orig_reset = P.reset_arena

    def reset_and_bias():
        orig_reset()
        bg = P.carve(F32, [1])
        P.dma("sp", bg[0:8, 0:1], P._bg_d, r=[], w=["bias"])
        for g in groups:
            if g.get("bias") == "BG":
                g["bias"] = bg

    P.reset_arena = reset_and_bias
    try:
        proj_phase(P, x1, wv, groups)
    finally:
        P.reset_arena = orig_reset


def pc_layout(v):
    sh = v.shape[:-1]
    return np.ascontiguousarray(v.reshape(sh + (16, 128)).swapaxes(-1, -2))


def odd_proj(P, x5, w_in, O):
    wv = fm(w_in)
    groups = [
        dict(mode="fm", col0=0, ncols=1024, dst=O["QCT"], dtype=F32),
        dict(mode="fm", col0=1024, ncols=1024, dst=O["FCT"], dtype=F32),
        dict(mode="fm", col0=4096, ncols=1024, dst=O["QDT"], dtype=F32),
        dict(mode="fm", col0=5120, ncols=1024, dst=O["KDT"], dtype=F32, scale=1.0 / 16.0),
        dict(mode="tm", col0=1024, ncols=1024, dtype=F32, dsts=[(O["FC"], 0, 0, 1024, False, False)]),
        dict(mode="tm", col0=2048, ncols=1024, dtype=BF16, dsts=[(O["IC"], 0, 0, 1024, False, False)]),
        dict(mode="tm", col0=3072, ncols=1024, dtype=F32, dsts=[(O["GC"], 0, 0, 1024, False, False)]),
        dict(mode="tm", col0=5120, ncols=1024, dtype=F32, dsts=[(O["KD"], 0, 0, 1024, False, False)]),
        dict(mode="tm", col0=6144, ncols=1024, dtype=BF16, dsts=[(O["VD"], 0, 0, 1024, False, False)]),
        dict(mode="tm", col0=7168, ncols=1024, dtype=F32, dsts=[(O["GD"], 0, 0, 1024, False, False)]),
    ]
    proj_phase(P, x5, wv, groups)


def build(stop_after=None, debug_out=(), mixers=True):
    P = Prog(debug_out=debug_out)
    xin = P.inp("xin", [D, NT])
    wup = P.inp("w_ffn_up", [2, 2, D, 2 * DFF])
    wdn = P.inp("w_ffn_down", [2, 2, DFF, D])
    lng = P.inp("ln_g", [2, 3, 128, 16])
    lnb = P.inp("ln_b", [2, 3, 128, 16])
    w_in_even = P.inp("w_in_even", [D, 6224])
    w_in_odd = P.inp("w_in_odd", [D, 8192])
    w_out = P.inp("w_out", [2, D, D])
    w_pg = P.inp("w_pe_gate", [2, D, D])
    w_pp = P.inp("w_pe_proj", [2, 256, D])
    pT = P.inp("pT", [2, 256, NT])
    bgate = P.inp("b_gate", [8, 1])
    x = [None] + [P.scratch("x%d" % i, [D, NT]) for i in range(1, 8)]
    yT_out = P.out("yT_out", [D, NT])
    YT0 = P.scratch("YT0", [D, NT], BF16)
    YT1 = P.scratch("YT1", [D, NT], BF16)
    O = {}
    for nm, shp, dt in [("QAT", [1024, NT], BF16), ("KAT", [1024, NT], BF16), ("GT", [8, NT], F32),
                        ("QBT", [1024, NT], BF16), ("KBT", [256, NT], BF16), ("QIT", [512, NT], BF16),
                        ("KIT", [64, NT], BF16), ("KA", [NT, 1024], BF16), ("VA", [NT, 1024], BF16),
                        ("OA", [NT, 1024], F32), ("WI", [NT, 8], F32),
                        ("QCT", [1024, NT], F32), ("FCT", [1024, NT], F32), ("QDT", [1024, NT], F32),
                        ("KDT", [1024, NT], F32), ("FC", [NT, 1024], F32), ("IC", [NT, 1024], BF16),
                        ("GC", [NT, 1024], F32), ("KD", [NT, 1024], F32), ("VD", [NT, 1024], BF16),
                        ("GD", [NT, 1024], F32)]:
        O[nm] = P.scratch(nm, shp, dt)
    for nm, shp in [("k_rows_p", [SEQ, 256]), ("k_rows_s", [1, 256]), ("v_rows_p", [SEQ, 256]),
                    ("v_rows_s", [1, 256]), ("ik_rows_p", [SEQ, 64]), ("ik_rows_s", [1, 64])]:
        O[nm] = P.out(nm, shp)
    P.O = O
    Cn = {}
    Cn["ident"] = P.inp("c_ident", [128, 128])
    Cn["maskT"] = P.inp("c_maskT", [128, 128])
    Cn["sel4"] = P.inp("c_sel4", [4, 4, 128])
    Cn["g_mlstm_bc"] = P.inp("g_mlstm_bc", [128, 1024])
    Cn["cmask"] = P.inp("c_cmask", [128, 128])
    Cn["page_col"] = P.inp("page_col", [128, 1], I32)
    Cn["cosT"] = P.inp("c_cosT", [128, NT])
    Cn["sinT"] = P.inp("c_sinT", [128, NT])
    Cn["cosK"] = P.inp("c_cosK", [NT, 128])
    Cn["sinK"] = P.inp("c_sinK", [NT, 128])
    Cn["ret_D"] = P.inp("c_ret_D", [4, 2, 128, 128])
    Cn["ret_tails"] = P.inp("c_ret_tails", [128, 4, 17])
    Cn["g_ret_bc"] = P.inp("g_ret_bc", [128, 1024])
    Cn["st_ret"] = P.inp("st_ret", [4, 256, 256])
    Cn["rt_p"] = P.out("rt_p", [4, 256, 256])
    Cn["rt_s"] = P.out("rt_s", [4, 256, 256])
    Cn["lb_bc"] = P.inp("lb_bc", [128, 2, 1024])
    Cn["lb_col"] = P.inp("lb_col", [128, 2, 8])
    Cn["g_hgrn_bc"] = P.inp("g_hgrn_bc", [128, 1024])
    Cn["st_hg"] = P.inp("st_hg", [8, 128, 128])
    Cn["hg_p"] = P.out("hg_p", [8, 128, 128])
    Cn["hg_s"] = P.out("hg_s", [8, 128, 128])
    Cn["cache_ik"] = P.inp("cache_ik", [1280, 8192])
    Cn["cache_k4"] = P.inp("cache_k4", [5120, 8192])
    Cn["cache_v4"] = P.inp("cache_v4", [5120, 8192])
    Cn["k_rows_s"] = O["k_rows_s"]
    Cn["v_rows_s"] = O["v_rows_s"]
    Cn["ik_rows_s"] = O["ik_rows_s"]
    Cn["mC_p"] = P.out("mC_p", [4, 256, 256])
    Cn["mn_p"] = P.out("mn_p", [4, 256])
    Cn["mm_p"] = P.out("mm_p", [4, 1])
    Cn["mC_s"] = P.out("mC_s", [4, 256, 256])
    Cn["mn_s"] = P.out("mn_s", [4, 256])
    Cn["mm_s"] = P.out("mm_s", [1, 4])
    Cn["st_C"] = P.inp("st_C", [4, 256, 256])
    Cn["st_n"] = P.inp("st_n", [4, 256])
    Cn["st_m"] = P.inp("st_m", [1, 4])
    P.C = Cn
    ffn_phase(P, xin, x[1], wup[0, 0], wdn[0, 0], lng[0, 0], lnb[0, 0])
    if stop_after == "ffn1":
        return P
    even_proj(P, x[1], w_in_even, bgate, O)
    if stop_after == "proj0":
        return P
    zero_dram(P, YT0, D, NT, BF16)
    zero_dram(P, YT1, D, NT, BF16)
    if mixers:
        mixers_even(P, O, YT0)
    if stop_after == "mix0":
        return P
    outproj_phase(P, YT0, x[1], x[2], w_out[0], lng[0, 1], lnb[0, 1])
    ffn_phase(P, x[2], x[3], wup[0, 1], wdn[0, 1], lng[0, 2], lnb[0, 2])
    pegate_phase(P, x[3], x[4], pT[0], w_pg[0], w_pp[0])
    ffn_phase(P, x[4], x[5], wup[1, 0], wdn[1, 0], lng[1, 0], lnb[1, 0])
    odd_proj(P, x[5], w_in_odd, O)
    if mixers:
        mixers_odd(P, O, YT1)
    outproj_phase(P, YT1, x[5], x[6], w_out[1], lng[1, 1], lnb[1, 1])
    ffn_phase(P, x[6], x[7], wup[1, 1], wdn[1, 1], lng[1, 2], lnb[1, 2])
    pegate_phase(P, x[7], yT_out, pT[1], w_pg[1], w_pp[1])
    return P


def scan_rows(P, src, tmp, n, op, ksrc, ktmp):
    cur, nxt, kc, kn = src, tmp, ksrc, ktmp
    k = 1
    while k < n:
        P.tt("dve", nxt[0:4, k:n], cur[0:4, k:n], cur[0:4, 0:n - k], op, r=[kc], w=[kn])
        P.copy("act", nxt[0:4, 0:k], cur[0:4, 0:k], r=[kc], w=[kn])
        cur, nxt, kc, kn = nxt, cur, kn, kc
        k *= 2
    return cur, kc


def mlstm_phase(P, O, YT0, C):
    P.reset_arena()
    PS = P.PS
    S_ = SEQ
    row = lambda: P.carve(F32, [NT])
    ig, fa, r0, r1, bt, at, Mt, negM, wrow, Erow = [row() for _ in range(10)]
    cols = P.carve(F32, [3, 16, 4])
    sel = P.carve(F32, [4, 128])
    ident = P.carve(F32, [128])
    identb = P.carve(BF16, [128])
    maskT = P.carve(F32, [128])
    gbc = P.carve(F32, [1024])
    QT = P.carve(BF16, [2, NT])
    KT = P.carve(BF16, [2, NT])
    KAt = P.carve(BF16, [17, 256])
    KW = P.carve(BF16, [16, 256])
    VA1 = P.carve(BF16, [17, 257])
    oat = [P.carve(F32, [256]) for _ in range(2)]
    dwt = [P.carve(F32, [128]) for _ in range(3)]
    pt = [P.carve(BF16, [128]) for _ in range(3)]
    hh = [P.carve(F32, [256]) for _ in range(2)]
    junk = P.carve(F32, [256])
    sm = [P.carve(F32, [8]) for _ in range(2)]
    yab = [P.carve(BF16, [256]) for _ in range(2)]
    yT = [P.carve(BF16, [2, 128]) for _ in range(2)]
    cst = [P.carve(F32, [257]) for _ in range(2)]
    Cin = P.carve(F32, [8, 256])
    Cinb = P.carve(BF16, [8, 256])
    nin = P.carve(F32, [8])
    ninb = P.carve(BF16, [8])
    srow = P.carve(F32, [64])
    bc8 = P.carve(F32, [8])
    qcol = P.carve(BF16, [8])
    kcolf = P.carve(F32, [8])
    nnew = P.carve(F32, [8])
    one1 = P.carve(F32, [1])
    one1b = P.carve(BF16, [1])
    yrow = P.carve(BF16, [1024])
    hrow = P.carve(F32, [1024])
    oarow = P.carve(F32, [1024])

    P.dma("sp", ident[:, :], C["ident"], r=[], w=["ident"])
    P.copy("dve", identb[:, :], ident[:, :], r=["ident"], w=["identb"])
    P.dma("sp", maskT[:, :], C["maskT"], r=[], w=["maskT"])
    P.dma("sp", sel[0:4, :, :], C["sel4"], r=[], w=["sel"])
    P.dma("sp", gbc[:, :], C["g_mlstm_bc"], r=[], w=["gbc"])
    P.dma("sp", ig[0:4, :], O["GT"][0:4, :], r=[("dram", "GT")], w=["ig"])
    P.dma("sp", fa[0:4, :], O["GT"][4:8, :], r=[("dram", "GT")], w=["fa"])
    P.memset("dve", one1[:, :], 1.0, w=["one1"])
    P.memset("dve", one1b[:, :], 1.0, w=["one1b"])
    P.act(fa[0:4, :], fa[0:4, :], AF.Exp, r=["fa"], w=["fa"], scale=-1.0)
    P.act(fa[0:4, :], fa[0:4, :], AF.Ln, r=["fa"], w=["fa"], bias=1.0)
    P.ts("dve", fa[0:4, :], fa[0:4, :], -1.0, None, ALU.mult, None, r=["fa"], w=["fa"])
    P.copy("dve", r0[0:4, 0:S_], fa[0:4, 0:S_], r=["fa"], w=["r0"])
    bres, kb = scan_rows(P, r0, r1, S_, ALU.add, "r0", "r1")
    P.copy("dve", bt[0:4, 0:S_], bres[0:4, 0:S_], r=[kb], w=["bt"])
    P.tt("dve", at[0:4, 0:S_], ig[0:4, 0:S_], bt[0:4, 0:S_], ALU.subtract, r=["ig", "bt"], w=["at"])
    P.copy("dve", r0[0:4, 0:S_], at[0:4, 0:S_], r=["at", kb], w=["r0"])
    P.S.add("dve", None, r=["r1"], w=["r1"])
    mres, km = scan_rows(P, r0, r1, S_, ALU.max, "r0", "r1")
    P.ts("dve", Mt[0:4, 0:S_], mres[0:4, 0:S_], 0.0, None, ALU.max, None, r=[km], w=["Mt"])
    P.ts("dve", negM[0:4, 0:S_], Mt[0:4, 0:S_], -1.0, None, ALU.mult, None, r=["Mt"], w=["negM"])
    P.tt("dve", Erow[0:4, 0:S_], bt[0:4, 0:S_], Mt[0:4, 0:S_], ALU.add, r=["bt", "Mt"], w=["Erow"])
    P.dma("sp", C["mm_p"], Erow[0:4, S_ - 1:S_], r=["Erow"], w=[("dram", "mm_p")])
    P.act(Erow[0:4, 0:S_], Erow[0:4, 0:S_], AF.Exp, r=["Erow"], w=["Erow"], scale=-1.0)
    P.act(wrow[0:4, 0:S_], at[0:4, 0:S_], AF.Exp, r=["at", "negM"], w=["wrow"], bias=negM[0:4, S_ - 1:S_])
    tp = PS[0]
    for qi_, (rt, kr) in enumerate([(at, "at"), (wrow, "wrow"), (Erow, "Erow")]):
        for blk in range(16):
            o = (qi_ * 16 + blk) * 4
            P.mm(tp[:, o:o + 4], rt[0:4, blk * 128:(blk + 1) * 128], ident[0:4, 0:4], True, True,
                 r=[kr, "ident"], w=["ps0_0"])
    P.copy("dve", cols[:, :, :, :].rearrange("p a b c -> p (a b c)"), tp[:, 0:192], r=["ps0_0"], w=["cols"])

    rp = PS[1][0:1, 0:8]
    P.mm(rp[:, 0:4], ig[0:4, S_:NT], ident[0:4, 0:4], True, True, r=["ig", "ident"], w=["ps1a"])
    P.mm(rp[:, 4:8], fa[0:4, S_:NT], ident[0:4, 0:4], True, True, r=["fa", "ident"], w=["ps1a"])
    P.copy("dve", srow[0:1, 0:8], rp, r=["ps1a"], w=["srow"])
    P.dma("sp", srow[0:1, 8:12], C["st_m"], r=[], w=["srow"])
    P.tt("dve", srow[0:1, 12:16], srow[0:1, 4:8], srow[0:1, 8:12], ALU.add, r=["srow"], w=["srow"])
    P.tt("dve", srow[0:1, 16:20], srow[0:1, 12:16], srow[0:1, 0:4], ALU.max, r=["srow"], w=["srow"])
    P.tt("dve", srow[0:1, 20:24], srow[0:1, 0:4], srow[0:1, 16:20], ALU.subtract, r=["srow"], w=["srow"])
    P.tt("dve", srow[0:1, 24:28], srow[0:1, 12:16], srow[0:1, 16:20], ALU.subtract, r=["srow"], w=["srow"])
    P.act(srow[0:1, 20:28], srow[0:1, 20:28], AF.Exp, r=["srow"], w=["srow"])
    P.act(srow[0:1, 28:32], srow[0:1, 16:20], AF.Exp, r=["srow"], w=["srow"], scale=-1.0)
    P.dma("sp", C["mm_s"], srow[0:1, 16:20], r=["srow"], w=[("dram", "mm_s")])
    bp = PS[1][:, 0:8]
    P.mm(bp, maskT[0:1, :], srow[0:1, 20:28], True, True, r=["maskT", "srow"], w=["ps1a"])
    P.copy("dve", bc8[:, :], bp, r=["ps1a"], w=["bc8"])
    P.dma("sp", oarow[0:1, :], O["OA"][S_:NT, :], r=[("dram", "OA")], w=["oarow"])
    P.act(oarow[0:1, :], oarow[0:1, :], AF.Sigmoid, r=["oarow"], w=["oarow"])
    P.tt("dve", oarow[0:1, :], oarow[0:1, :], gbc[0:1, :], ALU.mult, r=["oarow", "gbc"], w=["oarow"])

    QATv, KATv = fm(O["QAT"]), fm(O["KAT"])
    KAv = O["KA"][0:S_, :].rearrange("(b p) c -> p b c", p=128)
    VAv = O["VA"][0:S_, :].rearrange("(b p) c -> p b c", p=128)
    OAv = O["OA"][0:S_, :].rearrange("(b p) c -> p b c", p=128)
    YTv = fm(YT0)
    it = 0
    for h in range(4):
        hs = slice(h * 256, (h + 1) * 256)
        P.dma("sp", QT[:, :, :], QATv[:, 2 * h:2 * h + 2, :], r=[("dram", "QAT")], w=["QT"])
        P.dma("sp", KT[:, :, :], KATv[:, 2 * h:2 * h + 2, :], r=[("dram", "KAT")], w=["KT"])
        P.dma("sp", KAt[:, 0:16, :], KAv[:, :, hs], r=[("dram", "KA")], w=["KAt"])
        P.dma("sp", KAt[0:1, 16, :], O["KA"][S_:NT, hs], r=[("dram", "KA")], w=["KAt"])
        P.dma("sp", VA1[:, 0:16, 0:256], VAv[:, :, hs], r=[("dram", "VA")], w=["VA1"])
        P.dma("sp", VA1[0:1, 16, 0:256], O["VA"][S_:NT, hs], r=[("dram", "VA")], w=["VA1"])
        P.memset("pool", VA1[:, :, 256:257], 1.0, w=["VA1"])
        for tq in range(16):
            ts_ = slice(tq * 128, (tq + 1) * 128)
            nb_ps = PS[1][:, 0:128]
            P.mm(nb_ps, sel[0:4, h, :], negM[0:4, ts_], True, True, r=["sel", "negM"], w=["ps1a"])
            num_ps = PS[2 + (tq % 2)][:, 0:257]
            knum = "ps%d" % (2 + tq % 2)
            for sb in range(tq + 1):
                ss = slice(sb * 128, (sb + 1) * 128)
                sT = PS[0][:, (it % 2) * 512:(it % 2) * 512 + 128]
                ksT = "ps0_%d" % (it % 2)
                P.mm(sT, KT[:, 0, ss], QT[:, 0, ts_], True, False, r=["KT", "QT"], w=[ksT])
                P.mm(sT, KT[:, 1, ss], QT[:, 1, ts_], False, True, r=["KT", "QT"], w=[ksT])
                dw = dwt[it % 3]
                kdw = "dw%d" % (it % 3)
                P.act(dw[:, :], nb_ps, AF.Exp, r=["ps1a", "cols"], w=[kdw], bias=cols[:, 0, sb, h:h + 1])
                if sb == tq:
                    P.tt("pool", dw[:, :], dw[:, :], maskT[:, :], ALU.mult, r=[kdw, "maskT"], w=[kdw])
                p_ = pt[it % 3]
                kp = "pt%d" % (it % 3)
                P.tt("dve", p_[:, :], sT, dw[:, :], ALU.mult, r=[ksT, kdw], w=[kp])
                P.mm(num_ps, p_[:, :], VA1[:, sb, :], sb == 0, sb == tq, r=[kp, "VA1"], w=[knum])
                it += 1
            i2 = tq % 2
            smt = sm[i2]
            ksm = "sm%d" % i2
            P.act(smt[:, 0:1], num_ps[:, 256:257], AF.Abs, r=[knum], w=[ksm])
            P.tt("dve", smt[:, 1:2], smt[:, 0:1], cols[:, 2, tq, h:h + 1], ALU.max, r=[ksm, "cols"], w=[ksm])
            P.S.add("dve", lambda e, a=smt: e.reciprocal(a[:, 2:3], a[:, 1:2]), r=[ksm], w=[ksm])
            hht = hh[i2]
            khh = "hh%d" % i2
            P.ts("dve", hht[:, :], num_ps[:, 0:256], smt[:, 2:3], None, ALU.mult, None, r=[knum, ksm], w=[khh])
            P.act(junk[:, :], hht[:, :], AF.Square, r=[khh], w=["junk", ksm], accum_out=smt[:, 3:4])
            P.ts("dve", smt[:, 4:5], smt[:, 3:4], 1.0 / 256.0, float(LN_EPS), ALU.mult, ALU.add, r=[ksm], w=[ksm])
            P.act(smt[:, 4:5], smt[:, 4:5], AF.Sqrt, r=[ksm], w=[ksm])
            P.S.add("dve", lambda e, a=smt: e.reciprocal(a[:, 5:6], a[:, 4:5]), r=[ksm], w=[ksm])
            oa = oat[i2]
            koa = "oa%d" % i2
            P.dma("sp", oa[:, :], OAv[:, tq, hs], r=[("dram", "OA")], w=[koa])
            P.act(oa[:, :], oa[:, :], AF.Sigmoid, r=[koa], w=[koa])
            P.tt("pool", oa[:, :], oa[:, :], gbc[:, hs], ALU.mult, r=[koa, "gbc"], w=[koa])
            ya = yab[i2]
            kya = "ya%d" % i2
            P.stt(ya[:, :], hht[:, :], smt[:, 5:6], oa[:, :], ALU.mult, ALU.mult, r=[khh, ksm, koa], w=[kya])
            tps = PS[1][:, 512:768].bitcast(BF16)
            yTt = yT[i2]
            kyT = "yT%d" % i2
            for j in range(2):
                P.S.add("pe", lambda e, o=tps[:, j * 128:(j + 1) * 128], i=ya[:, j * 128:(j + 1) * 128]:
                        e.transpose(o, i, identb[:, :]), r=[kya, "identb"], w=["ps1b"])
            P.copy("act", yTt[:, :, :].rearrange("p a b -> p (a b)"), tps[:, 0:256], r=["ps1b"], w=[kyT])
            P.dma("sp", YTv[:, 2 * h:2 * h + 2, ts_], yTt[:, :, :], r=[kyT], w=[("dram", YT0.name)])
        for sb in range(16):
            P.ts("dve", KW[:, sb, :], KAt[:, sb, :], cols[:, 1, sb, h:h + 1], None, ALU.mult, None,
                 r=["KAt", "cols"], w=["KW"])
        for dcn in range(2):
            c_ps = PS[2 + dcn][:, 0:257]
            kc = "ps%d" % (2 + dcn)
            for sb in range(16):
                P.mm(c_ps, KW[:, sb, dcn * 128:(dcn + 1) * 128], VA1[:, sb, :], sb == 0, sb == 15,
                     r=["KW", "VA1"], w=[kc])
            ct = cst[dcn]
            kct = "cst%d" % dcn
            P.copy("act", ct[:, :], c_ps, r=[kc], w=[kct])
            P.dma("sp", C["mC_p"][h, dcn * 128:(dcn + 1) * 128, :], ct[:, 0:256], r=[kct], w=[("dram", "mC_p")])
            P.dma("sp", C["mn_p"][h:h + 1, dcn * 128:(dcn + 1) * 128].rearrange("a d -> d a"), ct[:, 256:257],
                  r=[kct], w=[("dram", "mn_p")])
        P.dma("sp", Cin[:, 0:2, :], C["st_C"][h].rearrange("(c p) v -> p c v", p=128), r=[], w=["Cin"])
        P.dma("sp", nin[:, 0:2], C["st_n"][h:h + 1, :].rearrange("a (c p) -> p (a c)", p=128), r=[], w=["nin"])
        P.copy("act", Cinb[:, 0:2, :], Cin[:, 0:2, :], r=["Cin"], w=["Cinb"])
        P.copy("dve", ninb[:, 0:2], nin[:, 0:2], r=["nin"], w=["ninb"])
        P.copy("dve", kcolf[:, 0:2], KT[:, :, S_], r=["KT"], w=["kcolf"])
        rps = PS[1][0:1, 0:512]
        for dcn in range(2):
            P.mm(rps[:, 0:256], QT[:, dcn, S_:NT], Cinb[:, dcn, :], dcn == 0, dcn == 1, r=["QT", "Cinb"], w=["ps1a"])
        for dcn in range(2):
            P.mm(rps[:, 256:257], QT[:, dcn, S_:NT], KT[:, dcn, S_:NT], dcn == 0, dcn == 1, r=["QT", "KT"], w=["ps1a"])
        for dcn in range(2):
            P.mm(rps[:, 257:258], QT[:, dcn, S_:NT], ninb[:, dcn:dcn + 1], dcn == 0, dcn == 1, r=["QT", "ninb"], w=["ps1a"])
        dwh, gwh, emh = srow[0:1, 20 + h:21 + h], srow[0:1, 24 + h:25 + h], srow[0:1, 28 + h:29 + h]
        sc = srow[0:1, 32:48]
        P.ts("dve", sc[:, 0:1], rps[:, 256:257], dwh, None, ALU.mult, None, r=["ps1a", "srow"], w=["sc"])
        P.ts("dve", sc[:, 1:2], rps[:, 257:258], gwh, None, ALU.mult, None, r=["ps1a", "srow"], w=["sc"])
        P.tt("dve", sc[:, 2:3], sc[:, 0:1], sc[:, 1:2], ALU.add, r=["sc"], w=["sc"])
        P.act(sc[:, 3:4], sc[:, 2:3], AF.Abs, r=["sc"], w=["sc"])
        P.tt("dve", sc[:, 4:5], sc[:, 3:4], emh, ALU.max, r=["sc", "srow"], w=["sc"])
        P.S.add("dve", lambda e, a=sc: e.reciprocal(a[:, 5:6], a[:, 4:5]), r=["sc"], w=["sc"])
        P.ts("dve", hrow[0:1, hs], rps[:, 0:256], gwh, None, ALU.mult, None, r=["ps1a", "srow"], w=["hrow"])
        P.stt(hrow[0:1, hs], VA1[0:1, 16, 0:256], sc[:, 0:1], hrow[0:1, hs], ALU.mult, ALU.add,
              r=["VA1", "sc", "hrow"], w=["hrow"])
        P.ts("dve", hrow[0:1, hs], hrow[0:1, hs], sc[:, 5:6], None, ALU.mult, None, r=["hrow", "sc"], w=["hrow"])
        P.act(junk[0:1, :], hrow[0:1, hs], AF.Square, r=["hrow"], w=["junk", "sc"], accum_out=sc[:, 6:7])
        P.ts("dve", sc[:, 7:8], sc[:, 6:7], 1.0 / 256.0, float(LN_EPS), ALU.mult, ALU.add, r=["sc"], w=["sc"])
        P.act(sc[:, 7:8], sc[:, 7:8], AF.Sqrt, r=["sc"], w=["sc"])
        P.S.add("dve", lambda e, a=sc: e.reciprocal(a[:, 8:9], a[:, 7:8]), r=["sc"], w=["sc"])
        P.stt(yrow[0:1, hs], hrow[0:1, hs], sc[:, 8:9], oarow[0:1, hs], ALU.mult, ALU.mult,
              r=["hrow", "sc", "oarow"], w=["yrow"])
        cps = PS[1][:, 512:514]
        for j in range(2):
            P.mm(cps[:, j:j + 1], yrow[0:1, h * 256 + j * 128: h * 256 + (j + 1) * 128], one1b[0:1, 0:1], True, True,
                 r=["yrow", "one1b"], w=["ps1b"])
        P.copy("dve", qcol[:, 0:2], cps, r=["ps1b"], w=["qcol"])
        P.dma("sp", YTv[:, 2 * h:2 * h + 2, S_], qcol[:, 0:2], r=["qcol"], w=[("dram", YT0.name)])
        P.ts("dve", yrow[0:1, 0:256] if False else KW[0:1, 0, :], KAt[0:1, 16, :], dwh, None, ALU.mult, None,
             r=["KAt", "srow", "KW"], w=["KW"])
        for dcn in range(2):
            kv_ps = PS[2 + dcn][:, 0:256]
            kc = "ps%d" % (2 + dcn)
            P.mm(kv_ps, KW[0:1, 0, dcn * 128:(dcn + 1) * 128], VA1[0:1, 16, 0:256], True, True,
                 r=["KW", "VA1"], w=[kc])
            ct = cst[dcn]
            kct = "cst%d" % dcn
            P.stt(ct[:, 0:256], Cin[:, dcn, :], bc8[:, 4 + h:5 + h], kv_ps, ALU.mult, ALU.add,
                  r=["Cin", "bc8", kc], w=[kct])
            P.dma("sp", C["mC_s"][h, dcn * 128:(dcn + 1) * 128, :], ct[:, 0:256], r=[kct], w=[("dram", "mC_s")])
        P.ts("dve", kcolf[:, 0:2], kcolf[:, 0:2], bc8[:, h:h + 1], None, ALU.mult, None, r=["kcolf", "bc8"], w=["kcolf"])
        P.stt(nnew[:, 0:2], nin[:, 0:2], bc8[:, 4 + h:5 + h], kcolf[:, 0:2], ALU.mult, ALU.add,
              r=["nin", "bc8", "kcolf"], w=["nnew"])
        P.dma("sp", C["mn_s"][h:h + 1, :].rearrange("a (c p) -> p (a c)", p=128), nnew[:, 0:2], r=["nnew"],
              w=[("dram", "mn_s")])
    P.mlstm_sample_done = True


def dsa_prompt_phase(P, O, YT0, C):
    P.reset_arena()
    PS = P.PS
    S_ = SEQ
    NSEL = 256
    QI = P.carve(BF16, [4, NT])
    KI2 = P.carve(BF16, [NT])
    QB = P.carve(BF16, [8, NT])
    KB = P.carve(BF16, [2, NT])
    VBf = P.carve(F32, [2, 256])
    VB = P.carve(BF16, [16, 256])
    WIc = P.carve(F32, [16, 8])
    acc = P.carve(F32, [S_])
    rl = [P.carve(F32, [S_]) for _ in range(2)]
    junk = P.carve(BF16, [S_])
    mask = P.carve(BF16, [S_])
    pp = [P.carve(BF16, [S_]) for _ in range(2)]
    pm = [P.carve(BF16, [S_]) for _ in range(2)]
    pT = [P.carve(BF16, [8, 128]) for _ in range(2)]
    ybt = P.carve(BF16, [1024])
    yT = P.carve(BF16, [8, 128])
    cmask = P.carve(F32, [128])
    ident = P.carve(F32, [128])
    identb = P.carve(BF16, [128])
    sm = P.carve(F32, [16])
    hsm = [P.carve(F32, [4]) for _ in range(2)]
    P.dma("sp", ident[:, :], C["ident"], r=[], w=["ident"])
    P.copy("dve", identb[:, :], ident[:, :], r=["ident"], w=["identb"])
    P.dma("sp", cmask[:, :], C["cmask"], r=[], w=["cmask"])
    P.dma("sp", QI[:, :, :], fm(O["QIT"]), r=[("dram", "QIT")], w=["QI"])
    P.dma("sp", KI2[0:64, :], O["KIT"], r=[("dram", "KIT")], w=["KI2"])
    P.dma("sp", KI2[64:128, :], O["KIT"], r=[("dram", "KIT")], w=["KI2"])
    P.dma("sp", QB[:, :, :], fm(O["QBT"]), r=[("dram", "QBT")], w=["QB"])
    P.dma("sp", KB[:, :, :], fm(O["KBT"]), r=[("dram", "KBT")], w=["KB"])
    P.dma("sp", WIc[:, :, :], O["WI"][0:S_, :].rearrange("(b p) h -> p b h", p=128), r=[("dram", "WI")], w=["WIc"])
    vv = O["v_rows_p"].rearrange("(b p) c -> p b c", p=128)
    for b2 in range(0, 16, 2):
        P.dma("sp", VBf[:, :, :], vv[:, b2:b2 + 2, :], r=[("dram", "v_rows_p")], w=["VBf"])
        P.copy("dve", VB[:, b2:b2 + 2, :], VBf[:, :, :], r=["VBf"], w=["VB"])
    YTv = fm(YT0)
    for tq in range(16):
        ts_ = slice(tq * 128, (tq + 1) * 128)
        nk = (tq + 1) * 128
        halves = [(o, min(1024, nk - o)) for o in range(0, nk, 1024)]
        for h in range(8):
            pb = (h % 2) * 64
            r_ = rl[h % 2]
            kr = "rl%d" % (h % 2)
            for (o, n) in halves:
                ps = PS[(h % 2) * 2 + o // 1024]
                kps = "ps%d" % ((h % 2) * 2 + o // 1024)
                for (bo, bn) in blocks(n):
                    P.mm(ps[:, bo:bo + bn], QI[pb:pb + 64, h // 2, ts_], KI2[pb:pb + 64, o + bo:o + bo + bn], True, True,
                         r=["QI", "KI2"], w=[kps])
                P.act(r_[:, o:o + n], ps[:, 0:n], AF.Relu, r=[kps], w=[kr])
            if h == 0:
                P.ts("dve", acc[:, 0:nk], r_[:, 0:nk], WIc[:, tq, 0:1], None, ALU.mult, None, r=[kr, "WIc"], w=["acc"])
            else:
                P.stt(acc[:, 0:nk], r_[:, 0:nk], WIc[:, tq, h:h + 1], acc[:, 0:nk], ALU.mult, ALU.add,
                      r=[kr, "WIc", "acc"], w=["acc"])
        P.S.add("dve", lambda e, a=acc, n_=nk: e.tensor_reduce(sm[:, 0:1], a[:, 0:n_], AX.X, ALU.min), r=["acc"], w=["sm"])
        P.S.add("dve", lambda e, a=acc, n_=nk: e.tensor_reduce(sm[:, 1:2], a[:, 0:n_], AX.X, ALU.max), r=["acc"], w=["sm"])
        P.tt("dve", acc[:, nk - 128:nk], acc[:, nk - 128:nk], cmask[:, :], ALU.add, r=["acc", "cmask"], w=["acc"])
        if tq >= 2:
            P.tt("dve", sm[:, 2:3], sm[:, 1:2], sm[:, 0:1], ALU.subtract, r=["sm"], w=["sm"])
            for it in range(26):
                ci = 0.5 ** (it + 1)
                P.ts("dve", sm[:, 3:4], sm[:, 2:3], float(ci), sm[:, 0:1], ALU.mult, ALU.add, r=["sm"], w=["sm"])
                P.ts("dve", junk[:, 0:nk], acc[:, 0:nk], sm[:, 3:4], 0.0, ALU.is_ge, ALU.add, r=["acc", "sm"],
                     w=["junk", "sm"], accum_out=sm[:, 4:5])
                P.ts("dve", sm[:, 5:6], sm[:, 4:5], NSEL - 0.5, float(ci), ALU.is_ge, ALU.mult, r=["sm"], w=["sm"])
                P.stt(sm[:, 0:1], sm[:, 5:6], sm[:, 2:3], sm[:, 0:1], ALU.mult, ALU.add, r=["sm"], w=["sm"])
        else:
            P.memset("dve", sm[:, 0:1], -1.0e29, w=["sm"])
        P.ts("dve", mask[:, 0:nk], acc[:, 0:nk], sm[:, 0:1], None, ALU.is_ge, None, r=["acc", "sm"], w=["mask"])
        for hq in range(8):
            kv = hq // 4
            hs_ = hsm[hq % 2]
            khs = "hsm%d" % (hq % 2)
            lps = []
            for (o, n) in halves:
                ps = PS[o // 1024]
                kps = "ps%d" % (o // 1024)
                for (bo, bn) in blocks(n):
                    P.mm(ps[:, bo:bo + bn], QB[:, hq, ts_], KB[:, kv, o + bo:o + bo + bn], True, True,
                         r=["QB", "KB"], w=[kps])
                lps.append((ps, kps, o, n))
            for i_, (ps, kps, o, n) in enumerate(lps):
                P.S.add("dve", lambda e, d=hs_[:, i_:i_ + 1], a=ps[:, 0:n]: e.tensor_reduce(d, a, AX.X, ALU.max),
                        r=[kps], w=[khs])
            if len(lps) == 2:
                P.tt("dve", hs_[:, 0:1], hs_[:, 0:1], hs_[:, 1:2], ALU.max, r=[khs], w=[khs])
            P.ts("dve", hs_[:, 2:3], hs_[:, 0:1], -1.0, None, ALU.mult, None, r=[khs], w=[khs])
            p_ = pp[hq % 2]
            kp = "pp%d" % (hq % 2)
            for (ps, kps, o, n) in lps:
                P.act(p_[:, o:o + n], ps[:, 0:n], AF.Exp, r=[kps, khs], w=[kp], bias=hs_[:, 2:3])
            pm_ = pm[hq % 2]
            kpm = "pm%d" % (hq % 2)
            P.stt(pm_[:, 0:nk], p_[:, 0:nk], 1.0, mask[:, 0:nk], ALU.mult, ALU.mult, r=[kp, "mask"], w=[kpm, khs],
                  accum_out=hs_[:, 3:4])
            o_ps = PS[3][:, 0:128]
            nsb = tq + 1
            for g0 in range(0, nsb, 8):
                gn = min(8, nsb - g0)
                gi = (g0 // 8) % 2
                tps = PS[2][:, gi * 512:(gi + 1) * 512].bitcast(BF16)
                ktp = "ps2_%d" % gi
                for j in range(gn):
                    sb = g0 + j
                    P.S.add("pe", lambda e, o_=tps[:, j * 128:(j + 1) * 128], i=pm_[:, sb * 128:(sb + 1) * 128]:
                            e.transpose(o_, i, identb[:, :]), r=[kpm, "identb"], w=[ktp])
                pT_ = pT[gi]
                kpT = "pT%d" % gi
                P.copy("act" if gi == 0 else "dve", pT_[:, 0:gn, :].rearrange("p a b -> p (a b)"), tps[:, 0:gn * 128],
                       r=[ktp], w=[kpT])
                for j in range(gn):
                    sb = g0 + j
                    P.mm(o_ps, pT_[:, j, :], VB[:, sb, kv * 128:(kv + 1) * 128], sb == 0, sb == nsb - 1,
                         r=[kpT, "VB"], w=["ps3a"])
            P.S.add("dve", lambda e, a=hs_: e.reciprocal(a[:, 3:4], a[:, 3:4]), r=[khs], w=[khs])
            P.ts("dve", ybt[:, hq * 128:(hq + 1) * 128], o_ps, hs_[:, 3:4], None, ALU.mult, None, r=["ps3a", khs],
                 w=["ybt"])
        tps = PS[3][:, 512:1024].bitcast(BF16)
        for hq in range(8):
            P.S.add("pe", lambda e, o_=tps[:, hq * 128:(hq + 1) * 128], i=ybt[:, hq * 128:(hq + 1) * 128]:
                    e.transpose(o_, i, identb[:, :]), r=["ybt", "identb"], w=["ps3b"])
        P.copy("act", yT[:, :, :].rearrange("p a b -> p (a b)"), tps[:, 0:1024], r=["ps3b"], w=["yT"])
        P.dma("sp", YTv[:, 8:16, ts_], yT[:, :, :], r=["yT"], w=[("dram", YT0.name)])


def row_from_cols(P, cols_bf, nchunk, identb, ps_row, kps, dst_row, kdst, rkeys):
    for i, cap in enumerate(cols_bf):
        P.mm(ps_row[0:1, i * 128:(i + 1) * 128], cap, identb[:, :], True, True, r=list(rkeys) + ["identb"], w=[kps])
    P.copy("dve", dst_row[0:1, 0:nchunk * 128], ps_row[0:1, 0:nchunk * 128], r=[kps], w=[kdst])


def dsa_sample_phase(P, O, YT0, C):
    P.reset_arena()
    PS = P.PS
    S_ = SEQ
    NSEL = 256
    KIg = P.carve(F32, [128, 64])
    tmp = P.carve(F32, [8192])
    KV = [P.carve(F32, [32, 256]) for _ in range(2)]
    sc = P.carve(F32, [129])
    dot = P.carve(F32, [128])
    lg = P.carve(F32, [8, 129])
    pmat = P.carve(F32, [8, 129])
    mask = P.carve(F32, [129])
    qicol = P.carve(BF16, [4])
    qbcol = P.carve(BF16, [8])
    qirow = P.carve(F32, [512])
    qbrow = P.carve(F32, [1024])
    qibc = P.carve(F32, [512])
    qbbc = P.carve(F32, [1024])
    wrow = P.carve(F32, [8])
    wbc = P.carve(F32, [8])
    kis = P.carve(F32, [64])
    ksf = P.carve(F32, [256])
    vsf = P.carve(F32, [256])
    srow = P.carve(F32, [64])
    ident = P.carve(F32, [128])
    identb = P.carve(BF16, [128])
    ones = P.carve(F32, [128])
    pt_i = P.carve(I32, [1])
    pt_f = P.carve(F32, [1])
    idx = [P.carve(I32, [1]) for _ in range(4)]
    sm = P.carve(F32, [16])
    sm8 = P.carve(F32, [32])
    dg = P.carve(F32, [8])
    yrow = P.carve(F32, [1024])
    yrowb = P.carve(BF16, [1024])
    ycol = P.carve(BF16, [8])
    junk = P.carve(F32, [129])
    P.dma("sp", ident[:, :], C["ident"], r=[], w=["ident"])
    P.copy("dve", identb[:, :], ident[:, :], r=["ident"], w=["identb"])
    P.memset("dve", ones[:, :], 1.0, w=["ones"])
    P.dma("sp", pt_i[:, :], C["page_col"], r=[], w=["pt_i"])
    P.copy("dve", pt_f[:, :], pt_i[:, :], r=["pt_i"], w=["pt_f"])
    for q in range(4):
        P.ts("dve", idx[q][:, :], pt_f[:, :], 4.0, float(q), ALU.mult, ALU.add, r=["pt_f"], w=["idx%d" % q])
    P.S.add("pool", lambda e: e.indirect_dma_start(
        out=KIg[:, :, :].rearrange("p a b -> p (a b)"), out_offset=None, in_=C["cache_ik"],
        in_offset=bass.IndirectOffsetOnAxis(ap=pt_i[:, :], axis=0)), r=["pt_i"], w=["KIg"], dma=True)
    P.dma("sp", qicol[:, :], fm(O["QIT"])[:, :, S_], r=[("dram", "QIT")], w=["qicol"])
    P.dma("sp", qbcol[:, :], fm(O["QBT"])[:, :, S_], r=[("dram", "QBT")], w=["qbcol"])
    rp = PS[0]
    row_from_cols(P, [qicol[:, i:i + 1] for i in range(4)], 4, identb, rp[:, 0:512], "ps0", qirow, "qirow", ["qicol"])
    row_from_cols(P, [qbcol[:, i:i + 1] for i in range(8)], 8, identb, rp[:, 0:1024] if False else PS[1][:, 0:1024],
                  "ps1", qbrow, "qbrow", ["qbcol"])
    P.dma("sp", wrow[0:1, :], O["WI"][S_:NT, :], r=[("dram", "WI")], w=["wrow"])
    P.mm(PS[2][:, 0:512], ones[0:1, :], qirow[0:1, :], True, True, r=["ones", "qirow"], w=["ps2"])
    P.copy("act", qibc[:, :], PS[2][:, 0:512], r=["ps2"], w=["qibc"])
    for i in range(2):
        P.mm(PS[3][:, i * 512:(i + 1) * 512], ones[0:1, :], qbrow[0:1, i * 512:(i + 1) * 512], True, True,
             r=["ones", "qbrow"], w=["ps3"])
    P.copy("act", qbbc[:, :], PS[3][:, 0:1024], r=["ps3"], w=["qbbc"])
    P.mm(PS[2][:, 512:520], ones[0:1, :], wrow[0:1, :], True, True, r=["ones", "wrow"], w=["ps2b"])
    P.copy("dve", wbc[:, :], PS[2][:, 512:520], r=["ps2b"], w=["wbc"])
    qi3 = qibc[:, :].rearrange("p (h d) -> p h d", h=8)
    t3 = tmp[:, :].rearrange("p (a b) -> p a b", a=128)
    for h in range(8):
        P.tt("dve", t3, KIg[:, :, :], qi3[:, h:h + 1, :].broadcast_to([128, 128, 64]), ALU.mult,
             r=["KIg", "qibc"], w=["tmp"])
        P.S.add("dve", lambda e: e.tensor_reduce(dot[:, :], t3, AX.X, ALU.add), r=["tmp"], w=["dot"])
        P.act(dot[:, :], dot[:, :], AF.Relu, r=["dot"], w=["dot"])
        if h == 0:
            P.ts("dve", sc[:, 0:128], dot[:, :], wbc[:, 0:1], None, ALU.mult, None, r=["dot", "wbc"], w=["sc"])
        else:
            P.stt(sc[:, 0:128], dot[:, :], wbc[:, h:h + 1], sc[:, 0:128], ALU.mult, ALU.add, r=["dot", "wbc", "sc"], w=["sc"])
    P.dma("sp", kis[0:1, :], C["ik_rows_s"], r=[("dram", "ik_rows_s")], w=["kis"])
    q3r = qirow[0:1, :].rearrange("p (h d) -> p h d", h=8)
    P.tt("dve", tmp[0:1, 0:512].rearrange("p (h d) -> p h d", h=8), q3r, kis[0:1, :].rearrange("p (a d) -> p a d", a=1).broadcast_to([1, 8, 64]),
         ALU.mult, r=["qirow", "kis", "tmp"], w=["tmp"])
    P.S.add("dve", lambda e: e.tensor_reduce(srow[0:1, 0:8], tmp[0:1, 0:512].rearrange("p (h d) -> p h d", h=8), AX.X, ALU.add),
            r=["tmp"], w=["srow"])
    P.act(srow[0:1, 0:8], srow[0:1, 0:8], AF.Relu, r=["srow"], w=["srow"])
    P.tt("dve", srow[0:1, 0:8], srow[0:1, 0:8], wrow[0:1, :], ALU.mult, r=["srow", "wrow"], w=["srow"])
    P.memset("dve", sc[:, 128:129], -1.0e30, w=["sc"])
    P.S.add("dve", lambda e: e.tensor_reduce(sc[0:1, 128:129], srow[0:1, 0:8], AX.X, ALU.add), r=["srow", "sc"], w=["sc"])
    P.S.add("dve", lambda e: e.tensor_reduce(sm[:, 0:1], sc[:, 0:128], AX.X, ALU.min), r=["sc"], w=["sm"])
    P.S.add("dve", lambda e: e.tensor_reduce(sm[:, 1:2], sc[:, 0:129], AX.X, ALU.max), r=["sc"], w=["sm"])
    P.mm(PS[0][0:1, 0:128], sm[:, 0:1], ident[:, :], True, True, r=["sm", "ident"], w=["ps0"])
    P.mm(PS[0][0:1, 128:256], sm[:, 1:2], ident[:, :], True, True, r=["sm", "ident"], w=["ps0"])
    P.S.add("dve", lambda e: e.tensor_reduce(srow[0:1, 8:9], PS[0][0:1, 0:128], AX.X, ALU.min), r=["ps0"], w=["srow"])
    P.S.add("dve", lambda e: e.tensor_reduce(srow[0:1, 9:10], PS[0][0:1, 128:256], AX.X, ALU.max), r=["ps0"], w=["srow"])
    P.tt("dve", srow[0:1, 9:10], srow[0:1, 9:10], srow[0:1, 8:9], ALU.subtract, r=["srow"], w=["srow"])
    P.mm(PS[0][:, 512:514], ones[0:1, :], srow[0:1, 8:10], True, True, r=["ones", "srow"], w=["ps0b"])
    P.copy("dve", sm[:, 0:1], PS[0][:, 512:513], r=["ps0b"], w=["sm"])
    P.copy("dve", sm[:, 2:3], PS[0][:, 513:514], r=["ps0b"], w=["sm"])
    for it in range(26):
        ci = 0.5 ** (it + 1)
        P.ts("dve", sm[:, 3:4], sm[:, 2:3], float(ci), sm[:, 0:1], ALU.mult, ALU.add, r=["sm"], w=["sm"])
        P.ts("dve", junk[:, 0:129], sc[:, 0:129], sm[:, 3:4], 0.0, ALU.is_ge, ALU.add, r=["sc", "sm"],
             w=["junk", "sm"], accum_out=sm[:, 4:5])
        P.mm(PS[0][:, 520:521], ones[:, :], sm[:, 4:5], True, True, r=["ones", "sm"], w=["ps0b"])
        P.ts("dve", sm[:, 5:6], PS[0][:, 520:521], NSEL - 0.5, float(ci), ALU.is_ge, ALU.mult, r=["ps0b"], w=["sm"])
        P.stt(sm[:, 0:1], sm[:, 5:6], sm[:, 2:3], sm[:, 0:1], ALU.mult, ALU.add, r=["sm"], w=["sm"])
    P.ts("dve", mask[:, 0:129], sc[:, 0:129], sm[:, 0:1], None, ALU.is_ge, None, r=["sc", "sm"], w=["mask"])
    qb3 = qbbc[:, :].rearrange("p (h d) -> p h d", h=8)
    ck = C["cache_k4"]
    cv = C["cache_v4"]
    tk = tmp[:, 0:4096].rearrange("p (a b) -> p a b", a=32)
    for q in range(4):
        kvb = KV[q % 2]
        kk = "KV%d" % (q % 2)
        P.S.add("pool", lambda e, kvb=kvb, q=q: e.indirect_dma_start(
            out=kvb[:, :, :].rearrange("p a b -> p (a b)"), out_offset=None, in_=ck,
            in_offset=bass.IndirectOffsetOnAxis(ap=idx[q][:, :], axis=0)), r=["idx%d" % q], w=[kk], dma=True)
        for hq in range(8):
            kv = hq // 4
            P.tt("dve", tk, kvb[:, :, kv * 128:(kv + 1) * 128], qb3[:, hq:hq + 1, :].broadcast_to([128, 32, 128]),
                 ALU.mult, r=[kk, "qbbc"], w=["tmp"])
            P.S.add("dve", lambda e, hq=hq, q=q: e.tensor_reduce(lg[:, hq, q * 32:(q + 1) * 32], tk, AX.X, ALU.add),
                    r=["tmp"], w=["lg"])
    P.dma("sp", ksf[0:1, :], C["k_rows_s"], r=[("dram", "k_rows_s")], w=["ksf"])
    P.dma("sp", vsf[0:1, :], C["v_rows_s"], r=[("dram", "v_rows_s")], w=["vsf"])
    P.memset("dve", lg[:, :, 128:129], 0.0, w=["lg"])
    for kv in range(2):
        P.tt("dve", tmp[0:1, kv * 512:(kv + 1) * 512].rearrange("p (h d) -> p h d", h=4),
             qbrow[0:1, kv * 512:(kv + 1) * 512].rearrange("p (h d) -> p h d", h=4),
             ksf[0:1, kv * 128:(kv + 1) * 128].rearrange("p (a d) -> p a d", a=1).broadcast_to([1, 4, 128]),
             ALU.mult, r=["qbrow", "ksf", "tmp"], w=["tmp"])
    P.S.add("dve", lambda e: e.tensor_reduce(lg[0:1, :, 128:129].rearrange("p h a -> p (h a)"),
                                             tmp[0:1, 0:1024].rearrange("p (h d) -> p h d", h=8), AX.X, ALU.add),
            r=["tmp", "lg"], w=["lg"])
    for hq in range(8):
        P.S.add("dve", lambda e, hq=hq: e.tensor_reduce(sm8[:, hq:hq + 1], lg[:, hq, :], AX.X, ALU.max), r=["lg"], w=["sm8"])
    P.mm(PS[1][0:8, 0:128], sm8[:, 0:8], ident[:, :], True, True, r=["sm8", "ident"], w=["ps1"])
    P.S.add("dve", lambda e: e.tensor_reduce(sm8[0:8, 8:9], PS[1][0:8, 0:128], AX.X, ALU.max), r=["ps1"], w=["sm8"])
    P.ts("dve", dg[0:8, :], ident[0:8, 0:8], sm8[0:8, 8:9], None, ALU.mult, None, r=["ident", "sm8"], w=["dg"])
    P.mm(PS[1][:, 512:520], ones[0:8, :], dg[0:8, :], True, True, r=["ones", "dg"], w=["ps1b"])
    P.ts("dve", sm8[:, 16:24], PS[1][:, 512:520], -1.0, None, ALU.mult, None, r=["ps1b"], w=["sm8"])
    for hq in range(8):
        P.act(pmat[:, hq, :], lg[:, hq, :], AF.Exp, r=["lg", "sm8"], w=["pmat"], bias=sm8[:, 16 + hq:17 + hq])
        P.stt(pmat[:, hq, :], pmat[:, hq, :], 1.0, mask[:, :], ALU.mult, ALU.mult, r=["pmat", "mask"], w=["pmat", "sm8"],
              accum_out=sm8[:, 24 + hq:25 + hq])
    P.mm(PS[1][:, 520:528], ones[:, :], sm8[:, 24:32], True, True, r=["ones", "sm8"], w=["ps1b"])
    P.copy("dve", srow[0:1, 16:24], PS[1][0:1, 520:528], r=["ps1b"], w=["srow"])
    P.S.add("dve", lambda e: e.reciprocal(srow[0:1, 16:24], srow[0:1, 16:24]), r=["srow"], w=["srow"])
    acc_reg = [PS[hq // 2][0:1, (hq % 2) * 512:(hq % 2) * 512 + 128] for hq in range(8)]
    acc_key = ["psacc%d" % hq for hq in range(8)]
    for q in range(4):
        kvb = KV[q % 2]
        kk = "KV%d" % (q % 2)
        P.S.add("pool", lambda e, kvb=kvb, q=q: e.indirect_dma_start(
            out=kvb[:, :, :].rearrange("p a b -> p (a b)"), out_offset=None, in_=cv,
            in_offset=bass.IndirectOffsetOnAxis(ap=idx[q][:, :], axis=0)), r=["idx%d" % q], w=[kk], dma=True)
        for hq in range(8):
            kv = hq // 4
            dst = acc_reg[hq]
            for o in range(32):
                P.mm(dst, pmat[:, hq, q * 32 + o:q * 32 + o + 1], kvb[:, o, kv * 128:(kv + 1) * 128],
                     q == 0 and o == 0, q == 3 and o == 31, r=["pmat", kk, "ps0", "ps1", "ps1b", "ps0b", "ps2", "ps3", "ps2b"],
                     w=[acc_key[hq]])
    for hq in range(8):
        kv = hq // 4
        src = acc_reg[hq]
        P.stt(yrow[0:1, hq * 128:(hq + 1) * 128], vsf[0:1, kv * 128:(kv + 1) * 128], pmat[0:1, hq, 128:129], src,
              ALU.mult, ALU.add, r=["vsf", "pmat", acc_key[hq]], w=["yrow"])
        P.ts("dve", yrowb[0:1, hq * 128:(hq + 1) * 128], yrow[0:1, hq * 128:(hq + 1) * 128], srow[0:1, 16 + hq:17 + hq],
             None, ALU.mult, None, r=["yrow", "srow"], w=["yrowb"])
    one1b = identb[0:1, 0:1]
    for hq in range(8):
        P.mm(PS[0][:, 600 + hq:601 + hq], yrowb[0:1, hq * 128:(hq + 1) * 128], one1b, True, True,
             r=["yrowb", "identb"], w=["ps0c"] + acc_key)
    P.copy("dve", ycol[:, :], PS[0][:, 600:608], r=["ps0c"], w=["ycol"])
    P.dma("sp", fm(YT0)[:, 8:16, S_], ycol[:, :], r=["ycol"], w=[("dram", YT0.name)])


def mixers_even(P, O, YT0):
    mlstm_phase(P, O, YT0, P.C)
    dsa_prompt_phase(P, O, YT0, P.C)
    dsa_sample_phase(P, O, YT0, P.C)


def hgrn_phase(P, O, YT1, C):
    P.reset_arena()
    PS = P.PS
    ident = P.carve(F32, [128])
    identb = P.carve(BF16, [128])
    maskT = P.carve(F32, [128])
    mask8 = P.carve(F32, [8, 128])
    lbb = P.carve(F32, [2, 1024])
    lbc = P.carve(F32, [2, 8])
    omlc = P.carve(F32, [8])
    gbc = P.carve(F32, [1024])
    St = P.carve(F32, [8, 128])
    Stmp = P.carve(F32, [8, 128])
    Sb = P.carve(BF16, [8, 128])
    junk = P.carve(F32, [1024])
    fct = [P.carve(F32, [1024]) for _ in range(2)]
    eb = [P.carve(F32, [8, 128]) for _ in range(2)]
    enb = [P.carve(F32, [8, 128]) for _ in range(2)]
    qct = [P.carve(F32, [8, 128]) for _ in range(2)]
    fcT = [P.carve(F32, [8, 128]) for _ in range(2)]
    gdt = [P.carve(F32, [1024]) for _ in range(2)]
    sm = [P.carve(F32, [32]) for _ in range(2)]
    QT = [P.carve(BF16, [8, 128]) for _ in range(2)]
    KT = [P.carve(BF16, [8, 128]) for _ in range(2)]
    Ktm = [P.carve(BF16, [8, 128]) for _ in range(2)]
    vt = [P.carve(BF16, [1024]) for _ in range(2)]
    AT = [P.carve(BF16, [8, 128]) for _ in range(2)]
    yab = [P.carve(BF16, [8, 128]) for _ in range(2)]
    yT = [P.carve(BF16, [8, 128]) for _ in range(2)]

    def flat(t):
        return t[:, :, :].rearrange("p a b -> p (a b)")

    P.dma("sp", ident[:, :], C["ident"], r=[], w=["ident"])
    P.copy("dve", identb[:, :], ident[:, :], r=["ident"], w=["identb"])
    P.dma("sp", maskT[:, :], C["maskT"], r=[], w=["maskT"])
    for h in range(8):
        P.copy("dve" if h % 2 == 0 else "pool", mask8[:, h, :], maskT[:, :], r=["maskT"], w=["mask8"])
    P.dma("sp", gbc[:, :], C["g_hgrn_bc"], r=[], w=["gbc"])
    P.dma("sp", lbb[:, :, :], C["lb_bc"], r=[], w=["lbb"])
    P.tt("dve", lbb[:, 0, :], lbb[:, 1, :], lbb[:, 0, :], ALU.subtract, r=["lbb"], w=["lbb"])
    P.act(lbb[:, 1, :], lbb[:, 0, :], AF.Sigmoid, r=["lbb"], w=["lbb"], scale=-1.0)
    P.act(lbb[:, 0, :], lbb[:, 0, :], AF.Sigmoid, r=["lbb"], w=["lbb"])
    P.dma("sp", lbc[:, :, :], C["lb_col"], r=[], w=["lbc"])
    P.tt("dve", lbc[:, 0, :], lbc[:, 1, :], lbc[:, 0, :], ALU.subtract, r=["lbc"], w=["lbc"])
    P.act(omlc[:, 0:8], lbc[:, 0, :], AF.Sigmoid, r=["lbc"], w=["omlc"], scale=-1.0)
    P.memset("dve", flat(St), 0.0, w=["St"])
    P.memset("pool", flat(Sb), 0.0, w=["Sb"])
    QCv, FCTv = fm(O["QCT"]), fm(O["FCT"])
    YTv = fm(YT1)
    b3 = PS[0][:, :].rearrange("p (h t) -> p h t", h=8)
    a3 = PS[1][:, :].rearrange("p (h t) -> p h t", h=8)
    s3 = PS[3][:, :].rearrange("p (h t) -> p h t", h=8)
    for blk, (c0, tw) in enumerate(TOKBLKS):
        i2 = blk % 2
        sfx = "%d" % i2
        if blk == 16:
            P.dma("sp", St[:, :, :], C["st_hg"].rearrange("h d v -> d h v"), r=[], w=["St"])
            P.copy("act", flat(Sb), flat(St), r=["St"], w=["Sb"])
        f_ = fct[i2]
        kf = "fct" + sfx
        P.dma("sp", f_[0:tw, :], O["FC"][c0:c0 + tw, :], r=[("dram", "FC")], w=[kf])
        P.act(f_[0:tw, :], f_[0:tw, :], AF.Sigmoid, r=[kf], w=[kf])
        P.tt("dve", f_[0:tw, :], f_[0:tw, :], lbb[0:tw, 1, :], ALU.mult, r=[kf, "lbb"], w=[kf])
        P.tt("pool", f_[0:tw, :], f_[0:tw, :], lbb[0:tw, 0, :], ALU.add, r=[kf, "lbb"], w=[kf])
        P.act(f_[0:tw, :], f_[0:tw, :], AF.Ln, r=[kf], w=[kf])
        for h in range(8):
            P.mm(PS[0][:, h * 128:h * 128 + tw], f_[0:tw, h * 128:(h + 1) * 128], maskT[0:tw, 0:tw], True, True,
                 r=[kf, "maskT"], w=["ps0"])
        e_, en_ = eb[i2], enb[i2]
        ke, ken = "eb" + sfx, "enb" + sfx
        P.act(e_[:, :, 0:tw], b3[:, :, 0:tw], AF.Exp, r=["ps0"], w=[ke])
        P.act(en_[:, :, 0:tw], b3[:, :, 0:tw], AF.Exp, r=["ps0"], w=[ken], scale=-1.0)
        q_ = qct[i2]
        kq = "qct" + sfx
        P.dma("pool", q_[:, :, 0:tw], QCv[:, 0:8, c0:c0 + tw], r=[("dram", "QCT")], w=[kq])
        P.act(q_[:, :, 0:tw], q_[:, :, 0:tw], AF.Silu, r=[kq], w=[kq])
        QT_ = QT[i2]
        kQT = "QT" + sfx
        P.tt("dve", QT_[:, :, 0:tw], q_[:, :, 0:tw], e_[:, :, 0:tw], ALU.mult, r=[kq, ke], w=[kQT])
        fT = fcT[i2]
        kfT = "fcT" + sfx
        P.dma("pool", fT[:, :, 0:tw], FCTv[:, 0:8, c0:c0 + tw], r=[("dram", "FCT")], w=[kfT])
        P.act(fT[:, :, 0:tw], fT[:, :, 0:tw], AF.Sigmoid, r=[kfT], w=[kfT], scale=-1.0)
        KT_ = KT[i2]
        kKT = "KT" + sfx
        for h in range(8):
            P.stt(KT_[:, h, 0:tw], fT[:, h, 0:tw], omlc[:, h:h + 1], en_[:, h, 0:tw], ALU.mult, ALU.mult,
                  r=[kfT, "omlc", ken], w=[kKT])
        for h in range(8):
            P.mm(PS[0][0:tw, h * 128:(h + 1) * 128], KT_[:, h, 0:tw], identb[:, :], True, True,
                 r=[kKT, "identb"], w=["ps0"])
        Ktm_ = Ktm[i2]
        kKtm = "Ktm" + sfx
        P.copy("act", flat(Ktm_)[0:tw, :], PS[0][0:tw, :], r=["ps0"], w=[kKtm])
        v_ = vt[i2]
        kv = "vt" + sfx
        P.dma("sp", v_[0:tw, :], O["IC"][c0:c0 + tw, :], r=[("dram", "IC")], w=[kv])
        for h in range(8):
            P.mm(PS[1][0:tw, h * 128:h * 128 + tw], KT_[:, h, 0:tw], QT_[:, h, 0:tw], True, True,
                 r=[kKT, kQT], w=["ps1"])
        AT_ = AT[i2]
        kAT = "AT" + sfx
        P.tt("dve", AT_[0:tw, :, 0:tw], a3[0:tw, :, 0:tw], mask8[0:tw, :, 0:tw], ALU.mult, r=["ps1", "mask8"], w=[kAT])
        for h in range(8):
            hs = slice(h * 128, (h + 1) * 128)
            P.mm(PS[2][0:tw, hs], AT_[0:tw, h, 0:tw], v_[0:tw, hs], True, False, r=[kAT, kv], w=["ps2"])
            P.mm(PS[2][0:tw, hs], QT_[:, h, 0:tw], Sb[:, h, :], False, True, r=[kQT, "Sb"], w=["ps2"])
        for h in range(8):
            hs = slice(h * 128, (h + 1) * 128)
            P.mm(PS[3][:, hs], Ktm_[0:tw, h, :], v_[0:tw, hs], True, True, r=[kKtm, kv], w=["ps3"])
        P.tt("dve", Stmp[:, :, :], St[:, :, :], s3, ALU.add, r=["St", "ps3"], w=["Stmp"])
        for h in range(8):
            P.ts("dve", St[:, h, :], Stmp[:, h, :], e_[:, h, tw - 1:tw], None, ALU.mult, None,
                 r=["Stmp", ke], w=["St"])
        P.copy("act", flat(Sb), flat(St), r=["St"], w=["Sb"])
        sm_ = sm[i2]
        ksm = "sm" + sfx
        P.act(junk[0:tw, :], PS[2][0:tw, :], AF.Square, r=["ps2"], w=["junk"])
        P.S.add("dve", lambda e, a=sm_, n=tw: e.tensor_reduce(
            a[0:n, 0:8], junk[0:n, :].rearrange("p (h v) -> p h v", h=8), AX.X, ALU.add), r=["junk"], w=[ksm])
        P.ts("dve", sm_[0:tw, 8:16], sm_[0:tw, 0:8], 1.0 / 128.0, float(LN_EPS), ALU.mult, ALU.add, r=[ksm], w=[ksm])
        P.act(sm_[0:tw, 8:16], sm_[0:tw, 8:16], AF.Sqrt, r=[ksm], w=[ksm])
        P.S.add("dve", lambda e, a=sm_, n=tw: e.reciprocal(a[0:n, 16:24], a[0:n, 8:16]), r=[ksm], w=[ksm])
        g_ = gdt[i2]
        kg = "gdt" + sfx
        P.dma("sp", g_[0:tw, :], O["GC"][c0:c0 + tw, :], r=[("dram", "GC")], w=[kg])
        P.act(g_[0:tw, :], g_[0:tw, :], AF.Silu, r=[kg], w=[kg])
        P.tt("pool", g_[0:tw, :], g_[0:tw, :], gbc[0:tw, :], ALU.mult, r=[kg, "gbc"], w=[kg])
        ya_ = yab[i2]
        kya = "ya" + sfx
        for h in range(8):
            hs = slice(h * 128, (h + 1) * 128)
            P.stt(ya_[0:tw, h, :], PS[2][0:tw, hs], sm_[0:tw, 16 + h:17 + h], g_[0:tw, hs], ALU.mult, ALU.mult,
                  r=["ps2", ksm, kg], w=[kya])
        for h in range(8):
            P.mm(PS[1][:, h * 128:h * 128 + tw], ya_[0:tw, h, :], identb[0:tw, 0:tw], True, True,
                 r=[kya, "identb"], w=["ps1"])
        yT_ = yT[i2]
        kyT = "yT" + sfx
        P.copy("act", yT_[:, :, 0:tw], a3[:, :, 0:tw], r=["ps1"], w=[kyT])
        P.dma("sp", YTv[:, 0:8, c0:c0 + tw], yT_[:, :, 0:tw], r=[kyT], w=[("dram", YT1.name)])
        if blk == 15:
            P.dma("sp", C["hg_p"].rearrange("h d v -> d h v"), St[:, :, :], r=["St"], w=[("dram", "hg_p")])
        if blk == 16:
            P.dma("sp", C["hg_s"].rearrange("h d v -> d h v"), St[:, :, :], r=["St"], w=[("dram", "hg_s")])


RET_GAMMA = [1.0 - 2.0 ** (-5.0 - h) for h in range(4)]


def ret_phase(P, O, YT1, C):
    P.reset_arena()
    PS = P.PS
    S_ = SEQ
    cosT = P.carve(F32, [NT])
    sinT = P.carve(F32, [NT])
    cosK = P.carve(F32, [17, 128])
    sinK = P.carve(F32, [17, 128])
    x1 = P.carve(F32, [NT])
    x2 = P.carve(F32, [NT])
    ta = P.carve(F32, [NT])
    tb = P.carve(F32, [NT])
    QR = P.carve(BF16, [2, NT])
    KR = P.carve(BF16, [2, NT])
    QRl = P.carve(BF16, [2, NT])
    KRl = P.carve(BF16, [2, NT])
    KDt = P.carve(F32, [17, 256])
    KRt = P.carve(F32, [17, 256])
    tk = P.carve(F32, [17, 128])
    KW = P.carve(BF16, [17, 256])
    VDt = P.carve(BF16, [17, 256])
    RD = P.carve(F32, [2, 128])
    tails = P.carve(F32, [4, 17])
    gbc = P.carve(F32, [1024])
    ident = P.carve(F32, [128])
    identb = P.carve(BF16, [128])
    gdt = [P.carve(F32, [256]) for _ in range(2)]
    pt = [P.carve(BF16, [128]) for _ in range(3)]
    junk = P.carve(F32, [256])
    sm = [P.carve(F32, [8]) for _ in range(2)]
    yab = [P.carve(BF16, [256]) for _ in range(2)]
    yT = [P.carve(BF16, [2, 128]) for _ in range(2)]
    cst = [P.carve(F32, [256]) for _ in range(2)]
    Sin = P.carve(F32, [2, 256])
    Sinb = P.carve(BF16, [2, 256])
    srow = P.carve(F32, [16])
    hrow = P.carve(F32, [256])
    gdrow = P.carve(F32, [1024])
    yrow = P.carve(BF16, [256])
    ycol = P.carve(BF16, [2])
    P.dma("sp", ident[:, :], C["ident"], r=[], w=["ident"])
    P.copy("dve", identb[:, :], ident[:, :], r=["ident"], w=["identb"])
    P.dma("sp", cosT[:, :], C["cosT"], r=[], w=["cosT"])
    P.dma("sp", sinT[:, :], C["sinT"], r=[], w=["sinT"])
    P.dma("sp", cosK[:, 0:16, :], C["cosK"][0:S_, :].rearrange("(b p) j -> p b j", p=128), r=[], w=["cosK"])
    P.dma("sp", cosK[0:1, 16, :], C["cosK"][S_:NT, :], r=[], w=["cosK"])
    P.dma("sp", sinK[:, 0:16, :], C["sinK"][0:S_, :].rearrange("(b p) j -> p b j", p=128), r=[], w=["sinK"])
    P.dma("sp", sinK[0:1, 16, :], C["sinK"][S_:NT, :], r=[], w=["sinK"])
    P.dma("sp", tails[:, :, :], C["ret_tails"], r=[], w=["tails"])
    P.dma("sp", gbc[:, :], C["g_ret_bc"], r=[], w=["gbc"])
    P.dma("sp", gdrow[0:1, :], O["GD"][S_:NT, :], r=[("dram", "GD")], w=["gdrow"])
    P.act(gdrow[0:1, :], gdrow[0:1, :], AF.Silu, r=["gdrow"], w=["gdrow"])
    P.tt("dve", gdrow[0:1, :], gdrow[0:1, :], gbc[0:1, :], ALU.mult, r=["gdrow", "gbc"], w=["gdrow"])
    QDv, KDv = fm(O["QDT"]), fm(O["KDT"])
    KDtv = O["KD"][0:S_, :].rearrange("(b p) c -> p b c", p=128)
    VDtv = O["VD"][0:S_, :].rearrange("(b p) c -> p b c", p=128)
    GDv = O["GD"][0:S_, :].rearrange("(b p) c -> p b c", p=128)
    YTv = fm(YT1)
    it = 0
    for h in range(4):
        hs = slice(h * 256, (h + 1) * 256)
        gam = RET_GAMMA[h]
        P.dma("sp", RD[:, :, :], C["ret_D"][h].rearrange("a s t -> s a t"), r=[], w=["RD"])
        for (srcv, dst, dstl, kd, kdl, nm) in ((QDv, QR, QRl, "QR", "QRl", "QDT"), (KDv, KR, KRl, "KR", "KRl", "KDT")):
            P.dma("sp", x1[:, :], srcv[:, 2 * h, :], r=[("dram", nm)], w=["x1"])
            P.dma("sp", x2[:, :], srcv[:, 2 * h + 1, :], r=[("dram", nm)], w=["x2"])
            for (half, sa, sb_, op) in ((0, cosT, sinT, ALU.subtract), (1, sinT, cosT, ALU.add)):
                ka, kb = ("cosT", "sinT") if half == 0 else ("sinT", "cosT")
                P.tt("dve", ta[:, :], x1[:, :], sa[:, :], ALU.mult, r=["x1", ka], w=["ta"])
                P.tt("pool", tb[:, :], x2[:, :], sb_[:, :], ALU.mult, r=["x2", kb], w=["tb"])
                P.tt("dve", ta[:, :], ta[:, :], tb[:, :], op, r=["ta", "tb"], w=["ta"])
                P.copy("act", dst[:, half, :], ta[:, :], r=["ta"], w=[kd])
                P.copy("act", tb[:, :], dst[:, half, :], r=[kd], w=["tb"])
                P.tt("dve", dstl[:, half, :], ta[:, :], tb[:, :], ALU.subtract, r=["ta", "tb"], w=[kdl])
        P.dma("sp", KDt[:, 0:16, :], KDtv[:, :, hs], r=[("dram", "KD")], w=["KDt"])
        P.dma("sp", KDt[0:1, 16, :], O["KD"][S_:NT, hs], r=[("dram", "KD")], w=["KDt"])
        P.dma("sp", VDt[:, 0:16, :], VDtv[:, :, hs], r=[("dram", "VD")], w=["VDt"])
        P.dma("sp", VDt[0:1, 16, :], O["VD"][S_:NT, hs], r=[("dram", "VD")], w=["VDt"])
        for (np_, bsl) in ((128, slice(0, 16)), (1, slice(16, 17))):
            k1, k2 = KDt[0:np_, bsl, 0:128], KDt[0:np_, bsl, 128:256]
            cK, sK = cosK[0:np_, bsl, :], sinK[0:np_, bsl, :]
            P.tt("dve", KRt[0:np_, bsl, 0:128], k1, cK, ALU.mult, r=["KDt", "cosK"], w=["KRt"])
            P.tt("pool", tk[0:np_, bsl, :], k2, sK, ALU.mult, r=["KDt", "sinK"], w=["tk"])
            P.tt("dve", KRt[0:np_, bsl, 0:128], KRt[0:np_, bsl, 0:128], tk[0:np_, bsl, :], ALU.subtract, r=["KRt", "tk"], w=["KRt"])
            P.tt("dve", KRt[0:np_, bsl, 128:256], k1, sK, ALU.mult, r=["KDt", "sinK"], w=["KRt"])
            P.tt("pool", tk[0:np_, bsl, :], k2, cK, ALU.mult, r=["KDt", "cosK", "KRt"], w=["tk"])
            P.tt("dve", KRt[0:np_, bsl, 128:256], KRt[0:np_, bsl, 128:256], tk[0:np_, bsl, :], ALU.add, r=["KRt", "tk"], w=["KRt"])
        for sb in range(17):
            np_ = 128 if sb < 16 else 1
            P.ts("dve", KW[0:np_, sb, :], KRt[0:np_, sb, :], tails[0:np_, h, sb:sb + 1], None, ALU.mult, None,
                 r=["KRt", "tails"], w=["KW"])
        for tq in range(16):
            ts_ = slice(tq * 128, (tq + 1) * 128)
            o_ps = PS[2 + (tq % 2)][:, 0:256]
            ko = "ps%d" % (2 + tq % 2)
            for sb in range(tq + 1):
                ss = slice(sb * 128, (sb + 1) * 128)
                sT = PS[0][:, (it % 2) * 512:(it % 2) * 512 + 128]
                ksT = "ps0_%d" % (it % 2)
                diag = (sb == tq)
                P.mm(sT, KR[:, 0, ss], QR[:, 0, ts_], True, False, r=["KR", "QR"], w=[ksT])
                P.mm(sT, KR[:, 1, ss], QR[:, 1, ts_], False, not diag, r=["KR", "QR"], w=[ksT])
                if diag:
                    P.mm(sT, KRl[:, 0, ss], QR[:, 0, ts_], False, False, r=["KRl", "QR"], w=[ksT])
                    P.mm(sT, KRl[:, 1, ss], QR[:, 1, ts_], False, False, r=["KRl", "QR"], w=[ksT])
                    P.mm(sT, KR[:, 0, ss], QRl[:, 0, ts_], False, False, r=["KR", "QRl"], w=[ksT])
                    P.mm(sT, KR[:, 1, ss], QRl[:, 1, ts_], False, True, r=["KR", "QRl"], w=[ksT])
                p_ = pt[it % 3]
                kp = "pt%d" % (it % 3)
                if sb == tq:
                    P.tt("dve", p_[:, :], sT, RD[:, 1, :], ALU.mult, r=[ksT, "RD"], w=[kp])
                else:
                    P.stt(p_[:, :], sT, float(gam ** (128 * (tq - sb))), RD[:, 0, :], ALU.mult, ALU.mult,
                          r=[ksT, "RD"], w=[kp])
                P.mm(o_ps, p_[:, :], VDt[:, sb, :], sb == 0, sb == tq, r=[kp, "VDt"], w=[ko])
                it += 1
            i2 = tq % 2
            smt = sm[i2]
            ksm = "sm%d" % i2
            P.act(junk[:, :], o_ps, AF.Square, r=[ko], w=["junk", ksm], accum_out=smt[:, 3:4])
            P.ts("dve", smt[:, 4:5], smt[:, 3:4], 1.0 / 256.0, float(LN_EPS), ALU.mult, ALU.add, r=[ksm], w=[ksm])
            P.act(smt[:, 4:5], smt[:, 4:5], AF.Sqrt, r=[ksm], w=[ksm])
            P.S.add("dve", lambda e, a=smt: e.reciprocal(a[:, 5:6], a[:, 4:5]), r=[ksm], w=[ksm])
            gd = gdt[i2]
            kgd = "gd%d" % i2
            P.dma("sp", gd[:, :], GDv[:, tq, hs], r=[("dram", "GD")], w=[kgd])
            P.act(gd[:, :], gd[:, :], AF.Silu, r=[kgd], w=[kgd])
            P.tt("pool", gd[:, :], gd[:, :], gbc[:, hs], ALU.mult, r=[kgd, "gbc"], w=[kgd])
            ya = yab[i2]
            kya = "ya%d" % i2
            P.stt(ya[:, :], o_ps, smt[:, 5:6], gd[:, :], ALU.mult, ALU.mult, r=[ko, ksm, kgd], w=[kya])
            tps = PS[1][:, 512:768].bitcast(BF16)
            yTt = yT[i2]
            kyT = "yT%d" % i2
            for j in range(2):
                P.S.add("pe", lambda e, o=tps[:, j * 128:(j + 1) * 128], i=ya[:, j * 128:(j + 1) * 128]:
                        e.transpose(o, i, identb[:, :]), r=[kya, "identb"], w=["ps1b"])
            P.copy("act", yTt[:, :, :].rearrange("p a b -> p (a b)"), tps[:, 0:256], r=["ps1b"], w=[kyT])
            P.dma("sp", YTv[:, 8 + 2 * h:8 + 2 * h + 2, ts_], yTt[:, :, :], r=[kyT], w=[("dram", YT1.name)])
        for dcn in range(2):
            c_ps = PS[2 + dcn][:, 0:256]
            kc = "ps%d" % (2 + dcn)
            for sb in range(16):
                P.mm(c_ps, KW[:, sb, dcn * 128:(dcn + 1) * 128], VDt[:, sb, :], sb == 0, sb == 15,
                     r=["KW", "VDt"], w=[kc])
            ct = cst[dcn]
            kct = "cst%d" % dcn
            P.copy("act", ct[:, :], c_ps, r=[kc], w=[kct])
            P.dma("sp", C["rt_p"][h, dcn * 128:(dcn + 1) * 128, :], ct[:, :], r=[kct], w=[("dram", "rt_p")])
        P.dma("sp", Sin[:, 0:2, :], C["st_ret"][h].rearrange("(c p) v -> p c v", p=128), r=[], w=["Sin"])
        P.copy("act", Sinb[:, 0:2, :], Sin[:, 0:2, :], r=["Sin"], w=["Sinb"])
        rps = PS[1][0:1, 0:512]
        for dcn in range(2):
            P.mm(rps[:, 0:256], QR[:, dcn, S_:NT], Sinb[:, dcn, :], dcn == 0, dcn == 1, r=["QR", "Sinb"], w=["ps1a"])
        for dcn in range(2):
            P.mm(rps[:, 256:257], QR[:, dcn, S_:NT], KR[:, dcn, S_:NT], dcn == 0, dcn == 1, r=["QR", "KR"], w=["ps1a"])
        P.copy("dve", srow[0:1, 0:1], rps[:, 256:257], r=["ps1a"], w=["srow"])
        P.ts("dve", hrow[0:1, :], rps[:, 0:256], float(gam), None, ALU.mult, None, r=["ps1a"], w=["hrow"])
        P.stt(hrow[0:1, :], VDt[0:1, 16, :], srow[0:1, 0:1], hrow[0:1, :], ALU.mult, ALU.add,
              r=["VDt", "srow", "hrow"], w=["hrow"])
        P.act(junk[0:1, :], hrow[0:1, :], AF.Square, r=["hrow"], w=["junk", "srow"], accum_out=srow[0:1, 1:2])
        P.ts("dve", srow[0:1, 2:3], srow[0:1, 1:2], 1.0 / 256.0, float(LN_EPS), ALU.mult, ALU.add, r=["srow"], w=["srow"])
        P.act(srow[0:1, 2:3], srow[0:1, 2:3], AF.Sqrt, r=["srow"], w=["srow"])
        P.S.add("dve", lambda e: e.reciprocal(srow[0:1, 3:4], srow[0:1, 2:3]), r=["srow"], w=["srow"])
        P.stt(yrow[0:1, :], hrow[0:1, :], srow[0:1, 3:4], gdrow[0:1, hs], ALU.mult, ALU.mult,
              r=["hrow", "srow", "gdrow"], w=["yrow"])
        cps = PS[1][:, 512:514]
        for j in range(2):
            P.mm(cps[:, j:j + 1], yrow[0:1, j * 128:(j + 1) * 128], identb[0:1, 0:1], True, True,
                 r=["yrow", "identb"], w=["ps1b"])
        P.copy("dve", ycol[:, 0:2], cps, r=["ps1b"], w=["ycol"])
        P.dma("sp", YTv[:, 8 + 2 * h:8 + 2 * h + 2, S_], ycol[:, 0:2], r=["ycol"], w=[("dram", YT1.name)])
        for dcn in range(2):
            kv_ps = PS[2 + dcn][:, 0:256]
            kc = "ps%d" % (2 + dcn)
            P.mm(kv_ps, KW[0:1, 16, dcn * 128:(dcn + 1) * 128], VDt[0:1, 16, :], True, True, r=["KW", "VDt"], w=[kc])
            ct = cst[dcn]
            kct = "cst%d" % dcn
            P.stt(ct[:, :], Sin[:, dcn, :], float(gam), kv_ps, ALU.mult, ALU.add, r=["Sin", kc], w=[kct])
            P.dma("sp", C["rt_s"][h, dcn * 128:(dcn + 1) * 128, :], ct[:, :], r=[kct], w=[("dram", "rt_s")])


def mixers_odd(P, O, YT1):
    hgrn_phase(P, O, YT1, P.C)
    ret_phase(P, O, YT1, P.C)


def finish(P):
    S = P.S
    S.barrier()
    st = ExitStack()
    S.emit(st)
    st.close()
    return P.nc


def make_in_maps(inputs):
    f = np.float32
    A = lambda k: np.ascontiguousarray(np.asarray(inputs[k], f))
    xp, xs = A("x_prompt"), A("x_sample")
    pp, ps = A("p_prompt"), A("p_sample")
    shared = {
        "w_ffn_up": A("w_ffn_up"), "w_ffn_down": A("w_ffn_down"),
        "ln_g": pc_layout(A("ln_g")), "ln_b": pc_layout(A("ln_b")),
        "w_in_even": np.ascontiguousarray(A("w_in_even")[0]),
        "w_in_odd": np.ascontiguousarray(A("w_in_odd")[0]),
        "w_out": A("w_out"), "w_pe_gate": A("w_pe_gate"), "w_pe_proj": A("w_pe_proj"),
        "b_gate": np.ascontiguousarray(A("b_gate_mlstm")[0].reshape(8, 1)),
        "c_ident": np.eye(128, dtype=f),
        "cache_ik": np.ascontiguousarray(A("cache_idx_k")[0].reshape(1280, 8192)),
        "cache_k4": np.ascontiguousarray(A("cache_k")[0].reshape(5120, 8192)),
        "cache_v4": np.ascontiguousarray(A("cache_v")[0].reshape(5120, 8192)),
        "c_maskT": np.triu(np.ones((128, 128), f)),
        "c_cmask": np.where(np.tril(np.ones((128, 128), bool)), 0.0, -1.0e30).astype(f),
        "c_sel4": np.ascontiguousarray(np.broadcast_to(np.eye(4, dtype=f)[:, :, None], (4, 4, 128))),
        "g_mlstm_bc": np.ascontiguousarray(np.broadcast_to(A("g_mlstm")[0][None, :], (128, 1024))),
    }
    pos = np.concatenate([np.arange(SEQ, dtype=np.float32), np.array([16384.0], np.float32)])
    inv = (1.0 / (np.float32(10000.0) ** np.linspace(0.0, 1.0, 128, dtype=np.float32))).astype(np.float32)
    ang = (pos[:, None] * inv[None, :]).astype(np.float32)
    shared["c_cosK"] = np.ascontiguousarray(np.cos(ang).astype(f))
    shared["c_sinK"] = np.ascontiguousarray(np.sin(ang).astype(f))
    shared["c_cosT"] = np.ascontiguousarray(shared["c_cosK"].T)
    shared["c_sinT"] = np.ascontiguousarray(shared["c_sinK"].T)
    lg = np.log1p(-np.exp2(-5.0 - np.arange(4, dtype=np.float64)))
    sl = np.arange(128)
    dexp = (sl[None, :] - sl[:, None]).astype(np.float64)
    D0 = np.exp(lg[:, None, None] * dexp[None])
    Dd = np.where(dexp[None] >= 0, D0, 0.0)
    shared["c_ret_D"] = np.ascontiguousarray(np.stack([D0, Dd], 1).astype(f))
    tl = np.zeros((128, 4, 17), np.float64)
    for hh in range(4):
        for bb in range(16):
            tl[:, hh, bb] = np.exp(lg[hh] * (SEQ - 1 - (bb * 128 + sl))) / 16.0
        tl[:, hh, 16] = 1.0 / 16.0
    shared["c_ret_tails"] = tl.astype(f)
    shared["g_ret_bc"] = np.ascontiguousarray(np.broadcast_to(A("g_ret")[0][None, :], (128, 1024)))
    shared["g_hgrn_bc"] = np.ascontiguousarray(np.broadcast_to(A("g_hgrn")[0][None, :], (128, 1024)))
    shared["lb_bc"] = np.ascontiguousarray(np.broadcast_to(A("hgrn_lb")[None], (128, 2, 1024)))
    shared["lb_col"] = np.ascontiguousarray(A("hgrn_lb").reshape(2, 8, 128).transpose(2, 0, 1))
    maps = []
    for c in range(8):
        b = c % 4
        m = dict(shared)
        m["st_ret"] = np.ascontiguousarray(A("state_ret")[0, c])
        m["st_hg"] = np.ascontiguousarray(A("state_hgrn")[0, c])
        m["xin"] = np.ascontiguousarray(np.concatenate([xp[b].T, xs[c].T], axis=1))
        m["page_col"] = np.ascontiguousarray(np.asarray(inputs["page_table"], np.int32)[c].reshape(128, 1))
        m["st_C"] = np.ascontiguousarray(A("state_mlstm_C")[0, c])
        m["st_n"] = np.ascontiguousarray(A("state_mlstm_n")[0, c])
        m["st_m"] = np.ascontiguousarray(A("state_mlstm_m")[0, c].reshape(1, 4))
        m["pT"] = np.ascontiguousarray(np.stack(
            [np.concatenate([pp[i, b].T, ps[i, c].T], axis=1) for i in range(2)]))
        maps.append(m)
    return maps


_NC_CACHE = {}


def kernel(**inputs):
    if "nc" not in _NC_CACHE:
        P = build()
        _NC_CACHE["nc"] = finish(P)
    nc = _NC_CACHE["nc"]
    maps = make_in_maps(inputs)
    res = run_bass_kernel_spmd(nc, maps, core_ids=list(range(8)))
    R = res.results
    f = np.float32

    def get(c, name, shape):
        if name in R[c]:
            return np.asarray(R[c][name], f).reshape(shape)
        return np.zeros(shape, f)

    y_p = np.stack([np.asarray(R[b]["yT_out"], f)[:, :SEQ].T for b in range(4)])
    y_s = np.stack([np.asarray(R[c]["yT_out"], f)[:, SEQ:].T for c in range(8)])
    P4, S8 = range(4), range(8)
    mC_p = np.stack([get(b, "mC_p", (4, 256, 256)) for b in P4])[None]
    mC_s = np.stack([get(c, "mC_s", (4, 256, 256)) for c in S8])[None]
    mn_p = np.stack([get(b, "mn_p", (4, 256)) for b in P4])[None]
    mn_s = np.stack([get(c, "mn_s", (4, 256)) for c in S8])[None]
    mm_p = np.stack([get(b, "mm_p", (4,)) for b in P4])[None]
    mm_s = np.stack([get(c, "mm_s", (4,)) for c in S8])[None]
    k_p = np.stack([get(b, "k_rows_p", (SEQ, 2, 128)) for b in P4])[None]
    k_s = np.stack([get(c, "k_rows_s", (1, 2, 128)) for c in S8])[None]
    v_p = np.stack([get(b, "v_rows_p", (SEQ, 2, 128)) for b in P4])[None]
    v_s = np.stack([get(c, "v_rows_s", (1, 2, 128)) for c in S8])[None]
    ik_p = np.stack([get(b, "ik_rows_p", (SEQ, 64)) for b in P4])[None]
    ik_s = np.stack([get(c, "ik_rows_s", (1, 64)) for c in S8])[None]
    hg_p = np.stack([get(b, "hg_p", (8, 128, 128)) for b in P4])[None]
    hg_s = np.stack([get(c, "hg_s", (8, 128, 128)) for c in S8])[None]
    rt_p = np.stack([get(b, "rt_p", (4, 256, 256)) for b in P4])[None]
    rt_s = np.stack([get(c, "rt_s", (4, 256, 256)) for c in S8])[None]
    return (y_p, y_s, mC_p, mC_s, mn_p, mn_s, mm_p, mm_s, k_p, k_s, v_p, v_s, ik_p, ik_s,
            hg_p, hg_s, rt_p, rt_s)
```

```python
import numpy as np
from contextlib import ExitStack
import concourse.bass as bass
import concourse.mybir as mybir
from concourse.bass_utils import run_bass_kernel_spmd

F32 = mybir.dt.float32
BF16 = mybir.dt.bfloat16
I32 = mybir.dt.int32
AF = mybir.ActivationFunctionType
ALU = mybir.AluOpType
AX = mybir.AxisListType

ENGS = ("pe", "act", "dve", "pool", "sp")

D = 2048
SEQ = 2048
NT = SEQ + 1
DFF = 5632
NFC = DFF // 128
DEPTH = 2
ALPHA = (2 * DEPTH) ** 0.25
LN_EPS = 1e-5
TILES = [(0, 512), (512, 512), (1024, 512), (1536, 513)]


def blocks(tw):
    out = []
    o = 0
    while o < tw:
        n = min(512, tw - o)
        out.append((o, n))
        o += n
    return out


class _Op:
    __slots__ = ("eng", "fn", "dma", "deps", "signaled", "sem", "val", "amt", "pre")

    def __init__(self, eng, fn, dma):
        self.eng = eng
        self.fn = fn
        self.dma = dma
        self.deps = []
        self.signaled = False
        self.sem = None
        self.val = 0
        self.amt = 0
        self.pre = None


class Sched:
    def __init__(self, nc, n_dma_sems=12):
        self.nc = nc
        self.ops = {e: [] for e in ENGS}
        self.last_w = {}
        self.readers = {}
        self.n_dma_sems = n_dma_sems

    def add(self, eng, fn, r=(), w=(), dma=False):
        op = _Op(eng, fn, dma)
        deps = set()
        for k in r:
            lw = self.last_w.get(k)
            if lw is not None:
                deps.add(lw)
        for k in w:
            lw = self.last_w.get(k)
            if lw is not None:
                deps.add(lw)
            for rd in self.readers.get(k, ()):
                deps.add(rd)
        op.deps = list(deps)
        for d in op.deps:
            d.signaled = True
        for k in w:
            self.last_w[k] = op
            self.readers[k] = []
        for k in r:
            if k not in w:
                self.readers.setdefault(k, []).append(op)
        if dma:
            op.signaled = True
        self.ops[eng].append(op)
        return op

    def barrier(self):
        pend = set()
        for v in self.last_w.values():
            if v is not None:
                pend.add(v)
        for v in self.readers.values():
            for rd in v:
                pend.add(rd)
        for e in ENGS:
            op = _Op(e, None, False)
            op.deps = list(pend)
            self.ops[e].append(op)
        for d in pend:
            d.signaled = True
        self.last_w = {}
        self.readers = {}

    def emit(self, stack):
        nc = self.nc
        sems = {e: stack.enter_context(nc.semaphore("s_" + e)) for e in ENGS}
        dsems = {}
        for e in ("sp", "pool", "act"):
            dsems[e] = [stack.enter_context(nc.semaphore("d_%s_%d" % (e, i))) for i in range(self.n_dma_sems)]
        cnt = {e: 0 for e in ENGS}
        dtot = {e: [0] * self.n_dma_sems for e in dsems}
        drr = {e: 0 for e in dsems}
        for e in ENGS:
            for op in self.ops[e]:
                if op.dma:
                    j = drr[e]
                    drr[e] = (j + 1) % self.n_dma_sems
                    op.sem = dsems[e][j]
                    if dtot[e][j] > 0:
                        op.pre = (op.sem, dtot[e][j])
                    dtot[e][j] += 16
                    op.val = dtot[e][j]
                    op.amt = 16
                elif op.signaled:
                    cnt[e] += 1
                    op.sem = sems[e]
                    op.val = cnt[e]
                    op.amt = 1
        block = stack.enter_context(nc.Block())

        def run(e, eng):
            waited = {}
            for op in self.ops[e]:
                need = {}
                if op.pre is not None:
                    need[id(op.pre[0])] = (op.pre[0], op.pre[1])
                for d in op.deps:
                    if d.sem is None:
                        continue
                    if d.eng == "pe" and e == "pe" and not d.dma:
                        continue
                    cur = need.get(id(d.sem))
                    if cur is None or cur[1] < d.val:
                        need[id(d.sem)] = (d.sem, d.val)
                for sid, (sem, val) in need.items():
                    if waited.get(sid, 0) >= val:
                        continue
                    eng.wait_ge(sem, val)
                    waited[sid] = val
                if op.fn is not None:
                    ins = op.fn(eng)
                    if op.signaled:
                        ins.then_inc(op.sem, op.amt)
                elif op.signaled:
                    eng.nop().then_inc(op.sem, op.amt)

        @block.tensor
        def _(eng):
            run("pe", eng)

        @block.scalar
        def _(eng):
            run("act", eng)

        @block.vector
        def _(eng):
            run("dve", eng)

        @block.gpsimd
        def _(eng):
            run("pool", eng)

        @block.sync
        def _(eng):
            run("sp", eng)


class Prog:
    ARENA = 94 * 1024

    def __init__(self, debug_out=None):
        self.nc = bass.Bass("TRN2", target_bir_lowering=False)
        nc = self.nc
        self.S = Sched(nc)
        self.arena = nc.alloc_sbuf_tensor("arena", [128, self.ARENA], BF16)
        self.aoff = 0
        self.PS = [nc.alloc_psum_tensor("ps%d" % i, [128, 1024], F32) for i in range(4)]
        self.ins = {}
        self.outs = {}
        self.debug_out = debug_out or ()
        self.uid = 0

    def reset_arena(self):
        self.S.barrier()
        self.aoff = 0

    def carve(self, dtype, fshape, parts=128):
        n = int(np.prod(fshape))
        nb = n * (2 if dtype == F32 or dtype == I32 else 1)
        nb = (nb + 15) // 16 * 16
        assert self.aoff + nb <= self.ARENA, ("arena overflow", self.aoff, nb)
        ap = self.arena[0:parts, self.aoff:self.aoff + nb]
        self.aoff += nb
        if dtype != BF16:
            ap = ap.bitcast(dtype)
        ap = ap[:, 0:n]
        if len(fshape) == 2:
            ap = ap.rearrange("p (a b) -> p a b", a=fshape[0])
        elif len(fshape) == 3:
            ap = ap.rearrange("p (a b c) -> p a b c", a=fshape[0], b=fshape[1])
        return ap

    def key(self, name):
        self.uid += 1
        return "%s#%d" % (name, self.uid)

    def inp(self, name, shape, dtype=F32):
        t = self.nc.dram_tensor(name, list(shape), dtype, kind="ExternalInput").ap()
        self.ins[name] = t
        return t

    def out(self, name, shape, dtype=F32):
        t = self.nc.dram_tensor(name, list(shape), dtype, kind="ExternalOutput").ap()
        self.outs[name] = t
        return t

    def scratch(self, name, shape, dtype=F32):
        kind = "ExternalOutput" if name in self.debug_out else "Internal"
        t = self.nc.dram_tensor(name, list(shape), dtype, kind=kind).ap()
        if kind == "ExternalOutput":
            self.outs[name] = t
        return t

    def dma(self, eng, out, in_, r, w):
        return self.S.add(eng, lambda e: e.dma_start(out=out, in_=in_, allow_slow_non_contiguous=True), r=r, w=w, dma=True)

    def mm(self, out, lhsT, rhs, start, stop, r, w):
        return self.S.add("pe", lambda e: e.matmul(out, lhsT, rhs, start=start, stop=stop), r=r, w=w)

    def act(self, out, in_, func, r, w, bias=None, scale=None, accum_out=None):
        kw = {}
        if bias is not None:
            kw["bias"] = bias
        if scale is not None:
            kw["scale"] = scale
        if accum_out is not None:
            kw["accum_out"] = accum_out
        return self.S.add("act", lambda e: e.activation(out, in_, func, **kw), r=r, w=w)

    def tt(self, eng, out, in0, in1, op, r, w):
        return self.S.add(eng, lambda e: e.tensor_tensor(out, in0, in1, op), r=r, w=w)

    def ts(self, eng, out, in0, s1, s2, op0, op1, r, w, accum_out=None):
        if op1 is None:
            return self.S.add(eng, lambda e: e.tensor_scalar(out, in0, s1, None, op0), r=r, w=w)
        if accum_out is not None:
            return self.S.add(eng, lambda e: e.tensor_scalar(out, in0, s1, s2, op0, op1, accum_out), r=r, w=w)
        return self.S.add(eng, lambda e: e.tensor_scalar(out, in0, s1, s2, op0, op1), r=r, w=w)

    def stt(self, out, in0, scalar, in1, op0, op1, r, w, accum_out=None):
        if accum_out is not None:
            return self.S.add("dve", lambda e: e.scalar_tensor_tensor(out, in0, scalar, in1, op0, op1, accum_out), r=r, w=w)
        return self.S.add("dve", lambda e: e.scalar_tensor_tensor(out, in0, scalar, in1, op0, op1), r=r, w=w)

    def copy(self, eng, out, in_, r, w):
        if eng == "act":
            return self.S.add("act", lambda e: e.copy(out, in_), r=r, w=w)
        return self.S.add(eng, lambda e: e.tensor_copy(out, in_), r=r, w=w)

    def memset(self, eng, ap, val, w):
        return self.S.add(eng, lambda e: e.memset(ap, val), w=w)


def fm(ap):
    return ap.rearrange("(c p) t -> p c t", p=128)


def load_cast_tile(P, src, c0, tw, xb, nch=16):
    S = P.S
    sv = fm(src)
    keys = []
    grp = 2
    for q in range(0, nch, grp):
        n = min(grp, nch - q)
        i = (q // grp) % 2
        stg = P.xstg[i]
        ks = "xstg%d" % i
        P.dma("sp", stg[:, 0:n, 0:tw], sv[:, q:q + n, c0:c0 + tw], r=[("dram", src.name)], w=[ks])
        kx = ("xb", id(xb), q)
        eng = "dve" if (q // grp) % 2 == 0 else "act"
        P.copy(eng, xb[:, q:q + n, 0:tw], stg[:, 0:n, 0:tw], r=[ks], w=[kx])
        keys.append(kx)
    return keys


def load_cast_steps(P, src, c0, tw, xb, nch=16):
    sv = fm(src)
    grp = 2
    keys, steps = [], []
    for q in range(0, nch, grp):
        n = min(grp, nch - q)
        i = (q // grp) % 2
        kx = ("xb", id(xb), q)
        keys.append(kx)

        def step(q=q, n=n, i=i, kx=kx):
            stg = P.xstg[i]
            ks = "xstg%d" % i
            P.dma("sp", stg[:, 0:n, 0:tw], sv[:, q:q + n, c0:c0 + tw], r=[("dram", src.name)], w=[ks])
            P.copy("dve" if i == 0 else "act", xb[:, q:q + n, 0:tw], stg[:, 0:n, 0:tw], r=[ks], w=[kx])
        steps.append(step)
    return keys, steps


def proj_res_ln(P, tilei, c0, tw, inT, in_keys, nk, wview, wcol0, res_src, scale, lng, lnb, dst,
                dst_bf=None, pre_steps=None, defer=False):
    S = P.S
    blks = blocks(tw)
    PS = P.PS
    rv = fm(res_src)
    dv = fm(dst)
    s1, s2 = PS[2], PS[3]
    pend = None

    def post(dmc, o_ps, ko):
        xf = P.xf[dmc % 2]
        kxf = "xf%d" % (dmc % 2)
        P.dma("sp", xf[:, 0:tw], rv[:, dmc, c0:c0 + tw], r=[("dram", res_src.name)], w=[kxf])
        kv = ("v", dmc)
        xa = P.xa[dmc % 2]
        kxa = "xa%d" % (dmc % 2)
        P.act(xa[:, 0:tw], xf[:, 0:tw], AF.Copy, r=[kxf], w=[kxa], scale=float(ALPHA))
        P.stt(P.v[:, dmc, 0:tw], o_ps[:, 0:tw], float(scale), xa[:, 0:tw], ALU.mult, ALU.add, r=[ko, kxa], w=[kv])
        sq = P.sq[dmc % 2]
        ksq = "sq%d" % (dmc % 2)
        P.act(sq[:, 0:tw], P.v[:, dmc, 0:tw], AF.Square, r=[kv], w=[ksq])
        return kv, ksq

    def stats(dmc, kv, ksq):
        sq = P.sq[dmc % 2]
        for (o, n) in blks:
            P.mm(s1[:, o:o + n], P.ones[:, :], P.v[:, dmc, o:o + n], dmc == 0, dmc == 15, r=[kv, "ones"], w=["ps2"])
            P.mm(s2[:, o:o + n], P.ones[:, :], sq[:, o:o + n], dmc == 0, dmc == 15, r=[ksq, "ones"], w=["ps3"])

    pre = list(pre_steps or [])
    for dmc in range(16):
        wd = P.wD[dmc % 2]
        kw = "wD%d" % (dmc % 2)
        P.dma("pool", wd[:, 0:nk, :], wview[:, :, wcol0 + dmc * 128: wcol0 + (dmc + 1) * 128], r=[], w=[kw])
        for _ in range(-(-len(pre) // (16 - dmc))):
            pre.pop(0)()
        o_ps = PS[dmc % 2]
        ko = "ps%d" % (dmc % 2)
        for (o, n) in blks:
            for k in range(nk):
                P.mm(o_ps[:, o:o + n], wd[:, k, :], inT[:, k, o:o + n], k == 0, k == nk - 1,
                     r=[kw] + list(in_keys), w=[ko])
        if pend is not None:
            stats(*pend)
        kv, ksq = post(dmc, o_ps, ko)
        pend = (dmc, kv, ksq)
    stats(*pend)
    mean, msq, var, rstd = P.lnt[0], P.lnt[1], P.lnt[2], P.lnt[3]
    P.ts("dve", mean[:, 0:tw], s1[:, 0:tw], 1.0 / D, None, ALU.mult, None, r=["ps2"], w=["mean"])
    P.tt("dve", msq[:, 0:tw], mean[:, 0:tw], mean[:, 0:tw], ALU.mult, r=["mean"], w=["msq"])
    P.stt(var[:, 0:tw], s2[:, 0:tw], 1.0 / D, msq[:, 0:tw], ALU.mult, ALU.subtract, r=["ps3", "msq"], w=["var"])
    P.ts("dve", var[:, 0:tw], var[:, 0:tw], float(LN_EPS), None, ALU.add, None, r=["var"], w=["var"])
    P.act(var[:, 0:tw], var[:, 0:tw], AF.Sqrt, r=["var"], w=["var"])
    P.S.add("dve", lambda e: e.reciprocal(rstd[:, 0:tw], var[:, 0:tw]), r=["var"], w=["rstd"])
    def body(dmc):
        t1 = P.t1[dmc % 2]
        k1 = "t1%d" % (dmc % 2)
        yo = P.yo[dmc % 2]
        ky = "yo%d" % (dmc % 2)
        P.tt("dve", t1[:, 0:tw], P.v[:, dmc, 0:tw], mean[:, 0:tw], ALU.subtract, r=[("v", dmc), "mean"], w=[k1])
        P.tt("dve", t1[:, 0:tw], t1[:, 0:tw], rstd[:, 0:tw], ALU.mult, r=[k1, "rstd"], w=[k1])
        P.act(yo[:, 0:tw], t1[:, 0:tw], AF.Identity, r=[k1, "lnp"], w=[ky],
              scale=lng[:, dmc:dmc + 1], bias=lnb[:, dmc:dmc + 1])
        P.dma("sp", dv[:, dmc, c0:c0 + tw], yo[:, 0:tw], r=[ky], w=[("dram", dst.name)])

    steps = [(lambda d=dmc: body(d)) for dmc in range(16)]
    if defer:
        return steps
    for st_ in steps:
        st_()
    return []


def dense_bufs(P, lng_d, lnb_d, full=True):
    P.reset_arena()
    P.xb = P.carve(BF16, [16, 513])
    if full:
        P.hT = P.carve(BF16, [NFC, 513])
        P.v = P.carve(F32, [16, 513])
        P.wA = [P.carve(BF16, [16, 256]) for _ in range(2)]
        P.wB = [P.carve(BF16, [16, 256]) for _ in range(2)]
    P.wD = [P.carve(BF16, [NFC, 128]) for _ in range(2)]
    P.xstg = [P.carve(F32, [2, 513]) for _ in range(2)]
    P.xf = [P.carve(F32, [513]) for _ in range(2)]
    P.xa = [P.carve(F32, [513]) for _ in range(2)]
    P.sq = [P.carve(F32, [513]) for _ in range(2)]
    P.sg = [P.carve(BF16, [513]) for _ in range(2)]
    P.lnt = [P.carve(F32, [513]) for _ in range(4)]
    P.t1 = [P.carve(F32, [513]) for _ in range(2)]
    P.yo = [P.carve(F32, [513]) for _ in range(2)]
    P.ones = P.carve(F32, [128])
    P.lng = P.carve(F32, [16])
    P.lnb = P.carve(F32, [16])
    P.memset("dve", P.ones[:, :], 1.0, w=["ones"])
    if lng_d is not None:
        P.dma("sp", P.lng[:, :], lng_d, r=[], w=["lnp"])
        P.dma("sp", P.lnb[:, :], lnb_d, r=[], w=["lnp"])


def outproj_phase(P, yT, res_src, dst, w_out, lng_d, lnb_d):
    dense_bufs(P, lng_d, lnb_d)
    wv = fm(w_out)
    yv = fm(yT)
    xbufs = [P.xb, P.hT[:, 0:16, :]]
    kxb = ["xb_y0", "xb_y1"]

    def load(ti):
        c0, tw = TILES[ti]
        P.dma("sp", xbufs[ti % 2][:, :, 0:tw], yv[:, :, c0:c0 + tw], r=[("dram", yT.name)], w=[kxb[ti % 2]])

    load(0)
    pending = []
    for ti, (c0, tw) in enumerate(TILES):
        pre = []
        if ti + 1 < len(TILES):
            pre.append(lambda t=ti + 1: load(t))
        pre += pending
        pending = proj_res_ln(P, ti, c0, tw, xbufs[ti % 2], [kxb[ti % 2]], 16, wv, 0, res_src, 1.0, P.lng, P.lnb, dst,
                              pre_steps=pre, defer=True)
    for st_ in pending:
        st_()


def pegate_phase(P, src, dst, pT, w_gate, w_proj):
    P.reset_arena()
    xb = P.carve(BF16, [16, NT])
    pb = P.carve(BF16, [2, NT])
    P.xstg = [P.carve(F32, [2, 513]) for _ in range(2)]
    pst = [P.carve(F32, [2, 513]) for _ in range(2)]
    wp = P.carve(BF16, [2, 2048])
    wD = [P.carve(BF16, [16, 128]) for _ in range(3)]
    NB = 4
    xf = [P.carve(F32, [513]) for _ in range(NB)]
    sg = [P.carve(F32, [513]) for _ in range(2)]
    t1 = [P.carve(F32, [513]) for _ in range(2)]
    yo = [P.carve(F32, [513]) for _ in range(NB)]
    wgv = fm(w_gate)
    wpv = fm(w_proj)
    pv = fm(pT)
    sv = fm(src)
    dv = fm(dst)
    PS = P.PS
    P.dma("pool", wp[:, :, :], wpv[:, :, :], r=[], w=["wp"])
    xkeys = load_all_bf16(P, src, xb)
    for ti, (c0, tw) in enumerate(TILES):
        ps_, kps = pst[ti % 2], "pst%d" % (ti % 2)
        P.dma("sp", ps_[:, :, 0:tw], pv[:, :, c0:c0 + tw], r=[], w=[kps])
        P.copy("dve" if ti % 2 == 0 else "act", pb[:, :, c0:c0 + tw], ps_[:, :, 0:tw], r=[kps], w=[("pb", ti)])
    its = [(dmc, ti, c0, tw) for dmc in range(16) for ti, (c0, tw) in enumerate(TILES)]
    PF = 2

    def load_res(j):
        dmc, ti, c0, tw = its[j]
        P.dma("sp", xf[j % NB][:, 0:tw], sv[:, dmc, c0:c0 + tw], r=[("dram", src.name)], w=["xf%d" % (j % NB)])

    for j in range(min(PF, len(its))):
        load_res(j)
    for it, (dmc, ti, c0, tw) in enumerate(its):
        wd, kw = wD[dmc % 3], "wD%d" % (dmc % 3)
        if ti == 0:
            P.dma("pool", wd[:, :, :], wgv[:, :, dmc * 128:(dmc + 1) * 128], r=[], w=[kw])
        if it + PF < len(its):
            load_res(it + PF)
        g_ps, p_ps = PS[(it % 2) * 2], PS[(it % 2) * 2 + 1]
        kg, kp = "ps%d" % ((it % 2) * 2), "ps%d" % ((it % 2) * 2 + 1)
        for (o, n) in blocks(tw):
            for dc in range(16):
                P.mm(g_ps[:, o:o + n], wd[:, dc, :], xb[:, dc, c0 + o:c0 + o + n], dc == 0, dc == 15,
                     r=[kw] + xkeys, w=[kg])
            for j in range(2):
                P.mm(p_ps[:, o:o + n], wp[:, j, dmc * 128:(dmc + 1) * 128], pb[:, j, c0 + o:c0 + o + n], j == 0, j == 1,
                     r=["wp", ("pb", ti)], w=[kp])
        sgm, ksg = sg[it % 2], "sg%d" % (it % 2)
        P.act(sgm[:, 0:tw], g_ps[:, 0:tw], AF.Sigmoid, r=[kg], w=[ksg])
        t_, k1 = t1[it % 2], "t1%d" % (it % 2)
        P.tt("dve", t_[:, 0:tw], sgm[:, 0:tw], p_ps[:, 0:tw], ALU.mult, r=[ksg, kp], w=[k1])
        y_, ky = yo[it % NB], "yo%d" % (it % NB)
        P.tt("dve", y_[:, 0:tw], t_[:, 0:tw], xf[it % NB][:, 0:tw], ALU.add, r=[k1, "xf%d" % (it % NB)], w=[ky])
        P.dma("sp", dv[:, dmc, c0:c0 + tw], y_[:, 0:tw], r=[ky], w=[("dram", dst.name)])


def zero_dram(P, dst, nrows, ncols, dtype):
    P.reset_arena()
    z = P.carve(dtype, [ncols])
    P.memset("dve", z[:, :], 0.0, w=["z"])
    for r0 in range(0, nrows, 128):
        n = min(128, nrows - r0)
        P.dma("sp", dst[r0:r0 + n, :], z[0:n, :], r=["z"], w=[("dram", dst.name)])


def ffn_phase(P, src, dst, wup, wdn, lng_d, lnb_d):
    dense_bufs(P, lng_d, lnb_d)
    lng, lnb = P.lng, P.lnb
    wuv = fm(wup)
    wdv = fm(wdn)
    PS = P.PS
    xkeys, st0 = load_cast_steps(P, src, TILES[0][0], TILES[0][1], P.xb)
    for st_ in st0:
        st_()
    pending = []
    for ti, (c0, tw) in enumerate(TILES):
        blks = blocks(tw)
        for fp in range(NFC // 2):
            wg, wu = P.wA[fp % 2], P.wB[fp % 2]
            kg, ku = "wA%d" % (fp % 2), "wB%d" % (fp % 2)
            P.dma("pool", wg[:, :, :], wuv[:, :, fp * 256:(fp + 1) * 256], r=[], w=[kg])
            P.dma("pool", wu[:, :, :], wuv[:, :, DFF + fp * 256: DFF + (fp + 1) * 256], r=[], w=[ku])
            for j in range(2):
                fc = fp * 2 + j
                g_ps, u_ps = PS[(fc % 2) * 2], PS[(fc % 2) * 2 + 1]
                kgp, kup = "ps%d" % ((fc % 2) * 2), "ps%d" % ((fc % 2) * 2 + 1)
                for (o, n) in blks:
                    for dc in range(16):
                        P.mm(g_ps[:, o:o + n], wg[:, dc, j * 128:(j + 1) * 128], P.xb[:, dc, o:o + n],
                             dc == 0, dc == 15, r=[kg] + xkeys, w=[kgp])
                for (o, n) in blks:
                    for dc in range(16):
                        P.mm(u_ps[:, o:o + n], wu[:, dc, j * 128:(j + 1) * 128], P.xb[:, dc, o:o + n],
                             dc == 0, dc == 15, r=[ku] + xkeys, w=[kup])
                sg = P.sg[fc % 2]
                ksg = "sg%d" % (fc % 2)
                P.act(sg[:, 0:tw], g_ps[:, 0:tw], AF.Silu, r=[kgp], w=[ksg])
                P.tt("dve", P.hT[:, fc, 0:tw], sg[:, 0:tw], u_ps[:, 0:tw], ALU.mult, r=[ksg, kup], w=[("hT", fc)])
            if pending:
                pending.pop(0)()
        while pending:
            pending.pop(0)()
        hkeys = [("hT", fc) for fc in range(NFC)]
        if ti + 1 < len(TILES):
            nkeys, nsteps = load_cast_steps(P, src, TILES[ti + 1][0], TILES[ti + 1][1], P.xb)
        else:
            nkeys, nsteps = None, []
        pending = proj_res_ln(P, ti, c0, tw, P.hT, hkeys, NFC, wdv, 0, src, 0.5, lng, lnb, dst,
                              pre_steps=nsteps, defer=True)
        xkeys = nkeys
    while pending:
        pending.pop(0)()


TOKBLKS = [(i * 128, 128) for i in range(16)] + [(2048, 1)]
COLBLKS = [(0, 512), (512, 512), (1024, 512), (1536, 512), (2048, 1)]


def load_all_bf16(P, src, xb_all):
    keys = []
    sv = fm(src)
    i = 0
    for (c0, tw) in TILES:
        for q in range(0, 16, 2):
            stg = P.xstg[i % 2]
            ks = "xstg%d" % (i % 2)
            P.dma("sp", stg[:, 0:2, 0:tw], sv[:, q:q + 2, c0:c0 + tw], r=[("dram", src.name)], w=[ks])
            kx = ("xball", q, c0)
            P.copy("dve" if i % 2 == 0 else "act", xb_all[:, q:q + 2, c0:c0 + tw], stg[:, 0:2, 0:tw], r=[ks], w=[kx])
            keys.append(kx)
            i += 1
    return keys


def proj_phase(P, src, wv, groups):
    P.reset_arena()
    xb = P.carve(BF16, [16, NT])
    wb = [P.carve(BF16, [16, 512]) for _ in range(2)]
    P.xstg = [P.carve(F32, [2, 513]) for _ in range(2)]
    stg_f = [P.carve(F32, [NT]) for _ in range(2)]
    stg_t = [P.carve(F32, [512]) for _ in range(2)]
    xkeys = load_all_bf16(P, src, xb)
    PS = P.PS
    wi = 0
    si = 0
    pi = 0
    for g in groups:
        col0, ncols = g["col0"], g["ncols"]
        for c in range(0, ncols, 512):
            n = min(512, ncols - c)
            w = wb[wi % 2]
            kw = "wb%d" % (wi % 2)
            wi += 1
            P.dma("pool", w[:, :, 0:n], wv[:, :, col0 + c: col0 + c + n], r=[], w=[kw])
            if g["mode"] == "fm":
                for cc in range(0, n, 128):
                    m = min(128, n - cc)
                    stg = stg_f[si % 2]
                    kst = "stgf%d" % (si % 2)
                    si += 1
                    sdt = stg if g["dtype"] == F32 else stg.bitcast(BF16)
                    for (o, nn) in COLBLKS:
                        ps = PS[pi % 4]
                        kp = "ps%d" % (pi % 4)
                        pi += 1
                        for dc in range(16):
                            P.mm(ps[0:m, 0:nn], w[:, dc, cc:cc + m], xb[:, dc, o:o + nn], dc == 0, dc == 15,
                                 r=[kw] + xkeys, w=[kp])
                        kwargs = {}
                        if g.get("bias") is not None:
                            P.act(sdt[0:m, o:o + nn], ps[0:m, 0:nn], AF.Identity, r=[kp, "bias"], w=[kst],
                                  bias=g["bias"][0:m, 0:1], scale=float(g.get("scale", 1.0)))
                        elif pi % 2 == 0:
                            P.act(sdt[0:m, o:o + nn], ps[0:m, 0:nn], AF.Copy, r=[kp], w=[kst],
                                  scale=float(g.get("scale", 1.0)))
                        else:
                            P.ts("dve", sdt[0:m, o:o + nn], ps[0:m, 0:nn], float(g.get("scale", 1.0)), None,
                                 ALU.mult, None, r=[kp], w=[kst])
                    dst = g["dst"]
                    P.dma("sp", dst[c + cc: c + cc + m, :], sdt[0:m, 0:NT], r=[kst], w=[("dram", dst.name)])
            else:
                for (t0, tn) in TOKBLKS:
                    ps = PS[pi % 4]
                    kp = "ps%d" % (pi % 4)
                    pi += 1
                    for dc in range(16):
                        P.mm(ps[0:tn, 0:n], xb[:, dc, t0:t0 + tn], w[:, dc, 0:n], dc == 0, dc == 15,
                             r=[kw] + xkeys, w=[kp])
                    stg = stg_t[si % 2]
                    kst = "stgt%d" % (si % 2)
                    si += 1
                    sdt = stg if g["dtype"] == F32 else stg.bitcast(BF16)
                    if pi % 2 == 0:
                        P.act(sdt[0:tn, 0:n], ps[0:tn, 0:n], AF.Copy, r=[kp], w=[kst])
                    else:
                        P.copy("dve", sdt[0:tn, 0:n], ps[0:tn, 0:n], r=[kp], w=[kst])
                    for (dst, dcol0, scol0, scn, prompt_only, sample_only) in g["dsts"]:
                        lo = max(scol0, c)
                        hi = min(scol0 + scn, c + n)
                        if lo >= hi:
                            continue
                        if tn == 1:
                            if prompt_only:
                                continue
                            row = 0 if sample_only else t0
                        else:
                            if sample_only:
                                continue
                            row = t0
                        P.dma("sp", dst[row:row + tn, dcol0 + lo - scol0: dcol0 + hi - scol0],
                              sdt[0:tn, lo - c: hi - c], r=[kst], w=[("dram", dst.name)])


def even_proj(P, x1, w_in, bgate_d, O):
    wv = fm(w_in)
    groups = [
        dict(mode="fm", col0=0, ncols=1024, dst=O["QAT"], dtype=BF16, scale=1.0 / 16.0),
        dict(mode="fm", col0=1024, ncols=1024, dst=O["KAT"], dtype=BF16),
        dict(mode="fm", col0=4096, ncols=8, dst=O["GT"], dtype=F32, bias="BG"),
        dict(mode="fm", col0=4104, ncols=1024, dst=O["QBT"], dtype=BF16, scale=128.0 ** -0.5),
        dict(mode="fm", col0=5128, ncols=256, dst=O["KBT"], dtype=BF16),
        dict(mode="fm", col0=5640, ncols=512, dst=O["QIT"], dtype=BF16, scale=64.0 ** -0.5),
        dict(mode="fm", col0=6152, ncols=64, dst=O["KIT"], dtype=BF16),
        dict(mode="tm", col0=1024, ncols=1024, dtype=BF16, dsts=[(O["KA"], 0, 0, 1024, False, False)]),
        dict(mode="tm", col0=2048, ncols=1024, dtype=BF16, dsts=[(O["VA"], 0, 0, 1024, False, False)]),
        dict(mode="tm", col0=3072, ncols=1024, dtype=F32, dsts=[(O["OA"], 0, 0, 1024, False, False)]),
        dict(mode="tm", col0=5128, ncols=512, dtype=F32, dsts=[
            (O["k_rows_p"], 0, 0, 256, True, False), (O["k_rows_s"], 0, 0, 256, False, True),
            (O["v_rows_p"], 0, 256, 256, True, False), (O["v_rows_s"], 0, 256, 256, False, True)]),
        dict(mode="tm", col0=6152, ncols=72, dtype=F32, dsts=[
            (O["ik_rows_p"], 0, 0, 64, True, False), (O["ik_rows_s"], 0, 0, 64, False, True),
            (O["WI"], 0, 64, 8, False, False)]),
    ]
    P._bg_d = bgate_d
    _proj_with_bias(P, x1, wv, groups)


def _proj_with_bias(P, x1, wv, groups):
    orig_reset = P.reset_arena

    def reset_and_bias():
        orig_reset()
        bg = P.carve(F32, [1])
        P.dma("sp", bg[0:8, 0:1], P._bg_d, r=[], w=["bias"])
        for g in groups:
            if g.get("bias") == "BG":
                g["bias"] = bg

    P.reset_arena = reset_and_bias
    try:
        proj_phase(P, x1, wv, groups)
    finally:
        P.reset_arena = orig_reset


def pc_layout(v):
    sh = v.shape[:-1]
    return np.ascontiguousarray(v.reshape(sh + (16, 128)).swapaxes(-1, -2))


def odd_proj(P, x5, w_in, O):
    wv = fm(w_in)
    groups = [
        dict(mode="fm", col0=0, ncols=1024, dst=O["QCT"], dtype=F32),
        dict(mode="fm", col0=1024, ncols=1024, dst=O["FCT"], dtype=F32),
        dict(mode="fm", col0=4096, ncols=1024, dst=O["QDT"], dtype=F32),
        dict(mode="fm", col0=5120, ncols=1024, dst=O["KDT"], dtype=F32, scale=1.0 / 16.0),
        dict(mode="tm", col0=1024, ncols=1024, dtype=F32, dsts=[(O["FC"], 0, 0, 1024, False, False)]),
        dict(mode="tm", col0=2048, ncols=1024, dtype=BF16, dsts=[(O["IC"], 0, 0, 1024, False, False)]),
        dict(mode="tm", col0=3072, ncols=1024, dtype=F32, dsts=[(O["GC"], 0, 0, 1024, False, False)]),
        dict(mode="tm", col0=5120, ncols=1024, dtype=F32, dsts=[(O["KD"], 0, 0, 1024, False, False)]),
        dict(mode="tm", col0=6144, ncols=1024, dtype=BF16, dsts=[(O["VD"], 0, 0, 1024, False, False)]),
        dict(mode="tm", col0=7168, ncols=1024, dtype=F32, dsts=[(O["GD"], 0, 0, 1024, False, False)]),
    ]
    proj_phase(P, x5, wv, groups)


def build(stop_after=None, debug_out=(), mixers=True):
    P = Prog(debug_out=debug_out)
    xin = P.inp("xin", [D, NT])
    wup = P.inp("w_ffn_up", [2, 2, D, 2 * DFF])
    wdn = P.inp("w_ffn_down", [2, 2, DFF, D])
    lng = P.inp("ln_g", [2, 3, 128, 16])
    lnb = P.inp("ln_b", [2, 3, 128, 16])
    w_in_even = P.inp("w_in_even", [D, 6224])
    w_in_odd = P.inp("w_in_odd", [D, 8192])
    w_out = P.inp("w_out", [2, D, D])
    w_pg = P.inp("w_pe_gate", [2, D, D])
    w_pp = P.inp("w_pe_proj", [2, 256, D])
    pT = P.inp("pT", [2, 256, NT])
    bgate = P.inp("b_gate", [8, 1])
    x = [None] + [P.scratch("x%d" % i, [D, NT]) for i in range(1, 8)]
    yT_out = P.out("yT_out", [D, NT])
    YT0 = P.scratch("YT0", [D, NT], BF16)
    YT1 = P.scratch("YT1", [D, NT], BF16)
    O = {}
    for nm, shp, dt in [("QAT", [1024, NT], BF16), ("KAT", [1024, NT], BF16), ("GT", [8, NT], F32),
                        ("QBT", [1024, NT], BF16), ("KBT", [256, NT], BF16), ("QIT", [512, NT], BF16),
                        ("KIT", [64, NT], BF16), ("KA", [NT, 1024], BF16), ("VA", [NT, 1024], BF16),
                        ("OA", [NT, 1024], F32), ("WI", [NT, 8], F32),
                        ("QCT", [1024, NT], F32), ("FCT", [1024, NT], F32), ("QDT", [1024, NT], F32),
                        ("KDT", [1024, NT], F32), ("FC", [NT, 1024], F32), ("IC", [NT, 1024], BF16),
                        ("GC", [NT, 1024], F32), ("KD", [NT, 1024], F32), ("VD", [NT, 1024], BF16),
                        ("GD", [NT, 1024], F32)]:
        O[nm] = P.scratch(nm, shp, dt)
    for nm, shp in [("k_rows_p", [SEQ, 256]), ("k_rows_s", [1, 256]), ("v_rows_p", [SEQ, 256]),
                    ("v_rows_s", [1, 256]), ("ik_rows_p", [SEQ, 64]), ("ik_rows_s", [1, 64])]:
        O[nm] = P.out(nm, shp)
    P.O = O
    Cn = {}
    Cn["ident"] = P.inp("c_ident", [128, 128])
    Cn["maskT"] = P.inp("c_maskT", [128, 128])
    Cn["sel4"] = P.inp("c_sel4", [4, 4, 128])
    Cn["g_mlstm_bc"] = P.inp("g_mlstm_bc", [128, 1024])
    Cn["cmask"] = P.inp("c_cmask", [128, 128])
    Cn["page_col"] = P.inp("page_col", [128, 1], I32)
    Cn["cosT"] = P.inp("c_cosT", [128, NT])
    Cn["sinT"] = P.inp("c_sinT", [128, NT])
    Cn["cosK"] = P.inp("c_cosK", [NT, 128])
    Cn["sinK"] = P.inp("c_sinK", [NT, 128])
    Cn["ret_D"] = P.inp("c_ret_D", [4, 2, 128, 128])
    Cn["ret_tails"] = P.inp("c_ret_tails", [128, 4, 17])
    Cn["g_ret_bc"] = P.inp("g_ret_bc", [128, 1024])
    Cn["st_ret"] = P.inp("st_ret", [4, 256, 256])
    Cn["rt_p"] = P.out("rt_p", [4, 256, 256])
    Cn["rt_s"] = P.out("rt_s", [4, 256, 256])
    Cn["lb_bc"] = P.inp("lb_bc", [128, 2, 1024])
    Cn["lb_col"] = P.inp("lb_col", [128, 2, 8])
    Cn["g_hgrn_bc"] = P.inp("g_hgrn_bc", [128, 1024])
    Cn["st_hg"] = P.inp("st_hg", [8, 128, 128])
    Cn["hg_p"] = P.out("hg_p", [8, 128, 128])
    Cn["hg_s"] = P.out("hg_s", [8, 128, 128])
    Cn["cache_ik"] = P.inp("cache_ik", [1280, 8192])
    Cn["cache_k4"] = P.inp("cache_k4", [5120, 8192])
    Cn["cache_v4"] = P.inp("cache_v4", [5120, 8192])
    Cn["k_rows_s"] = O["k_rows_s"]
    Cn["v_rows_s"] = O["v_rows_s"]
    Cn["ik_rows_s"] = O["ik_rows_s"]
    Cn["mC_p"] = P.out("mC_p", [4, 256, 256])
    Cn["mn_p"] = P.out("mn_p", [4, 256])
    Cn["mm_p"] = P.out("mm_p", [4, 1])
    Cn["mC_s"] = P.out("mC_s", [4, 256, 256])
    Cn["mn_s"] = P.out("mn_s", [4, 256])
    Cn["mm_s"] = P.out("mm_s", [1, 4])
    Cn["st_C"] = P.inp("st_C", [4, 256, 256])
    Cn["st_n"] = P.inp("st_n", [4, 256])
    Cn["st_m"] = P.inp("st_m", [1, 4])
    P.C = Cn
    ffn_phase(P, xin, x[1], wup[0, 0], wdn[0, 0], lng[0, 0], lnb[0, 0])
    if stop_after == "ffn1":
        return P
    even_proj(P, x[1], w_in_even, bgate, O)
    if stop_after == "proj0":
        return P
    zero_dram(P, YT0, D, NT, BF16)
    zero_dram(P, YT1, D, NT, BF16)
    if mixers:
        mixers_even(P, O, YT0)
    if stop_after == "mix0":
        return P
    outproj_phase(P, YT0, x[1], x[2], w_out[0], lng[0, 1], lnb[0, 1])
    ffn_phase(P, x[2], x[3], wup[0, 1], wdn[0, 1], lng[0, 2], lnb[0, 2])
    pegate_phase(P, x[3], x[4], pT[0], w_pg[0], w_pp[0])
    ffn_phase(P, x[4], x[5], wup[1, 0], wdn[1, 0], lng[1, 0], lnb[1, 0])
    odd_proj(P, x[5], w_in_odd, O)
    if mixers:
        mixers_odd(P, O, YT1)
    outproj_phase(P, YT1, x[5], x[6], w_out[1], lng[1, 1], lnb[1, 1])
    ffn_phase(P, x[6], x[7], wup[1, 1], wdn[1, 1], lng[1, 2], lnb[1, 2])
    pegate_phase(P, x[7], yT_out, pT[1], w_pg[1], w_pp[1])
    return P


def scan_rows(P, src, tmp, n, op, ksrc, ktmp):
    cur, nxt, kc, kn = src, tmp, ksrc, ktmp
    k = 1
    while k < n:
        P.tt("dve", nxt[0:4, k:n], cur[0:4, k:n], cur[0:4, 0:n - k], op, r=[kc], w=[kn])
        P.copy("act", nxt[0:4, 0:k], cur[0:4, 0:k], r=[kc], w=[kn])
        cur, nxt, kc, kn = nxt, cur, kn, kc
        k *= 2
    return cur, kc


def mlstm_phase(P, O, YT0, C):
    P.reset_arena()
    PS = P.PS
    S_ = SEQ
    row = lambda: P.carve(F32, [NT])
    ig, fa, r0, r1, bt, at, Mt, negM, wrow, Erow = [row() for _ in range(10)]
    cols = P.carve(F32, [3, 16, 4])
    sel = P.carve(F32, [4, 128])
    ident = P.carve(F32, [128])
    identb = P.carve(BF16, [128])
    maskT = P.carve(F32, [128])
    gbc = P.carve(F32, [1024])
    QT = P.carve(BF16, [2, NT])
    KT = P.carve(BF16, [2, NT])
    KAt = P.carve(BF16, [17, 256])
    KW = P.carve(BF16, [16, 256])
    VA1 = P.carve(BF16, [17, 257])
    oat = [P.carve(F32, [256]) for _ in range(2)]
    dwt = [P.carve(F32, [128]) for _ in range(3)]
    pt = [P.carve(BF16, [128]) for _ in range(3)]
    hh = [P.carve(F32, [256]) for _ in range(2)]
    junk = P.carve(F32, [256])
    sm = [P.carve(F32, [8]) for _ in range(2)]
    yab = [P.carve(BF16, [256]) for _ in range(2)]
    yT = [P.carve(BF16, [2, 128]) for _ in range(2)]
    cst = [P.carve(F32, [257]) for _ in range(2)]
    Cin = P.carve(F32, [8, 256])
    Cinb = P.carve(BF16, [8, 256])
    nin = P.carve(F32, [8])
    ninb = P.carve(BF16, [8])
    srow = P.carve(F32, [64])
    bc8 = P.carve(F32, [8])
    qcol = P.carve(BF16, [8])
    kcolf = P.carve(F32, [8])
    nnew = P.carve(F32, [8])
    one1 = P.carve(F32, [1])
    one1b = P.carve(BF16, [1])
    yrow = P.carve(BF16, [1024])
    hrow = P.carve(F32, [1024])
    oarow = P.carve(F32, [1024])

    P.dma("sp", ident[:, :], C["ident"], r=[], w=["ident"])
    P.copy("dve", identb[:, :], ident[:, :], r=["ident"], w=["identb"])
    P.dma("sp", maskT[:, :], C["maskT"], r=[], w=["maskT"])
    P.dma("sp", sel[0:4, :, :], C["sel4"], r=[], w=["sel"])
    P.dma("sp", gbc[:, :], C["g_mlstm_bc"], r=[], w=["gbc"])
    P.dma("sp", ig[0:4, :], O["GT"][0:4, :], r=[("dram", "GT")], w=["ig"])
    P.dma("sp", fa[0:4, :], O["GT"][4:8, :], r=[("dram", "GT")], w=["fa"])
    P.memset("dve", one1[:, :], 1.0, w=["one1"])
    P.memset("dve", one1b[:, :], 1.0, w=["one1b"])
    P.act(fa[0:4, :], fa[0:4, :], AF.Exp, r=["fa"], w=["fa"], scale=-1.0)
    P.act(fa[0:4, :], fa[0:4, :], AF.Ln, r=["fa"], w=["fa"], bias=1.0)
    P.ts("dve", fa[0:4, :], fa[0:4, :], -1.0, None, ALU.mult, None, r=["fa"], w=["fa"])
    P.copy("dve", r0[0:4, 0:S_], fa[0:4, 0:S_], r=["fa"], w=["r0"])
    bres, kb = scan_rows(P, r0, r1, S_, ALU.add, "r0", "r1")
    P.copy("dve", bt[0:4, 0:S_], bres[0:4, 0:S_], r=[kb], w=["bt"])
    P.tt("dve", at[0:4, 0:S_], ig[0:4, 0:S_], bt[0:4, 0:S_], ALU.subtract, r=["ig", "bt"], w=["at"])
    P.copy("dve", r0[0:4, 0:S_], at[0:4, 0:S_], r=["at", kb], w=["r0"])
    P.S.add("dve", None, r=["r1"], w=["r1"])
    mres, km = scan_rows(P, r0, r1, S_, ALU.max, "r0", "r1")
    P.ts("dve", Mt[0:4, 0:S_], mres[0:4, 0:S_], 0.0, None, ALU.max, None, r=[km], w=["Mt"])
    P.ts("dve", negM[0:4, 0:S_], Mt[0:4, 0:S_], -1.0, None, ALU.mult, None, r=["Mt"], w=["negM"])
    P.tt("dve", Erow[0:4, 0:S_], bt[0:4, 0:S_], Mt[0:4, 0:S_], ALU.add, r=["bt", "Mt"], w=["Erow"])
    P.dma("sp", C["mm_p"], Erow[0:4, S_ - 1:S_], r=["Erow"], w=[("dram", "mm_p")])
    P.act(Erow[0:4, 0:S_], Erow[0:4, 0:S_], AF.Exp, r=["Erow"], w=["Erow"], scale=-1.0)
    P.act(wrow[0:4, 0:S_], at[0:4, 0:S_], AF.Exp, r=["at", "negM"], w=["wrow"], bias=negM[0:4, S_ - 1:S_])
    tp = PS[0]
    for qi_, (rt, kr) in enumerate([(at, "at"), (wrow, "wrow"), (Erow, "Erow")]):
        for blk in range(16):
            o = (qi_ * 16 + blk) * 4
            P.mm(tp[:, o:o + 4], rt[0:4, blk * 128:(blk + 1) * 128], ident[0:4, 0:4], True, True,
                 r=[kr, "ident"], w=["ps0_0"])
    P.copy("dve", cols[:, :, :, :].rearrange("p a b c -> p (a b c)"), tp[:, 0:192], r=["ps0_0"], w=["cols"])

    rp = PS[1][0:1, 0:8]
    P.mm(rp[:, 0:4], ig[0:4, S_:NT], ident[0:4, 0:4], True, True, r=["ig", "ident"], w=["ps1a"])
    P.mm(rp[:, 4:8], fa[0:4, S_:NT], ident[0:4, 0:4], True, True, r=["fa", "ident"], w=["ps1a"])
    P.copy("dve", srow[0:1, 0:8], rp, r=["ps1a"], w=["srow"])
    P.dma("sp", srow[0:1, 8:12], C["st_m"], r=[], w=["srow"])
    P.tt("dve", srow[0:1, 12:16], srow[0:1, 4:8], srow[0:1, 8:12], ALU.add, r=["srow"], w=["srow"])
    P.tt("dve", srow[0:1, 16:20], srow[0:1, 12:16], srow[0:1, 0:4], ALU.max, r=["srow"], w=["srow"])
    P.tt("dve", srow[0:1, 20:24], srow[0:1, 0:4], srow[0:1, 16:20], ALU.subtract, r=["srow"], w=["srow"])
    P.tt("dve", srow[0:1, 24:28], srow[0:1, 12:16], srow[0:1, 16:20], ALU.subtract, r=["srow"], w=["srow"])
    P.act(srow[0:1, 20:28], srow[0:1, 20:28], AF.Exp, r=["srow"], w=["srow"])
    P.act(srow[0:1, 28:32], srow[0:1, 16:20], AF.Exp, r=["srow"], w=["srow"], scale=-1.0)
    P.dma("sp", C["mm_s"], srow[0:1, 16:20], r=["srow"], w=[("dram", "mm_s")])
    bp = PS[1][:, 0:8]
    P.mm(bp, maskT[0:1, :], srow[0:1, 20:28], True, True, r=["maskT", "srow"], w=["ps1a"])
    P.copy("dve", bc8[:, :], bp, r=["ps1a"], w=["bc8"])
    P.dma("sp", oarow[0:1, :], O["OA"][S_:NT, :], r=[("dram", "OA")], w=["oarow"])
    P.act(oarow[0:1, :], oarow[0:1, :], AF.Sigmoid, r=["oarow"], w=["oarow"])
    P.tt("dve", oarow[0:1, :], oarow[0:1, :], gbc[0:1, :], ALU.mult, r=["oarow", "gbc"], w=["oarow"])

    QATv, KATv = fm(O["QAT"]), fm(O["KAT"])
    KAv = O["KA"][0:S_, :].rearrange("(b p) c -> p b c", p=128)
    VAv = O["VA"][0:S_, :].rearrange("(b p) c -> p b c", p=128)
    OAv = O["OA"][0:S_, :].rearrange("(b p) c -> p b c", p=128)
    YTv = fm(YT0)
    it = 0
    for h in range(4):
        hs = slice(h * 256, (h + 1) * 256)
        P.dma("sp", QT[:, :, :], QATv[:, 2 * h:2 * h + 2, :], r=[("dram", "QAT")], w=["QT"])
        P.dma("sp", KT[:, :, :], KATv[:, 2 * h:2 * h + 2, :], r=[("dram", "KAT")], w=["KT"])
        P.dma("sp", KAt[:, 0:16, :], KAv[:, :, hs], r=[("dram", "KA")], w=["KAt"])
        P.dma("sp", KAt[0:1, 16, :], O["KA"][S_:NT, hs], r=[("dram", "KA")], w=["KAt"])
        P.dma("sp", VA1[:, 0:16, 0:256], VAv[:, :, hs], r=[("dram", "VA")], w=["VA1"])
        P.dma("sp", VA1[0:1, 16, 0:256], O["VA"][S_:NT, hs], r=[("dram", "VA")], w=["VA1"])
        P.memset("pool", VA1[:, :, 256:257], 1.0, w=["VA1"])
        for tq in range(16):
            ts_ = slice(tq * 128, (tq + 1) * 128)
            nb_ps = PS[1][:, 0:128]
            P.mm(nb_ps, sel[0:4, h, :], negM[0:4, ts_], True, True, r=["sel", "negM"], w=["ps1a"])
            num_ps = PS[2 + (tq % 2)][:, 0:257]
            knum = "ps%d" % (2 + tq % 2)
            for sb in range(tq + 1):
                ss = slice(sb * 128, (sb + 1) * 128)
                sT = PS[0][:, (it % 2) * 512:(it % 2) * 512 + 128]
                ksT = "ps0_%d" % (it % 2)
                P.mm(sT, KT[:, 0, ss], QT[:, 0, ts_], True, False, r=["KT", "QT"], w=[ksT])
                P.mm(sT, KT[:, 1, ss], QT[:, 1, ts_], False, True, r=["KT", "QT"], w=[ksT])
                dw = dwt[it % 3]
                kdw = "dw%d" % (it % 3)
                P.act(dw[:, :], nb_ps, AF.Exp, r=["ps1a", "cols"], w=[kdw], bias=cols[:, 0, sb, h:h + 1])
                if sb == tq:
                    P.tt("pool", dw[:, :], dw[:, :], maskT[:, :], ALU.mult, r=[kdw, "maskT"], w=[kdw])
                p_ = pt[it % 3]
                kp = "pt%d" % (it % 3)
                P.tt("dve", p_[:, :], sT, dw[:, :], ALU.mult, r=[ksT, kdw], w=[kp])
                P.mm(num_ps, p_[:, :], VA1[:, sb, :], sb == 0, sb == tq, r=[kp, "VA1"], w=[knum])
                it += 1
            i2 = tq % 2
            smt = sm[i2]
            ksm = "sm%d" % i2
            P.act(smt[:, 0:1], num_ps[:, 256:257], AF.Abs, r=[knum], w=[ksm])
            P.tt("dve", smt[:, 1:2], smt[:, 0:1], cols[:, 2, tq, h:h + 1], ALU.max, r=[ksm, "cols"], w=[ksm])
            P.S.add("dve", lambda e, a=smt: e.reciprocal(a[:, 2:3], a[:, 1:2]), r=[ksm], w=[ksm])
            hht = hh[i2]
            khh = "hh%d" % i2
            P.ts("dve", hht[:, :], num_ps[:, 0:256], smt[:, 2:3], None, ALU.mult, None, r=[knum, ksm], w=[khh])
            P.act(junk[:, :], hht[:, :], AF.Square, r=[khh], w=["junk", ksm], accum_out=smt[:, 3:4])
            P.ts("dve", smt[:, 4:5], smt[:, 3:4], 1.0 / 256.0, float(LN_EPS), ALU.mult, ALU.add, r=[ksm], w=[ksm])
            P.act(smt[:, 4:5], smt[:, 4:5], AF.Sqrt, r=[ksm], w=[ksm])
            P.S.add("dve", lambda e, a=smt: e.reciprocal(a[:, 5:6], a[:, 4:5]), r=[ksm], w=[ksm])
            oa = oat[i2]
            koa = "oa%d" % i2
            P.dma("sp", oa[:, :], OAv[:, tq, hs], r=[("dram", "OA")], w=[koa])
            P.act(oa[:, :], oa[:, :], AF.Sigmoid, r=[koa], w=[koa])
            P.tt("pool", oa[:, :], oa[:, :], gbc[:, hs], ALU.mult, r=[koa, "gbc"], w=[koa])
            ya = yab[i2]
            kya = "ya%d" % i2
            P.stt(ya[:, :], hht[:, :], smt[:, 5:6], oa[:, :], ALU.mult, ALU.mult, r=[khh, ksm, koa], w=[kya])
            tps = PS[1][:, 512:768].bitcast(BF16)
            yTt = yT[i2]
            kyT = "yT%d" % i2
            for j in range(2):
                P.S.add("pe", lambda e, o=tps[:, j * 128:(j + 1) * 128], i=ya[:, j * 128:(j + 1) * 128]:
                        e.transpose(o, i, identb[:, :]), r=[kya, "identb"], w=["ps1b"])
            P.copy("act", yTt[:, :, :].rearrange("p a b -> p (a b)"), tps[:, 0:256], r=["ps1b"], w=[kyT])
            P.dma("sp", YTv[:, 2 * h:2 * h + 2, ts_], yTt[:, :, :], r=[kyT], w=[("dram", YT0.name)])
        for sb in range(16):
            P.ts("dve", KW[:, sb, :], KAt[:, sb, :], cols[:, 1, sb, h:h + 1], None, ALU.mult, None,
                 r=["KAt", "cols"], w=["KW"])
        for dcn in range(2):
            c_ps = PS[2 + dcn][:, 0:257]
            kc = "ps%d" % (2 + dcn)
            for sb in range(16):
                P.mm(c_ps, KW[:, sb, dcn * 128:(dcn + 1) * 128], VA1[:, sb, :], sb == 0, sb == 15,
                     r=["KW", "VA1"], w=[kc])
            ct = cst[dcn]
            kct = "cst%d" % dcn
            P.copy("act", ct[:, :], c_ps, r=[kc], w=[kct])
            P.dma("sp", C["mC_p"][h, dcn * 128:(dcn + 1) * 128, :], ct[:, 0:256], r=[kct], w=[("dram", "mC_p")])
            P.dma("sp", C["mn_p"][h:h + 1, dcn * 128:(dcn + 1) * 128].rearrange("a d -> d a"), ct[:, 256:257],
                  r=[kct], w=[("dram", "mn_p")])
        P.dma("sp", Cin[:, 0:2, :], C["st_C"][h].rearrange("(c p) v -> p c v", p=128), r=[], w=["Cin"])
        P.dma("sp", nin[:, 0:2], C["st_n"][h:h + 1, :].rearrange("a (c p) -> p (a c)", p=128), r=[], w=["nin"])
        P.copy("act", Cinb[:, 0:2, :], Cin[:, 0:2, :], r=["Cin"], w=["Cinb"])
        P.copy("dve", ninb[:, 0:2], nin[:, 0:2], r=["nin"], w=["ninb"])
        P.copy("dve", kcolf[:, 0:2], KT[:, :, S_], r=["KT"], w=["kcolf"])
        rps = PS[1][0:1, 0:512]
        for dcn in range(2):
            P.mm(rps[:, 0:256], QT[:, dcn, S_:NT], Cinb[:, dcn, :], dcn == 0, dcn == 1, r=["QT", "Cinb"], w=["ps1a"])
        for dcn in range(2):
            P.mm(rps[:, 256:257], QT[:, dcn, S_:NT], KT[:, dcn, S_:NT], dcn == 0, dcn == 1, r=["QT", "KT"], w=["ps1a"])
        for dcn in range(2):
            P.mm(rps[:, 257:258], QT[:, dcn, S_:NT], ninb[:, dcn:dcn + 1], dcn == 0, dcn == 1, r=["QT", "ninb"], w=["ps1a"])
        dwh, gwh, emh = srow[0:1, 20 + h:21 + h], srow[0:1, 24 + h:25 + h], srow[0:1, 28 + h:29 + h]
        sc = srow[0:1, 32:48]
        P.ts("dve", sc[:, 0:1], rps[:, 256:257], dwh, None, ALU.mult, None, r=["ps1a", "srow"], w=["sc"])
        P.ts("dve", sc[:, 1:2], rps[:, 257:258], gwh, None, ALU.mult, None, r=["ps1a", "srow"], w=["sc"])
        P.tt("dve", sc[:, 2:3], sc[:, 0:1], sc[:, 1:2], ALU.add, r=["sc"], w=["sc"])
        P.act(sc[:, 3:4], sc[:, 2:3], AF.Abs, r=["sc"], w=["sc"])
        P.tt("dve", sc[:, 4:5], sc[:, 3:4], emh, ALU.max, r=["sc", "srow"], w=["sc"])
        P.S.add("dve", lambda e, a=sc: e.reciprocal(a[:, 5:6], a[:, 4:5]), r=["sc"], w=["sc"])
        P.ts("dve", hrow[0:1, hs], rps[:, 0:256], gwh, None, ALU.mult, None, r=["ps1a", "srow"], w=["hrow"])
        P.stt(hrow[0:1, hs], VA1[0:1, 16, 0:256], sc[:, 0:1], hrow[0:1, hs], ALU.mult, ALU.add,
              r=["VA1", "sc", "hrow"], w=["hrow"])
        P.ts("dve", hrow[0:1, hs], hrow[0:1, hs], sc[:, 5:6], None, ALU.mult, None, r=["hrow", "sc"], w=["hrow"])
        P.act(junk[0:1, :], hrow[0:1, hs], AF.Square, r=["hrow"], w=["junk", "sc"], accum_out=sc[:, 6:7])
        P.ts("dve", sc[:, 7:8], sc[:, 6:7], 1.0 / 256.0, float(LN_EPS), ALU.mult, ALU.add, r=["sc"], w=["sc"])
        P.act(sc[:, 7:8], sc[:, 7:8], AF.Sqrt, r=["sc"], w=["sc"])
        P.S.add("dve", lambda e, a=sc: e.reciprocal(a[:, 8:9], a[:, 7:8]), r=["sc"], w=["sc"])
        P.stt(yrow[0:1, hs], hrow[0:1, hs], sc[:, 8:9], oarow[0:1, hs], ALU.mult, ALU.mult,
              r=["hrow", "sc", "oarow"], w=["yrow"])
        cps = PS[1][:, 512:514]
        for j in range(2):
            P.mm(cps[:, j:j + 1], yrow[0:1, h * 256 + j * 128: h * 256 + (j + 1) * 128], one1b[0:1, 0:1], True, True,
                 r=["yrow", "one1b"], w=["ps1b"])
        P.copy("dve", qcol[:, 0:2], cps, r=["ps1b"], w=["qcol"])
        P.dma("sp", YTv[:, 2 * h:2 * h + 2, S_], qcol[:, 0:2], r=["qcol"], w=[("dram", YT0.name)])
        P.ts("dve", yrow[0:1, 0:256] if False else KW[0:1, 0, :], KAt[0:1, 16, :], dwh, None, ALU.mult, None,
             r=["KAt", "srow", "KW"], w=["KW"])
        for dcn in range(2):
            kv_ps = PS[2 + dcn][:, 0:256]
            kc = "ps%d" % (2 + dcn)
            P.mm(kv_ps, KW[0:1, 0, dcn * 128:(dcn + 1) * 128], VA1[0:1, 16, 0:256], True, True,
                 r=["KW", "VA1"], w=[kc])
            ct = cst[dcn]
            kct = "cst%d" % dcn
            P.stt(ct[:, 0:256], Cin[:, dcn, :], bc8[:, 4 + h:5 + h], kv_ps, ALU.mult, ALU.add,
                  r=["Cin", "bc8", kc], w=[kct])
            P.dma("sp", C["mC_s"][h, dcn * 128:(dcn + 1) * 128, :], ct[:, 0:256], r=[kct], w=[("dram", "mC_s")])
        P.ts("dve", kcolf[:, 0:2], kcolf[:, 0:2], bc8[:, h:h + 1], None, ALU.mult, None, r=["kcolf", "bc8"], w=["kcolf"])
        P.stt(nnew[:, 0:2], nin[:, 0:2], bc8[:, 4 + h:5 + h], kcolf[:, 0:2], ALU.mult, ALU.add,
              r=["nin", "bc8", "kcolf"], w=["nnew"])
        P.dma("sp", C["mn_s"][h:h + 1, :].rearrange("a (c p) -> p (a c)", p=128), nnew[:, 0:2], r=["nnew"],
              w=[("dram", "mn_s")])
    P.mlstm_sample_done = True


def dsa_prompt_phase(P, O, YT0, C):
    P.reset_arena()
    PS = P.PS
    S_ = SEQ
    NSEL = 256
    QI = P.carve(BF16, [4, NT])
    KI2 = P.carve(BF16, [NT])
    QB = P.carve(BF16, [8, NT])
    KB = P.carve(BF16, [2, NT])
    VBf = P.carve(F32, [2, 256])
    VB = P.carve(BF16, [16, 256])
    WIc = P.carve(F32, [16, 8])
    acc = P.carve(F32, [S_])
    rl = [P.carve(F32, [S_]) for _ in range(2)]
    junk = P.carve(BF16, [S_])
    mask = P.carve(BF16, [S_])
    pp = [P.carve(BF16, [S_]) for _ in range(2)]
    pm = [P.carve(BF16, [S_]) for _ in range(2)]
    pT = [P.carve(BF16, [8, 128]) for _ in range(2)]
    ybt = P.carve(BF16, [1024])
    yT = P.carve(BF16, [8, 128])
    cmask = P.carve(F32, [128])
    ident = P.carve(F32, [128])
    identb = P.carve(BF16, [128])
    sm = P.carve(F32, [16])
    hsm = [P.carve(F32, [4]) for _ in range(2)]
    P.dma("sp", ident[:, :], C["ident"], r=[], w=["ident"])
    P.copy("dve", identb[:, :], ident[:, :], r=["ident"], w=["identb"])
    P.dma("sp", cmask[:, :], C["cmask"], r=[], w=["cmask"])
    P.dma("sp", QI[:, :, :], fm(O["QIT"]), r=[("dram", "QIT")], w=["QI"])
    P.dma("sp", KI2[0:64, :], O["KIT"], r=[("dram", "KIT")], w=["KI2"])
    P.dma("sp", KI2[64:128, :], O["KIT"], r=[("dram", "KIT")], w=["KI2"])
    P.dma("sp", QB[:, :, :], fm(O["QBT"]), r=[("dram", "QBT")], w=["QB"])
    P.dma("sp", KB[:, :, :], fm(O["KBT"]), r=[("dram", "KBT")], w=["KB"])
    P.dma("sp", WIc[:, :, :], O["WI"][0:S_, :].rearrange("(b p) h -> p b h", p=128), r=[("dram", "WI")], w=["WIc"])
    vv = O["v_rows_p"].rearrange("(b p) c -> p b c", p=128)
    for b2 in range(0, 16, 2):
        P.dma("sp", VBf[:, :, :], vv[:, b2:b2 + 2, :], r=[("dram", "v_rows_p")], w=["VBf"])
        P.copy("dve", VB[:, b2:b2 + 2, :], VBf[:, :, :], r=["VBf"], w=["VB"])
    YTv = fm(YT0)
    for tq in range(16):
        ts_ = slice(tq * 128, (tq + 1) * 128)
        nk = (tq + 1) * 128
        halves = [(o, min(1024, nk - o)) for o in range(0, nk, 1024)]
        for h in range(8):
            pb = (h % 2) * 64
            r_ = rl[h % 2]
            kr = "rl%d" % (h % 2)
            for (o, n) in halves:
                ps = PS[(h % 2) * 2 + o // 1024]
                kps = "ps%d" % ((h % 2) * 2 + o // 1024)
                for (bo, bn) in blocks(n):
                    P.mm(ps[:, bo:bo + bn], QI[pb:pb + 64, h // 2, ts_], KI2[pb:pb + 64, o + bo:o + bo + bn], True, True,
                         r=["QI", "KI2"], w=[kps])
                P.act(r_[:, o:o + n], ps[:, 0:n], AF.Relu, r=[kps], w=[kr])
            if h == 0:
                P.ts("dve", acc[:, 0:nk], r_[:, 0:nk], WIc[:, tq, 0:1], None, ALU.mult, None, r=[kr, "WIc"], w=["acc"])
            else:
                P.stt(acc[:, 0:nk], r_[:, 0:nk], WIc[:, tq, h:h + 1], acc[:, 0:nk], ALU.mult, ALU.add,
                      r=[kr, "WIc", "acc"], w=["acc"])
        P.S.add("dve", lambda e, a=acc, n_=nk: e.tensor_reduce(sm[:, 0:1], a[:, 0:n_], AX.X, ALU.min), r=["acc"], w=["sm"])
        P.S.add("dve", lambda e, a=acc, n_=nk: e.tensor_reduce(sm[:, 1:2], a[:, 0:n_], AX.X, ALU.max), r=["acc"], w=["sm"])
        P.tt("dve", acc[:, nk - 128:nk], acc[:, nk - 128:nk], cmask[:, :], ALU.add, r=["acc", "cmask"], w=["acc"])
        if tq >= 2:
            P.tt("dve", sm[:, 2:3], sm[:, 1:2], sm[:, 0:1], ALU.subtract, r=["sm"], w=["sm"])
            for it in range(26):
                ci = 0.5 ** (it + 1)
                P.ts("dve", sm[:, 3:4], sm[:, 2:3], float(ci), sm[:, 0:1], ALU.mult, ALU.add, r=["sm"], w=["sm"])
                P.ts("dve", junk[:, 0:nk], acc[:, 0:nk], sm[:, 3:4], 0.0, ALU.is_ge, ALU.add, r=["acc", "sm"],
                     w=["junk", "sm"], accum_out=sm[:, 4:5])
                P.ts("dve", sm[:, 5:6], sm[:, 4:5], NSEL - 0.5, float(ci), ALU.is_ge, ALU.mult, r=["sm"], w=["sm"])
                P.stt(sm[:, 0:1], sm[:, 5:6], sm[:, 2:3], sm[:, 0:1], ALU.mult, ALU.add, r=["sm"], w=["sm"])
        else:
            P.memset("dve", sm[:, 0:1], -1.0e29, w=["sm"])
        P.ts("dve", mask[:, 0:nk], acc[:, 0:nk], sm[:, 0:1], None, ALU.is_ge, None, r=["acc", "sm"], w=["mask"])
        for hq in range(8):
            kv = hq // 4
            hs_ = hsm[hq % 2]
            khs = "hsm%d" % (hq % 2)
            lps = []
            for (o, n) in halves:
                ps = PS[o // 1024]
                kps = "ps%d" % (o // 1024)
                for (bo, bn) in blocks(n):
                    P.mm(ps[:, bo:bo + bn], QB[:, hq, ts_], KB[:, kv, o + bo:o + bo + bn], True, True,
                         r=["QB", "KB"], w=[kps])
                lps.append((ps, kps, o, n))
            for i_, (ps, kps, o, n) in enumerate(lps):
                P.S.add("dve", lambda e, d=hs_[:, i_:i_ + 1], a=ps[:, 0:n]: e.tensor_reduce(d, a, AX.X, ALU.max),
                        r=[kps], w=[khs])
            if len(lps) == 2:
                P.tt("dve", hs_[:, 0:1], hs_[:, 0:1], hs_[:, 1:2], ALU.max, r=[khs], w=[khs])
            P.ts("dve", hs_[:, 2:3], hs_[:, 0:1], -1.0, None, ALU.mult, None, r=[khs], w=[khs])
            p_ = pp[hq % 2]
            kp = "pp%d" % (hq % 2)
            for (ps, kps, o, n) in lps:
                P.act(p_[:, o:o + n], ps[:, 0:n], AF.Exp, r=[kps, khs], w=[kp], bias=hs_[:, 2:3])
            pm_ = pm[hq % 2]
            kpm = "pm%d" % (hq % 2)
            P.stt(pm_[:, 0:nk], p_[:, 0:nk], 1.0, mask[:, 0:nk], ALU.mult, ALU.mult, r=[kp, "mask"], w=[kpm, khs],
                  accum_out=hs_[:, 3:4])
            o_ps = PS[3][:, 0:128]
            nsb = tq + 1
            for g0 in range(0, nsb, 8):
                gn = min(8, nsb - g0)
                gi = (g0 // 8) % 2
                tps = PS[2][:, gi * 512:(gi + 1) * 512].bitcast(BF16)
                ktp = "ps2_%d" % gi
                for j in range(gn):
                    sb = g0 + j
                    P.S.add("pe", lambda e, o_=tps[:, j * 128:(j + 1) * 128], i=pm_[:, sb * 128:(sb + 1) * 128]:
                            e.transpose(o_, i, identb[:, :]), r=[kpm, "identb"], w=[ktp])
                pT_ = pT[gi]
                kpT = "pT%d" % gi
                P.copy("act" if gi == 0 else "dve", pT_[:, 0:gn, :].rearrange("p a b -> p (a b)"), tps[:, 0:gn * 128],
                       r=[ktp], w=[kpT])
                for j in range(gn):
                    sb = g0 + j
                    P.mm(o_ps, pT_[:, j, :], VB[:, sb, kv * 128:(kv + 1) * 128], sb == 0, sb == nsb - 1,
                         r=[kpT, "VB"], w=["ps3a"])
            P.S.add("dve", lambda e, a=hs_: e.reciprocal(a[:, 3:4], a[:, 3:4]), r=[khs], w=[khs])
            P.ts("dve", ybt[:, hq * 128:(hq + 1) * 128], o_ps, hs_[:, 3:4], None, ALU.mult, None, r=["ps3a", khs],
                 w=["ybt"])
        tps = PS[3][:, 512:1024].bitcast(BF16)
        for hq in range(8):
            P.S.add("pe", lambda e, o_=tps[:, hq * 128:(hq + 1) * 128], i=ybt[:, hq * 128:(hq + 1) * 128]:
                    e.transpose(o_, i, identb[:, :]), r=["ybt", "identb"], w=["ps3b"])
        P.copy("act", yT[:, :, :].rearrange("p a b -> p (a b)"), tps[:, 0:1024], r=["ps3b"], w=["yT"])
        P.dma("sp", YTv[:, 8:16, ts_], yT[:, :, :], r=["yT"], w=[("dram", YT0.name)])


def row_from_cols(P, cols_bf, nchunk, identb, ps_row, kps, dst_row, kdst, rkeys):
    for i, cap in enumerate(cols_bf):
        P.mm(ps_row[0:1, i * 128:(i + 1) * 128], cap, identb[:, :], True, True, r=list(rkeys) + ["identb"], w=[kps])
    P.copy("dve", dst_row[0:1, 0:nchunk * 128], ps_row[0:1, 0:nchunk * 128], r=[kps], w=[kdst])


def dsa_sample_phase(P, O, YT0, C):
    P.reset_arena()
    PS = P.PS
    S_ = SEQ
    NSEL = 256
    KIg = P.carve(F32, [128, 64])
    tmp = P.carve(F32, [8192])
    KV = [P.carve(F32, [32, 256]) for _ in range(2)]
    sc = P.carve(F32, [129])
    dot = P.carve(F32, [128])
    lg = P.carve(F32, [8, 129])
    pmat = P.carve(F32, [8, 129])
    mask = P.carve(F32, [129])
    qicol = P.carve(BF16, [4])
    qbcol = P.carve(BF16, [8])
    qirow = P.carve(F32, [512])
    qbrow = P.carve(F32, [1024])
    qibc = P.carve(F32, [512])
    qbbc = P.carve(F32, [1024])
    wrow = P.carve(F32, [8])
    wbc = P.carve(F32, [8])
    kis = P.carve(F32, [64])
    ksf = P.carve(F32, [256])
    vsf = P.carve(F32, [256])
    srow = P.carve(F32, [64])
    ident = P.carve(F32, [128])
    identb = P.carve(BF16, [128])
    ones = P.carve(F32, [128])
    pt_i = P.carve(I32, [1])
    pt_f = P.carve(F32, [1])
    idx = [P.carve(I32, [1]) for _ in range(4)]
    sm = P.carve(F32, [16])
    sm8 = P.carve(F32, [32])
    dg = P.carve(F32, [8])
    yrow = P.carve(F32, [1024])
    yrowb = P.carve(BF16, [1024])
    ycol = P.carve(BF16, [8])
    junk = P.carve(F32, [129])
    P.dma("sp", ident[:, :], C["ident"], r=[], w=["ident"])
    P.copy("dve", identb[:, :], ident[:, :], r=["ident"], w=["identb"])
    P.memset("dve", ones[:, :], 1.0, w=["ones"])
    P.dma("sp", pt_i[:, :], C["page_col"], r=[], w=["pt_i"])
    P.copy("dve", pt_f[:, :], pt_i[:, :], r=["pt_i"], w=["pt_f"])
    for q in range(4):
        P.ts("dve", idx[q][:, :], pt_f[:, :], 4.0, float(q), ALU.mult, ALU.add, r=["pt_f"], w=["idx%d" % q])
    P.S.add("pool", lambda e: e.indirect_dma_start(
        out=KIg[:, :, :].rearrange("p a b -> p (a b)"), out_offset=None, in_=C["cache_ik"],
        in_offset=bass.IndirectOffsetOnAxis(ap=pt_i[:, :], axis=0)), r=["pt_i"], w=["KIg"], dma=True)
    P.dma("sp", qicol[:, :], fm(O["QIT"])[:, :, S_], r=[("dram", "QIT")], w=["qicol"])
    P.dma("sp", qbcol[:, :], fm(O["QBT"])[:, :, S_], r=[("dram", "QBT")], w=["qbcol"])
    rp = PS[0]
    row_from_cols(P, [qicol[:, i:i + 1] for i in range(4)], 4, identb, rp[:, 0:512], "ps0", qirow, "qirow", ["qicol"])
    row_from_cols(P, [qbcol[:, i:i + 1] for i in range(8)], 8, identb, rp[:, 0:1024] if False else PS[1][:, 0:1024],
                  "ps1", qbrow, "qbrow", ["qbcol"])
    P.dma("sp", wrow[0:1, :], O["WI"][S_:NT, :], r=[("dram", "WI")], w=["wrow"])
    P.mm(PS[2][:, 0:512], ones[0:1, :], qirow[0:1, :], True, True, r=["ones", "qirow"], w=["ps2"])
    P.copy("act", qibc[:, :], PS[2][:, 0:512], r=["ps2"], w=["qibc"])
    for i in range(2):
        P.mm(PS[3][:, i * 512:(i + 1) * 512], ones[0:1, :], qbrow[0:1, i * 512:(i + 1) * 512], True, True,
             r=["ones", "qbrow"], w=["ps3"])
    P.copy("act", qbbc[:, :], PS[3][:, 0:1024], r=["ps3"], w=["qbbc"])
    P.mm(PS[2][:, 512:520], ones[0:1, :], wrow[0:1, :], True, True, r=["ones", "wrow"], w=["ps2b"])
    P.copy("dve", wbc[:, :], PS[2][:, 512:520], r=["ps2b"], w=["wbc"])
    qi3 = qibc[:, :].rearrange("p (h d) -> p h d", h=8)
    t3 = tmp[:, :].rearrange("p (a b) -> p a b", a=128)
    for h in range(8):
        P.tt("dve", t3, KIg[:, :, :], qi3[:, h:h + 1, :].broadcast_to([128, 128, 64]), ALU.mult,
             r=["KIg", "qibc"], w=["tmp"])
        P.S.add("dve", lambda e: e.tensor_reduce(dot[:, :], t3, AX.X, ALU.add), r=["tmp"], w=["dot"])
        P.act(dot[:, :], dot[:, :], AF.Relu, r=["dot"], w=["dot"])
        if h == 0:
            P.ts("dve", sc[:, 0:128], dot[:, :], wbc[:, 0:1], None, ALU.mult, None, r=["dot", "wbc"], w=["sc"])
        else:
            P.stt(sc[:, 0:128], dot[:, :], wbc[:, h:h + 1], sc[:, 0:128], ALU.mult, ALU.add, r=["dot", "wbc", "sc"], w=["sc"])
    P.dma("sp", kis[0:1, :], C["ik_rows_s"], r=[("dram", "ik_rows_s")], w=["kis"])
    q3r = qirow[0:1, :].rearrange("p (h d) -> p h d", h=8)
    P.tt("dve", tmp[0:1, 0:512].rearrange("p (h d) -> p h d", h=8), q3r, kis[0:1, :].rearrange("p (a d) -> p a d", a=1).broadcast_to([1, 8, 64]),
         ALU.mult, r=["qirow", "kis", "tmp"], w=["tmp"])
    P.S.add("dve", lambda e: e.tensor_reduce(srow[0:1, 0:8], tmp[0:1, 0:512].rearrange("p (h d) -> p h d", h=8), AX.X, ALU.add),
            r=["tmp"], w=["srow"])
    P.act(srow[0:1, 0:8], srow[0:1, 0:8], AF.Relu, r=["srow"], w=["srow"])
    P.tt("dve", srow[0:1, 0:8], srow[0:1, 0:8], wrow[0:1, :], ALU.mult, r=["srow", "wrow"], w=["srow"])
    P.memset("dve", sc[:, 128:129], -1.0e30, w=["sc"])
    P.S.add("dve", lambda e: e.tensor_reduce(sc[0:1, 128:129], srow[0:1, 0:8], AX.X, ALU.add), r=["srow", "sc"], w=["sc"])
    P.S.add("dve", lambda e: e.tensor_reduce(sm[:, 0:1], sc[:, 0:128], AX.X, ALU.min), r=["sc"], w=["sm"])
    P.S.add("dve", lambda e: e.tensor_reduce(sm[:, 1:2], sc[:, 0:129], AX.X, ALU.max), r=["sc"], w=["sm"])
    P.mm(PS[0][0:1, 0:128], sm[:, 0:1], ident[:, :], True, True, r=["sm", "ident"], w=["ps0"])
    P.mm(PS[0][0:1, 128:256], sm[:, 1:2], ident[:, :], True, True, r=["sm", "ident"], w=["ps0"])
    P.S.add("dve", lambda e: e.tensor_reduce(srow[0:1, 8:9], PS[0][0:1, 0:128], AX.X, ALU.min), r=["ps0"], w=["srow"])
    P.S.add("dve", lambda e: e.tensor_reduce(srow[0:1, 9:10], PS[0][0:1, 128:256], AX.X, ALU.max), r=["ps0"], w=["srow"])
    P.tt("dve", srow[0:1, 9:10], srow[0:1, 9:10], srow[0:1, 8:9], ALU.subtract, r=["srow"], w=["srow"])
    P.mm(PS[0][:, 512:514], ones[0:1, :], srow[0:1, 8:10], True, True, r=["ones", "srow"], w=["ps0b"])
    P.copy("dve", sm[:, 0:1], PS[0][:, 512:513], r=["ps0b"], w=["sm"])
    P.copy("dve", sm[:, 2:3], PS[0][:, 513:514], r=["ps0b"], w=["sm"])
    for it in range(26):
        ci = 0.5 ** (it + 1)
        P.ts("dve", sm[:, 3:4], sm[:, 2:3], float(ci), sm[:, 0:1], ALU.mult, ALU.add, r=["sm"], w=["sm"])
        P.ts("dve", junk[:, 0:129], sc[:, 0:129], sm[:, 3:4], 0.0, ALU.is_ge, ALU.add, r=["sc", "sm"],
             w=["junk", "sm"], accum_out=sm[:, 4:5])
        P.mm(PS[0][:, 520:521], ones[:, :], sm[:, 4:5], True, True, r=["ones", "sm"], w=["ps0b"])
        P.ts("dve", sm[:, 5:6], PS[0][:, 520:521], NSEL - 0.5, float(ci), ALU.is_ge, ALU.mult, r=["ps0b"], w=["sm"])
        P.stt(sm[:, 0:1], sm[:, 5:6], sm[:, 2:3], sm[:, 0:1], ALU.mult, ALU.add, r=["sm"], w=["sm"])
    P.ts("dve", mask[:, 0:129], sc[:, 0:129], sm[:, 0:1], None, ALU.is_ge, None, r=["sc", "sm"], w=["mask"])
    qb3 = qbbc[:, :].rearrange("p (h d) -> p h d", h=8)
    ck = C["cache_k4"]
    cv = C["cache_v4"]
    tk = tmp[:, 0:4096].rearrange("p (a b) -> p a b", a=32)
    for q in range(4):
        kvb = KV[q % 2]
        kk = "KV%d" % (q % 2)
        P.S.add("pool", lambda e, kvb=kvb, q=q: e.indirect_dma_start(
            out=kvb[:, :, :].rearrange("p a b -> p (a b)"), out_offset=None, in_=ck,
            in_offset=bass.IndirectOffsetOnAxis(ap=idx[q][:, :], axis=0)), r=["idx%d" % q], w=[kk], dma=True)
        for hq in range(8):
            kv = hq // 4
            P.tt("dve", tk, kvb[:, :, kv * 128:(kv + 1) * 128], qb3[:, hq:hq + 1, :].broadcast_to([128, 32, 128]),
                 ALU.mult, r=[kk, "qbbc"], w=["tmp"])
            P.S.add("dve", lambda e, hq=hq, q=q: e.tensor_reduce(lg[:, hq, q * 32:(q + 1) * 32], tk, AX.X, ALU.add),
                    r=["tmp"], w=["lg"])
    P.dma("sp", ksf[0:1, :], C["k_rows_s"], r=[("dram", "k_rows_s")], w=["ksf"])
    P.dma("sp", vsf[0:1, :], C["v_rows_s"], r=[("dram", "v_rows_s")], w=["vsf"])
    P.memset("dve", lg[:, :, 128:129], 0.0, w=["lg"])
    for kv in range(2):
        P.tt("dve", tmp[0:1, kv * 512:(kv + 1) * 512].rearrange("p (h d) -> p h d", h=4),
             qbrow[0:1, kv * 512:(kv + 1) * 512].rearrange("p (h d) -> p h d", h=4),
             ksf[0:1, kv * 128:(kv + 1) * 128].rearrange("p (a d) -> p a d", a=1).broadcast_to([1, 4, 128]),
             ALU.mult, r=["qbrow", "ksf", "tmp"], w=["tmp"])
    P.S.add("dve", lambda e: e.tensor_reduce(lg[0:1, :, 128:129].rearrange("p h a -> p (h a)"),
                                             tmp[0:1, 0:1024].rearrange("p (h d) -> p h d", h=8), AX.X, ALU.add),
            r=["tmp", "lg"], w=["lg"])
    for hq in range(8):
        P.S.add("dve", lambda e, hq=hq: e.tensor_reduce(sm8[:, hq:hq + 1], lg[:, hq, :], AX.X, ALU.max), r=["lg"], w=["sm8"])
    P.mm(PS[1][0:8, 0:128], sm8[:, 0:8], ident[:, :], True, True, r=["sm8", "ident"], w=["ps1"])
    P.S.add("dve", lambda e: e.tensor_reduce(sm8[0:8, 8:9], PS[1][0:8, 0:128], AX.X, ALU.max), r=["ps1"], w=["sm8"])
    P.ts("dve", dg[0:8, :], ident[0:8, 0:8], sm8[0:8, 8:9], None, ALU.mult, None, r=["ident", "sm8"], w=["dg"])
    P.mm(PS[1][:, 512:520], ones[0:8, :], dg[0:8, :], True, True, r=["ones", "dg"], w=["ps1b"])
    P.ts("dve", sm8[:, 16:24], PS[1][:, 512:520], -1.0, None, ALU.mult, None, r=["ps1b"], w=["sm8"])
    for hq in range(8):
        P.act(pmat[:, hq, :], lg[:, hq, :], AF.Exp, r=["lg", "sm8"], w=["pmat"], bias=sm8[:, 16 + hq:17 + hq])
        P.stt(pmat[:, hq, :], pmat[:, hq, :], 1.0, mask[:, :], ALU.mult, ALU.mult, r=["pmat", "mask"], w=["pmat", "sm8"],
              accum_out=sm8[:, 24 + hq:25 + hq])
    P.mm(PS[1][:, 520:528], ones[:, :], sm8[:, 24:32], True, True, r=["ones", "sm8"], w=["ps1b"])
    P.copy("dve", srow[0:1, 16:24], PS[1][0:1, 520:528], r=["ps1b"], w=["srow"])
    P.S.add("dve", lambda e: e.reciprocal(srow[0:1, 16:24], srow[0:1, 16:24]), r=["srow"], w=["srow"])
    acc_reg = [PS[hq // 2][0:1, (hq % 2) * 512:(hq % 2) * 512 + 128] for hq in range(8)]
    acc_key = ["psacc%d" % hq for hq in range(8)]
    for q in range(4):
        kvb = KV[q % 2]
        kk = "KV%d" % (q % 2)
        P.S.add("pool", lambda e, kvb=kvb, q=q: e.indirect_dma_start(
            out=kvb[:, :, :].rearrange("p a b -> p (a b)"), out_offset=None, in_=cv,
            in_offset=bass.IndirectOffsetOnAxis(ap=idx[q][:, :], axis=0)), r=["idx%d" % q], w=[kk], dma=True)
        for hq in range(8):
            kv = hq // 4
            dst = acc_reg[hq]
            for o in range(32):
                P.mm(dst, pmat[:, hq, q * 32 + o:q * 32 + o + 1], kvb[:, o, kv * 128:(kv + 1) * 128],
                     q == 0 and o == 0, q == 3 and o == 31, r=["pmat", kk, "ps0", "ps1", "ps1b", "ps0b", "ps2", "ps3", "ps2b"],
                     w=[acc_key[hq]])
    for hq in range(8):
        kv = hq // 4
        src = acc_reg[hq]
        P.stt(yrow[0:1, hq * 128:(hq + 1) * 128], vsf[0:1, kv * 128:(kv + 1) * 128], pmat[0:1, hq, 128:129], src,
              ALU.mult, ALU.add, r=["vsf", "pmat", acc_key[hq]], w=["yrow"])
        P.ts("dve", yrowb[0:1, hq * 128:(hq + 1) * 128], yrow[0:1, hq * 128:(hq + 1) * 128], srow[0:1, 16 + hq:17 + hq],
             None, ALU.mult, None, r=["yrow", "srow"], w=["yrowb"])
    one1b = identb[0:1, 0:1]
    for hq in range(8):
        P.mm(PS[0][:, 600 + hq:601 + hq], yrowb[0:1, hq * 128:(hq + 1) * 128], one1b, True, True,
             r=["yrowb", "identb"], w=["ps0c"] + acc_key)
    P.copy("dve", ycol[:, :], PS[0][:, 600:608], r=["ps0c"], w=["ycol"])
    P.dma("sp", fm(YT0)[:, 8:16, S_], ycol[:, :], r=["ycol"], w=[("dram", YT0.name)])


def mixers_even(P, O, YT0):
    mlstm_phase(P, O, YT0, P.C)
    dsa_prompt_phase(P, O, YT0, P.C)
    dsa_sample_phase(P, O, YT0, P.C)


def hgrn_phase(P, O, YT1, C):
    P.reset_arena()
    PS = P.PS
    ident = P.carve(F32, [128])
    identb = P.carve(BF16, [128])
    maskT = P.carve(F32, [128])
    mask8 = P.carve(F32, [8, 128])
    lbb = P.carve(F32, [2, 1024])
    lbc = P.carve(F32, [2, 8])
    omlc = P.carve(F32, [8])
    gbc = P.carve(F32, [1024])
    St = P.carve(F32, [8, 128])
    Stmp = P.carve(F32, [8, 128])
    Sb = P.carve(BF16, [8, 128])
    junk = P.carve(F32, [1024])
    fct = [P.carve(F32, [1024]) for _ in range(2)]
    eb = [P.carve(F32, [8, 128]) for _ in range(2)]
    enb = [P.carve(F32, [8, 128]) for _ in range(2)]
    qct = [P.carve(F32, [8, 128]) for _ in range(2)]
    fcT = [P.carve(F32, [8, 128]) for _ in range(2)]
    gdt = [P.carve(F32, [1024]) for _ in range(2)]
    sm = [P.carve(F32, [32]) for _ in range(2)]
    QT = [P.carve(BF16, [8, 128]) for _ in range(2)]
    KT = [P.carve(BF16, [8, 128]) for _ in range(2)]
    Ktm = [P.carve(BF16, [8, 128]) for _ in range(2)]
    vt = [P.carve(BF16, [1024]) for _ in range(2)]
    AT = [P.carve(BF16, [8, 128]) for _ in range(2)]
    yab = [P.carve(BF16, [8, 128]) for _ in range(2)]
    yT = [P.carve(BF16, [8, 128]) for _ in range(2)]

    def flat(t):
        return t[:, :, :].rearrange("p a b -> p (a b)")

    P.dma("sp", ident[:, :], C["ident"], r=[], w=["ident"])
    P.copy("dve", identb[:, :], ident[:, :], r=["ident"], w=["identb"])
    P.dma("sp", maskT[:, :], C["maskT"], r=[], w=["maskT"])
    for h in range(8):
        P.copy("dve" if h % 2 == 0 else "pool", mask8[:, h, :], maskT[:, :], r=["maskT"], w=["mask8"])
    P.dma("sp", gbc[:, :], C["g_hgrn_bc"], r=[], w=["gbc"])
    P.dma("sp", lbb[:, :, :], C["lb_bc"], r=[], w=["lbb"])
    P.tt("dve", lbb[:, 0, :], lbb[:, 1, :], lbb[:, 0, :], ALU.subtract, r=["lbb"], w=["lbb"])
    P.act(lbb[:, 1, :], lbb[:, 0, :], AF.Sigmoid, r=["lbb"], w=["lbb"], scale=-1.0)
    P.act(lbb[:, 0, :], lbb[:, 0, :], AF.Sigmoid, r=["lbb"], w=["lbb"])
    P.dma("sp", lbc[:, :, :], C["lb_col"], r=[], w=["lbc"])
    P.tt("dve", lbc[:, 0, :], lbc[:, 1, :], lbc[:, 0, :], ALU.subtract, r=["lbc"], w=["lbc"])
    P.act(omlc[:, 0:8], lbc[:, 0, :], AF.Sigmoid, r=["lbc"], w=["omlc"], scale=-1.0)
    P.memset("dve", flat(St), 0.0, w=["St"])
    P.memset("pool", flat(Sb), 0.0, w=["Sb"])
    QCv, FCTv = fm(O["QCT"]), fm(O["FCT"])
    YTv = fm(YT1)
    b3 = PS[0][:, :].rearrange("p (h t) -> p h t", h=8)
    a3 = PS[1][:, :].rearrange("p (h t) -> p h t", h=8)
    s3 = PS[3][:, :].rearrange("p (h t) -> p h t", h=8)
    def names(blk):
        i2 = blk % 2
        sfx = "%d" % i2
        return (i2, sfx, eb[i2], "eb" + sfx, QT[i2], "QT" + sfx, Ktm[i2], "Ktm" + sfx, vt[i2], "vt" + sfx,
                AT[i2], "AT" + sfx)

    def head(blk):
        c0, tw = TOKBLKS[blk]
        i2, sfx, e_, ke, QT_, kQT, Ktm_, kKtm, v_, kv, AT_, kAT = names(blk)
        f_ = fct[i2]
        kf = "fct" + sfx
        P.dma("sp", f_[0:tw, :], O["FC"][c0:c0 + tw, :], r=[("dram", "FC")], w=[kf])
        P.act(f_[0:tw, :], f_[0:tw, :], AF.Sigmoid, r=[kf], w=[kf])
        P.tt("dve", f_[0:tw, :], f_[0:tw, :], lbb[0:tw, 1, :], ALU.mult, r=[kf, "lbb"], w=[kf])
        P.tt("pool", f_[0:tw, :], f_[0:tw, :], lbb[0:tw, 0, :], ALU.add, r=[kf, "lbb"], w=[kf])
        P.act(f_[0:tw, :], f_[0:tw, :], AF.Ln, r=[kf], w=[kf])
        for h in range(8):
            P.mm(PS[0][:, h * 128:h * 128 + tw], f_[0:tw, h * 128:(h + 1) * 128], maskT[0:tw, 0:tw], True, True,
                 r=[kf, "maskT"], w=["ps0"])
        en_ = enb[i2]
        ken = "enb" + sfx
        P.act(e_[:, :, 0:tw], b3[:, :, 0:tw], AF.Exp, r=["ps0"], w=[ke])
        P.act(en_[:, :, 0:tw], b3[:, :, 0:tw], AF.Exp, r=["ps0"], w=[ken], scale=-1.0)
        q_ = qct[i2]
        kq = "qct" + sfx
        P.dma("pool", q_[:, :, 0:tw], QCv[:, 0:8, c0:c0 + tw], r=[("dram", "QCT")], w=[kq])
        P.act(q_[:, :, 0:tw], q_[:, :, 0:tw], AF.Silu, r=[kq], w=[kq])
        P.tt("dve", QT_[:, :, 0:tw], q_[:, :, 0:tw], e_[:, :, 0:tw], ALU.mult, r=[kq, ke], w=[kQT])
        fT = fcT[i2]
        kfT = "fcT" + sfx
        P.dma("pool", fT[:, :, 0:tw], FCTv[:, 0:8, c0:c0 + tw], r=[("dram", "FCT")], w=[kfT])
        P.act(fT[:, :, 0:tw], fT[:, :, 0:tw], AF.Sigmoid, r=[kfT], w=[kfT], scale=-1.0)
        KT_ = KT[i2]
        kKT = "KT" + sfx
        for h in range(8):
            P.stt(KT_[:, h, 0:tw], fT[:, h, 0:tw], omlc[:, h:h + 1], en_[:, h, 0:tw], ALU.mult, ALU.mult,
                  r=[kfT, "omlc", ken], w=[kKT])
        for h in range(8):
            P.mm(PS[0][0:tw, h * 128:(h + 1) * 128], KT_[:, h, 0:tw], identb[:, :], True, True,
                 r=[kKT, "identb"], w=["ps0"])
        P.copy("act", flat(Ktm_)[0:tw, :], PS[0][0:tw, :], r=["ps0"], w=[kKtm])
        P.dma("sp", v_[0:tw, :], O["IC"][c0:c0 + tw, :], r=[("dram", "IC")], w=[kv])
        for h in range(8):
            P.mm(PS[1][0:tw, h * 128:h * 128 + tw], KT_[:, h, 0:tw], QT_[:, h, 0:tw], True, True,
                 r=[kKT, kQT], w=["ps1"])
        P.tt("dve", AT_[0:tw, :, 0:tw], a3[0:tw, :, 0:tw], mask8[0:tw, :, 0:tw], ALU.mult, r=["ps1", "mask8"], w=[kAT])

    def tail(blk):
        c0, tw = TOKBLKS[blk]
        i2, sfx, e_, ke, QT_, kQT, Ktm_, kKtm, v_, kv, AT_, kAT = names(blk)
        if blk == 16:
            P.dma("sp", St[:, :, :], C["st_hg"].rearrange("h d v -> d h v"), r=[], w=["St"])
            P.copy("act", flat(Sb), flat(St), r=["St"], w=["Sb"])
        for h in range(8):
            hs = slice(h * 128, (h + 1) * 128)
            P.mm(PS[2][0:tw, hs], AT_[0:tw, h, 0:tw], v_[0:tw, hs], True, False, r=[kAT, kv], w=["ps2"])
            P.mm(PS[2][0:tw, hs], QT_[:, h, 0:tw], Sb[:, h, :], False, True, r=[kQT, "Sb"], w=["ps2"])
        for h in range(8):
            hs = slice(h * 128, (h + 1) * 128)
            P.mm(PS[3][:, hs], Ktm_[0:tw, h, :], v_[0:tw, hs], True, True, r=[kKtm, kv], w=["ps3"])
        P.tt("dve", Stmp[:, :, :], St[:, :, :], s3, ALU.add, r=["St", "ps3"], w=["Stmp"])
        for h in range(8):
            P.ts("dve", St[:, h, :], Stmp[:, h, :], e_[:, h, tw - 1:tw], None, ALU.mult, None,
                 r=["Stmp", ke], w=["St"])
        P.copy("act", flat(Sb), flat(St), r=["St"], w=["Sb"])
        sm_ = sm[i2]
        ksm = "sm" + sfx
        P.act(junk[0:tw, :], PS[2][0:tw, :], AF.Square, r=["ps2"], w=["junk"])
        P.S.add("dve", lambda e, a=sm_, n=tw: e.tensor_reduce(
            a[0:n, 0:8], junk[0:n, :].rearrange("p (h v) -> p h v", h=8), AX.X, ALU.add), r=["junk"], w=[ksm])
        P.ts("dve", sm_[0:tw, 8:16], sm_[0:tw, 0:8], 1.0 / 128.0, float(LN_EPS), ALU.mult, ALU.add, r=[ksm], w=[ksm])
        P.act(sm_[0:tw, 8:16], sm_[0:tw, 8:16], AF.Sqrt, r=[ksm], w=[ksm])
        P.S.add("dve", lambda e, a=sm_, n=tw: e.reciprocal(a[0:n, 16:24], a[0:n, 8:16]), r=[ksm], w=[ksm])
        g_ = gdt[i2]
        kg = "gdt" + sfx
        P.dma("sp", g_[0:tw, :], O["GC"][c0:c0 + tw, :], r=[("dram", "GC")], w=[kg])
        P.act(g_[0:tw, :], g_[0:tw, :], AF.Silu, r=[kg], w=[kg])
        P.tt("pool", g_[0:tw, :], g_[0:tw, :], gbc[0:tw, :], ALU.mult, r=[kg, "gbc"], w=[kg])
        ya_ = yab[i2]
        kya = "ya" + sfx
        for h in range(8):
            hs = slice(h * 128, (h + 1) * 128)
            P.stt(ya_[0:tw, h, :], PS[2][0:tw, hs], sm_[0:tw, 16 + h:17 + h], g_[0:tw, hs], ALU.mult, ALU.mult,
                  r=["ps2", ksm, kg], w=[kya])
        for h in range(8):
            P.mm(PS[1][:, h * 128:h * 128 + tw], ya_[0:tw, h, :], identb[0:tw, 0:tw], True, True,
                 r=[kya, "identb"], w=["ps1"])
        yT_ = yT[i2]
        kyT = "yT" + sfx
        P.copy("act", yT_[:, :, 0:tw], a3[:, :, 0:tw], r=["ps1"], w=[kyT])
        P.dma("sp", YTv[:, 0:8, c0:c0 + tw], yT_[:, :, 0:tw], r=[kyT], w=[("dram", YT1.name)])
        if blk == 15:
            P.dma("sp", C["hg_p"].rearrange("h d v -> d h v"), St[:, :, :], r=["St"], w=[("dram", "hg_p")])
        if blk == 16:
            P.dma("sp", C["hg_s"].rearrange("h d v -> d h v"), St[:, :, :], r=["St"], w=[("dram", "hg_s")])

    head(0)
    for blk in range(len(TOKBLKS)):
        if blk + 1 < len(TOKBLKS):
            head(blk + 1)
        tail(blk)


RET_GAMMA = [1.0 - 2.0 ** (-5.0 - h) for h in range(4)]


def ret_phase(P, O, YT1, C):
    P.reset_arena()
    PS = P.PS
    S_ = SEQ
    cosT = P.carve(F32, [NT])
    sinT = P.carve(F32, [NT])
    cosK = P.carve(F32, [17, 128])
    sinK = P.carve(F32, [17, 128])
    x1 = P.carve(F32, [NT])
    x2 = P.carve(F32, [NT])
    ta = P.carve(F32, [NT])
    tb = P.carve(F32, [NT])
    QR = P.carve(BF16, [2, NT])
    KR = P.carve(BF16, [2, NT])
    QRl = P.carve(BF16, [2, NT])
    KRl = P.carve(BF16, [2, NT])
    KDt = P.carve(F32, [17, 256])
    KRt = P.carve(F32, [17, 256])
    tk = P.carve(F32, [17, 128])
    KW = P.carve(BF16, [17, 256])
    VDt = P.carve(BF16, [17, 256])
    RD = P.carve(F32, [2, 128])
    tails = P.carve(F32, [4, 17])
    gbc = P.carve(F32, [1024])
    ident = P.carve(F32, [128])
    identb = P.carve(BF16, [128])
    gdt = [P.carve(F32, [256]) for _ in range(2)]
    pt = [P.carve(BF16, [128]) for _ in range(3)]
    junk = P.carve(F32, [256])
    sm = [P.carve(F32, [8]) for _ in range(2)]
    yab = [P.carve(BF16, [256]) for _ in range(2)]
    yT = [P.carve(BF16, [2, 128]) for _ in range(2)]
    cst = [P.carve(F32, [256]) for _ in range(2)]
    Sin = P.carve(F32, [2, 256])
    Sinb = P.carve(BF16, [2, 256])
    srow = P.carve(F32, [16])
    hrow = P.carve(F32, [256])
    gdrow = P.carve(F32, [1024])
    yrow = P.carve(BF16, [256])
    ycol = P.carve(BF16, [2])
    P.dma("sp", ident[:, :], C["ident"], r=[], w=["ident"])
    P.copy("dve", identb[:, :], ident[:, :], r=["ident"], w=["identb"])
    P.dma("sp", cosT[:, :], C["cosT"], r=[], w=["cosT"])
    P.dma("sp", sinT[:, :], C["sinT"], r=[], w=["sinT"])
    P.dma("sp", cosK[:, 0:16, :], C["cosK"][0:S_, :].rearrange("(b p) j -> p b j", p=128), r=[], w=["cosK"])
    P.dma("sp", cosK[0:1, 16, :], C["cosK"][S_:NT, :], r=[], w=["cosK"])
    P.dma("sp", sinK[:, 0:16, :], C["sinK"][0:S_, :].rearrange("(b p) j -> p b j", p=128), r=[], w=["sinK"])
    P.dma("sp", sinK[0:1, 16, :], C["sinK"][S_:NT, :], r=[], w=["sinK"])
    P.dma("sp", tails[:, :, :], C["ret_tails"], r=[], w=["tails"])
    P.dma("sp", gbc[:, :], C["g_ret_bc"], r=[], w=["gbc"])
    P.dma("sp", gdrow[0:1, :], O["GD"][S_:NT, :], r=[("dram", "GD")], w=["gdrow"])
    P.act(gdrow[0:1, :], gdrow[0:1, :], AF.Silu, r=["gdrow"], w=["gdrow"])
    P.tt("dve", gdrow[0:1, :], gdrow[0:1, :], gbc[0:1, :], ALU.mult, r=["gdrow", "gbc"], w=["gdrow"])
    QDv, KDv = fm(O["QDT"]), fm(O["KDT"])
    KDtv = O["KD"][0:S_, :].rearrange("(b p) c -> p b c", p=128)
    VDtv = O["VD"][0:S_, :].rearrange("(b p) c -> p b c", p=128)
    GDv = O["GD"][0:S_, :].rearrange("(b p) c -> p b c", p=128)
    YTv = fm(YT1)
    it = 0
    for h in range(4):
        hs = slice(h * 256, (h + 1) * 256)
        gam = RET_GAMMA[h]
        P.dma("sp", RD[:, :, :], C["ret_D"][h].rearrange("a s t -> s a t"), r=[], w=["RD"])
        for (srcv, dst, dstl, kd, kdl, nm) in ((QDv, QR, QRl, "QR", "QRl", "QDT"), (KDv, KR, KRl, "KR", "KRl", "KDT")):
            P.dma("sp", x1[:, :], srcv[:, 2 * h, :], r=[("dram", nm)], w=["x1"])
            P.dma("sp", x2[:, :], srcv[:, 2 * h + 1, :], r=[("dram", nm)], w=["x2"])
            for (half, sa, sb_, op) in ((0, cosT, sinT, ALU.subtract), (1, sinT, cosT, ALU.add)):
                ka, kb = ("cosT", "sinT") if half == 0 else ("sinT", "cosT")
                P.tt("dve", ta[:, :], x1[:, :], sa[:, :], ALU.mult, r=["x1", ka], w=["ta"])
                P.tt("pool", tb[:, :], x2[:, :], sb_[:, :], ALU.mult, r=["x2", kb], w=["tb"])
                P.tt("dve", ta[:, :], ta[:, :], tb[:, :], op, r=["ta", "tb"], w=["ta"])
                P.copy("act", dst[:, half, :], ta[:, :], r=["ta"], w=[kd])
                P.copy("act", tb[:, :], dst[:, half, :], r=[kd], w=["tb"])
                P.tt("dve", dstl[:, half, :], ta[:, :], tb[:, :], ALU.subtract, r=["ta", "tb"], w=[kdl])
        P.dma("sp", KDt[:, 0:16, :], KDtv[:, :, hs], r=[("dram", "KD")], w=["KDt"])
        P.dma("sp", KDt[0:1, 16, :], O["KD"][S_:NT, hs], r=[("dram", "KD")], w=["KDt"])
        P.dma("sp", VDt[:, 0:16, :], VDtv[:, :, hs], r=[("dram", "VD")], w=["VDt"])
        P.dma("sp", VDt[0:1, 16, :], O["VD"][S_:NT, hs], r=[("dram", "VD")], w=["VDt"])
        for (np_, bsl) in ((128, slice(0, 16)), (1, slice(16, 17))):
            k1, k2 = KDt[0:np_, bsl, 0:128], KDt[0:np_, bsl, 128:256]
            cK, sK = cosK[0:np_, bsl, :], sinK[0:np_, bsl, :]
            P.tt("dve", KRt[0:np_, bsl, 0:128], k1, cK, ALU.mult, r=["KDt", "cosK"], w=["KRt"])
            P.tt("pool", tk[0:np_, bsl, :], k2, sK, ALU.mult, r=["KDt", "sinK"], w=["tk"])
            P.tt("dve", KRt[0:np_, bsl, 0:128], KRt[0:np_, bsl, 0:128], tk[0:np_, bsl, :], ALU.subtract, r=["KRt", "tk"], w=["KRt"])
            P.tt("dve", KRt[0:np_, bsl, 128:256], k1, sK, ALU.mult, r=["KDt", "sinK"], w=["KRt"])
            P.tt("pool", tk[0:np_, bsl, :], k2, cK, ALU.mult, r=["KDt", "cosK", "KRt"], w=["tk"])
            P.tt("dve", KRt[0:np_, bsl, 128:256], KRt[0:np_, bsl, 128:256], tk[0:np_, bsl, :], ALU.add, r=["KRt", "tk"], w=["KRt"])
        for sb in range(17):
            np_ = 128 if sb < 16 else 1
            P.ts("dve", KW[0:np_, sb, :], KRt[0:np_, sb, :], tails[0:np_, h, sb:sb + 1], None, ALU.mult, None,
                 r=["KRt", "tails"], w=["KW"])
        def epilogue(tq, ts_, o_ps, ko):
            i2 = tq % 2
            smt = sm[i2]
            ksm = "sm%d" % i2
            P.act(junk[:, :], o_ps, AF.Square, r=[ko], w=["junk", ksm], accum_out=smt[:, 3:4])
            P.ts("dve", smt[:, 4:5], smt[:, 3:4], 1.0 / 256.0, float(LN_EPS), ALU.mult, ALU.add, r=[ksm], w=[ksm])
            P.act(smt[:, 4:5], smt[:, 4:5], AF.Sqrt, r=[ksm], w=[ksm])
            P.S.add("dve", lambda e, a=smt: e.reciprocal(a[:, 5:6], a[:, 4:5]), r=[ksm], w=[ksm])
            gd = gdt[i2]
            kgd = "gd%d" % i2
            P.dma("sp", gd[:, :], GDv[:, tq, hs], r=[("dram", "GD")], w=[kgd])
            P.act(gd[:, :], gd[:, :], AF.Silu, r=[kgd], w=[kgd])
            P.tt("pool", gd[:, :], gd[:, :], gbc[:, hs], ALU.mult, r=[kgd, "gbc"], w=[kgd])
            ya = yab[i2]
            kya = "ya%d" % i2
            P.stt(ya[:, :], o_ps, smt[:, 5:6], gd[:, :], ALU.mult, ALU.mult, r=[ko, ksm, kgd], w=[kya])
            tps = PS[1][:, 512:768].bitcast(BF16)
            yTt = yT[i2]
            kyT = "yT%d" % i2
            for j in range(2):
                P.S.add("pe", lambda e, o=tps[:, j * 128:(j + 1) * 128], i=ya[:, j * 128:(j + 1) * 128]:
                        e.transpose(o, i, identb[:, :]), r=[kya, "identb"], w=["ps1b"])
            P.copy("act", yTt[:, :, :].rearrange("p a b -> p (a b)"), tps[:, 0:256], r=["ps1b"], w=[kyT])
            P.dma("sp", YTv[:, 8 + 2 * h:8 + 2 * h + 2, ts_], yTt[:, :, :], r=[kyT], w=[("dram", YT1.name)])

        epi_pending = []
        for tq in range(16):
            ts_ = slice(tq * 128, (tq + 1) * 128)
            o_ps = PS[2 + (tq % 2)][:, 0:256]
            ko = "ps%d" % (2 + tq % 2)
            for sb in range(tq + 1):
                if epi_pending and sb == min(2, tq):
                    epi_pending.pop(0)()
                ss = slice(sb * 128, (sb + 1) * 128)
                sT = PS[0][:, (it % 2) * 512:(it % 2) * 512 + 128]
                ksT = "ps0_%d" % (it % 2)
                diag = (sb == tq)
                P.mm(sT, KR[:, 0, ss], QR[:, 0, ts_], True, False, r=["KR", "QR"], w=[ksT])
                P.mm(sT, KR[:, 1, ss], QR[:, 1, ts_], False, not diag, r=["KR", "QR"], w=[ksT])
                if diag:
                    P.mm(sT, KRl[:, 0, ss], QR[:, 0, ts_], False, False, r=["KRl", "QR"], w=[ksT])
                    P.mm(sT, KRl[:, 1, ss], QR[:, 1, ts_], False, False, r=["KRl", "QR"], w=[ksT])
                    P.mm(sT, KR[:, 0, ss], QRl[:, 0, ts_], False, False, r=["KR", "QRl"], w=[ksT])
                    P.mm(sT, KR[:, 1, ss], QRl[:, 1, ts_], False, True, r=["KR", "QRl"], w=[ksT])
                p_ = pt[it % 3]
                kp = "pt%d" % (it % 3)
                if sb == tq:
                    P.tt("dve", p_[:, :], sT, RD[:, 1, :], ALU.mult, r=[ksT, "RD"], w=[kp])
                else:
                    P.stt(p_[:, :], sT, float(gam ** (128 * (tq - sb))), RD[:, 0, :], ALU.mult, ALU.mult,
                          r=[ksT, "RD"], w=[kp])
                P.mm(o_ps, p_[:, :], VDt[:, sb, :], sb == 0, sb == tq, r=[kp, "VDt"], w=[ko])
                it += 1
            epi_pending.append(lambda tq=tq, ts_=ts_, o_ps=o_ps, ko=ko: epilogue(tq, ts_, o_ps, ko))
        while epi_pending:
            epi_pending.pop(0)()
        for dcn in range(2):
            c_ps = PS[2 + dcn][:, 0:256]
            kc = "ps%d" % (2 + dcn)
            for sb in range(16):
                P.mm(c_ps, KW[:, sb, dcn * 128:(dcn + 1) * 128], VDt[:, sb, :], sb == 0, sb == 15,
                     r=["KW", "VDt"], w=[kc])
            ct = cst[dcn]
            kct = "cst%d" % dcn
            P.copy("act", ct[:, :], c_ps, r=[kc], w=[kct])
            P.dma("sp", C["rt_p"][h, dcn * 128:(dcn + 1) * 128, :], ct[:, :], r=[kct], w=[("dram", "rt_p")])
        P.dma("sp", Sin[:, 0:2, :], C["st_ret"][h].rearrange("(c p) v -> p c v", p=128), r=[], w=["Sin"])
        P.copy("act", Sinb[:, 0:2, :], Sin[:, 0:2, :], r=["Sin"], w=["Sinb"])
        rps = PS[1][0:1, 0:512]
        for dcn in range(2):
            P.mm(rps[:, 0:256], QR[:, dcn, S_:NT], Sinb[:, dcn, :], dcn == 0, dcn == 1, r=["QR", "Sinb"], w=["ps1a"])
        for dcn in range(2):
            P.mm(rps[:, 256:257], QR[:, dcn, S_:NT], KR[:, dcn, S_:NT], dcn == 0, dcn == 1, r=["QR", "KR"], w=["ps1a"])
        P.copy("dve", srow[0:1, 0:1], rps[:, 256:257], r=["ps1a"], w=["srow"])
        P.ts("dve", hrow[0:1, :], rps[:, 0:256], float(gam), None, ALU.mult, None, r=["ps1a"], w=["hrow"])
        P.stt(hrow[0:1, :], VDt[0:1, 16, :], srow[0:1, 0:1], hrow[0:1, :], ALU.mult, ALU.add,
              r=["VDt", "srow", "hrow"], w=["hrow"])
        P.act(junk[0:1, :], hrow[0:1, :], AF.Square, r=["hrow"], w=["junk", "srow"], accum_out=srow[0:1, 1:2])
        P.ts("dve", srow[0:1, 2:3], srow[0:1, 1:2], 1.0 / 256.0, float(LN_EPS), ALU.mult, ALU.add, r=["srow"], w=["srow"])
        P.act(srow[0:1, 2:3], srow[0:1, 2:3], AF.Sqrt, r=["srow"], w=["srow"])
        P.S.add("dve", lambda e: e.reciprocal(srow[0:1, 3:4], srow[0:1, 2:3]), r=["srow"], w=["srow"])
        P.stt(yrow[0:1, :], hrow[0:1, :], srow[0:1, 3:4], gdrow[0:1, hs], ALU.mult, ALU.mult,
              r=["hrow", "srow", "gdrow"], w=["yrow"])
        cps = PS[1][:, 512:514]
        for j in range(2):
            P.mm(cps[:, j:j + 1], yrow[0:1, j * 128:(j + 1) * 128], identb[0:1, 0:1], True, True,
                 r=["yrow", "identb"], w=["ps1b"])
        P.copy("dve", ycol[:, 0:2], cps, r=["ps1b"], w=["ycol"])
        P.dma("sp", YTv[:, 8 + 2 * h:8 + 2 * h + 2, S_], ycol[:, 0:2], r=["ycol"], w=[("dram", YT1.name)])
        for dcn in range(2):
            kv_ps = PS[2 + dcn][:, 0:256]
            kc = "ps%d" % (2 + dcn)
            P.mm(kv_ps, KW[0:1, 16, dcn * 128:(dcn + 1) * 128], VDt[0:1, 16, :], True, True, r=["KW", "VDt"], w=[kc])
            ct = cst[dcn]
            kct = "cst%d" % dcn
            P.stt(ct[:, :], Sin[:, dcn, :], float(gam), kv_ps, ALU.mult, ALU.add, r=["Sin", kc], w=[kct])
            P.dma("sp", C["rt_s"][h, dcn * 128:(dcn + 1) * 128, :], ct[:, :], r=[kct], w=[("dram", "rt_s")])


def mixers_odd(P, O, YT1):
    hgrn_phase(P, O, YT1, P.C)
    ret_phase(P, O, YT1, P.C)


def finish(P):
    S = P.S
    S.barrier()
    st = ExitStack()
    S.emit(st)
    st.close()
    return P.nc


def make_in_maps(inputs):
    f = np.float32
    A = lambda k: np.ascontiguousarray(np.asarray(inputs[k], f))
    xp, xs = A("x_prompt"), A("x_sample")
    pp, ps = A("p_prompt"), A("p_sample")
    shared = {
        "w_ffn_up": A("w_ffn_up"), "w_ffn_down": A("w_ffn_down"),
        "ln_g": pc_layout(A("ln_g")), "ln_b": pc_layout(A("ln_b")),
        "w_in_even": np.ascontiguousarray(A("w_in_even")[0]),
        "w_in_odd": np.ascontiguousarray(A("w_in_odd")[0]),
        "w_out": A("w_out"), "w_pe_gate": A("w_pe_gate"), "w_pe_proj": A("w_pe_proj"),
        "b_gate": np.ascontiguousarray(A("b_gate_mlstm")[0].reshape(8, 1)),
        "c_ident": np.eye(128, dtype=f),
        "cache_ik": np.ascontiguousarray(A("cache_idx_k")[0].reshape(1280, 8192)),
        "cache_k4": np.ascontiguousarray(A("cache_k")[0].reshape(5120, 8192)),
        "cache_v4": np.ascontiguousarray(A("cache_v")[0].reshape(5120, 8192)),
        "c_maskT": np.triu(np.ones((128, 128), f)),
        "c_cmask": np.where(np.tril(np.ones((128, 128), bool)), 0.0, -1.0e30).astype(f),
        "c_sel4": np.ascontiguousarray(np.broadcast_to(np.eye(4, dtype=f)[:, :, None], (4, 4, 128))),
        "g_mlstm_bc": np.ascontiguousarray(np.broadcast_to(A("g_mlstm")[0][None, :], (128, 1024))),
    }
    pos = np.concatenate([np.arange(SEQ, dtype=np.float32), np.array([16384.0], np.float32)])
    inv = (1.0 / (np.float32(10000.0) ** np.linspace(0.0, 1.0, 128, dtype=np.float32))).astype(np.float32)
    ang = (pos[:, None] * inv[None, :]).astype(np.float32)
    shared["c_cosK"] = np.ascontiguousarray(np.cos(ang).astype(f))
    shared["c_sinK"] = np.ascontiguousarray(np.sin(ang).astype(f))
    shared["c_cosT"] = np.ascontiguousarray(shared["c_cosK"].T)
    shared["c_sinT"] = np.ascontiguousarray(shared["c_sinK"].T)
    lg = np.log1p(-np.exp2(-5.0 - np.arange(4, dtype=np.float64)))
    sl = np.arange(128)
    dexp = (sl[None, :] - sl[:, None]).astype(np.float64)
    D0 = np.exp(lg[:, None, None] * dexp[None])
    Dd = np.where(dexp[None] >= 0, D0, 0.0)
    shared["c_ret_D"] = np.ascontiguousarray(np.stack([D0, Dd], 1).astype(f))
    tl = np.zeros((128, 4, 17), np.float64)
    for hh in range(4):
        for bb in range(16):
            tl[:, hh, bb] = np.exp(lg[hh] * (SEQ - 1 - (bb * 128 + sl))) / 16.0
        tl[:, hh, 16] = 1.0 / 16.0
    shared["c_ret_tails"] = tl.astype(f)
    shared["g_ret_bc"] = np.ascontiguousarray(np.broadcast_to(A("g_ret")[0][None, :], (128, 1024)))
    shared["g_hgrn_bc"] = np.ascontiguousarray(np.broadcast_to(A("g_hgrn")[0][None, :], (128, 1024)))
    shared["lb_bc"] = np.ascontiguousarray(np.broadcast_to(A("hgrn_lb")[None], (128, 2, 1024)))
    shared["lb_col"] = np.ascontiguousarray(A("hgrn_lb").reshape(2, 8, 128).transpose(2, 0, 1))
    maps = []
    for c in range(8):
        b = c % 4
        m = dict(shared)
        m["st_ret"] = np.ascontiguousarray(A("state_ret")[0, c])
        m["st_hg"] = np.ascontiguousarray(A("state_hgrn")[0, c])
        m["xin"] = np.ascontiguousarray(np.concatenate([xp[b].T, xs[c].T], axis=1))
        m["page_col"] = np.ascontiguousarray(np.asarray(inputs["page_table"], np.int32)[c].reshape(128, 1))
        m["st_C"] = np.ascontiguousarray(A("state_mlstm_C")[0, c])
        m["st_n"] = np.ascontiguousarray(A("state_mlstm_n")[0, c])
        m["st_m"] = np.ascontiguousarray(A("state_mlstm_m")[0, c].reshape(1, 4))
        m["pT"] = np.ascontiguousarray(np.stack(
            [np.concatenate([pp[i, b].T, ps[i, c].T], axis=1) for i in range(2)]))
        maps.append(m)
    return maps


_NC_CACHE = {}


def kernel(**inputs):
    if "nc" not in _NC_CACHE:
        P = build()
        _NC_CACHE["nc"] = finish(P)
    nc = _NC_CACHE["nc"]
    maps = make_in_maps(inputs)
    res = run_bass_kernel_spmd(nc, maps, core_ids=list(range(8)))
    R = res.results
    f = np.float32

    def get(c, name, shape):
        if name in R[c]:
            return np.asarray(R[c][name], f).reshape(shape)
        return np.zeros(shape, f)

    y_p = np.stack([np.asarray(R[b]["yT_out"], f)[:, :SEQ].T for b in range(4)])
    y_s = np.stack([np.asarray(R[c]["yT_out"], f)[:, SEQ:].T for c in range(8)])
    P4, S8 = range(4), range(8)
    mC_p = np.stack([get(b, "mC_p", (4, 256, 256)) for b in P4])[None]
    mC_s = np.stack([get(c, "mC_s", (4, 256, 256)) for c in S8])[None]
    mn_p = np.stack([get(b, "mn_p", (4, 256)) for b in P4])[None]
    mn_s = np.stack([get(c, "mn_s", (4, 256)) for c in S8])[None]
    mm_p = np.stack([get(b, "mm_p", (4,)) for b in P4])[None]
    mm_s = np.stack([get(c, "mm_s", (4,)) for c in S8])[None]
    k_p = np.stack([get(b, "k_rows_p", (SEQ, 2, 128)) for b in P4])[None]
    k_s = np.stack([get(c, "k_rows_s", (1, 2, 128)) for c in S8])[None]
    v_p = np.stack([get(b, "v_rows_p", (SEQ, 2, 128)) for b in P4])[None]
    v_s = np.stack([get(c, "v_rows_s", (1, 2, 128)) for c in S8])[None]
    ik_p = np.stack([get(b, "ik_rows_p", (SEQ, 64)) for b in P4])[None]
    ik_s = np.stack([get(c, "ik_rows_s", (1, 64)) for c in S8])[None]
    hg_p = np.stack([get(b, "hg_p", (8, 128, 128)) for b in P4])[None]
    hg_s = np.stack([get(c, "hg_s", (8, 128, 128)) for c in S8])[None]
    rt_p = np.stack([get(b, "rt_p", (4, 256, 256)) for b in P4])[None]
    rt_s = np.stack([get(c, "rt_s", (4, 256, 256)) for c in S8])[None]
    return (y_p, y_s, mC_p, mC_s, mn_p, mn_s, mm_p, mm_s, k_p, k_s, v_p, v_s, ik_p, ik_s,
            hg_p, hg_s, rt_p, rt_s)
```
